# Optimizing a Trainium2 kernel written in Bass

```python
import math
import jax, jax.numpy as jnp
from jax import lax
import numpy as np

D_MODEL = 2048
BATCH = 4
SEQ = 2048
DEPTH = 2
DEC_BATCH = 128
DEC_SEQ = 1
PAST_LEN = 16384
PAGE_SIZE = 128

N_MIXERS = 2
N_DELTA_LAYERS = (DEPTH + 1) // 2
N_SSM_LAYERS = DEPTH // 2
RMS_EPS = 1e-6

GDN_QK_HEADS = 16
GDN_V_HEADS = 32
GDN_DK = 128
GDN_DV = 128
GDN_KEY_DIM = GDN_QK_HEADS * GDN_DK
GDN_VAL_DIM = GDN_V_HEADS * GDN_DV
GDN_CONV_DIM = 2 * GDN_KEY_DIM + GDN_VAL_DIM
GDN_CONV_W = 4
GDN_CHUNK = 64
GDN_IN_DIM = GDN_CONV_DIM + GDN_VAL_DIM + 2 * GDN_V_HEADS

SSM_EXPAND = 2
SSM_WIDTH = SSM_EXPAND * D_MODEL
SSM_GROUP = 16
SSM_GROUPS = SSM_WIDTH // SSM_GROUP
SSM_STATE = 64
SSM_BLOCK = 256
DT_MIN = 1e-3
DT_MAX = 1e-1

kernel_name = "gdn_s5_hybrid_step"


def rms_norm(x, g):
    xf = x.astype(jnp.float32)
    y = xf * lax.rsqrt(jnp.mean(xf * xf, axis=-1, keepdims=True) + RMS_EPS)
    return (y * g.astype(jnp.float32)).astype(x.dtype)


def l2_normalize(x):
    xf = x.astype(jnp.float32)
    return xf * lax.rsqrt(jnp.sum(xf * xf, axis=-1, keepdims=True) + 1e-6)


def causal_conv_silu(x, buf, w):
    T = x.shape[1]
    xp = jnp.concatenate([buf.astype(x.dtype), x], axis=1)
    y = xp[:, 0:T] * w[0]
    for j in range(1, GDN_CONV_W):
        y = y + xp[:, j:j + T] * w[j]
    return jax.nn.silu(y), xp[:, -(GDN_CONV_W - 1):]


def gated_delta_rule(q, k, v, g, beta, s0):
    Bsz, T, H, _ = q.shape
    C = min(GDN_CHUNK, T)
    n = -(-T // C)
    pad = n * C - T

    def prep(a):
        a = jnp.pad(a.astype(jnp.float32), [(0, 0), (0, pad)] + [(0, 0)] * (a.ndim - 2))
        a = a.reshape((Bsz, n, C) + a.shape[2:])
        return jnp.moveaxis(a, 3, 1)

    q, k, v, g, beta = prep(q), prep(k), prep(v), prep(g), prep(beta)
    dv = v.shape[-1]
    gc = jnp.cumsum(g, axis=-1)
    causal = jnp.tril(jnp.ones((C, C), bool))
    strict = jnp.tril(jnp.ones((C, C), bool), -1)
    decay = jnp.exp(jnp.where(causal, gc[..., :, None] - gc[..., None, :], -jnp.inf))
    kb = k * beta[..., None]
    lower = jnp.where(strict, jnp.einsum('bhncd,bhnsd->bhncs', kb, k) * decay, 0.0)
    eye = jnp.eye(C, dtype=jnp.float32)
    rhs = jnp.concatenate([v * beta[..., None], kb * jnp.exp(gc)[..., None]], axis=-1)
    sol = lax.linalg.triangular_solve(lower + eye, rhs, left_side=True, lower=True)
    u, w = sol[..., :dv], sol[..., dv:]
    attn = jnp.einsum('bhncd,bhnsd->bhncs', q, k) * decay
    q_dec = q * jnp.exp(gc)[..., None]
    k_dec = k * jnp.exp(gc[..., -1:] - gc)[..., None]
    g_last = jnp.exp(gc[..., -1])
    xs = tuple(jnp.moveaxis(a, 2, 0) for a in (u, w, attn, q_dec, k_dec, g_last))

    def step(S, inp):
        u_c, w_c, a_c, qd_c, kd_c, gl_c = inp
        v_new = u_c - jnp.einsum('bhcd,bhde->bhce', w_c, S)
        o = jnp.einsum('bhcd,bhde->bhce', qd_c, S) + jnp.einsum('bhcs,bhse->bhce', a_c, v_new)
        S = S * gl_c[..., None, None] + jnp.einsum('bhcd,bhce->bhde', kd_c, v_new)
        return S, o

    S, o = lax.scan(step, s0.astype(jnp.float32), xs)
    o = jnp.transpose(o, (1, 0, 3, 2, 4)).reshape(Bsz, n * C, H, dv)[:, :T]
    return o, S


def gdn_branch(h, conv_buf, s0, w_in, conv_w, a_log, dt_bias, o_gain, w_out):
    Bsz, T, _ = h.shape
    f32 = jnp.float32
    proj = h @ w_in
    i1 = GDN_CONV_DIM
    i2 = i1 + GDN_VAL_DIM
    i3 = i2 + GDN_V_HEADS
    qkv, z, b, a = jnp.split(proj, [i1, i2, i3], axis=-1)
    qkv, new_buf = causal_conv_silu(qkv, conv_buf, conv_w)
    q, k, v = jnp.split(qkv, [GDN_KEY_DIM, 2 * GDN_KEY_DIM], axis=-1)
    rep = GDN_V_HEADS // GDN_QK_HEADS
    q = jnp.repeat(l2_normalize(q.reshape(Bsz, T, GDN_QK_HEADS, GDN_DK)), rep, axis=2) * (GDN_DK ** -0.5)
    k = jnp.repeat(l2_normalize(k.reshape(Bsz, T, GDN_QK_HEADS, GDN_DK)), rep, axis=2)
    v = v.reshape(Bsz, T, GDN_V_HEADS, GDN_DV)
    beta = jax.nn.sigmoid(b.astype(f32))
    g = -jnp.exp(a_log.astype(f32)) * jax.nn.softplus(a.astype(f32) + dt_bias.astype(f32))
    o, s_new = gated_delta_rule(q, k, v, g, beta, s0)
    zg = z.reshape(Bsz, T, GDN_V_HEADS, GDN_DV).astype(f32)
    o = o * lax.rsqrt(jnp.mean(o * o, axis=-1, keepdims=True) + RMS_EPS) * o_gain.astype(f32) * jax.nn.silu(zg)
    out = o.reshape(Bsz, T, GDN_VAL_DIM).astype(h.dtype) @ w_out
    return out, new_buf, s_new


def s5_branch(h, h0_re, h0_im, w_in, lam_re, lam_im, b_re, b_im, c_re, c_im, d_skip, log_dt, w_glu, b_glu, w_out):
    Bsz, T, _ = h.shape
    f32 = jnp.float32
    u, z = jnp.split(h @ w_in, 2, axis=-1)
    lam = lax.complex(jnp.minimum(lam_re.astype(f32), -1e-4), lam_im.astype(f32))
    dt = jnp.exp(log_dt.astype(f32))[:, None]
    a_bar = jnp.exp(lam * dt)
    b_bar = ((a_bar - 1.0) / lam)[..., None] * lax.complex(b_re.astype(f32), b_im.astype(f32))
    cmat = lax.complex(c_re.astype(f32), c_im.astype(f32))
    blk = math.gcd(T, SSM_BLOCK)
    nb = T // blk
    ub = jnp.moveaxis(u.astype(f32).reshape(Bsz, nb, blk, SSM_GROUPS, SSM_GROUP), 1, 0)

    def combine(e1, e2):
        a1, b1 = e1
        a2, b2 = e2
        return a2 * a1, a2 * b1 + b2

    def block(hc, u_blk):
        bu = jnp.einsum('gpc,btgc->btgp', b_bar, u_blk)
        bu = bu.at[:, 0].add(a_bar * hc)
        _, hs = lax.associative_scan(combine, (jnp.broadcast_to(a_bar, bu.shape), bu), axis=1)
        y = jnp.real(jnp.einsum('gcp,btgp->btgc', cmat, hs))
        return hs[:, -1], y

    hc0 = lax.complex(h0_re.astype(f32), h0_im.astype(f32))
    h_last, y = lax.scan(block, hc0, ub)
    y = jnp.moveaxis(y, 0, 1).reshape(Bsz, T, SSM_WIDTH) + d_skip.astype(f32) * u.astype(f32)
    y = jax.nn.gelu(y)
    y = y * jax.nn.sigmoid((y.astype(h.dtype) @ w_glu + b_glu).astype(f32))
    y = y * jax.nn.silu(z.astype(f32))
    out = y.astype(h.dtype) @ w_out
    return out, jnp.real(h_last), jnp.imag(h_last)


def trunk(x, conv0, delta0, re0, im0, gdn_w, ssm_w, norm_final):
    (norm_gdn, w_in_gdn, conv_gdn, a_log_gdn, dt_bias_gdn, onorm_gdn, w_out_gdn) = gdn_w
    (norm_ssm, w_in_ssm, lam_re, lam_im, b_re, b_im, c_re, c_im, d_ssm, log_dt_ssm,
     w_glu_ssm, b_glu_ssm, w_out_ssm) = ssm_w
    convs, deltas, res, ims = [], [], [], []
    for i in range(DEPTH):
        j = i // N_MIXERS
        if i % N_MIXERS == 0:
            h = rms_norm(x, norm_gdn[j])
            out, cb, ds = gdn_branch(h, conv0[j], delta0[j], w_in_gdn[j], conv_gdn[j], a_log_gdn[j],
                                     dt_bias_gdn[j], onorm_gdn[j], w_out_gdn[j])
            convs.append(cb)
            deltas.append(ds)
        else:
            h = rms_norm(x, norm_ssm[j])
            out, hr, hi = s5_branch(h, re0[j], im0[j], w_in_ssm[j], lam_re[j], lam_im[j], b_re[j], b_im[j],
                                    c_re[j], c_im[j], d_ssm[j], log_dt_ssm[j], w_glu_ssm[j], b_glu_ssm[j],
                                    w_out_ssm[j])
            res.append(hr)
            ims.append(hi)
        x = x + out.astype(x.dtype)
    y = rms_norm(x, norm_final)
    return y, jnp.stack(convs), jnp.stack(deltas), jnp.stack(res), jnp.stack(ims)


def setup_inputs(seed: int = 0) -> dict:
    key = jax.random.key(seed)
    ks = jax.random.split(key, 32)
    f32 = jnp.float32
    nrm = lambda k, s, sc: jax.random.normal(k, s, f32) * sc
    NA, NB = N_DELTA_LAYERS, N_SSM_LAYERS
    dt0 = jnp.exp(jax.random.uniform(ks[10], (NA, GDN_V_HEADS), f32, math.log(DT_MIN), math.log(DT_MAX)))
    lam_im = jnp.pi * jnp.arange(SSM_STATE, dtype=f32) + nrm(ks[16], (NB, SSM_GROUPS, SSM_STATE), 0.01)
    return {
        "x_prompt": nrm(ks[0], (BATCH, SEQ, D_MODEL), 1.0),
        "x_sample": nrm(ks[1], (DEC_BATCH, DEC_SEQ, D_MODEL), 1.0),
        "state_gdn_conv": nrm(ks[2], (NA, DEC_BATCH, GDN_CONV_W - 1, GDN_CONV_DIM), 1.0),
        "state_gdn_delta": nrm(ks[3], (NA, DEC_BATCH, GDN_V_HEADS, GDN_DK, GDN_DV), GDN_DK ** -0.5),
        "state_ssm_re": nrm(ks[4], (NB, DEC_BATCH, SSM_GROUPS, SSM_STATE), 0.5),
        "state_ssm_im": nrm(ks[5], (NB, DEC_BATCH, SSM_GROUPS, SSM_STATE), 0.5),
        "norm_gdn": 1.0 + nrm(ks[6], (NA, D_MODEL), 0.02),
        "w_in_gdn": nrm(ks[7], (NA, D_MODEL, GDN_IN_DIM), D_MODEL ** -0.5),
        "conv_gdn": nrm(ks[8], (NA, GDN_CONV_W, GDN_CONV_DIM), GDN_CONV_W ** -0.5),
        "a_log_gdn": jnp.log(jax.random.uniform(ks[9], (NA, GDN_V_HEADS), f32, 1.0, 16.0)),
        "dt_bias_gdn": dt0 + jnp.log(-jnp.expm1(-dt0)),
        "onorm_gdn": 1.0 + nrm(ks[11], (NA, GDN_DV), 0.02),
        "w_out_gdn": nrm(ks[12], (NA, GDN_VAL_DIM, D_MODEL), GDN_VAL_DIM ** -0.5),
        "norm_ssm": 1.0 + nrm(ks[13], (NB, D_MODEL), 0.02),
        "w_in_ssm": nrm(ks[14], (NB, D_MODEL, 2 * SSM_WIDTH), D_MODEL ** -0.5),
        "lam_re": -0.5 + nrm(ks[15], (NB, SSM_GROUPS, SSM_STATE), 0.01),
        "lam_im": lam_im,
        "b_re": nrm(ks[17], (NB, SSM_GROUPS, SSM_STATE, SSM_GROUP), (2 * SSM_GROUP) ** -0.5),
        "b_im": nrm(ks[18], (NB, SSM_GROUPS, SSM_STATE, SSM_GROUP), (2 * SSM_GROUP) ** -0.5),
        "c_re": nrm(ks[19], (NB, SSM_GROUPS, SSM_GROUP, SSM_STATE), SSM_STATE ** -0.5),
        "c_im": nrm(ks[20], (NB, SSM_GROUPS, SSM_GROUP, SSM_STATE), SSM_STATE ** -0.5),
        "d_ssm": nrm(ks[21], (NB, SSM_WIDTH), 1.0),
        "log_dt_ssm": jax.random.uniform(ks[22], (NB, SSM_GROUPS), f32, math.log(DT_MIN), math.log(DT_MAX)),
        "w_glu_ssm": nrm(ks[23], (NB, SSM_WIDTH, SSM_WIDTH), SSM_WIDTH ** -0.5),
        "b_glu_ssm": nrm(ks[24], (NB, SSM_WIDTH), 0.01),
        "w_out_ssm": nrm(ks[25], (NB, SSM_WIDTH, D_MODEL), SSM_WIDTH ** -0.5),
        "norm_final": 1.0 + nrm(ks[26], (D_MODEL,), 0.02),
    }


def reference(x_prompt, x_sample, state_gdn_conv, state_gdn_delta, state_ssm_re, state_ssm_im,
              norm_gdn, w_in_gdn, conv_gdn, a_log_gdn, dt_bias_gdn, onorm_gdn, w_out_gdn,
              norm_ssm, w_in_ssm, lam_re, lam_im, b_re, b_im, c_re, c_im, d_ssm, log_dt_ssm,
              w_glu_ssm, b_glu_ssm, w_out_ssm, norm_final):
    gdn_w = (norm_gdn, w_in_gdn, conv_gdn, a_log_gdn, dt_bias_gdn, onorm_gdn, w_out_gdn)
    ssm_w = (norm_ssm, w_in_ssm, lam_re, lam_im, b_re, b_im, c_re, c_im, d_ssm, log_dt_ssm,
             w_glu_ssm, b_glu_ssm, w_out_ssm)
    Bp = x_prompt.shape[0]
    conv_p0 = jnp.zeros((N_DELTA_LAYERS, Bp, GDN_CONV_W - 1, GDN_CONV_DIM), x_prompt.dtype)
    delta_p0 = jnp.zeros((N_DELTA_LAYERS, Bp, GDN_V_HEADS, GDN_DK, GDN_DV), jnp.float32)
    ssm_p0 = jnp.zeros((N_SSM_LAYERS, Bp, SSM_GROUPS, SSM_STATE), jnp.float32)
    y_prompt, conv_p, delta_p, re_p, im_p = trunk(x_prompt, conv_p0, delta_p0, ssm_p0, ssm_p0,
                                                  gdn_w, ssm_w, norm_final)
    y_sample, conv_s, delta_s, re_s, im_s = trunk(x_sample, state_gdn_conv, state_gdn_delta, state_ssm_re,
                                                  state_ssm_im, gdn_w, ssm_w, norm_final)
    return (y_prompt, y_sample, conv_p, delta_p, re_p, im_p, conv_s, delta_s, re_s, im_s)
```

```python
import contextlib
import math
import numpy as np
import concourse.bass as bass
import concourse.mybir as mybir
from concourse.bass_utils import run_bass_kernel_spmd

F32 = mybir.dt.float32
BF16 = mybir.dt.bfloat16
AF = mybir.ActivationFunctionType
ALU = mybir.AluOpType
AX = mybir.AxisListType

FULL = dict(D=2048, T=2048, NS=16, HQK=16, G=256)


class Cfg:
    def __init__(self, D, T, NS, HQK, G):
        self.D, self.T, self.NS, self.HQK, self.G = D, T, NS, HQK, G
        self.KD = D // 128
        self.HV = 2 * HQK
        self.KEY = HQK * 128
        self.VAL = self.HV * 128
        self.CONV = 2 * self.KEY + self.VAL
        self.IN = self.CONV + self.VAL + 2 * self.HV
        self.W = 16 * G
        self.KW = self.W // 128
        self.TT = T + NS
        self.NCH = T // 128


class TR:
    def __init__(self, nc, es):
        self.nc, self.es = nc, es
        self.eng = dict(pe=nc.tensor, act=nc.scalar, dve=nc.vector, pool=nc.gpsimd, sp=nc.sync)
        self.sem = {}
        self.cnt = {}
        for k in ("pe", "act", "dve", "pool"):
            self.sem[k] = es.enter_context(nc.semaphore("s_" + k))
            self.cnt[k] = 0
        self.waited = {k: {} for k in self.eng}
        self.lastw = {}
        self.reads = {}
        self.nchan = 0

    def chan(self):
        self.nchan += 1
        k = "d%d" % self.nchan
        self.sem[k] = self.es.enter_context(self.nc.semaphore("s_" + k))
        self.cnt[k] = 0
        return k

    def _deps(self, e, reads, writes):
        deps = {}
        def add(ev):
            if ev is None:
                return
            k, v = ev
            if deps.get(k, 0) < v:
                deps[k] = v
        for r in reads:
            add(self.lastw.get(r))
            if r.startswith("pf") or r.startswith("pb"):
                for k, v in self.reads.get(r, {}).items():
                    if k != e:
                        add((k, v))
        for w in writes:
            add(self.lastw.get(w))
            for k, v in self.reads.get(w, {}).items():
                add((k, v))
        pend = []
        for k, v in deps.items():
            if k == "pe" and e == "pe":
                continue
            if self.waited[e].get(k, 0) >= v:
                continue
            pend.append((k, v))
            self.waited[e][k] = v
        for k, v in pend[:-1]:
            self.eng[e].wait_ge(self.sem[k], v)
        return pend[-1] if pend else None

    def _mark(self, ev, reads, writes):
        k, v = ev
        for r in reads:
            self.reads.setdefault(r, {})[k] = v
        for w in writes:
            self.lastw[w] = ev
            self.reads[w] = {}

    def op(self, e, reads, writes, fn):
        lw = self._deps(e, reads, writes)
        ins = fn(self.eng[e])
        if lw is not None:
            ins._wait_ge(self.sem[lw[0]], lw[1])
        self.cnt[e] += 1
        ins.then_inc(self.sem[e], 1)
        self._mark((e, self.cnt[e]), reads, writes)

    def dma(self, q, ch, reads, writes, out, in_, **kw):
        lw = self._deps(q, reads, writes)
        ins = self.eng[q].dma_start(out=out, in_=in_, **kw)
        if lw is not None:
            ins._wait_ge(self.sem[lw[0]], lw[1])
        ins.then_inc(self.sem[ch], 16)
        self.cnt[ch] += 16
        self._mark((ch, self.cnt[ch]), reads, writes)

    def barrier(self):
        for e in ("pe", "act", "dve", "pool", "sp"):
            for k in self.sem:
                if k != e and self.cnt[k] > 0 and self.waited[e].get(k, 0) < self.cnt[k]:
                    self.eng[e].wait_ge(self.sem[k], self.cnt[k])
                    self.waited[e][k] = self.cnt[k]

    def finish(self, q="sp"):
        for k in self.sem:
            if k.startswith("d") and self.cnt[k] > 0:
                self.eng[q].wait_ge(self.sem[k], self.cnt[k])
        for k in ("pe", "act", "dve", "pool"):
            if self.cnt[k] > 0:
                self.eng[q].wait_ge(self.sem[k], self.cnt[k])


def r3(ap, a, b):
    return ap.rearrange("p (a b) -> p a b", a=a, b=b)


class _Stop(Exception):
    pass


def build(cfg):
    c = cfg
    nc = bass.Bass("TRN2", target_bir_lowering=False)
    D, T, NS, TT, KD, HV, G = c.D, c.T, c.NS, c.TT, c.KD, c.HV, c.G

    def din(name, shape, dt=F32):
        return nc.dram_tensor(name, list(shape), dt, kind="ExternalInput").ap()

    def dout(name, shape, dt=F32):
        return nc.dram_tensor(name, list(shape), dt, kind="ExternalOutput").ap()

    def dscr(name, shape, dt):
        return nc.dram_tensor(name, list(shape), dt, kind="Internal").ap()

    xin = din("xin", [TT, D])
    conv_s = din("conv_s", [NS, 3, c.CONV])
    delta_s = din("delta_s", [NS, HV, 128, 128])
    re_s = din("re_s", [NS, G, 64])
    im_s = din("im_s", [NS, G, 64])
    norm_gdn = din("norm_gdn", [1, D])
    w_in_gdn = din("w_in_gdn", [D, c.IN])
    conv_w = din("conv_w", [4, c.CONV])
    a_log = din("a_log", [1, HV])
    dt_bias = din("dt_bias", [1, HV])
    onorm = din("onorm", [1, 128])
    w_out_gdn = din("w_out_gdn", [c.VAL, D])
    norm_ssm = din("norm_ssm", [1, D])
    w_in_ssm = din("w_in_ssm", [D, 2 * c.W])
    lam_re = din("lam_re", [G, 64])
    lam_im = din("lam_im", [G, 64])
    b_re = din("b_re", [G, 64, 16])
    b_im = din("b_im", [G, 64, 16])
    c_re = din("c_re", [G, 16, 64])
    c_im = din("c_im", [G, 16, 64])
    d_ssm = din("d_ssm", [c.W, 1])
    log_dt = din("log_dt", [1, G])
    w_glu = din("w_glu", [c.W, c.W])
    b_glu = din("b_glu", [c.W, 1])
    w_out_ssm = din("w_out_ssm", [c.W, D])
    norm_final = din("norm_final", [1, D])
    consts = din("consts", [128, 8 * 128])

    y_out = dout("y_out", [TT, D])
    conv_p = dout("conv_p", [3, c.CONV])
    delta_p = dout("delta_p", [HV, 128, 128])
    re_p = dout("re_p", [G, 64])
    im_p = dout("im_p", [G, 64])
    conv_so = dout("conv_so", [NS, 3, c.CONV])
    delta_so = dout("delta_so", [NS, HV, 128, 128])
    re_so = dout("re_so", [NS, G, 64])
    im_so = dout("im_so", [NS, G, 64])

    oT_scr = dscr("oT_scr", [c.VAL, TT], BF16)
    x1_scr = dscr("x1_scr", [TT, D], F32)
    yT_scr = dscr("yT_scr", [c.W, TT], BF16)
    y2_scr = dscr("y2_scr", [c.W, TT], BF16)
    x2_scr = dscr("x2_scr", [TT, D], F32)

    es = contextlib.ExitStack()
    with es:
        tr = TR(nc, es)
        cur = [es]
        try:

            def chk(k):
                if getattr(c, "stop", None) == k:
                    raise _Stop()

            def sb(name, shape, dt=F32):
                return cur[0].enter_context(nc.sbuf_tensor(name, list(shape), dt))

            def phase_begin():
                tr.barrier()
                cur[0] = contextlib.ExitStack()

            def phase_end():
                tr.barrier()
                cur[0].close()
                cur[0] = es

            def ps(name, shape, dt=F32):
                return es.enter_context(nc.psum_tensor(name, list(shape), dt))

            cst = sb("cst", [128, 8 * 128])
            ch_c = tr.chan()
            tr.dma("sp", ch_c, [], ["cst"], cst[:], consts[:, :])
            ident = cst[:, 0:128]
            triU = cst[:, 128:256]
            ones = cst[:, 256:384]
            MBIG = cst[:, 384:512]
            MNEG = cst[:, 512:640]
            CMASK = cst[:, 640:768]
            BD = cst[:, 768:896]
            GSEL = cst[:, 896:904]
            GG = cst[:, 904:968]
            cstb = sb("cstb", [128, 256], BF16)
            identb = cstb[:, 0:128]
            onesb = cstb[:, 128:256]
            tr.op("dve", ["cst"], ["cstb"], lambda e: e.tensor_copy(out=cstb[:, 0:128], in_=ident))
            tr.op("dve", ["cst"], ["cstb"], lambda e: e.tensor_copy(out=cstb[:, 128:256], in_=ones))

            def bcast_load(name, src, n):
                t = sb(name, [128, n])
                ch = tr.chan()
                tr.dma("sp", ch, [], [name], t[:], src[0:1, :].broadcast_to([128, n]))
                return t

            chg = tr.chan()
            nrm = {}
            alog_bc = bcast_load("alog_bc", a_log, HV)
            dtb_bc = bcast_load("dtb_bc", dt_bias, HV)
            ogain_bc = bcast_load("ogain_bc", onorm, 128)
            negA = sb("negA", [128, HV])
            tr.op("act", ["alog_bc"], ["negA"], lambda e: e.activation(out=negA[:], in_=alog_bc[:], func=AF.Exp))
            tr.op("dve", ["negA"], ["negA"], lambda e: e.tensor_scalar(out=negA[:], in0=negA[:], scalar1=-1.0, scalar2=None, op0=ALU.mult))

            hT = sb("hT", [128, KD * TT], BF16)
            hT3 = r3(hT[:], KD, TT)

            PF = [ps("pf%d" % i, [128, 512]) for i in range(6)]
            PB = [ps("pb%d" % i, [128, 1024], BF16) for i in range(2)]
            pf_i = [0]
            pb_i = [0]

            def pf():
                pf_i[0] = (pf_i[0] + 1) % len(PF)
                return PF[pf_i[0]], "pf%d" % pf_i[0]

            def pb():
                pb_i[0] = (pb_i[0] + 1) % len(PB)
                return PB[pb_i[0]], "pb%d" % pb_i[0]

            xtc = [tr.chan() for i in range(1)]
            stat = sb("stat", [128, 8])

            def tok_tiles():
                tl = [(i * 128, 128) for i in range(T // 128)]
                tl.append((T, NS))
                return tl

            def norm_to_hT(src, gsrc, srcname, addsrc=None, final_out=None):
                phase_begin()
                nrm["i"] = nrm.get("i", 0) + 1
                gbuf = sb("gbuf%d" % nrm["i"], [128, D])
                xt = [sb("xt%d_%d" % (nrm["i"], 0), [128, D])]
                hb = [sb("hb%d_%d" % (nrm["i"], 0), [128, D], BF16)]
                _norm_body(src, gsrc, srcname, final_out, gbuf, xt, hb)
                phase_end()

            def _norm_body(src, gsrc, srcname, final_out, gbuf, xt, hb):
                gain = gbuf
                gname = "gbuf"
                tr.dma("sp", chg, [], ["gbuf"], gbuf[:], gsrc[0:1, :].broadcast_to([128, D]))
                for it, (t0, n) in enumerate(tok_tiles()):
                    b = 0
                    tr.dma("sp", xtc[b], [srcname], ["xt%d" % b], xt[b][0:n, :], src[t0:t0 + n, :])
                    tr.op("act", ["xt%d" % b], ["hb%d" % b, "stat"], lambda e: e.activation(out=hb[b][0:n, :], in_=xt[b][0:n, :], func=AF.Square, accum_out=stat[0:n, 0:1]))
                    tr.op("dve", ["stat"], ["stat"], lambda e: e.tensor_scalar(out=stat[0:n, 1:2], in0=stat[0:n, 0:1], scalar1=1.0 / D, scalar2=1e-6, op0=ALU.mult, op1=ALU.add))
                    tr.op("act", ["stat"], ["stat"], lambda e: e.activation(out=stat[0:n, 3:4], in_=stat[0:n, 1:2], func=AF.Sqrt))
                    tr.op("dve", ["stat"], ["stat"], lambda e: e.reciprocal(out=stat[0:n, 2:3], in_=stat[0:n, 3:4]))
                    if final_out is not None:
                        tr.op("dve", ["xt%d" % b, "stat", gname], ["xt%d" % b], lambda e: e.scalar_tensor_tensor(out=xt[b][0:n, :], in0=xt[b][0:n, :], scalar=stat[0:n, 2:3], in1=gain[0:n, :], op0=ALU.mult, op1=ALU.mult))
                        tr.dma("sp", xtc[b], ["xt%d" % b], ["y_out"], final_out[t0:t0 + n, :], xt[b][0:n, :])
                        continue
                    tr.op("dve", ["xt%d" % b, "stat", gname], ["hb%d" % b], lambda e: e.scalar_tensor_tensor(out=hb[b][0:n, :], in0=xt[b][0:n, :], scalar=stat[0:n, 2:3], in1=gain[0:n, :], op0=ALU.mult, op1=ALU.mult))
                    for k0 in range(0, KD, 8):
                        kk = min(8, KD - k0)
                        p, pn = pb()
                        for k in range(kk):
                            tr.op("pe", ["hb%d" % b, "cstb"], [pn], lambda e: e.transpose(out=p[:, k * 128:k * 128 + n], in_=hb[b][0:n, (k0 + k) * 128:(k0 + k + 1) * 128], identity=identb[0:n, 0:n]))
                        tr.op("act" if (k0 // 8) % 2 == 0 else "dve", [pn], ["hT"],
                              (lambda e: e.activation(out=hT3[:, k0:k0 + kk, t0:t0 + n], in_=r3(p[:, 0:kk * 128], kk, 128)[:, :, 0:n], func=AF.Copy)) if (k0 // 8) % 2 == 0 else
                              (lambda e: e.tensor_copy(out=hT3[:, k0:k0 + kk, t0:t0 + n], in_=r3(p[:, 0:kk * 128], kk, 128)[:, :, 0:n])))

            chk(0)
            norm_to_hT(xin, norm_gdn, "xin")
            chk(1)

            phase_begin()
            NCOL = 772
            wj = [sb("wj%d" % i, [128, KD * NCOL], BF16) for i in range(1)]
            wjc = [tr.chan() for i in range(1)]
            NBLK = c.CONV // 128
            NR = 4 * NBLK
            cwT = sb("cwT", [128, NR])
            cwr = sb("cwr", [128, 128])
            chx = tr.chan()
            cw_rows = conv_w.rearrange("j (b c) -> (j b) c", c=128)
            for r0 in range(0, NR, 128):
                nr = min(128, NR - r0)
                tr.dma("sp", chx, [], ["cwr"], cwr[0:nr, :], cw_rows[r0:r0 + nr, :])
                p, pn = pf()
                tr.op("pe", ["cwr", "cst"], [pn], lambda e: e.transpose(out=p[:, 0:nr], in_=cwr[0:nr, :], identity=ident[0:nr, 0:nr]))
                tr.op("dve", [pn], ["cwT"], lambda e: e.tensor_copy(out=cwT[:, r0:r0 + nr], in_=p[:, 0:nr]))
            pre = [sb("pre0", [128, 3 + TT])] * 4
            if 3 + TT >= 1032:
                tail = pre[0][0:NS + 3, 8:520]
                cst48 = pre[0][0:NS * 3, 520:1032]
            else:
                tail = sb("tailx", [NS + 3, 512])[:, :]
                cst48 = sb("cst48x", [NS * 3, 512])[:, :]
            xp4 = [sb("xp4_0", [128, NS * 4])] * 4
            xs3 = sb("xs3", [128, 4 * NS * 3])
            ch48 = tr.chan()
            cv = [sb("cv0", [128, TT])] * 4
            tmpc = sb("tmpc", [128, TT])
            qT = sb("qT", [128, TT], BF16)
            kT = sb("kT", [128, TT], BF16)
            chtail = tr.chan()
            zba = sb("zba", [128, (c.NCH) * 260], BF16)
            zbas = sb("zbas", [1, NS * 260], BF16)
            gates = {}
            for nm, Cc, nch in (("p", 128, c.NCH), ("s", 1, NS)):
                for f in ("beta", "g", "gc", "egc", "bg", "ekd", "gl", "egl128", "tmp"):
                    gates[(nm, f)] = sb("gt_%s_%s" % (nm, f), [128, nch * 2])
            LANES = []
            for li in range(2):
                ln = {}
                ln["Sst"] = sb("Sst%d" % li, [128, 128]); ln["Sbf"] = sb("Sbf%d" % li, [128, 128], BF16)
                ln["chS"] = tr.chan(); ln["chSo"] = tr.chan(); ln["choT"] = tr.chan()
                ln["oTst"] = sb("oTst%d" % li, [128, TT], BF16)
                Wl = {}
                for nm, dt in (("kbg", F32), ("E1", F32), ("E2", F32), ("L", F32), ("N", F32),
                               ("P", F32), ("L2a", F32), ("L2b", F32), ("N2a", F32), ("N2b", F32), ("vn", BF16),
                               ("av", F32), ("gz", F32), ("og", BF16), ("sq", BF16)):
                    Wl[nm] = sb("w%d_%s" % (li, nm), [128, 128], dt)
                Wl["dg"] = sb("w%d_dg" % li, [128, 128])
                ln["H"] = []
                for par in range(2):
                    Hd = {}
                    for nm, dt in (("vb", F32), ("kd", BF16), ("AT", BF16), ("u", F32), ("wT", BF16)):
                        Hd[nm] = sb("h%d_%d_%s" % (li, par, nm), [128, 128], dt)
                    ln["H"].append(Hd)
                ln["A_done"] = 0
                ln["B_done"] = 0
                ln["C_done"] = 0
                ln["O"] = [sb("ho%d_%d" % (li, par), [128, 128]) for par in range(2)]
                ln["SS"] = [(sb("Sss%d_%d" % (li, par), [128, 128]), sb("Ssb%d_%d" % (li, par), [128, 128], BF16), tr.chan(), tr.chan()) for par in range(2)]
                ln["W"] = Wl
                ln["colst"] = sb("colst%d" % li, [128, 8])
                ln["id"] = li
                ln["pfb"] = [3 * li, 3 * li + 1, 3 * li + 2]
                ln["pfi"] = [0]
                ln["pbb"] = li
                LANES.append(ln)

            def load_wj(j, b):
                base = wj[b]
                w3 = r3(base[:], KD, NCOL)
                segs = [(0, j * 128, 128), (128, c.KEY + j * 128, 128), (256, 2 * c.KEY + j * 256, 256),
                        (512, c.CONV + j * 256, 256), (768, c.CONV + c.VAL + 2 * j, 2), (770, c.CONV + c.VAL + HV + 2 * j, 2)]
                for (o, s0, n) in segs:
                    tr.dma("pool", wjc[b], [], ["wj%d" % b], w3[:, :, o:o + n], w_in_gdn[:, s0:s0 + n].rearrange("(k p) n -> p k n", p=128))

            def chunk(ln, part, seq, nm, Cc, ci, cols, j, hh, first, last, n_idx):
                c0 = cols
                W = ln["W"]; Sst = ln["Sst"]; Sbf = ln["Sbf"]; chS = ln["chS"]; chSo = ln["chSo"]; oTst = ln["oTst"]; colst = ln["colst"]
                WN = "w%d_" % ln["id"]; SN = "Sst%d" % ln["id"]; BN = "Sbf%d" % ln["id"]; ON = "oTst%d" % ln["id"]; CN = "colst%d" % ln["id"]

                H = ln["H"][seq % 2]
                HN = "h%d_%d_" % (ln["id"], seq % 2)

                def pf():
                    if part == "B":
                        i_ = ln["pfb"][2]
                    else:
                        ln["pfi"][0] = (ln["pfi"][0] + 1) % 2
                        i_ = ln["pfb"][ln["pfi"][0]]
                    return PF[i_], "pf%d" % i_

                def pb():
                    return PB[ln["pbb"]], "pb%d" % ln["pbb"]
                h = 2 * j + hh
                gi = ci * 2 + hh
                G_ = lambda f: gates[(nm, f)]
                gname = lambda f: "gt_%s_%s" % (nm, f)
                if part == "A":
                    while ln["B_done"] < seq - 1:
                        yield "blocked"
                    p, pn = pb()
                    tr.op("pe", ["kT", "cstb"], [pn], lambda e: e.transpose(out=p[0:Cc, 0:128], in_=kT[:, c0:c0 + Cc], identity=identb))
                    yield
                    tr.op("pe", ["cvb%d" % hh, "cstb"], [pn], lambda e: e.transpose(out=p[0:Cc, 128:256], in_=cvb[hh][:, c0:c0 + Cc], identity=identb))
                    yield
                    tr.op("dve", [pn, gname("beta")], [HN + "vb"], lambda e: e.tensor_scalar(out=H["vb"][0:Cc, :], in0=p[0:Cc, 128:256], scalar1=G_("beta")[0:Cc, gi:gi + 1], scalar2=None, op0=ALU.mult))
                    yield
                    tr.op("dve", [pn, gname("bg")], [WN + "kbg"], lambda e: e.tensor_scalar(out=W["kbg"][0:Cc, :], in0=p[0:Cc, 0:128], scalar1=G_("bg")[0:Cc, gi:gi + 1], scalar2=None, op0=ALU.mult))
                    yield
                    tr.op("act", [pn, gname("ekd")], [HN + "kd"], lambda e: e.activation(out=H["kd"][0:Cc, :], in_=p[0:Cc, 0:128], func=AF.Copy, scale=G_("ekd")[0:Cc, gi:gi + 1]))
                    yield
                    chk(10)
                    if Cc > 1:
                        pk, pkn = pf()
                        tr.op("pe", ["kT"], [pkn], lambda e: e.matmul(pk[0:Cc, 0:Cc], lhsT=kT[:, c0:c0 + Cc], rhs=kT[:, c0:c0 + Cc], start=True, stop=True))
                        yield
                        tr.op("pe", ["kT", "qT"], [pkn], lambda e: e.matmul(pk[0:Cc, 128:128 + Cc], lhsT=kT[:, c0:c0 + Cc], rhs=qT[:, c0:c0 + Cc], start=True, stop=True))
                        yield
                        tr.op("dve", ["cst", gname("gc")], [WN + "dg"], lambda e: e.tensor_scalar(out=W["dg"][0:Cc, 0:Cc], in0=ident[0:Cc, 0:Cc], scalar1=G_("gc")[0:Cc, gi:gi + 1], scalar2=None, op0=ALU.mult))
                        yield
                        tr.op("pe", ["cst", WN + "dg"], [pkn], lambda e: e.matmul(pk[0:Cc, 256:256 + Cc], lhsT=ones[0:Cc, 0:Cc], rhs=W["dg"][0:Cc, 0:Cc], start=True, stop=True))
                        yield
                        R = pk[0:Cc, 256:256 + Cc]
                        chk(11)
                        tr.op("dve", [pkn, gname("gc"), "cst"], [WN + "E1"], lambda e: e.scalar_tensor_tensor(out=W["E1"][0:Cc, 0:Cc], in0=R, scalar=G_("gc")[0:Cc, gi:gi + 1], in1=MBIG[0:Cc, 0:Cc], op0=ALU.subtract, op1=ALU.max))
                        yield
                        tr.op("dve", [pkn, gname("gc"), "cst"], [WN + "E2"], lambda e: e.scalar_tensor_tensor(out=W["E2"][0:Cc, 0:Cc], in0=R, scalar=G_("gc")[0:Cc, gi:gi + 1], in1=MNEG[0:Cc, 0:Cc], op0=ALU.subtract, op1=ALU.min))
                        yield
                        tr.op("act", [WN + "E1"], [WN + "E1"], lambda e: e.activation(out=W["E1"][0:Cc, 0:Cc], in_=W["E1"][0:Cc, 0:Cc], func=AF.Exp, scale=-1.0))
                        yield
                        tr.op("act", [WN + "E2"], [WN + "E2"], lambda e: e.activation(out=W["E2"][0:Cc, 0:Cc], in_=W["E2"][0:Cc, 0:Cc], func=AF.Exp))
                        yield
                        tr.op("dve", [pkn, gname("beta"), WN + "E1"], [WN + "L"], lambda e: e.scalar_tensor_tensor(out=W["L"][0:Cc, 0:Cc], in0=pk[0:Cc, 0:Cc], scalar=G_("beta")[0:Cc, gi:gi + 1], in1=W["E1"][0:Cc, 0:Cc], op0=ALU.mult, op1=ALU.mult))
                        yield
                        tr.op("dve", [pkn, WN + "E2"], [HN + "AT"], lambda e: e.tensor_tensor(out=H["AT"][0:Cc, 0:Cc], in0=pk[0:Cc, 128:128 + Cc], in1=W["E2"][0:Cc, 0:Cc], op=ALU.mult))
                        yield
                    else:
                        pk, pkn = pf()
                        tr.op("pe", ["kT", "qT"], [pkn], lambda e: e.matmul(pk[0:1, 128:129], lhsT=kT[:, c0:c0 + 1], rhs=qT[:, c0:c0 + 1], start=True, stop=True))
                        yield
                        tr.op("dve", [pkn], [HN + "AT"], lambda e: e.tensor_copy(out=H["AT"][0:1, 0:1], in_=pk[0:1, 128:129]))
                        yield
                    chk(12)
                    if Cc > 1:
                        p2, p2n = pf()
                        tr.op("pe", [WN + "L", "cst"], [p2n], lambda e: e.transpose(out=p2[0:Cc, 0:Cc], in_=W["L"][0:Cc, 0:Cc], identity=ident[0:Cc, 0:Cc]))
                        yield
                        tr.op("act", [p2n], [WN + "N"], lambda e: e.activation(out=W["N"][0:Cc, 0:Cc], in_=p2[0:Cc, 0:Cc], func=AF.Copy))
                        yield
                        tr.op("dve", ["cstb", WN + "N"], [WN + "P"], lambda e: e.tensor_tensor(out=W["P"][0:Cc, 0:Cc], in0=ident[0:Cc, 0:Cc], in1=W["N"][0:Cc, 0:Cc], op=ALU.subtract))
                        yield
                        chk(120)
                        Lk, Nk = "L", "N"
                        nsteps = int(math.log2(Cc)) - 1
                        for st in range(nsteps):
                            L2, N2 = ("L2a", "N2a") if st % 2 == 0 else ("L2b", "N2b")
                            pq, pqn = pf()
                            tr.op("pe", [WN + Nk, WN + Lk], [pqn], lambda e: e.matmul(pq[0:Cc, 0:Cc], lhsT=W[Nk][0:Cc, 0:Cc], rhs=W[Lk][0:Cc, 0:Cc], start=True, stop=True))
                            yield
                            if st < nsteps - 1:
                                tr.op("pe", [WN + Nk, WN + Lk], [pqn], lambda e: e.matmul(pq[0:Cc, 128:128 + Cc], lhsT=W[Lk][0:Cc, 0:Cc], rhs=W[Nk][0:Cc, 0:Cc], start=True, stop=True))
                                yield
                            tr.op("act", [pqn], [WN + L2], lambda e: e.activation(out=W[L2][0:Cc, 0:Cc], in_=pq[0:Cc, 0:Cc], func=AF.Copy))
                            yield
                            if st < nsteps - 1:
                                tr.op("dve", [pqn], [WN + N2], lambda e: e.tensor_copy(out=W[N2][0:Cc, 0:Cc], in_=pq[0:Cc, 128:128 + Cc]))
                                yield
                            tr.op("pe", [WN + L2, WN + "P"], [pqn], lambda e: e.matmul(pq[0:Cc, 256:256 + Cc], lhsT=W[L2][0:Cc, 0:Cc], rhs=W["P"][0:Cc, 0:Cc], start=True, stop=True))
                            yield
                            tr.op("dve", [pqn, WN + "P"], [WN + "P"], lambda e: e.tensor_tensor(out=W["P"][0:Cc, 0:Cc], in0=pq[0:Cc, 256:256 + Cc], in1=W["P"][0:Cc, 0:Cc], op=ALU.add))
                            yield
                            Lk, Nk = L2, N2
                            chk(121 + st)
                        TTm, TTn = W["P"][0:Cc, 0:Cc], WN + "P"
                    else:
                        TTm, TTn = ident[0:1, 0:1], "cst"
                    chk(13)
                    pu, pun = pf()
                    if Cc > 1:
                        tr.op("pe", [TTn, HN + "vb"], [pun], lambda e: e.matmul(pu[0:Cc, 0:128], lhsT=TTm, rhs=H["vb"][0:Cc, :], start=True, stop=True))
                        yield
                        tr.op("act", [pun], [HN + "u"], lambda e: e.activation(out=H["u"][0:Cc, :], in_=pu[0:Cc, 0:128], func=AF.Copy))
                        yield
                        Uap, Un = H["u"], HN + "u"
                    else:
                        Uap, Un = H["vb"], HN + "vb"
                    tr.op("pe", [TTn, WN + "kbg"], [pun], lambda e: e.matmul(pu[:, 128:128 + Cc], lhsT=W["kbg"][0:Cc, :], rhs=TTm, start=True, stop=True))
                    yield
                    tr.op("act", [pun], [HN + "wT"], lambda e: e.activation(out=H["wT"][:, 0:Cc], in_=pu[:, 128:128 + Cc], func=AF.Copy))
                    yield
                    chk(14)
                    ln["A_done"] = seq + 1
                    return
                Otile = ln["O"][seq % 2]
                OnN = "ho%d_%d" % (ln["id"], seq % 2)
                if nm == "s":
                    Sst, Sbf, chS, chSo = ln["SS"][seq % 2]
                    SN = "Sss%d_%d" % (ln["id"], seq % 2)
                    BN = "Ssb%d_%d" % (ln["id"], seq % 2)
                if part == "C":
                    while ln["B_done"] <= seq:
                        yield "blocked"
                else:
                    while ln["A_done"] <= seq or ln["C_done"] < seq - 1:
                        yield "blocked"
                    if Cc > 1:
                        Uap, Un = H["u"], HN + "u"
                    else:
                        Uap, Un = H["vb"], HN + "vb"
                    if nm == "s":
                        tr.dma("sp", chS, ["delta_s"], [SN], Sst[:], delta_s[n_idx, h, :, :])
                        yield
                        tr.op("act", [SN], [BN], lambda e: e.activation(out=Sbf[:], in_=Sst[:], func=AF.Copy))
                        yield
                    elif first:
                        tr.op("pool", [], [SN], lambda e: e.memset(Sst[:], 0.0))
                        yield
                        tr.op("pool", [], [BN], lambda e: e.memset(Sbf[:], 0.0))
                        yield
                    pw, pwn = pf()
                    tr.op("pe", [HN + "wT", BN], [pwn], lambda e: e.matmul(pw[0:Cc, 0:128], lhsT=H["wT"][:, 0:Cc], rhs=Sbf[:], start=True, stop=True))
                    yield
                    tr.op("pe", ["qT", BN], [pwn], lambda e: e.matmul(pw[0:Cc, 128:256], lhsT=qT[:, c0:c0 + Cc], rhs=Sbf[:], start=True, stop=True))
                    yield
                    tr.op("dve", [pwn, Un], [WN + "vn"], lambda e: e.tensor_tensor(out=W["vn"][0:Cc, :], in0=Uap[0:Cc, :], in1=pw[0:Cc, 0:128], op=ALU.subtract))
                    yield
                    tr.op("pe", [HN + "kd", WN + "vn"], [pwn], lambda e: e.matmul(pw[:, 384:512], lhsT=H["kd"][0:Cc, :], rhs=W["vn"][0:Cc, :], start=True, stop=True))
                    yield
                    tr.op("pe", [HN + "AT", WN + "vn"], [pwn], lambda e: e.matmul(pw[0:Cc, 256:384], lhsT=H["AT"][0:Cc, 0:Cc], rhs=W["vn"][0:Cc, :], start=True, stop=True))
                    yield
                    tr.op("dve", [pwn, SN, gname("egl128")], [SN], lambda e: e.scalar_tensor_tensor(out=Sst[:], in0=Sst[:], scalar=G_("egl128")[:, gi:gi + 1], in1=pw[:, 384:512], op0=ALU.mult, op1=ALU.add))
                    yield
                    if nm == "s":
                        tr.dma("sp", chSo, [SN], ["delta_so"], delta_so[n_idx, h, :, :], Sst[:])
                        yield
                    elif last:
                        tr.dma("sp", chSo, [SN], ["delta_p"], delta_p[h, :, :], Sst[:])
                        yield
                    else:
                        tr.op("act", [SN], [BN], lambda e: e.activation(out=Sbf[:], in_=Sst[:], func=AF.Copy))
                        yield
                    tr.op("act", [pwn], [WN + "av"], lambda e: e.activation(out=W["av"][0:Cc, :], in_=pw[0:Cc, 256:384], func=AF.Copy))
                    yield
                    tr.op("dve", [pwn, WN + "av", gname("egc")], [OnN], lambda e: e.scalar_tensor_tensor(out=Otile[0:Cc, :], in0=pw[0:Cc, 128:256], scalar=G_("egc")[0:Cc, gi:gi + 1], in1=W["av"][0:Cc, :], op0=ALU.mult, op1=ALU.add))
                    yield
                    ln["B_done"] = seq + 1
                    return
                zsrc = (zba if nm == "p" else zbas)
                zname = "zba" if nm == "p" else "zbas"
                zap = zsrc[0:Cc, ci * 260 + hh * 128: ci * 260 + hh * 128 + 128]
                tr.op("act", [OnN], [WN + "sq", CN], lambda e: e.activation(out=W["sq"][0:Cc, :], in_=Otile[0:Cc, :], func=AF.Square, accum_out=colst[0:Cc, 0:1]))
                yield
                tr.op("dve", [CN], [CN], lambda e: e.tensor_scalar(out=colst[0:Cc, 1:2], in0=colst[0:Cc, 0:1], scalar1=1.0 / 128, scalar2=1e-6, op0=ALU.mult, op1=ALU.add))
                yield
                tr.op("act", [CN], [CN], lambda e: e.activation(out=colst[0:Cc, 3:4], in_=colst[0:Cc, 1:2], func=AF.Sqrt))
                yield
                tr.op("dve", [CN], [CN], lambda e: e.reciprocal(out=colst[0:Cc, 2:3], in_=colst[0:Cc, 3:4]))
                yield
                tr.op("act", [zname], [WN + "gz"], lambda e: e.activation(out=W["gz"][0:Cc, :], in_=zap, func=AF.Silu))
                yield
                tr.op("pool", [WN + "gz", "ogain_bc"], [WN + "gz"], lambda e: e.tensor_tensor(out=W["gz"][0:Cc, :], in0=W["gz"][0:Cc, :], in1=ogain_bc[0:Cc, :], op=ALU.mult))
                yield
                tr.op("dve", [OnN, CN, WN + "gz"], [WN + "og"], lambda e: e.scalar_tensor_tensor(out=W["og"][0:Cc, :], in0=Otile[0:Cc, :], scalar=colst[0:Cc, 2:3], in1=W["gz"][0:Cc, :], op0=ALU.mult, op1=ALU.mult))
                yield
                p3, p3n = pb()
                tr.op("pe", [WN + "og", "cstb"], [p3n], lambda e: e.transpose(out=p3[:, 512:512 + Cc], in_=W["og"][0:Cc, :], identity=identb[0:Cc, 0:Cc]))
                yield
                tr.op("act", [p3n], [ON], lambda e: e.activation(out=oTst[:, c0:c0 + Cc], in_=p3[:, 512:512 + Cc], func=AF.Copy))
                yield
                ln["C_done"] = seq + 1

            cvb = [sb("cvb%d" % i, [128, TT], BF16) for i in range(2)]

            for j in range(c.HQK):
                chk(100 + j)
                b = 0
                load_wj(j, b)
                w3 = r3(wj[b][:], KD, NCOL)
                wn = "wj%d" % b
                chk(2)
                for (lo, m, dname) in ((T - 3, 3, "conv_p"), (T, NS, "conv_so")):
                    p, pn = pf()
                    for k in range(KD):
                        tr.op("pe", [wn, "hT"], [pn], lambda e: e.matmul(p[0:m, 0:512], lhsT=hT3[:, k, lo:lo + m], rhs=w3[:, k, 0:512], start=(k == 0), stop=(k == KD - 1)))
                    tr.op("act", [pn], ["pre0"], lambda e: e.activation(out=tail[0:m, :], in_=p[0:m, 0:512], func=AF.Copy))
                    for (o, s0, n) in ((0, j * 128, 128), (128, c.KEY + j * 128, 128), (256, 2 * c.KEY + j * 256, 256)):
                        if dname == "conv_p":
                            tr.dma("sp", chtail, ["pre0"], ["conv_p"], conv_p[0:3, s0:s0 + n], tail[0:3, o:o + n])
                        else:
                            tr.dma("sp", chtail, ["pre0"], ["conv_so"], conv_so[:, 2, s0:s0 + n], tail[0:NS, o:o + n])
                            tr.dma("sp", chtail, [], ["conv_so"], conv_so[:, 0:2, s0:s0 + n], conv_s[:, 1:3, s0:s0 + n])
                chk(3)
                for (o, s0, n) in ((0, j * 128, 128), (128, c.KEY + j * 128, 128), (256, 2 * c.KEY + j * 256, 256)):
                    tr.dma("sp", ch48, [], ["pre0"], cst48[:, o:o + n], conv_s[:, :, s0:s0 + n].rearrange("n j c -> (n j) c"))
                x4 = r3(xp4[0][:], NS, 4)
                for fb in range(4):
                    p, pn = pf()
                    tr.op("pe", ["pre0", "cst"], [pn], lambda e: e.transpose(out=p[:, 0:NS * 3], in_=cst48[:, fb * 128:(fb + 1) * 128], identity=ident[0:NS * 3, 0:NS * 3]))
                    tr.op("dve", [pn], ["xs3"], lambda e: e.tensor_copy(out=xs3[:, fb * NS * 3:(fb + 1) * NS * 3], in_=p[:, 0:NS * 3]))
                tr.op("pool", [], ["pre0"], lambda e: e.memset(pre[0][:, 0:3], 0.0))
                for fb in range(4):
                    for t0 in range(0, TT, 512):
                        n = min(512, TT - t0)
                        p, pn = pf()
                        for k in range(KD):
                            tr.op("pe", [wn, "hT"], [pn], lambda e: e.matmul(p[:, 0:n], lhsT=w3[:, k, fb * 128:(fb + 1) * 128], rhs=hT3[:, k, t0:t0 + n], start=(k == 0), stop=(k == KD - 1)))
                        np_ = max(0, min(n, T - t0))
                        if np_ > 0:
                            tr.op("act", [pn], ["pre0"], lambda e: e.activation(out=pre[0][:, 3 + t0:3 + t0 + np_], in_=p[:, 0:np_], func=AF.Copy))
                        if np_ < n:
                            s0 = t0 + np_ - T
                            ns_ = n - np_
                            tr.op("dve", [pn], ["xp4_0"], lambda e: e.tensor_copy(out=x4[:, s0:s0 + ns_, 3], in_=p[:, np_:n]))
                    tr.op("dve", ["xs3"], ["xp4_0"], lambda e: e.tensor_copy(out=x4[:, :, 0:3], in_=r3(xs3[:, fb * NS * 3:(fb + 1) * NS * 3], NS, 3)))
                    blk = [j, c.KEY // 128 + j, 2 * c.KEY // 128 + 2 * j, 2 * c.KEY // 128 + 2 * j + 1][fb]
                    cwb = lambda tp: cwT[:, tp * NBLK + blk:tp * NBLK + blk + 1]
                    acc, an, eng = tmpc, "tmpc", "dve"
                    tr.op(eng, ["pre0", "cwT"], [an], lambda e: e.tensor_scalar(out=acc[:, 0:T], in0=pre[0][:, 0:T], scalar1=cwb(0), scalar2=None, op0=ALU.mult))
                    for tp in (1, 2, 3):
                        tr.op(eng, ["pre0", "cwT", an], [an], lambda e: e.scalar_tensor_tensor(out=acc[:, 0:T], in0=pre[0][:, tp:tp + T], scalar=cwb(tp), in1=acc[:, 0:T], op0=ALU.mult, op1=ALU.add))
                    tr.op(eng, ["xp4_0", "cwT"], [an], lambda e: e.tensor_scalar(out=acc[:, T:TT], in0=x4[:, :, 0], scalar1=cwb(0), scalar2=None, op0=ALU.mult))
                    for tp in (1, 2, 3):
                        tr.op(eng, ["xp4_0", "cwT", an], [an], lambda e: e.scalar_tensor_tensor(out=acc[:, T:TT], in0=x4[:, :, tp], scalar=cwb(tp), in1=acc[:, T:TT], op0=ALU.mult, op1=ALU.add))
                    tr.op("act", [an], ["cv0"], lambda e: e.activation(out=cv[0][:], in_=acc[:], func=AF.Silu))
                    if fb < 2:
                        dstT, dn, scl = ((qT, "qT", 128 ** -0.5), (kT, "kT", 1.0))[fb]
                        sqb = pre[0][:, 3:3 + TT]
                        rinv = tmpc
                        tr.op("act", ["cv0"], ["pre0"], lambda e: e.activation(out=sqb, in_=cv[0][:], func=AF.Square))
                        for t0 in range(0, TT, 512):
                            n = min(512, TT - t0)
                            p, pn = pf()
                            tr.op("pe", ["cst", "pre0"], [pn], lambda e: e.matmul(p[:, 0:n], lhsT=ones, rhs=sqb[:, t0:t0 + n], start=True, stop=True))
                            tr.op("dve", [pn], ["tmpc"], lambda e: e.tensor_scalar(out=rinv[:, t0:t0 + n], in0=p[:, 0:n], scalar1=1e-6, scalar2=None, op0=ALU.add))
                            tr.op("act", ["tmpc"], ["tmpc"], lambda e: e.activation(out=rinv[:, t0:t0 + n], in_=rinv[:, t0:t0 + n], func=AF.Sqrt))
                            tr.op("dve", ["tmpc"], ["tmpc"], lambda e: e.reciprocal(out=rinv[:, t0:t0 + n], in_=rinv[:, t0:t0 + n]))
                        tr.op("dve", ["cv0", "tmpc"], [dn], lambda e: e.scalar_tensor_tensor(out=dstT[:], in0=cv[0][:], scalar=scl, in1=rinv[:], op0=ALU.mult, op1=ALU.mult))
                    else:
                        hh = fb - 2
                        tr.op("pool", ["cv0"], ["cvb%d" % hh], lambda e: e.tensor_copy(out=cvb[hh][:], in_=cv[0][:]))
                chk(4)
                for nm, Cc, nch, zt, zn in (("p", 128, c.NCH, zba, "zba"), ("s", 1, NS, zbas, "zbas")):
                    for ci in range(nch):
                        c0 = ci * 128 if nm == "p" else T + ci
                        p, pn = pf()
                        for k in range(KD):
                            tr.op("pe", [wn, "hT"], [pn], lambda e: e.matmul(p[0:Cc, 0:260], lhsT=hT3[:, k, c0:c0 + Cc], rhs=w3[:, k, 512:772], start=(k == 0), stop=(k == KD - 1)))
                        tr.op("act", [pn], [zn], lambda e: e.activation(out=zt[0:Cc, ci * 260:(ci + 1) * 260], in_=p[0:Cc, 0:260], func=AF.Copy))
                    z3 = r3(zt[0:Cc, 0:nch * 260], nch, 260)
                    Gt = lambda f: r3(gates[(nm, f)][:, 0:nch * 2], nch, 2)
                    gn = lambda f: "gt_%s_%s" % (nm, f)
                    tr.op("act", [zn], [gn("beta")], lambda e: e.activation(out=Gt("beta")[0:Cc], in_=z3[:, :, 256:258], func=AF.Sigmoid))
                    for hh in range(2):
                        tr.op("act", [zn, "dtb_bc"], [gn("tmp")], lambda e: e.activation(out=Gt("tmp")[0:Cc, :, hh], in_=z3[:, :, 258 + hh], func=AF.Exp, bias=dtb_bc[0:Cc, 2 * j + hh:2 * j + hh + 1]))
                    tr.op("act", [gn("tmp")], [gn("tmp")], lambda e: e.activation(out=gates[(nm, "tmp")][0:Cc, 0:nch * 2], in_=gates[(nm, "tmp")][0:Cc, 0:nch * 2], func=AF.Ln, bias=1.0))
                    for hh in range(2):
                        tr.op("dve", [gn("tmp"), "negA"], [gn("g")], lambda e: e.tensor_scalar(out=Gt("g")[0:Cc, :, hh], in0=Gt("tmp")[0:Cc, :, hh], scalar1=negA[0:Cc, 2 * j + hh:2 * j + hh + 1], scalar2=None, op0=ALU.mult))
                    p, pn = pf()
                    n2 = nch * 2
                    gg = gates[(nm, "g")]
                    tr.op("pe", ["cst", gn("g")], [pn], lambda e: e.matmul(p[0:Cc, 0:n2], lhsT=triU[0:Cc, 0:Cc], rhs=gg[0:Cc, 0:n2], start=True, stop=True))
                    tr.op("pe", ["cst", gn("g")], [pn], lambda e: e.matmul(p[0:Cc, 64:64 + n2], lhsT=ones[0:Cc, 0:Cc], rhs=gg[0:Cc, 0:n2], start=True, stop=True))
                    tr.op("pe", ["cst", gn("g")], [pn], lambda e: e.matmul(p[:, 128:128 + n2], lhsT=ones[0:Cc, :], rhs=gg[0:Cc, 0:n2], start=True, stop=True))
                    gt = lambda f: gates[(nm, f)]
                    tr.op("dve", [pn], [gn("gc")], lambda e: e.tensor_copy(out=gt("gc")[0:Cc, 0:n2], in_=p[0:Cc, 0:n2]))
                    tr.op("act", [pn], [gn("egc")], lambda e: e.activation(out=gt("egc")[0:Cc, 0:n2], in_=p[0:Cc, 0:n2], func=AF.Exp))
                    tr.op("dve", [gn("egc"), gn("beta")], [gn("bg")], lambda e: e.tensor_tensor(out=gt("bg")[0:Cc, 0:n2], in0=gt("egc")[0:Cc, 0:n2], in1=gt("beta")[0:Cc, 0:n2], op=ALU.mult))
                    tr.op("dve", [pn, gn("gc")], [gn("gl")], lambda e: e.tensor_tensor(out=gt("gl")[0:Cc, 0:n2], in0=p[0:Cc, 64:64 + n2], in1=gt("gc")[0:Cc, 0:n2], op=ALU.subtract))
                    tr.op("act", [gn("gl")], [gn("ekd")], lambda e: e.activation(out=gt("ekd")[0:Cc, 0:n2], in_=gt("gl")[0:Cc, 0:n2], func=AF.Exp))
                    tr.op("act", [pn], [gn("egl128")], lambda e: e.activation(out=gt("egl128")[:, 0:n2], in_=p[:, 128:128 + n2], func=AF.Exp))
                chk(5)
                def lane_gen(hh, part):
                    ln = LANES[hh]
                    seq = 0
                    for ci in range(c.NCH):
                        yield from chunk(ln, part, seq, "p", 128, ci, ci * 128, j, hh, ci == 0, ci == c.NCH - 1, None)
                        seq += 1
                    for n_ in range(NS):
                        yield from chunk(ln, part, seq, "s", 1, n_, T + n_, j, hh, False, False, n_)
                        seq += 1
                    if part == "C":
                        h = 2 * j + hh
                        tr.dma("sp", ln["choT"], ["oTst%d" % hh], ["oT_scr"], oT_scr[h * 128:(h + 1) * 128, :], ln["oTst"][:])

                for ln_ in LANES:
                    ln_["A_done"] = 0
                    ln_["B_done"] = 0
                    ln_["C_done"] = 0
                active = [lane_gen(0, "A"), lane_gen(1, "A"), lane_gen(0, "B"), lane_gen(1, "B"), lane_gen(0, "C"), lane_gen(1, "C")]
                while active:
                    for g_ in list(active):
                        try:
                            next(g_)
                        except StopIteration:
                            active.remove(g_)

            phase_end()
            chk(20)

            def outproj(srcT, sname, KS, wsrc, resid, rname, dst, dname, tag):
                phase_begin()
                CB = min(1024, D)
                NWB = 1 if CB > 512 else 2
                wo = [sb("wo%s%d" % (tag, i), [128, KS * CB], BF16) for i in range(NWB)]
                woc = [tr.chan() for i in range(NWB)]
                ot = [sb("ot%s%d" % (tag, i), [128, KS * 128], BF16) for i in range(2)]
                otc = [tr.chan() for i in range(2)]
                xr = [sb("xr%s%d" % (tag, i), [128, CB]) for i in range(2)]
                xrc = [tr.chan() for i in range(2)]
                it = 0
                for cb in range(D // CB):
                    wb = cb % NWB
                    tr.dma("pool", woc[wb], [], ["wo%s%d" % (tag, wb)], r3(wo[wb][:], KS, CB), wsrc[:, cb * CB:(cb + 1) * CB].rearrange("(k p) n -> p k n", p=128))
                    w3_ = r3(wo[wb][:], KS, CB)
                    for (t0, n) in tok_tiles():
                        b = it % 2
                        it += 1
                        o3 = r3(ot[b][:], KS, 128)
                        tr.dma("sp", otc[b], [sname], ["ot%s%d" % (tag, b)], o3[:, :, 0:n], srcT[:, t0:t0 + n].rearrange("(k p) t -> p k t", p=128))
                        tr.dma("sp", xrc[b], [rname], ["xr%s%d" % (tag, b)], xr[b][0:n, :], resid[t0:t0 + n, cb * CB:(cb + 1) * CB])
                        for h0 in range(0, CB, 512):
                            hn = min(512, CB - h0)
                            p, pn = pf()
                            for k in range(KS):
                                tr.op("pe", ["ot%s%d" % (tag, b), "wo%s%d" % (tag, wb)], [pn], lambda e: e.matmul(p[0:n, 0:hn], lhsT=o3[:, k, 0:n], rhs=w3_[:, k, h0:h0 + hn], start=(k == 0), stop=(k == KS - 1)))
                            tr.op("dve", [pn, "xr%s%d" % (tag, b)], ["xr%s%d" % (tag, b)], lambda e: e.tensor_tensor(out=xr[b][0:n, h0:h0 + hn], in0=p[0:n, 0:hn], in1=xr[b][0:n, h0:h0 + hn], op=ALU.add))
                        tr.dma("sp", xrc[b], ["xr%s%d" % (tag, b)], [dname], dst[t0:t0 + n, cb * CB:(cb + 1) * CB], xr[b][0:n, :])
                phase_end()

            outproj(oT_scr, "oT_scr", c.VAL // 128, w_out_gdn, xin, "xin", x1_scr, "x1_scr", "a")
            chk(21)
            norm_to_hT(x1_scr, norm_ssm, "x1_scr")
            chk(22)

            phase_begin()
            GB = min(16, G)
            NT = GB // 8
            NCK = T // 8
            NC1 = 1 + NCK
            PI = math.pi

            def ew(e, out, in0, in1, op, r=(), w=()):
                tr.op(e, list(r), list(w), lambda en: en.tensor_tensor(out=out, in0=in0, in1=in1, op=op))

            tb = {k: sb("s5_" + k, [64, G]) for k in ("lamr", "lami", "dt", "ar", "ai", "fr", "fi", "t1", "t2", "t3")}
            lt = sb("s5_lt", [128, 64])
            chl = tr.chan()
            chdt = tr.chan()
            chdc = tr.chan()
            chBi = tr.chan()
            chcr = tr.chan()
            for (src_, dst_) in ((lam_re, "lamr"), (lam_im, "lami")):
                for r0 in range(0, G, 128):
                    nr = min(128, G - r0)
                    tr.dma("sp", chl, [], ["s5_lt"], lt[0:nr, :], src_[r0:r0 + nr, :])
                    p, pn = pf()
                    tr.op("pe", ["s5_lt", "cst"], [pn], lambda e: e.transpose(out=p[0:64, 0:nr], in_=lt[0:nr, :], identity=ident[0:nr, 0:nr]))
                    tr.op("dve", [pn], ["s5_" + dst_], lambda e: e.tensor_copy(out=tb[dst_][:, r0:r0 + nr], in_=p[0:64, 0:nr]))
            tr.dma("sp", chdt, [], ["s5_dt"], tb["dt"][:], log_dt[0:1, :].broadcast_to([64, G]))
            tr.op("act", ["s5_dt"], ["s5_dt"], lambda e: e.activation(out=tb["dt"][:], in_=tb["dt"][:], func=AF.Exp))
            tr.op("dve", ["s5_lamr"], ["s5_lamr"], lambda e: e.tensor_scalar(out=tb["lamr"][:], in0=tb["lamr"][:], scalar1=-1e-4, scalar2=None, op0=ALU.min))
            ew("dve", tb["t1"][:], tb["lamr"][:], tb["dt"][:], ALU.mult, ["s5_lamr", "s5_dt"], ["s5_t1"])
            tr.op("act", ["s5_t1"], ["s5_t1"], lambda e: e.activation(out=tb["t1"][:], in_=tb["t1"][:], func=AF.Exp))
            ew("dve", tb["t2"][:], tb["lami"][:], tb["dt"][:], ALU.mult, ["s5_lami", "s5_dt"], ["s5_t2"])
            tr.op("act", ["s5_t2"], ["s5_ai"], lambda e: e.activation(out=tb["ai"][:], in_=tb["t2"][:], func=AF.Sin, scale=1.0 / 32))
            tr.op("dve", ["s5_t2"], ["s5_t3"], lambda e: e.tensor_scalar(out=tb["t3"][:], in0=tb["t2"][:], scalar1=1.0 / 32, scalar2=PI / 2, op0=ALU.mult, op1=ALU.add))
            tr.op("act", ["s5_t3"], ["s5_ar"], lambda e: e.activation(out=tb["ar"][:], in_=tb["t3"][:], func=AF.Sin))
            for _ in range(5):
                ew("dve", tb["t3"][:], tb["ar"][:], tb["ar"][:], ALU.mult, ["s5_ar"], ["s5_t3"])
                ew("dve", tb["t2"][:], tb["ai"][:], tb["ai"][:], ALU.mult, ["s5_ai"], ["s5_t2"])
                tr.op("dve", ["s5_ar", "s5_ai"], ["s5_ai"], lambda e: e.scalar_tensor_tensor(out=tb["ai"][:], in0=tb["ar"][:], scalar=2.0, in1=tb["ai"][:], op0=ALU.mult, op1=ALU.mult))
                ew("dve", tb["ar"][:], tb["t3"][:], tb["t2"][:], ALU.subtract, ["s5_t3", "s5_t2"], ["s5_ar"])
            ew("dve", tb["ar"][:], tb["ar"][:], tb["t1"][:], ALU.mult, ["s5_ar", "s5_t1"], ["s5_ar"])
            ew("dve", tb["ai"][:], tb["ai"][:], tb["t1"][:], ALU.mult, ["s5_ai", "s5_t1"], ["s5_ai"])
            tr.op("dve", ["s5_ar"], ["s5_t1"], lambda e: e.tensor_scalar(out=tb["t1"][:], in0=tb["ar"][:], scalar1=-1.0, scalar2=None, op0=ALU.add))
            ew("dve", tb["t2"][:], tb["lamr"][:], tb["lamr"][:], ALU.mult, ["s5_lamr"], ["s5_t2"])
            ew("dve", tb["t3"][:], tb["lami"][:], tb["lami"][:], ALU.mult, ["s5_lami"], ["s5_t3"])
            ew("dve", tb["t2"][:], tb["t2"][:], tb["t3"][:], ALU.add, ["s5_t2", "s5_t3"], ["s5_t2"])
            tr.op("dve", ["s5_t2"], ["s5_t2"], lambda e: e.reciprocal(out=tb["t2"][:], in_=tb["t2"][:]))
            ew("dve", tb["fr"][:], tb["t1"][:], tb["lamr"][:], ALU.mult, ["s5_t1", "s5_lamr"], ["s5_fr"])
            ew("dve", tb["t3"][:], tb["ai"][:], tb["lami"][:], ALU.mult, ["s5_ai", "s5_lami"], ["s5_t3"])
            ew("dve", tb["fr"][:], tb["fr"][:], tb["t3"][:], ALU.add, ["s5_fr", "s5_t3"], ["s5_fr"])
            ew("dve", tb["fr"][:], tb["fr"][:], tb["t2"][:], ALU.mult, ["s5_fr", "s5_t2"], ["s5_fr"])
            ew("dve", tb["fi"][:], tb["ai"][:], tb["lamr"][:], ALU.mult, ["s5_ai", "s5_lamr"], ["s5_fi"])
            ew("dve", tb["t3"][:], tb["t1"][:], tb["lami"][:], ALU.mult, ["s5_t1", "s5_lami"], ["s5_t3"])
            ew("dve", tb["fi"][:], tb["fi"][:], tb["t3"][:], ALU.subtract, ["s5_fi", "s5_t3"], ["s5_fi"])
            ew("dve", tb["fi"][:], tb["fi"][:], tb["t2"][:], ALU.mult, ["s5_fi", "s5_t2"], ["s5_fi"])

            dcol = sb("s5_dcol", [128, c.KW])
            with nc.allow_non_contiguous_dma(reason="tiny per-channel vector"):
                tr.dma("sp", chdc, [], ["s5_dcol"], dcol[:], d_ssm.rearrange("(k p) o -> p (k o)", p=128))
            wuy = sb("s5_wu", [128, max(KD * GB * 16, NT * TT)], BF16)
            wu3 = r3(wuy[:, 0:KD * GB * 16], KD, GB * 16)
            chwu = tr.chan()
            uT = sb("s5_uT", [128, NT * TT], BF16)
            uT3 = r3(uT[:], NT, TT)
            yT3 = r3(wuy[:, 0:NT * TT], NT, TT)
            chy = tr.chan()
            PWR = sb("s5_pwr", [64, 9 * GB])
            PWI = sb("s5_pwi", [64, 9 * GB])
            pw_r = lambda m: PWR[:, m * GB:(m + 1) * GB]
            pw_i = lambda m: PWI[:, m * GB:(m + 1) * GB]
            GC = GB * 16
            Bt = {k: sb("s5_" + k, [64, GC]) for k in ("Br", "Bi", "Bbr", "Bbi", "CTr", "CTi", "e1", "e2")}
            chB = tr.chan()
            XR = sb("s5_XR", [64, 8 * GC], BF16)
            XI = sb("s5_XI", [64, 8 * GC], BF16)
            CPR = sb("s5_CPR", [64, 8 * GC], BF16)
            CPI = sb("s5_CPI", [64, 8 * GC], BF16)
            CTrb = sb("s5_CTrb", [64, GC], BF16)
            NCTib = sb("s5_NCTib", [64, GC], BF16)
            crow = sb("s5_crow", [128, 64])
            Kbd = sb("s5_Kbd", [128, 8 * 128], BF16)
            YP = [sb("s5_YP%d" % i, [128, 8 * 8 * 64], BF16) for i in range(2)]
            YTs = sb("s5_YTs", [128, 128], BF16)
            CPpr = sb("s5_CPpr", [64, 8 * 128], BF16)
            CPpi = sb("s5_CPpi", [64, 8 * 128], BF16)
            VB = sb("s5_VB", [64, 2 * GB * NC1])
            VB4 = VB[:].rearrange("p (a g n) -> p a g n", a=2, g=GB, n=NC1)
            XH = sb("s5_XH", [64, 2 * GB * NCK], BF16)
            XH4 = XH[:].rearrange("p (a g n) -> p a g n", a=2, g=GB, n=NCK)
            A8a = sb("s5_A8a", [64, 2 * GB])
            A8b = sb("s5_A8b", [64, 2 * GB])
            A1a = sb("s5_A1a", [64, 2 * GB])
            A1b = sb("s5_A1b", [64, 2 * GB])
            sc1 = sb("s5_sc1", [64, 2 * GB])
            sc2 = sb("s5_sc2", [64, 2 * GB])
            VS = sb("s5_VS", [64, 2 * NS * GB])
            VS4 = VS[:].rearrange("p (a n g) -> p a n g", a=2, n=NS, g=GB)
            XS = sb("s5_XS", [64, 2 * NS * GB])
            XS4 = XS[:].rearrange("p (a n g) -> p a n g", a=2, n=NS, g=GB)
            XSb = sb("s5_XSb", [64, 2 * NS * GB], BF16)
            XSb4 = XSb[:].rearrange("p (a n g) -> p a n g", a=2, n=NS, g=GB)
            stmp = sb("s5_stmp", [64, max(2 * NS * GB, 2 * GB * 32)])
            ss1 = stmp
            APW = sb("s5_APW", [64, int(math.log2(NCK)) * 4 * GB])
            srow = sb("s5_srow", [128, 64])
            chs = tr.chan()
            orow = sb("s5_orow", [128, 64])
            cho = tr.chan()
            ytmp = sb("s5_ytmp", [128, max(NCK + NS, 256)])
            YT8 = ytmp[:, 0:256].bitcast(BF16)

            def bc3(ap2, n):
                return ap2.unsqueeze(2).to_broadcast([64, GB, n])

            for blk in range(G // GB):
                g0 = blk * GB
                ch0 = g0 * 16
                tr.dma("pool", chwu, [], ["s5_wu"], wu3, w_in_ssm[:, ch0:ch0 + GC].rearrange("(k p) n -> p k n", p=128))
                for tl in range(NT):
                    for t0 in range(0, TT, 512):
                        n = min(512, TT - t0)
                        p, pn = pf()
                        for k in range(KD):
                            tr.op("pe", ["s5_wu", "hT"], [pn], lambda e: e.matmul(p[:, 0:n], lhsT=wu3[:, k, tl * 128:(tl + 1) * 128], rhs=hT3[:, k, t0:t0 + n], start=(k == 0), stop=(k == KD - 1)))
                        tr.op("act", [pn], ["s5_uT"], lambda e: e.activation(out=uT3[:, tl, t0:t0 + n], in_=p[:, 0:n], func=AF.Copy))
                tr.op("pool", [], ["s5_pwr"], lambda e: e.memset(pw_r(0), 1.0))
                tr.op("pool", [], ["s5_pwi"], lambda e: e.memset(pw_i(0), 0.0))
                arb = tb["ar"][:, g0:g0 + GB]
                aib = tb["ai"][:, g0:g0 + GB]
                e1 = Bt["e1"][:, 0:GB]
                e2 = Bt["e2"][:, 0:GB]
                for m in range(8):
                    ew("dve", e1, pw_r(m), arb, ALU.mult, ["s5_pwr", "s5_ar"], ["s5_e1"])
                    ew("dve", e2, pw_i(m), aib, ALU.mult, ["s5_pwi", "s5_ai"], ["s5_e2"])
                    ew("dve", pw_r(m + 1), e1, e2, ALU.subtract, ["s5_e1", "s5_e2"], ["s5_pwr"])
                    ew("dve", e1, pw_r(m), aib, ALU.mult, ["s5_pwr", "s5_ai"], ["s5_e1"])
                    ew("dve", e2, pw_i(m), arb, ALU.mult, ["s5_pwi", "s5_ar"], ["s5_e2"])
                    ew("dve", pw_i(m + 1), e1, e2, ALU.add, ["s5_e1", "s5_e2"], ["s5_pwi"])
                for (Aa, Ab, an, bn, m) in ((A8a, A8b, "s5_A8a", "s5_A8b", 8), (A1a, A1b, "s5_A1a", "s5_A1b", 1)):
                    tr.op("dve", ["s5_pwr"], [an], lambda e: e.tensor_copy(out=Aa[:, 0:GB], in_=pw_r(m)))
                    tr.op("dve", ["s5_pwi"], [an], lambda e: e.tensor_copy(out=Aa[:, GB:2 * GB], in_=pw_i(m)))
                    tr.op("dve", ["s5_pwi"], [bn], lambda e: e.tensor_scalar(out=Ab[:, 0:GB], in0=pw_i(m), scalar1=-1.0, scalar2=None, op0=ALU.mult))
                    tr.op("dve", ["s5_pwr"], [bn], lambda e: e.tensor_copy(out=Ab[:, GB:2 * GB], in_=pw_r(m)))
                B3 = lambda k: r3(Bt[k][:], GB, 16)
                tr.dma("sp", chB, [], ["s5_Br"], B3("Br"), b_re[g0:g0 + GB].rearrange("g p c -> p g c"))
                tr.dma("sp", chBi, [], ["s5_Bi"], B3("Bi"), b_im[g0:g0 + GB].rearrange("g p c -> p g c"))
                frb = bc3(tb["fr"][:, g0:g0 + GB], 16)
                fib = bc3(tb["fi"][:, g0:g0 + GB], 16)
                ew("dve", B3("Bbr"), B3("Br"), frb, ALU.mult, ["s5_Br", "s5_fr"], ["s5_Bbr"])
                ew("dve", B3("e1"), B3("Bi"), fib, ALU.mult, ["s5_Bi", "s5_fi"], ["s5_e1"])
                ew("dve", B3("Bbr"), B3("Bbr"), B3("e1"), ALU.subtract, ["s5_Bbr", "s5_e1"], ["s5_Bbr"])
                ew("dve", B3("Bbi"), B3("Bi"), frb, ALU.mult, ["s5_Bi", "s5_fr"], ["s5_Bbi"])
                ew("dve", B3("e1"), B3("Br"), fib, ALU.mult, ["s5_Br", "s5_fi"], ["s5_e1"])
                ew("dve", B3("Bbi"), B3("Bbi"), B3("e1"), ALU.add, ["s5_Bbi", "s5_e1"], ["s5_Bbi"])
                for tau in range(8):
                    prb = bc3(pw_r(tau), 16)
                    pib = bc3(pw_i(tau), 16)
                    xr_o = r3(XR[:, tau * GC:(tau + 1) * GC], GB, 16)
                    xi_o = r3(XI[:, tau * GC:(tau + 1) * GC], GB, 16)
                    ew("dve", B3("e1"), B3("Bbr"), prb, ALU.mult, ["s5_Bbr", "s5_pwr"], ["s5_e1"])
                    ew("pool", B3("e2"), B3("Bbi"), pib, ALU.mult, ["s5_Bbi", "s5_pwi"], ["s5_e2"])
                    ew("dve", xr_o, B3("e1"), B3("e2"), ALU.subtract, ["s5_e1", "s5_e2"], ["s5_XR"])
                    ew("dve", B3("e1"), B3("Bbi"), prb, ALU.mult, ["s5_Bbi", "s5_pwr"], ["s5_e1"])
                    ew("pool", B3("e2"), B3("Bbr"), pib, ALU.mult, ["s5_Bbr", "s5_pwi"], ["s5_e2"])
                    ew("dve", xi_o, B3("e1"), B3("e2"), ALU.add, ["s5_e1", "s5_e2"], ["s5_XI"])
                for (src_, dk) in ((c_re, "CTr"), (c_im, "CTi")):
                    rows = src_[g0:g0 + GB].rearrange("g c p -> (g c) p")
                    for r0 in range(0, GC, 128):
                        tr.dma("sp", chcr, [], ["s5_crow"], crow[:], rows[r0:r0 + 128, :])
                        p, pn = pf()
                        tr.op("pe", ["s5_crow", "cst"], [pn], lambda e: e.transpose(out=p[0:64, 0:128], in_=crow[:], identity=ident))
                        tr.op("dve", [pn], ["s5_" + dk], lambda e: e.tensor_copy(out=Bt[dk][:, r0:r0 + 128], in_=p[0:64, 0:128]))
                tr.op("dve", ["s5_CTr"], ["s5_CTrb"], lambda e: e.tensor_copy(out=CTrb[:], in_=Bt["CTr"][:]))
                tr.op("dve", ["s5_CTi"], ["s5_NCTib"], lambda e: e.tensor_scalar(out=NCTib[:], in0=Bt["CTi"][:], scalar1=-1.0, scalar2=None, op0=ALU.mult))
                for r_ in range(8):
                    prb = bc3(pw_r(r_ + 1), 16)
                    pib = bc3(pw_i(r_ + 1), 16)
                    cr_o = r3(CPR[:, r_ * GC:(r_ + 1) * GC], GB, 16)
                    ci_o = r3(CPI[:, r_ * GC:(r_ + 1) * GC], GB, 16)
                    ew("dve", B3("e1"), B3("CTr"), prb, ALU.mult, ["s5_CTr", "s5_pwr"], ["s5_e1"])
                    ew("pool", B3("e2"), B3("CTi"), pib, ALU.mult, ["s5_CTi", "s5_pwi"], ["s5_e2"])
                    ew("dve", cr_o, B3("e1"), B3("e2"), ALU.subtract, ["s5_e1", "s5_e2"], ["s5_CPR"])
                    ew("dve", B3("e1"), B3("CTr"), pib, ALU.mult, ["s5_CTr", "s5_pwi"], ["s5_e1"])
                    ew("pool", B3("e2"), B3("CTi"), prb, ALU.mult, ["s5_CTi", "s5_pwr"], ["s5_e2"])
                    ew("dve", B3("e1"), B3("e1"), B3("e2"), ALU.add, ["s5_e1", "s5_e2"], ["s5_e1"])
                    tr.op("dve", ["s5_e1"], ["s5_CPI"], lambda e: e.tensor_scalar(out=ci_o, in0=B3("e1"), scalar1=-1.0, scalar2=None, op0=ALU.mult))
                tr.op("pool", [], ["s5_VB"], lambda e: e.memset(VB[:], 0.0))
                for tl in range(NT):
                    for part, Xs, xn in ((0, XR, "s5_XR"), (1, XI, "s5_XI")):
                        Y4 = YP[part][:].rearrange("p (t g s) -> p t g s", t=8, g=8, s=64)
                        ypn = "s5_YP%d" % part
                        p, pn = pb()
                        for tau in range(8):
                            tr.op("pe", [xn, "cstb"], [pn], lambda e: e.transpose(out=p[:, tau * 64:(tau + 1) * 64], in_=Xs[:, tau * GC + tl * 128: tau * GC + (tl + 1) * 128], identity=identb[0:64, 0:64]))
                        tr.op("act", [pn], ["s5_ytmp"], lambda e: e.activation(out=YT8, in_=p[:, 0:512], func=AF.Copy))
                        tr.op("dve", ["s5_ytmp", "cst"], [ypn], lambda e: e.tensor_tensor(out=Y4, in0=YT8.rearrange("p (t s) -> p t s", t=8, s=64).unsqueeze(2).to_broadcast([128, 8, 8, 64]), in1=GSEL.unsqueeze(1).unsqueeze(3).to_broadcast([128, 8, 8, 64]), op=ALU.mult))
                        for g in range(8):
                            p, pn = pf()
                            for tau in range(8):
                                tr.op("pe", [ypn, "s5_uT"], [pn], lambda e: e.matmul(p[0:64, 0:NCK], lhsT=Y4[:, tau, g, :], rhs=uT3[:, tl, (7 - tau):T:8], start=(tau == 0), stop=(tau == 7)))
                            tr.op("pe", [ypn, "s5_uT"], [pn], lambda e: e.matmul(p[0:64, NCK:NCK + NS], lhsT=Y4[:, 0, g, :], rhs=uT3[:, tl, T:TT], start=True, stop=True))
                            tr.op("act", [pn], ["s5_VB"], lambda e: e.activation(out=VB4[:, part, tl * 8 + g, 1:1 + NCK], in_=p[0:64, 0:NCK], func=AF.Copy))
                            tr.op("dve", [pn], ["s5_VS"], lambda e: e.tensor_copy(out=VS4[:, part, :, tl * 8 + g], in_=p[0:64, NCK:NCK + NS]))
                chk(30)
                A8a3 = A8a[:].rearrange("p (a g) -> p a g", a=2, g=GB)
                A8b3 = A8b[:].rearrange("p (a g) -> p a g", a=2, g=GB)
                LV = int(math.log2(NCK))
                assert (1 << LV) == NCK
                PWT = 32
                APW4 = APW[:].rearrange("p (l q g) -> p l q g", l=LV, q=4, g=GB)
                tr.op("dve", ["s5_A8a"], ["s5_APW"], lambda e: e.tensor_copy(out=APW4[:, 0, 0:2, :], in_=A8a3))
                tr.op("dve", ["s5_A8b"], ["s5_APW"], lambda e: e.tensor_copy(out=APW4[:, 0, 2:4, :], in_=A8b3))
                for l in range(1, LV):
                    pr_, pi_ = APW4[:, l - 1, 0, :], APW4[:, l - 1, 1, :]
                    ew("dve", sc1[:, 0:GB], pr_, pr_, ALU.mult, ["s5_APW"], ["s5_sc1"])
                    ew("dve", sc1[:, GB:2 * GB], pi_, pi_, ALU.mult, ["s5_APW"], ["s5_sc1"])
                    ew("dve", APW4[:, l, 0, :], sc1[:, 0:GB], sc1[:, GB:2 * GB], ALU.subtract, ["s5_sc1"], ["s5_APW"])
                    tr.op("dve", ["s5_APW"], ["s5_APW"], lambda e: e.scalar_tensor_tensor(out=APW4[:, l, 1, :], in0=pr_, scalar=2.0, in1=pi_, op0=ALU.mult, op1=ALU.mult))
                    tr.op("dve", ["s5_APW"], ["s5_APW"], lambda e: e.tensor_copy(out=APW4[:, l, 3, :], in_=APW4[:, l, 0, :]))
                    tr.op("dve", ["s5_APW"], ["s5_APW"], lambda e: e.tensor_scalar(out=APW4[:, l, 2, :], in0=APW4[:, l, 1, :], scalar1=-1.0, scalar2=None, op0=ALU.mult))

                def cacc(l, tgt_sl, src_sl, cnt):
                    for q0 in range(0, cnt, PWT):
                        qn = min(PWT, cnt - q0)
                        t_lo, t_st = tgt_sl
                        s_lo, s_st = src_sl
                        tg = VB4[:, :, :, t_lo + q0 * t_st: t_lo + (q0 + qn - 1) * t_st + 1: t_st]
                        sr = VB4[:, 0:1, :, s_lo + q0 * s_st: s_lo + (q0 + qn - 1) * s_st + 1: s_st].to_broadcast([64, 2, GB, qn])
                        si = VB4[:, 1:2, :, s_lo + q0 * s_st: s_lo + (q0 + qn - 1) * s_st + 1: s_st].to_broadcast([64, 2, GB, qn])
                        pa = APW4[:, l, 0:2, :].unsqueeze(3).to_broadcast([64, 2, GB, qn])
                        pb_ = APW4[:, l, 2:4, :].unsqueeze(3).to_broadcast([64, 2, GB, qn])
                        tm = stmp[:, 0:2 * GB * qn].rearrange("p (a g n) -> p a g n", a=2, g=GB, n=qn)
                        ew("dve", tm, pa, sr, ALU.mult, ["s5_APW", "s5_VB"], ["s5_stmp"])
                        ew("dve", tg, tg, tm, ALU.add, ["s5_VB", "s5_stmp"], ["s5_VB"])
                        ew("dve", tm, pb_, si, ALU.mult, ["s5_APW", "s5_VB"], ["s5_stmp"])
                        ew("dve", tg, tg, tm, ALU.add, ["s5_VB", "s5_stmp"], ["s5_VB"])

                for l in range(LV):
                    s_ = 1 << l
                    cacc(l, (2 * s_, 2 * s_), (s_, 2 * s_), NCK // (2 * s_))
                for l in range(LV - 2, -1, -1):
                    s_ = 1 << l
                    cacc(l, (3 * s_, 2 * s_), (2 * s_, 2 * s_), NCK // (2 * s_) - 1)
                tr.op("act", ["s5_VB"], ["s5_XH"], lambda e: e.activation(out=XH4, in_=VB4[:, :, :, 0:NCK], func=AF.Copy))
                for part, dst_, dn_ in ((0, re_p, "re_p"), (1, im_p, "im_p")):
                    tr.op("dve", ["s5_VB"], ["s5_sc1"], lambda e: e.tensor_copy(out=sc1[:, 0:GB], in_=VB4[:, part, :, NCK]))
                    p, pn = pf()
                    tr.op("pe", ["s5_sc1", "cst"], [pn], lambda e: e.transpose(out=p[0:GB, 0:64], in_=sc1[:, 0:GB], identity=ident[0:64, 0:64]))
                    tr.op("act", [pn], ["s5_orow"], lambda e: e.activation(out=orow[0:GB, :], in_=p[0:GB, 0:64], func=AF.Copy))
                    tr.dma("sp", cho, ["s5_orow"], [dn_], dst_[g0:g0 + GB, :], orow[0:GB, :])
                RW = min(128, NS * GB)
                for part, src_ in ((0, re_s), (1, im_s)):
                    for r0 in range(0, NS * GB, RW):
                        n0 = r0 // GB
                        nn = RW // GB
                        tr.dma("sp", chs, [], ["s5_srow"], srow[0:RW, :], src_[n0:n0 + nn, g0:g0 + GB, :])
                        p, pn = pf()
                        tr.op("pe", ["s5_srow", "cst"], [pn], lambda e: e.transpose(out=p[0:64, 0:RW], in_=srow[0:RW, :], identity=ident[0:RW, 0:RW]))
                        tr.op("dve", [pn], ["s5_XS"], lambda e: e.tensor_copy(out=XS[:, part * NS * GB + r0: part * NS * GB + r0 + RW], in_=p[0:64, 0:RW]))
                tr.op("act", ["s5_XS"], ["s5_XSb"], lambda e: e.activation(out=XSb[:], in_=XS[:], func=AF.Copy))
                A1a4 = A1a[:].rearrange("p (a g) -> p a g", a=2, g=GB).unsqueeze(2).to_broadcast([64, 2, NS, GB])
                A1b4 = A1b[:].rearrange("p (a g) -> p a g", a=2, g=GB).unsqueeze(2).to_broadcast([64, 2, NS, GB])
                ss4 = stmp[:, 0:2 * NS * GB].rearrange("p (a n g) -> p a n g", a=2, n=NS, g=GB)
                ew("dve", ss4, A1a4, XS4[:, 0:1].to_broadcast([64, 2, NS, GB]), ALU.mult, ["s5_A1a", "s5_XS"], ["s5_stmp"])
                ew("dve", VS4, VS4, ss4, ALU.add, ["s5_VS", "s5_stmp"], ["s5_VS"])
                ew("dve", ss4, A1b4, XS4[:, 1:2].to_broadcast([64, 2, NS, GB]), ALU.mult, ["s5_A1b", "s5_XS"], ["s5_stmp"])
                ew("dve", VS4, VS4, ss4, ALU.add, ["s5_VS", "s5_stmp"], ["s5_VS"])
                for part, dst_, dn_ in ((0, re_so, "re_so"), (1, im_so, "im_so")):
                    for r0 in range(0, NS * GB, RW):
                        n0 = r0 // GB
                        nn = RW // GB
                        p, pn = pf()
                        tr.op("pe", ["s5_VS", "cst"], [pn], lambda e: e.transpose(out=p[0:RW, 0:64], in_=VS[:, part * NS * GB + r0: part * NS * GB + r0 + RW], identity=ident[0:64, 0:64]))
                        tr.op("act", [pn], ["s5_orow"], lambda e: e.activation(out=orow[0:RW, :], in_=p[0:RW, 0:64], func=AF.Copy))
                        tr.dma("sp", cho, ["s5_orow"], [dn_], dst_[n0:n0 + nn, g0:g0 + GB, :], orow[0:RW, :])
                chk(31)
                for tl in range(NT):
                    for t4 in range(0, 8, 4):
                        p, pn = pf()
                        for tau in range(t4, t4 + 4):
                            o_ = p[:, (tau - t4) * 128:(tau - t4 + 1) * 128]
                            tr.op("pe", ["s5_XR", "s5_CTrb"], [pn], lambda e: e.matmul(o_, lhsT=XR[:, tau * GC + tl * 128: tau * GC + (tl + 1) * 128], rhs=CTrb[:, tl * 128:(tl + 1) * 128], start=True, stop=False))
                            tr.op("pe", ["s5_XI", "s5_NCTib"], [pn], lambda e: e.matmul(o_, lhsT=XI[:, tau * GC + tl * 128: tau * GC + (tl + 1) * 128], rhs=NCTib[:, tl * 128:(tl + 1) * 128], start=False, stop=True))
                        tr.op("dve", [pn, "cst"], ["s5_Kbd"], lambda e: e.tensor_tensor(out=r3(Kbd[:, t4 * 128:(t4 + 4) * 128], 4, 128), in0=r3(p[:, 0:512], 4, 128), in1=BD.unsqueeze(1).to_broadcast([128, 4, 128]), op=ALU.mult))
                    for r_ in range(8):
                        for CPs, CPp, cn in ((CPR, CPpr, "s5_CPpr"), (CPI, CPpi, "s5_CPpi")):
                            src4 = CPs[:, r_ * GC + tl * 128: r_ * GC + (tl + 1) * 128].rearrange("p (g c) -> p g c", g=8, c=16).unsqueeze(2).to_broadcast([64, 8, 8, 16])
                            gg4 = GG[0:64, :].rearrange("p (g h) -> p g h", g=8, h=8).unsqueeze(3).to_broadcast([64, 8, 8, 16])
                            tr.op("pool" if cn == "s5_CPpr" else "dve", ["s5_CPR", "s5_CPI", "cst"], [cn], lambda e: e.tensor_tensor(out=CPp[:].rearrange("p (g h c) -> p g h c", g=8, h=8, c=16), in0=src4, in1=gg4, op=ALU.mult))
                        p, pn = pf()
                        mms = [(Kbd[:, tau * 128:(tau + 1) * 128], uT3[:, tl, (r_ - tau):T:8], ["s5_Kbd", "s5_uT"]) for tau in range(r_ + 1)]
                        for g in range(8):
                            mms.append((CPpr[:, g * 128:(g + 1) * 128], XH4[:, 0, tl * 8 + g, :], ["s5_CPpr", "s5_XH"]))
                            mms.append((CPpi[:, g * 128:(g + 1) * 128], XH4[:, 1, tl * 8 + g, :], ["s5_CPpi", "s5_XH"]))
                        for i_, (l_, rh_, rd_) in enumerate(mms):
                            tr.op("pe", rd_, [pn], lambda e: e.matmul(p[:, 0:NCK], lhsT=l_, rhs=rh_, start=(i_ == 0), stop=(i_ == len(mms) - 1)))
                        dsc = dcol[:, blk * NT + tl: blk * NT + tl + 1]
                        tr.op("dve", [pn, "s5_uT", "s5_dcol"], ["s5_ytmp"], lambda e: e.scalar_tensor_tensor(out=ytmp[:, 0:NCK], in0=uT3[:, tl, r_:T:8], scalar=dsc, in1=p[:, 0:NCK], op0=ALU.mult, op1=ALU.add))
                        tr.op("act", ["s5_ytmp"], ["s5_wu"], lambda e: e.activation(out=yT3[:, tl, r_:T:8], in_=ytmp[:, 0:NCK], func=AF.Gelu))
                        if r_ == 0:
                            mms = [(Kbd[:, 0:128], uT3[:, tl, T:TT], ["s5_Kbd", "s5_uT"])]
                            for g in range(8):
                                mms.append((CPpr[:, g * 128:(g + 1) * 128], XSb4[:, 0, :, tl * 8 + g], ["s5_CPpr", "s5_XSb"]))
                                mms.append((CPpi[:, g * 128:(g + 1) * 128], XSb4[:, 1, :, tl * 8 + g], ["s5_CPpi", "s5_XSb"]))
                            p2, p2n = pf()
                            for i_, (l_, rh_, rd_) in enumerate(mms):
                                tr.op("pe", rd_, [p2n], lambda e: e.matmul(p2[:, 0:NS], lhsT=l_, rhs=rh_, start=(i_ == 0), stop=(i_ == len(mms) - 1)))
                            tr.op("dve", [p2n, "s5_uT", "s5_dcol"], ["s5_ytmp"], lambda e: e.scalar_tensor_tensor(out=ytmp[:, NCK:NCK + NS], in0=uT3[:, tl, T:TT], scalar=dsc, in1=p2[:, 0:NS], op0=ALU.mult, op1=ALU.add))
                            tr.op("act", ["s5_ytmp"], ["s5_wu"], lambda e: e.activation(out=yT3[:, tl, T:TT], in_=ytmp[:, NCK:NCK + NS], func=AF.Gelu))
                    tr.dma("sp", chy, ["s5_wu"], ["yT_scr"], yT_scr[ch0 + tl * 128: ch0 + (tl + 1) * 128, :], yT3[:, tl, :])
                chk(32)
            phase_end()
            chk(33)

            phase_begin()
            KW = c.KW
            TBM = min(TT, 1040)
            yTa = sb("g_yTa", [128, KW * TBM], BF16)
            yTa3 = r3(yTa[:], KW, TBM)
            chya = tr.chan()
            wg = [sb("g_wg%d" % i, [128, KW * 128], BF16) for i in range(2)]
            wgc = [tr.chan() for i in range(2)]
            wz = [sb("g_wz%d" % i, [128, KD * 128], BF16) for i in range(2)]
            wzc = [tr.chan() for i in range(2)]
            bgl = sb("g_bgl", [128, KW])
            chbg = tr.chan()
            with nc.allow_non_contiguous_dma(reason="tiny per-channel vector"):
                tr.dma("sp", chbg, [], ["g_bgl"], bgl[:], b_glu.rearrange("(k p) o -> p (k o)", p=128))
            sgt = sb("g_sg", [128, 512])
            szt = sb("g_sz", [128, 512])
            y2t = [sb("g_y2%d" % i, [128, TBM], BF16) for i in range(2)]
            y2c = [tr.chan() for i in range(2)]
            it = 0
            for b0 in range(0, TT, TBM):
                bn = min(TBM, TT - b0)
                tr.dma("sp", chya, ["yT_scr"], ["g_yTa"], yTa3[:, :, 0:bn], yT_scr[:, b0:b0 + bn].rearrange("(k p) t -> p k t", p=128))
                for m in range(KW):
                    wb = it % 2
                    it += 1
                    wg3 = r3(wg[wb][:], KW, 128)
                    wz3 = r3(wz[wb][:], KD, 128)
                    tr.dma("pool", wgc[wb], [], ["g_wg%d" % wb], wg3, w_glu[:, m * 128:(m + 1) * 128].rearrange("(k p) n -> p k n", p=128))
                    tr.dma("pool", wzc[wb], [], ["g_wz%d" % wb], wz3, w_in_ssm[:, c.W + m * 128: c.W + (m + 1) * 128].rearrange("(k p) n -> p k n", p=128))
                    for t0 in range(0, bn, 512):
                        n = min(512, bn - t0)
                        pg, pgn = pf()
                        for k in range(KW):
                            tr.op("pe", ["g_wg%d" % wb, "g_yTa"], [pgn], lambda e: e.matmul(pg[:, 0:n], lhsT=wg3[:, k, :], rhs=yTa3[:, k, t0:t0 + n], start=(k == 0), stop=(k == KW - 1)))
                        pz, pzn = pf()
                        for k in range(KD):
                            tr.op("pe", ["g_wz%d" % wb, "hT"], [pzn], lambda e: e.matmul(pz[:, 0:n], lhsT=wz3[:, k, :], rhs=hT3[:, k, b0 + t0:b0 + t0 + n], start=(k == 0), stop=(k == KD - 1)))
                        tr.op("act", [pgn, "g_bgl"], ["g_sg"], lambda e: e.activation(out=sgt[:, 0:n], in_=pg[:, 0:n], func=AF.Sigmoid, bias=bgl[:, m:m + 1]))
                        tr.op("act", [pzn], ["g_sz"], lambda e: e.activation(out=szt[:, 0:n], in_=pz[:, 0:n], func=AF.Silu))
                        tr.op("pool", ["g_sg", "g_sz"], ["g_sg"], lambda e: e.tensor_tensor(out=sgt[:, 0:n], in0=sgt[:, 0:n], in1=szt[:, 0:n], op=ALU.mult))
                        tr.op("dve", ["g_sg", "g_yTa"], ["g_y2%d" % wb], lambda e: e.tensor_tensor(out=y2t[wb][:, t0:t0 + n], in0=sgt[:, 0:n], in1=yTa3[:, m, t0:t0 + n], op=ALU.mult))
                    tr.dma("sp", y2c[wb], ["g_y2%d" % wb], ["y2_scr"], y2_scr[m * 128:(m + 1) * 128, b0:b0 + bn], y2t[wb][:, 0:bn])
            phase_end()
            chk(34)
            outproj(y2_scr, "y2_scr", c.KW, w_out_ssm, x1_scr, "x1_scr", x2_scr, "x2_scr", "b")
            chk(35)
            norm_to_hT(x2_scr, norm_final, "x2_scr", final_out=y_out)
        except _Stop:
            if cur[0] is not es:
                cur[0].close()
                cur[0] = es
        tr.finish()
    return nc


def make_consts():
    cs = np.zeros((128, 8 * 128), np.float32)
    i = np.arange(128)
    cs[:, 0:128] = np.eye(128)
    cs[:, 128:256] = (i[:, None] <= i[None, :])
    cs[:, 256:384] = 1.0
    cs[:, 384:512] = np.where(i[None, :] < i[:, None], 0.0, 30000.0)
    cs[:, 512:640] = np.where(i[:, None] <= i[None, :], 0.0, -30000.0)
    cs[:, 640:768] = ((i[None, :] // 16) >= (i[:, None] // 16))
    cs[:, 768:896] = ((i[None, :] // 16) == (i[:, None] // 16))
    cs[:, 896:904] = ((i[:, None] // 16) == np.arange(8)[None, :])
    cs[:, 904:968] = np.eye(8).reshape(1, 64)
    return cs


_NC_CACHE = {}


def kernel(x_prompt, x_sample, state_gdn_conv, state_gdn_delta, state_ssm_re, state_ssm_im,
           norm_gdn, w_in_gdn, conv_gdn, a_log_gdn, dt_bias_gdn, onorm_gdn, w_out_gdn,
           norm_ssm, w_in_ssm, lam_re, lam_im, b_re, b_im, c_re, c_im, d_ssm, log_dt_ssm,
           w_glu_ssm, b_glu_ssm, w_out_ssm, norm_final):
    cfg = Cfg(**FULL)
    f = lambda a: np.ascontiguousarray(np.asarray(a, dtype=np.float32))
    NS, T = cfg.NS, cfg.T
    B = x_prompt.shape[0]
    ncores = 8
    if "nc" not in _NC_CACHE:
        _NC_CACHE["nc"] = build(cfg)
    nc = _NC_CACHE["nc"]
    shared = {
        "norm_gdn": f(norm_gdn).reshape(1, -1), "w_in_gdn": f(w_in_gdn[0]), "conv_w": f(conv_gdn[0]),
        "a_log": f(a_log_gdn).reshape(1, -1), "dt_bias": f(dt_bias_gdn).reshape(1, -1), "onorm": f(onorm_gdn).reshape(1, -1),
        "w_out_gdn": f(w_out_gdn[0]), "norm_ssm": f(norm_ssm).reshape(1, -1), "w_in_ssm": f(w_in_ssm[0]),
        "lam_re": f(lam_re[0]), "lam_im": f(lam_im[0]), "b_re": f(b_re[0]), "b_im": f(b_im[0]), "c_re": f(c_re[0]), "c_im": f(c_im[0]),
        "d_ssm": f(d_ssm[0]).reshape(-1, 1), "log_dt": f(log_dt_ssm).reshape(1, -1), "w_glu": f(w_glu_ssm[0]),
        "b_glu": f(b_glu_ssm[0]).reshape(-1, 1), "w_out_ssm": f(w_out_ssm[0]), "norm_final": f(norm_final).reshape(1, -1),
        "consts": make_consts(),
    }
    in_maps = []
    for i in range(ncores):
        sq = i % B
        sl = slice(i * NS, (i + 1) * NS)
        m = dict(shared)
        m["xin"] = np.ascontiguousarray(np.concatenate([f(x_prompt[sq]), f(x_sample[sl, 0])], axis=0))
        m["conv_s"] = f(state_gdn_conv[0, sl])
        m["delta_s"] = f(state_gdn_delta[0, sl])
        m["re_s"] = f(state_ssm_re[0, sl])
        m["im_s"] = f(state_ssm_im[0, sl])
        in_maps.append(m)
    res = run_bass_kernel_spmd(nc, in_maps, core_ids=list(range(ncores))).results
    g = lambda i, k: np.asarray(res[i][k], dtype=np.float32)
    y_prompt = np.stack([g(i, "y_out")[:T] for i in range(B)])
    y_sample = np.concatenate([g(i, "y_out")[T:] for i in range(ncores)])[:, None, :]
    conv_prompt = np.stack([g(i, "conv_p") for i in range(B)])[None]
    delta_prompt = np.stack([g(i, "delta_p") for i in range(B)])[None]
    re_prompt = np.stack([g(i, "re_p") for i in range(B)])[None]
    im_prompt = np.stack([g(i, "im_p") for i in range(B)])[None]
    conv_sample = np.concatenate([g(i, "conv_so") for i in range(ncores)])[None]
    delta_sample = np.concatenate([g(i, "delta_so") for i in range(ncores)])[None]
    re_sample = np.concatenate([g(i, "re_so") for i in range(ncores)])[None]
    im_sample = np.concatenate([g(i, "im_so") for i in range(ncores)])[None]
    return (y_prompt, y_sample, conv_prompt, delta_prompt, re_prompt, im_prompt,
            conv_sample, delta_sample, re_sample, im_sample)
```

```python
import contextlib
import math
import numpy as np
import concourse.bass as bass
import concourse.mybir as mybir
from concourse.bass_utils import run_bass_kernel_spmd

F32 = mybir.dt.float32
BF16 = mybir.dt.bfloat16
AF = mybir.ActivationFunctionType
ALU = mybir.AluOpType
AX = mybir.AxisListType

FULL = dict(D=2048, T=2048, NS=16, HQK=16, G=256)


class Cfg:
    def __init__(self, D, T, NS, HQK, G):
        self.D, self.T, self.NS, self.HQK, self.G = D, T, NS, HQK, G
        self.KD = D // 128
        self.HV = 2 * HQK
        self.KEY = HQK * 128
        self.VAL = self.HV * 128
        self.CONV = 2 * self.KEY + self.VAL
        self.IN = self.CONV + self.VAL + 2 * self.HV
        self.W = 16 * G
        self.KW = self.W // 128
        self.TT = T + NS
        self.NCH = T // 128


class TR:
    def __init__(self, nc, es):
        self.nc, self.es = nc, es
        self.eng = dict(pe=nc.tensor, act=nc.scalar, dve=nc.vector, pool=nc.gpsimd, sp=nc.sync)
        self.sem = {}
        self.cnt = {}
        for k in ("pe", "act", "dve", "pool"):
            self.sem[k] = es.enter_context(nc.semaphore("s_" + k))
            self.cnt[k] = 0
        self.waited = {k: {} for k in self.eng}
        self.lastw = {}
        self.reads = {}
        self.nchan = 0

    def chan(self):
        self.nchan += 1
        k = "d%d" % self.nchan
        self.sem[k] = self.es.enter_context(self.nc.semaphore("s_" + k))
        self.cnt[k] = 0
        return k

    def _deps(self, e, reads, writes):
        deps = {}
        def add(ev):
            if ev is None:
                return
            k, v = ev
            if deps.get(k, 0) < v:
                deps[k] = v
        for r in reads:
            add(self.lastw.get(r))
            if r.startswith("pf") or r.startswith("pb"):
                for k, v in self.reads.get(r, {}).items():
                    if k != e:
                        add((k, v))
        for w in writes:
            add(self.lastw.get(w))
            for k, v in self.reads.get(w, {}).items():
                add((k, v))
        pend = []
        for k, v in deps.items():
            if k == "pe" and e == "pe":
                continue
            if self.waited[e].get(k, 0) >= v:
                continue
            pend.append((k, v))
            self.waited[e][k] = v
        for k, v in pend[:-1]:
            self.eng[e].wait_ge(self.sem[k], v)
        return pend[-1] if pend else None

    def _mark(self, ev, reads, writes):
        k, v = ev
        for r in reads:
            self.reads.setdefault(r, {})[k] = v
        for w in writes:
            self.lastw[w] = ev
            self.reads[w] = {}

    def op(self, e, reads, writes, fn):
        lw = self._deps(e, reads, writes)
        ins = fn(self.eng[e])
        if lw is not None:
            ins._wait_ge(self.sem[lw[0]], lw[1])
        self.cnt[e] += 1
        ins.then_inc(self.sem[e], 1)
        self._mark((e, self.cnt[e]), reads, writes)

    def dma(self, q, ch, reads, writes, out, in_, **kw):
        lw = self._deps(q, reads, writes)
        ins = self.eng[q].dma_start(out=out, in_=in_, **kw)
        if lw is not None:
            ins._wait_ge(self.sem[lw[0]], lw[1])
        ins.then_inc(self.sem[ch], 16)
        self.cnt[ch] += 16
        self._mark((ch, self.cnt[ch]), reads, writes)

    def barrier(self):
        for e in ("pe", "act", "dve", "pool", "sp"):
            for k in self.sem:
                if k != e and self.cnt[k] > 0 and self.waited[e].get(k, 0) < self.cnt[k]:
                    self.eng[e].wait_ge(self.sem[k], self.cnt[k])
                    self.waited[e][k] = self.cnt[k]

    def finish(self, q="sp"):
        for k in self.sem:
            if k.startswith("d") and self.cnt[k] > 0:
                self.eng[q].wait_ge(self.sem[k], self.cnt[k])
        for k in ("pe", "act", "dve", "pool"):
            if self.cnt[k] > 0:
                self.eng[q].wait_ge(self.sem[k], self.cnt[k])


def r3(ap, a, b):
    return ap.rearrange("p (a b) -> p a b", a=a, b=b)


class _Stop(Exception):
    pass


def build(cfg):
    c = cfg
    nc = bass.Bass("TRN2", target_bir_lowering=False)
    D, T, NS, TT, KD, HV, G = c.D, c.T, c.NS, c.TT, c.KD, c.HV, c.G

    def din(name, shape, dt=F32):
        return nc.dram_tensor(name, list(shape), dt, kind="ExternalInput").ap()

    def dout(name, shape, dt=F32):
        return nc.dram_tensor(name, list(shape), dt, kind="ExternalOutput").ap()

    def dscr(name, shape, dt):
        return nc.dram_tensor(name, list(shape), dt, kind="Internal").ap()

    xin = din("xin", [TT, D])
    conv_s = din("conv_s", [NS, 3, c.CONV])
    delta_s = din("delta_s", [NS, HV, 128, 128])
    re_s = din("re_s", [NS, G, 64])
    im_s = din("im_s", [NS, G, 64])
    norm_gdn = din("norm_gdn", [1, D])
    w_in_gdn = din("w_in_gdn", [D, c.IN])
    conv_w = din("conv_w", [4, c.CONV])
    a_log = din("a_log", [1, HV])
    dt_bias = din("dt_bias", [1, HV])
    onorm = din("onorm", [1, 128])
    w_out_gdn = din("w_out_gdn", [c.VAL, D])
    norm_ssm = din("norm_ssm", [1, D])
    w_in_ssm = din("w_in_ssm", [D, 2 * c.W])
    lam_re = din("lam_re", [G, 64])
    lam_im = din("lam_im", [G, 64])
    b_re = din("b_re", [G, 64, 16])
    b_im = din("b_im", [G, 64, 16])
    c_re = din("c_re", [G, 16, 64])
    c_im = din("c_im", [G, 16, 64])
    d_ssm = din("d_ssm", [c.W, 1])
    log_dt = din("log_dt", [1, G])
    w_glu = din("w_glu", [c.W, c.W])
    b_glu = din("b_glu", [c.W, 1])
    w_out_ssm = din("w_out_ssm", [c.W, D])
    norm_final = din("norm_final", [1, D])
    consts = din("consts", [128, 8 * 128])

    y_out = dout("y_out", [TT, D])
    conv_p = dout("conv_p", [3, c.CONV])
    delta_p = dout("delta_p", [HV, 128, 128])
    re_p = dout("re_p", [G, 64])
    im_p = dout("im_p", [G, 64])
    conv_so = dout("conv_so", [NS, 3, c.CONV])
    delta_so = dout("delta_so", [NS, HV, 128, 128])
    re_so = dout("re_so", [NS, G, 64])
    im_so = dout("im_so", [NS, G, 64])

    oT_scr = dscr("oT_scr", [c.VAL, TT], BF16)
    x1_scr = dscr("x1_scr", [TT, D], F32)
    yT_scr = dscr("yT_scr", [c.W, TT], BF16)
    y2_scr = dscr("y2_scr", [c.W, TT], BF16)
    x2_scr = dscr("x2_scr", [TT, D], F32)

    es = contextlib.ExitStack()
    with es:
        tr = TR(nc, es)
        cur = [es]
        try:

            def chk(k):
                if getattr(c, "stop", None) == k:
                    raise _Stop()

            def sb(name, shape, dt=F32):
                return cur[0].enter_context(nc.sbuf_tensor(name, list(shape), dt))

            def phase_begin():
                tr.barrier()
                cur[0] = contextlib.ExitStack()

            def phase_end():
                tr.barrier()
                cur[0].close()
                cur[0] = es

            def ps(name, shape, dt=F32):
                return es.enter_context(nc.psum_tensor(name, list(shape), dt))

            cst = sb("cst", [128, 8 * 128])
            ch_c = tr.chan()
            tr.dma("sp", ch_c, [], ["cst"], cst[:], consts[:, :])
            ident = cst[:, 0:128]
            triU = cst[:, 128:256]
            ones = cst[:, 256:384]
            MBIG = cst[:, 384:512]
            MNEG = cst[:, 512:640]
            CMASK = cst[:, 640:768]
            BD = cst[:, 768:896]
            GSEL = cst[:, 896:904]
            GG = cst[:, 904:968]
            cstb = sb("cstb", [128, 256], BF16)
            identb = cstb[:, 0:128]
            onesb = cstb[:, 128:256]
            tr.op("dve", ["cst"], ["cstb"], lambda e: e.tensor_copy(out=cstb[:, 0:128], in_=ident))
            tr.op("dve", ["cst"], ["cstb"], lambda e: e.tensor_copy(out=cstb[:, 128:256], in_=ones))

            def bcast_load(name, src, n):
                t = sb(name, [128, n])
                ch = tr.chan()
                tr.dma("sp", ch, [], [name], t[:], src[0:1, :].broadcast_to([128, n]))
                return t

            chg = tr.chan()
            nrm = {}
            alog_bc = bcast_load("alog_bc", a_log, HV)
            dtb_bc = bcast_load("dtb_bc", dt_bias, HV)
            ogain_bc = bcast_load("ogain_bc", onorm, 128)
            negA = sb("negA", [128, HV])
            tr.op("act", ["alog_bc"], ["negA"], lambda e: e.activation(out=negA[:], in_=alog_bc[:], func=AF.Exp))
            tr.op("dve", ["negA"], ["negA"], lambda e: e.tensor_scalar(out=negA[:], in0=negA[:], scalar1=-1.0, scalar2=None, op0=ALU.mult))

            hT = sb("hT", [128, KD * TT], BF16)
            hT3 = r3(hT[:], KD, TT)

            PF = [ps("pf%d" % i, [128, 512]) for i in range(6)]
            PB = [ps("pb%d" % i, [128, 1024], BF16) for i in range(2)]
            pf_i = [0]
            pb_i = [0]

            def pf():
                pf_i[0] = (pf_i[0] + 1) % len(PF)
                return PF[pf_i[0]], "pf%d" % pf_i[0]

            def pb():
                pb_i[0] = (pb_i[0] + 1) % len(PB)
                return PB[pb_i[0]], "pb%d" % pb_i[0]

            xtc = [tr.chan() for i in range(1)]
            stat = sb("stat", [128, 8])

            def tok_tiles():
                tl = [(i * 128, 128) for i in range(T // 128)]
                tl.append((T, NS))
                return tl

            def norm_to_hT(src, gsrc, srcname, addsrc=None, final_out=None):
                phase_begin()
                nrm["i"] = nrm.get("i", 0) + 1
                gbuf = sb("gbuf%d" % nrm["i"], [128, D])
                xt = [sb("xt%d_%d" % (nrm["i"], 0), [128, D])]
                hb = [sb("hb%d_%d" % (nrm["i"], 0), [128, D], BF16)]
                _norm_body(src, gsrc, srcname, final_out, gbuf, xt, hb)
                phase_end()

            def _norm_body(src, gsrc, srcname, final_out, gbuf, xt, hb):
                gain = gbuf
                gname = "gbuf"
                tr.dma("sp", chg, [], ["gbuf"], gbuf[:], gsrc[0:1, :].broadcast_to([128, D]))
                for it, (t0, n) in enumerate(tok_tiles()):
                    b = 0
                    tr.dma("sp", xtc[b], [srcname], ["xt%d" % b], xt[b][0:n, :], src[t0:t0 + n, :])
                    tr.op("act", ["xt%d" % b], ["hb%d" % b, "stat"], lambda e: e.activation(out=hb[b][0:n, :], in_=xt[b][0:n, :], func=AF.Square, accum_out=stat[0:n, 0:1]))
                    tr.op("dve", ["stat"], ["stat"], lambda e: e.tensor_scalar(out=stat[0:n, 1:2], in0=stat[0:n, 0:1], scalar1=1.0 / D, scalar2=1e-6, op0=ALU.mult, op1=ALU.add))
                    tr.op("act", ["stat"], ["stat"], lambda e: e.activation(out=stat[0:n, 3:4], in_=stat[0:n, 1:2], func=AF.Sqrt))
                    tr.op("dve", ["stat"], ["stat"], lambda e: e.reciprocal(out=stat[0:n, 2:3], in_=stat[0:n, 3:4]))
                    if final_out is not None:
                        tr.op("dve", ["xt%d" % b, "stat", gname], ["xt%d" % b], lambda e: e.scalar_tensor_tensor(out=xt[b][0:n, :], in0=xt[b][0:n, :], scalar=stat[0:n, 2:3], in1=gain[0:n, :], op0=ALU.mult, op1=ALU.mult))
                        tr.dma("sp", xtc[b], ["xt%d" % b], ["y_out"], final_out[t0:t0 + n, :], xt[b][0:n, :])
                        continue
                    tr.op("dve", ["xt%d" % b, "stat", gname], ["hb%d" % b], lambda e: e.scalar_tensor_tensor(out=hb[b][0:n, :], in0=xt[b][0:n, :], scalar=stat[0:n, 2:3], in1=gain[0:n, :], op0=ALU.mult, op1=ALU.mult))
                    for k0 in range(0, KD, 8):
                        kk = min(8, KD - k0)
                        p, pn = pb()
                        for k in range(kk):
                            tr.op("pe", ["hb%d" % b, "cstb"], [pn], lambda e: e.transpose(out=p[:, k * 128:k * 128 + n], in_=hb[b][0:n, (k0 + k) * 128:(k0 + k + 1) * 128], identity=identb[0:n, 0:n]))
                        tr.op("act" if (k0 // 8) % 2 == 0 else "dve", [pn], ["hT"],
                              (lambda e: e.activation(out=hT3[:, k0:k0 + kk, t0:t0 + n], in_=r3(p[:, 0:kk * 128], kk, 128)[:, :, 0:n], func=AF.Copy)) if (k0 // 8) % 2 == 0 else
                              (lambda e: e.tensor_copy(out=hT3[:, k0:k0 + kk, t0:t0 + n], in_=r3(p[:, 0:kk * 128], kk, 128)[:, :, 0:n])))

            chk(0)
            norm_to_hT(xin, norm_gdn, "xin")
            chk(1)

            phase_begin()
            NCOL = 772
            wj = [sb("wj%d" % i, [128, KD * NCOL], BF16) for i in range(1)]
            wjc = [tr.chan() for i in range(1)]
            NBLK = c.CONV // 128
            NR = 4 * NBLK
            cwT = sb("cwT", [128, NR])
            cwr = sb("cwr", [128, 128])
            chx = tr.chan()
            cw_rows = conv_w.rearrange("j (b c) -> (j b) c", c=128)
            for r0 in range(0, NR, 128):
                nr = min(128, NR - r0)
                tr.dma("sp", chx, [], ["cwr"], cwr[0:nr, :], cw_rows[r0:r0 + nr, :])
                p, pn = pf()
                tr.op("pe", ["cwr", "cst"], [pn], lambda e: e.transpose(out=p[:, 0:nr], in_=cwr[0:nr, :], identity=ident[0:nr, 0:nr]))
                tr.op("dve", [pn], ["cwT"], lambda e: e.tensor_copy(out=cwT[:, r0:r0 + nr], in_=p[:, 0:nr]))
            pre = [sb("pre0", [128, 3 + TT])] * 4
            if 3 + TT >= 1032:
                tail = pre[0][0:NS + 3, 8:520]
                cst48 = pre[0][0:NS * 3, 520:1032]
            else:
                tail = sb("tailx", [NS + 3, 512])[:, :]
                cst48 = sb("cst48x", [NS * 3, 512])[:, :]
            xp4 = [sb("xp4_0", [128, NS * 4])] * 4
            xs3 = sb("xs3", [128, 4 * NS * 3])
            ch48 = tr.chan()
            cv = [sb("cv0", [128, TT])] * 4
            tmpc = sb("tmpc", [128, TT])
            qT = sb("qT", [128, TT], BF16)
            kT = sb("kT", [128, TT], BF16)
            chtail = tr.chan()
            zba = sb("zba", [128, (c.NCH) * 260], BF16)
            zbas = sb("zbas", [1, NS * 260], BF16)
            gates = {}
            for nm, Cc, nch in (("p", 128, c.NCH), ("s", 1, NS)):
                for f in ("beta", "g", "gc", "egc", "bg", "ekd", "gl", "egl128", "tmp"):
                    gates[(nm, f)] = sb("gt_%s_%s" % (nm, f), [128, nch * 2])
            LANES = []
            for li in range(2):
                ln = {}
                ln["Sst"] = sb("Sst%d" % li, [128, 128]); ln["Sbf"] = sb("Sbf%d" % li, [128, 128], BF16)
                ln["chS"] = tr.chan(); ln["chSo"] = tr.chan(); ln["choT"] = tr.chan()
                ln["oTst"] = sb("oTst%d" % li, [128, TT], BF16)
                Wl = {}
                for nm, dt in (("kbg", F32), ("E1", F32), ("E2", F32), ("L", F32), ("N", F32),
                               ("P", F32), ("L2a", F32), ("L2b", F32), ("N2a", F32), ("N2b", F32), ("vn", BF16),
                               ("av", F32), ("gz", F32), ("og", BF16), ("sq", BF16)):
                    Wl[nm] = sb("w%d_%s" % (li, nm), [128, 128], dt)
                Wl["dg"] = sb("w%d_dg" % li, [128, 128])
                Wo = dict(Wl)
                for nm in ("kbg", "E1", "E2", "L", "N", "P", "L2a", "L2b", "N2a", "N2b", "dg"):
                    Wo[nm] = sb("w%do_%s" % (li, nm), [128, 128], F32)
                ln["WA"] = [Wl, Wo]
                ln["Adone"] = set()
                ln["H"] = []
                for par in range(2):
                    Hd = {}
                    for nm, dt in (("vb", F32), ("kd", BF16), ("AT", BF16), ("u", F32), ("wT", BF16)):
                        Hd[nm] = sb("h%d_%d_%s" % (li, par, nm), [128, 128], dt)
                    ln["H"].append(Hd)
                ln["A_done"] = 0
                ln["B_done"] = 0
                ln["C_done"] = 0
                ln["O"] = [sb("ho%d_%d" % (li, par), [128, 128]) for par in range(2)]
                ln["SS"] = [(sb("Sss%d_%d" % (li, par), [128, 128]), sb("Ssb%d_%d" % (li, par), [128, 128], BF16), tr.chan(), tr.chan()) for par in range(2)]
                ln["W"] = Wl
                ln["colst"] = sb("colst%d" % li, [128, 8])
                ln["id"] = li
                ln["pfb"] = [3 * li, 3 * li + 1, 3 * li + 2]
                ln["pfi"] = [0]
                ln["pbb"] = li
                LANES.append(ln)

            def load_wj(j, b):
                base = wj[b]
                w3 = r3(base[:], KD, NCOL)
                segs = [(0, j * 128, 128), (128, c.KEY + j * 128, 128), (256, 2 * c.KEY + j * 256, 256),
                        (512, c.CONV + j * 256, 256), (768, c.CONV + c.VAL + 2 * j, 2), (770, c.CONV + c.VAL + HV + 2 * j, 2)]
                for (o, s0, n) in segs:
                    tr.dma("pool", wjc[b], [], ["wj%d" % b], w3[:, :, o:o + n], w_in_gdn[:, s0:s0 + n].rearrange("(k p) n -> p k n", p=128))

            def chunk(ln, part, seq, nm, Cc, ci, cols, j, hh, first, last, n_idx):
                c0 = cols
                W = ln["W"]; Sst = ln["Sst"]; Sbf = ln["Sbf"]; chS = ln["chS"]; chSo = ln["chSo"]; oTst = ln["oTst"]; colst = ln["colst"]
                WN = "w%d_" % ln["id"]; SN = "Sst%d" % ln["id"]; BN = "Sbf%d" % ln["id"]; ON = "oTst%d" % ln["id"]; CN = "colst%d" % ln["id"]

                H = ln["H"][seq % 2]
                HN = "h%d_%d_" % (ln["id"], seq % 2)
                KO = 256 * (seq % 2)
                if part == "A":
                    W = ln["WA"][seq % 2]
                    WN = "w%d%s_" % (ln["id"], "o" if seq % 2 else "")

                def pf():
                    if part == "B":
                        i_ = ln["pfb"][2]
                    else:
                        i_ = ln["pfb"][seq % 2]
                    return PF[i_], "pf%d" % i_

                def pb():
                    return PB[ln["pbb"]], "pb%d" % ln["pbb"]
                h = 2 * j + hh
                gi = ci * 2 + hh
                G_ = lambda f: gates[(nm, f)]
                gname = lambda f: "gt_%s_%s" % (nm, f)
                if part == "A":
                    while ln["B_done"] < seq - 1:
                        yield "blocked"
                    p, pn = pb()
                    tr.op("pe", ["kT", "cstb"], [pn], lambda e: e.transpose(out=p[0:Cc, KO:KO + 128], in_=kT[:, c0:c0 + Cc], identity=identb))
                    yield
                    tr.op("pe", ["cvb%d" % hh, "cstb"], [pn], lambda e: e.transpose(out=p[0:Cc, KO + 128:KO + 256], in_=cvb[hh][:, c0:c0 + Cc], identity=identb))
                    yield
                    tr.op("dve", [pn, gname("beta")], [HN + "vb"], lambda e: e.tensor_scalar(out=H["vb"][0:Cc, :], in0=p[0:Cc, KO + 128:KO + 256], scalar1=G_("beta")[0:Cc, gi:gi + 1], scalar2=None, op0=ALU.mult))
                    yield
                    tr.op("dve", [pn, gname("bg")], [WN + "kbg"], lambda e: e.tensor_scalar(out=W["kbg"][0:Cc, :], in0=p[0:Cc, KO:KO + 128], scalar1=G_("bg")[0:Cc, gi:gi + 1], scalar2=None, op0=ALU.mult))
                    yield
                    tr.op("act", [pn, gname("ekd")], [HN + "kd"], lambda e: e.activation(out=H["kd"][0:Cc, :], in_=p[0:Cc, KO:KO + 128], func=AF.Copy, scale=G_("ekd")[0:Cc, gi:gi + 1]))
                    yield
                    chk(10)
                    if Cc > 1:
                        pk, pkn = pf()
                        tr.op("pe", ["kT"], [pkn], lambda e: e.matmul(pk[0:Cc, 0:Cc], lhsT=kT[:, c0:c0 + Cc], rhs=kT[:, c0:c0 + Cc], start=True, stop=True))
                        yield
                        tr.op("pe", ["kT", "qT"], [pkn], lambda e: e.matmul(pk[0:Cc, 128:128 + Cc], lhsT=kT[:, c0:c0 + Cc], rhs=qT[:, c0:c0 + Cc], start=True, stop=True))
                        yield
                        tr.op("dve", ["cst", gname("gc")], [WN + "dg"], lambda e: e.tensor_scalar(out=W["dg"][0:Cc, 0:Cc], in0=ident[0:Cc, 0:Cc], scalar1=G_("gc")[0:Cc, gi:gi + 1], scalar2=None, op0=ALU.mult))
                        yield
                        tr.op("pe", ["cst", WN + "dg"], [pkn], lambda e: e.matmul(pk[0:Cc, 256:256 + Cc], lhsT=ones[0:Cc, 0:Cc], rhs=W["dg"][0:Cc, 0:Cc], start=True, stop=True))
                        yield
                        R = pk[0:Cc, 256:256 + Cc]
                        chk(11)
                        tr.op("dve", [pkn, gname("gc"), "cst"], [WN + "E1"], lambda e: e.scalar_tensor_tensor(out=W["E1"][0:Cc, 0:Cc], in0=R, scalar=G_("gc")[0:Cc, gi:gi + 1], in1=MBIG[0:Cc, 0:Cc], op0=ALU.subtract, op1=ALU.max))
                        yield
                        tr.op("dve", [pkn, gname("gc"), "cst"], [WN + "E2"], lambda e: e.scalar_tensor_tensor(out=W["E2"][0:Cc, 0:Cc], in0=R, scalar=G_("gc")[0:Cc, gi:gi + 1], in1=MNEG[0:Cc, 0:Cc], op0=ALU.subtract, op1=ALU.min))
                        yield
                        tr.op("act", [WN + "E1"], [WN + "E1"], lambda e: e.activation(out=W["E1"][0:Cc, 0:Cc], in_=W["E1"][0:Cc, 0:Cc], func=AF.Exp, scale=-1.0))
                        yield
                        tr.op("act", [WN + "E2"], [WN + "E2"], lambda e: e.activation(out=W["E2"][0:Cc, 0:Cc], in_=W["E2"][0:Cc, 0:Cc], func=AF.Exp))
                        yield
                        tr.op("dve", [pkn, gname("beta"), WN + "E1"], [WN + "L"], lambda e: e.scalar_tensor_tensor(out=W["L"][0:Cc, 0:Cc], in0=pk[0:Cc, 0:Cc], scalar=G_("beta")[0:Cc, gi:gi + 1], in1=W["E1"][0:Cc, 0:Cc], op0=ALU.mult, op1=ALU.mult))
                        yield
                        tr.op("dve", [pkn, WN + "E2"], [HN + "AT"], lambda e: e.tensor_tensor(out=H["AT"][0:Cc, 0:Cc], in0=pk[0:Cc, 128:128 + Cc], in1=W["E2"][0:Cc, 0:Cc], op=ALU.mult))
                        yield
                    else:
                        pk, pkn = pf()
                        tr.op("pe", ["kT", "qT"], [pkn], lambda e: e.matmul(pk[0:1, 128:129], lhsT=kT[:, c0:c0 + 1], rhs=qT[:, c0:c0 + 1], start=True, stop=True))
                        yield
                        tr.op("dve", [pkn], [HN + "AT"], lambda e: e.tensor_copy(out=H["AT"][0:1, 0:1], in_=pk[0:1, 128:129]))
                        yield
                    chk(12)
                    if Cc > 1:
                        p2, p2n = pf()
                        tr.op("pe", [WN + "L", "cst"], [p2n], lambda e: e.transpose(out=p2[0:Cc, 0:Cc], in_=W["L"][0:Cc, 0:Cc], identity=ident[0:Cc, 0:Cc]))
                        yield
                        tr.op("act", [p2n], [WN + "N"], lambda e: e.activation(out=W["N"][0:Cc, 0:Cc], in_=p2[0:Cc, 0:Cc], func=AF.Copy))
                        yield
                        tr.op("dve", ["cstb", WN + "N"], [WN + "P"], lambda e: e.tensor_tensor(out=W["P"][0:Cc, 0:Cc], in0=ident[0:Cc, 0:Cc], in1=W["N"][0:Cc, 0:Cc], op=ALU.subtract))
                        yield
                        chk(120)
                        Lk, Nk = "L", "N"
                        nsteps = int(math.log2(Cc)) - 1
                        for st in range(nsteps):
                            L2, N2 = ("L2a", "N2a") if st % 2 == 0 else ("L2b", "N2b")
                            pq, pqn = pf()
                            tr.op("pe", [WN + Nk, WN + Lk], [pqn], lambda e: e.matmul(pq[0:Cc, 0:Cc], lhsT=W[Nk][0:Cc, 0:Cc], rhs=W[Lk][0:Cc, 0:Cc], start=True, stop=True))
                            yield
                            if st < nsteps - 1:
                                tr.op("pe", [WN + Nk, WN + Lk], [pqn], lambda e: e.matmul(pq[0:Cc, 128:128 + Cc], lhsT=W[Lk][0:Cc, 0:Cc], rhs=W[Nk][0:Cc, 0:Cc], start=True, stop=True))
                                yield
                            tr.op("act", [pqn], [WN + L2], lambda e: e.activation(out=W[L2][0:Cc, 0:Cc], in_=pq[0:Cc, 0:Cc], func=AF.Copy))
                            yield
                            if st < nsteps - 1:
                                tr.op("dve", [pqn], [WN + N2], lambda e: e.tensor_copy(out=W[N2][0:Cc, 0:Cc], in_=pq[0:Cc, 128:128 + Cc]))
                                yield
                            tr.op("pe", [WN + L2, WN + "P"], [pqn], lambda e: e.matmul(pq[0:Cc, 256:256 + Cc], lhsT=W[L2][0:Cc, 0:Cc], rhs=W["P"][0:Cc, 0:Cc], start=True, stop=True))
                            yield
                            tr.op("dve", [pqn, WN + "P"], [WN + "P"], lambda e: e.tensor_tensor(out=W["P"][0:Cc, 0:Cc], in0=pq[0:Cc, 256:256 + Cc], in1=W["P"][0:Cc, 0:Cc], op=ALU.add))
                            yield
                            Lk, Nk = L2, N2
                            chk(121 + st)
                        TTm, TTn = W["P"][0:Cc, 0:Cc], WN + "P"
                    else:
                        TTm, TTn = ident[0:1, 0:1], "cst"
                    chk(13)
                    pu, pun = pf()
                    if Cc > 1:
                        tr.op("pe", [TTn, HN + "vb"], [pun], lambda e: e.matmul(pu[0:Cc, 0:128], lhsT=TTm, rhs=H["vb"][0:Cc, :], start=True, stop=True))
                        yield
                        tr.op("act", [pun], [HN + "u"], lambda e: e.activation(out=H["u"][0:Cc, :], in_=pu[0:Cc, 0:128], func=AF.Copy))
                        yield
                        Uap, Un = H["u"], HN + "u"
                    else:
                        Uap, Un = H["vb"], HN + "vb"
                    tr.op("pe", [TTn, WN + "kbg"], [pun], lambda e: e.matmul(pu[:, 128:128 + Cc], lhsT=W["kbg"][0:Cc, :], rhs=TTm, start=True, stop=True))
                    yield
                    tr.op("act", [pun], [HN + "wT"], lambda e: e.activation(out=H["wT"][:, 0:Cc], in_=pu[:, 128:128 + Cc], func=AF.Copy))
                    yield
                    chk(14)
                    ln["Adone"].add(seq)
                    return
                Otile = ln["O"][seq % 2]
                OnN = "ho%d_%d" % (ln["id"], seq % 2)
                if nm == "s":
                    Sst, Sbf, chS, chSo = ln["SS"][seq % 2]
                    SN = "Sss%d_%d" % (ln["id"], seq % 2)
                    BN = "Ssb%d_%d" % (ln["id"], seq % 2)
                if part == "C":
                    while ln["B_done"] <= seq:
                        yield "blocked"
                else:
                    while (seq not in ln["Adone"]) or ln["C_done"] < seq - 1:
                        yield "blocked"
                    if Cc > 1:
                        Uap, Un = H["u"], HN + "u"
                    else:
                        Uap, Un = H["vb"], HN + "vb"
                    if nm == "s":
                        tr.dma("sp", chS, ["delta_s"], [SN], Sst[:], delta_s[n_idx, h, :, :])
                        yield
                        tr.op("act", [SN], [BN], lambda e: e.activation(out=Sbf[:], in_=Sst[:], func=AF.Copy))
                        yield
                    elif first:
                        tr.op("pool", [], [SN], lambda e: e.memset(Sst[:], 0.0))
                        yield
                        tr.op("pool", [], [BN], lambda e: e.memset(Sbf[:], 0.0))
                        yield
                    pw, pwn = pf()
                    tr.op("pe", [HN + "wT", BN], [pwn], lambda e: e.matmul(pw[0:Cc, 0:128], lhsT=H["wT"][:, 0:Cc], rhs=Sbf[:], start=True, stop=True))
                    yield
                    tr.op("pe", ["qT", BN], [pwn], lambda e: e.matmul(pw[0:Cc, 128:256], lhsT=qT[:, c0:c0 + Cc], rhs=Sbf[:], start=True, stop=True))
                    yield
                    tr.op("dve", [pwn, Un], [WN + "vn"], lambda e: e.tensor_tensor(out=W["vn"][0:Cc, :], in0=Uap[0:Cc, :], in1=pw[0:Cc, 0:128], op=ALU.subtract))
                    yield
                    tr.op("pe", [HN + "kd", WN + "vn"], [pwn], lambda e: e.matmul(pw[:, 384:512], lhsT=H["kd"][0:Cc, :], rhs=W["vn"][0:Cc, :], start=True, stop=True))
                    yield
                    tr.op("pe", [HN + "AT", WN + "vn"], [pwn], lambda e: e.matmul(pw[0:Cc, 256:384], lhsT=H["AT"][0:Cc, 0:Cc], rhs=W["vn"][0:Cc, :], start=True, stop=True))
                    yield
                    tr.op("dve", [pwn, SN, gname("egl128")], [SN], lambda e: e.scalar_tensor_tensor(out=Sst[:], in0=Sst[:], scalar=G_("egl128")[:, gi:gi + 1], in1=pw[:, 384:512], op0=ALU.mult, op1=ALU.add))
                    yield
                    if nm == "s":
                        tr.dma("sp", chSo, [SN], ["delta_so"], delta_so[n_idx, h, :, :], Sst[:])
                        yield
                    elif last:
                        tr.dma("sp", chSo, [SN], ["delta_p"], delta_p[h, :, :], Sst[:])
                        yield
                    else:
                        tr.op("act", [SN], [BN], lambda e: e.activation(out=Sbf[:], in_=Sst[:], func=AF.Copy))
                        yield
                    tr.op("act", [pwn], [WN + "av"], lambda e: e.activation(out=W["av"][0:Cc, :], in_=pw[0:Cc, 256:384], func=AF.Copy))
                    yield
                    tr.op("dve", [pwn, WN + "av", gname("egc")], [OnN], lambda e: e.scalar_tensor_tensor(out=Otile[0:Cc, :], in0=pw[0:Cc, 128:256], scalar=G_("egc")[0:Cc, gi:gi + 1], in1=W["av"][0:Cc, :], op0=ALU.mult, op1=ALU.add))
                    yield
                    ln["B_done"] = seq + 1
                    return
                zsrc = (zba if nm == "p" else zbas)
                zname = "zba" if nm == "p" else "zbas"
                zap = zsrc[0:Cc, ci * 260 + hh * 128: ci * 260 + hh * 128 + 128]
                tr.op("act", [OnN], [WN + "sq", CN], lambda e: e.activation(out=W["sq"][0:Cc, :], in_=Otile[0:Cc, :], func=AF.Square, accum_out=colst[0:Cc, 0:1]))
                yield
                tr.op("dve", [CN], [CN], lambda e: e.tensor_scalar(out=colst[0:Cc, 1:2], in0=colst[0:Cc, 0:1], scalar1=1.0 / 128, scalar2=1e-6, op0=ALU.mult, op1=ALU.add))
                yield
                tr.op("act", [CN], [CN], lambda e: e.activation(out=colst[0:Cc, 3:4], in_=colst[0:Cc, 1:2], func=AF.Sqrt))
                yield
                tr.op("dve", [CN], [CN], lambda e: e.reciprocal(out=colst[0:Cc, 2:3], in_=colst[0:Cc, 3:4]))
                yield
                tr.op("act", [zname], [WN + "gz"], lambda e: e.activation(out=W["gz"][0:Cc, :], in_=zap, func=AF.Silu))
                yield
                tr.op("pool", [WN + "gz", "ogain_bc"], [WN + "gz"], lambda e: e.tensor_tensor(out=W["gz"][0:Cc, :], in0=W["gz"][0:Cc, :], in1=ogain_bc[0:Cc, :], op=ALU.mult))
                yield
                tr.op("dve", [OnN, CN, WN + "gz"], [WN + "og"], lambda e: e.scalar_tensor_tensor(out=W["og"][0:Cc, :], in0=Otile[0:Cc, :], scalar=colst[0:Cc, 2:3], in1=W["gz"][0:Cc, :], op0=ALU.mult, op1=ALU.mult))
                yield
                p3, p3n = pb()
                tr.op("pe", [WN + "og", "cstb"], [p3n], lambda e: e.transpose(out=p3[:, 512:512 + Cc], in_=W["og"][0:Cc, :], identity=identb[0:Cc, 0:Cc]))
                yield
                tr.op("act", [p3n], [ON], lambda e: e.activation(out=oTst[:, c0:c0 + Cc], in_=p3[:, 512:512 + Cc], func=AF.Copy))
                yield
                ln["C_done"] = seq + 1

            cvb = [sb("cvb%d" % i, [128, TT], BF16) for i in range(2)]

            for j in range(c.HQK):
                chk(100 + j)
                b = 0
                load_wj(j, b)
                w3 = r3(wj[b][:], KD, NCOL)
                wn = "wj%d" % b
                chk(2)
                for (lo, m, dname) in ((T - 3, 3, "conv_p"), (T, NS, "conv_so")):
                    p, pn = pf()
                    for k in range(KD):
                        tr.op("pe", [wn, "hT"], [pn], lambda e: e.matmul(p[0:m, 0:512], lhsT=hT3[:, k, lo:lo + m], rhs=w3[:, k, 0:512], start=(k == 0), stop=(k == KD - 1)))
                    tr.op("act", [pn], ["pre0"], lambda e: e.activation(out=tail[0:m, :], in_=p[0:m, 0:512], func=AF.Copy))
                    for (o, s0, n) in ((0, j * 128, 128), (128, c.KEY + j * 128, 128), (256, 2 * c.KEY + j * 256, 256)):
                        if dname == "conv_p":
                            tr.dma("sp", chtail, ["pre0"], ["conv_p"], conv_p[0:3, s0:s0 + n], tail[0:3, o:o + n])
                        else:
                            tr.dma("sp", chtail, ["pre0"], ["conv_so"], conv_so[:, 2, s0:s0 + n], tail[0:NS, o:o + n])
                            tr.dma("sp", chtail, [], ["conv_so"], conv_so[:, 0:2, s0:s0 + n], conv_s[:, 1:3, s0:s0 + n])
                chk(3)
                for (o, s0, n) in ((0, j * 128, 128), (128, c.KEY + j * 128, 128), (256, 2 * c.KEY + j * 256, 256)):
                    tr.dma("sp", ch48, [], ["pre0"], cst48[:, o:o + n], conv_s[:, :, s0:s0 + n].rearrange("n j c -> (n j) c"))
                x4 = r3(xp4[0][:], NS, 4)
                for fb in range(4):
                    p, pn = pf()
                    tr.op("pe", ["pre0", "cst"], [pn], lambda e: e.transpose(out=p[:, 0:NS * 3], in_=cst48[:, fb * 128:(fb + 1) * 128], identity=ident[0:NS * 3, 0:NS * 3]))
                    tr.op("dve", [pn], ["xs3"], lambda e: e.tensor_copy(out=xs3[:, fb * NS * 3:(fb + 1) * NS * 3], in_=p[:, 0:NS * 3]))
                tr.op("pool", [], ["pre0"], lambda e: e.memset(pre[0][:, 0:3], 0.0))
                for fb in range(4):
                    for t0 in range(0, TT, 512):
                        n = min(512, TT - t0)
                        p, pn = pf()
                        for k in range(KD):
                            tr.op("pe", [wn, "hT"], [pn], lambda e: e.matmul(p[:, 0:n], lhsT=w3[:, k, fb * 128:(fb + 1) * 128], rhs=hT3[:, k, t0:t0 + n], start=(k == 0), stop=(k == KD - 1)))
                        np_ = max(0, min(n, T - t0))
                        if np_ > 0:
                            tr.op("act", [pn], ["pre0"], lambda e: e.activation(out=pre[0][:, 3 + t0:3 + t0 + np_], in_=p[:, 0:np_], func=AF.Copy))
                        if np_ < n:
                            s0 = t0 + np_ - T
                            ns_ = n - np_
                            tr.op("dve", [pn], ["xp4_0"], lambda e: e.tensor_copy(out=x4[:, s0:s0 + ns_, 3], in_=p[:, np_:n]))
                    tr.op("dve", ["xs3"], ["xp4_0"], lambda e: e.tensor_copy(out=x4[:, :, 0:3], in_=r3(xs3[:, fb * NS * 3:(fb + 1) * NS * 3], NS, 3)))
                    blk = [j, c.KEY // 128 + j, 2 * c.KEY // 128 + 2 * j, 2 * c.KEY // 128 + 2 * j + 1][fb]
                    cwb = lambda tp: cwT[:, tp * NBLK + blk:tp * NBLK + blk + 1]
                    acc, an, eng = tmpc, "tmpc", "dve"
                    tr.op(eng, ["pre0", "cwT"], [an], lambda e: e.tensor_scalar(out=acc[:, 0:T], in0=pre[0][:, 0:T], scalar1=cwb(0), scalar2=None, op0=ALU.mult))
                    for tp in (1, 2, 3):
                        tr.op(eng, ["pre0", "cwT", an], [an], lambda e: e.scalar_tensor_tensor(out=acc[:, 0:T], in0=pre[0][:, tp:tp + T], scalar=cwb(tp), in1=acc[:, 0:T], op0=ALU.mult, op1=ALU.add))
                    tr.op(eng, ["xp4_0", "cwT"], [an], lambda e: e.tensor_scalar(out=acc[:, T:TT], in0=x4[:, :, 0], scalar1=cwb(0), scalar2=None, op0=ALU.mult))
                    for tp in (1, 2, 3):
                        tr.op(eng, ["xp4_0", "cwT", an], [an], lambda e: e.scalar_tensor_tensor(out=acc[:, T:TT], in0=x4[:, :, tp], scalar=cwb(tp), in1=acc[:, T:TT], op0=ALU.mult, op1=ALU.add))
                    tr.op("act", [an], ["cv0"], lambda e: e.activation(out=cv[0][:], in_=acc[:], func=AF.Silu))
                    if fb < 2:
                        dstT, dn, scl = ((qT, "qT", 128 ** -0.5), (kT, "kT", 1.0))[fb]
                        sqb = pre[0][:, 3:3 + TT]
                        rinv = tmpc
                        tr.op("act", ["cv0"], ["pre0"], lambda e: e.activation(out=sqb, in_=cv[0][:], func=AF.Square))
                        for t0 in range(0, TT, 512):
                            n = min(512, TT - t0)
                            p, pn = pf()
                            tr.op("pe", ["cst", "pre0"], [pn], lambda e: e.matmul(p[:, 0:n], lhsT=ones, rhs=sqb[:, t0:t0 + n], start=True, stop=True))
                            tr.op("dve", [pn], ["tmpc"], lambda e: e.tensor_scalar(out=rinv[:, t0:t0 + n], in0=p[:, 0:n], scalar1=1e-6, scalar2=None, op0=ALU.add))
                            tr.op("act", ["tmpc"], ["tmpc"], lambda e: e.activation(out=rinv[:, t0:t0 + n], in_=rinv[:, t0:t0 + n], func=AF.Sqrt))
                            tr.op("dve", ["tmpc"], ["tmpc"], lambda e: e.reciprocal(out=rinv[:, t0:t0 + n], in_=rinv[:, t0:t0 + n]))
                        tr.op("dve", ["cv0", "tmpc"], [dn], lambda e: e.scalar_tensor_tensor(out=dstT[:], in0=cv[0][:], scalar=scl, in1=rinv[:], op0=ALU.mult, op1=ALU.mult))
                    else:
                        hh = fb - 2
                        tr.op("pool", ["cv0"], ["cvb%d" % hh], lambda e: e.tensor_copy(out=cvb[hh][:], in_=cv[0][:]))
                chk(4)
                for nm, Cc, nch, zt, zn in (("p", 128, c.NCH, zba, "zba"), ("s", 1, NS, zbas, "zbas")):
                    for ci in range(nch):
                        c0 = ci * 128 if nm == "p" else T + ci
                        p, pn = pf()
                        for k in range(KD):
                            tr.op("pe", [wn, "hT"], [pn], lambda e: e.matmul(p[0:Cc, 0:260], lhsT=hT3[:, k, c0:c0 + Cc], rhs=w3[:, k, 512:772], start=(k == 0), stop=(k == KD - 1)))
                        tr.op("act", [pn], [zn], lambda e: e.activation(out=zt[0:Cc, ci * 260:(ci + 1) * 260], in_=p[0:Cc, 0:260], func=AF.Copy))
                    z3 = r3(zt[0:Cc, 0:nch * 260], nch, 260)
                    Gt = lambda f: r3(gates[(nm, f)][:, 0:nch * 2], nch, 2)
                    gn = lambda f: "gt_%s_%s" % (nm, f)
                    tr.op("act", [zn], [gn("beta")], lambda e: e.activation(out=Gt("beta")[0:Cc], in_=z3[:, :, 256:258], func=AF.Sigmoid))
                    for hh in range(2):
                        tr.op("act", [zn, "dtb_bc"], [gn("tmp")], lambda e: e.activation(out=Gt("tmp")[0:Cc, :, hh], in_=z3[:, :, 258 + hh], func=AF.Exp, bias=dtb_bc[0:Cc, 2 * j + hh:2 * j + hh + 1]))
                    tr.op("act", [gn("tmp")], [gn("tmp")], lambda e: e.activation(out=gates[(nm, "tmp")][0:Cc, 0:nch * 2], in_=gates[(nm, "tmp")][0:Cc, 0:nch * 2], func=AF.Ln, bias=1.0))
                    for hh in range(2):
                        tr.op("dve", [gn("tmp"), "negA"], [gn("g")], lambda e: e.tensor_scalar(out=Gt("g")[0:Cc, :, hh], in0=Gt("tmp")[0:Cc, :, hh], scalar1=negA[0:Cc, 2 * j + hh:2 * j + hh + 1], scalar2=None, op0=ALU.mult))
                    p, pn = pf()
                    n2 = nch * 2
                    gg = gates[(nm, "g")]
                    tr.op("pe", ["cst", gn("g")], [pn], lambda e: e.matmul(p[0:Cc, 0:n2], lhsT=triU[0:Cc, 0:Cc], rhs=gg[0:Cc, 0:n2], start=True, stop=True))
                    tr.op("pe", ["cst", gn("g")], [pn], lambda e: e.matmul(p[0:Cc, 64:64 + n2], lhsT=ones[0:Cc, 0:Cc], rhs=gg[0:Cc, 0:n2], start=True, stop=True))
                    tr.op("pe", ["cst", gn("g")], [pn], lambda e: e.matmul(p[:, 128:128 + n2], lhsT=ones[0:Cc, :], rhs=gg[0:Cc, 0:n2], start=True, stop=True))
                    gt = lambda f: gates[(nm, f)]
                    tr.op("dve", [pn], [gn("gc")], lambda e: e.tensor_copy(out=gt("gc")[0:Cc, 0:n2], in_=p[0:Cc, 0:n2]))
                    tr.op("act", [pn], [gn("egc")], lambda e: e.activation(out=gt("egc")[0:Cc, 0:n2], in_=p[0:Cc, 0:n2], func=AF.Exp))
                    tr.op("dve", [gn("egc"), gn("beta")], [gn("bg")], lambda e: e.tensor_tensor(out=gt("bg")[0:Cc, 0:n2], in0=gt("egc")[0:Cc, 0:n2], in1=gt("beta")[0:Cc, 0:n2], op=ALU.mult))
                    tr.op("dve", [pn, gn("gc")], [gn("gl")], lambda e: e.tensor_tensor(out=gt("gl")[0:Cc, 0:n2], in0=p[0:Cc, 64:64 + n2], in1=gt("gc")[0:Cc, 0:n2], op=ALU.subtract))
                    tr.op("act", [gn("gl")], [gn("ekd")], lambda e: e.activation(out=gt("ekd")[0:Cc, 0:n2], in_=gt("gl")[0:Cc, 0:n2], func=AF.Exp))
                    tr.op("act", [pn], [gn("egl128")], lambda e: e.activation(out=gt("egl128")[:, 0:n2], in_=p[:, 128:128 + n2], func=AF.Exp))
                chk(5)
                def lane_gen(hh, part, parity=None):
                    ln = LANES[hh]
                    seq = 0
                    for ci in range(c.NCH):
                        if parity is None or seq % 2 == parity:
                            yield from chunk(ln, part, seq, "p", 128, ci, ci * 128, j, hh, ci == 0, ci == c.NCH - 1, None)
                        seq += 1
                    for n_ in range(NS):
                        if parity is None or seq % 2 == parity:
                            yield from chunk(ln, part, seq, "s", 1, n_, T + n_, j, hh, False, False, n_)
                        seq += 1
                    if part == "C":
                        h = 2 * j + hh
                        tr.dma("sp", ln["choT"], ["oTst%d" % hh], ["oT_scr"], oT_scr[h * 128:(h + 1) * 128, :], ln["oTst"][:])

                for ln_ in LANES:
                    ln_["A_done"] = 0
                    ln_["B_done"] = 0
                    ln_["C_done"] = 0
                    ln_["Adone"] = set()
                active = [lane_gen(0, "A", 0), lane_gen(1, "A", 0), lane_gen(0, "A", 1), lane_gen(1, "A", 1),
                          lane_gen(0, "B"), lane_gen(1, "B"), lane_gen(0, "C"), lane_gen(1, "C")]
                while active:
                    for g_ in list(active):
                        try:
                            next(g_)
                        except StopIteration:
                            active.remove(g_)

            phase_end()
            chk(20)

            def outproj(srcT, sname, KS, wsrc, resid, rname, dst, dname, tag):
                phase_begin()
                CB = min(1024, D)
                NWB = 1 if CB > 512 else 2
                wo = [sb("wo%s%d" % (tag, i), [128, KS * CB], BF16) for i in range(NWB)]
                woc = [tr.chan() for i in range(NWB)]
                ot = [sb("ot%s%d" % (tag, i), [128, KS * 128], BF16) for i in range(2)]
                otc = [tr.chan() for i in range(2)]
                xr = [sb("xr%s%d" % (tag, i), [128, CB]) for i in range(2)]
                xrc = [tr.chan() for i in range(2)]
                it = 0
                for cb in range(D // CB):
                    wb = cb % NWB
                    tr.dma("pool", woc[wb], [], ["wo%s%d" % (tag, wb)], r3(wo[wb][:], KS, CB), wsrc[:, cb * CB:(cb + 1) * CB].rearrange("(k p) n -> p k n", p=128))
                    w3_ = r3(wo[wb][:], KS, CB)
                    for (t0, n) in tok_tiles():
                        b = it % 2
                        it += 1
                        o3 = r3(ot[b][:], KS, 128)
                        tr.dma("sp", otc[b], [sname], ["ot%s%d" % (tag, b)], o3[:, :, 0:n], srcT[:, t0:t0 + n].rearrange("(k p) t -> p k t", p=128))
                        tr.dma("sp", xrc[b], [rname], ["xr%s%d" % (tag, b)], xr[b][0:n, :], resid[t0:t0 + n, cb * CB:(cb + 1) * CB])
                        for h0 in range(0, CB, 512):
                            hn = min(512, CB - h0)
                            p, pn = pf()
                            for k in range(KS):
                                tr.op("pe", ["ot%s%d" % (tag, b), "wo%s%d" % (tag, wb)], [pn], lambda e: e.matmul(p[0:n, 0:hn], lhsT=o3[:, k, 0:n], rhs=w3_[:, k, h0:h0 + hn], start=(k == 0), stop=(k == KS - 1)))
                            tr.op("dve", [pn, "xr%s%d" % (tag, b)], ["xr%s%d" % (tag, b)], lambda e: e.tensor_tensor(out=xr[b][0:n, h0:h0 + hn], in0=p[0:n, 0:hn], in1=xr[b][0:n, h0:h0 + hn], op=ALU.add))
                        tr.dma("sp", xrc[b], ["xr%s%d" % (tag, b)], [dname], dst[t0:t0 + n, cb * CB:(cb + 1) * CB], xr[b][0:n, :])
                phase_end()

            outproj(oT_scr, "oT_scr", c.VAL // 128, w_out_gdn, xin, "xin", x1_scr, "x1_scr", "a")
            chk(21)
            norm_to_hT(x1_scr, norm_ssm, "x1_scr")
            chk(22)

            phase_begin()
            GB = min(16, G)
            NT = GB // 8
            NCK = T // 8
            NC1 = 1 + NCK
            PI = math.pi

            def ew(e, out, in0, in1, op, r=(), w=()):
                tr.op(e, list(r), list(w), lambda en: en.tensor_tensor(out=out, in0=in0, in1=in1, op=op))

            tb = {k: sb("s5_" + k, [64, G]) for k in ("lamr", "lami", "dt", "ar", "ai", "fr", "fi", "t1", "t2", "t3")}
            lt = sb("s5_lt", [128, 64])
            chl = tr.chan()
            chdt = tr.chan()
            chdc = tr.chan()
            chBi = tr.chan()
            chcr = tr.chan()
            for (src_, dst_) in ((lam_re, "lamr"), (lam_im, "lami")):
                for r0 in range(0, G, 128):
                    nr = min(128, G - r0)
                    tr.dma("sp", chl, [], ["s5_lt"], lt[0:nr, :], src_[r0:r0 + nr, :])
                    p, pn = pf()
                    tr.op("pe", ["s5_lt", "cst"], [pn], lambda e: e.transpose(out=p[0:64, 0:nr], in_=lt[0:nr, :], identity=ident[0:nr, 0:nr]))
                    tr.op("dve", [pn], ["s5_" + dst_], lambda e: e.tensor_copy(out=tb[dst_][:, r0:r0 + nr], in_=p[0:64, 0:nr]))
            tr.dma("sp", chdt, [], ["s5_dt"], tb["dt"][:], log_dt[0:1, :].broadcast_to([64, G]))
            tr.op("act", ["s5_dt"], ["s5_dt"], lambda e: e.activation(out=tb["dt"][:], in_=tb["dt"][:], func=AF.Exp))
            tr.op("dve", ["s5_lamr"], ["s5_lamr"], lambda e: e.tensor_scalar(out=tb["lamr"][:], in0=tb["lamr"][:], scalar1=-1e-4, scalar2=None, op0=ALU.min))
            ew("dve", tb["t1"][:], tb["lamr"][:], tb["dt"][:], ALU.mult, ["s5_lamr", "s5_dt"], ["s5_t1"])
            tr.op("act", ["s5_t1"], ["s5_t1"], lambda e: e.activation(out=tb["t1"][:], in_=tb["t1"][:], func=AF.Exp))
            ew("dve", tb["t2"][:], tb["lami"][:], tb["dt"][:], ALU.mult, ["s5_lami", "s5_dt"], ["s5_t2"])
            tr.op("act", ["s5_t2"], ["s5_ai"], lambda e: e.activation(out=tb["ai"][:], in_=tb["t2"][:], func=AF.Sin, scale=1.0 / 32))
            tr.op("dve", ["s5_t2"], ["s5_t3"], lambda e: e.tensor_scalar(out=tb["t3"][:], in0=tb["t2"][:], scalar1=1.0 / 32, scalar2=PI / 2, op0=ALU.mult, op1=ALU.add))
            tr.op("act", ["s5_t3"], ["s5_ar"], lambda e: e.activation(out=tb["ar"][:], in_=tb["t3"][:], func=AF.Sin))
            for _ in range(5):
                ew("dve", tb["t3"][:], tb["ar"][:], tb["ar"][:], ALU.mult, ["s5_ar"], ["s5_t3"])
                ew("dve", tb["t2"][:], tb["ai"][:], tb["ai"][:], ALU.mult, ["s5_ai"], ["s5_t2"])
                tr.op("dve", ["s5_ar", "s5_ai"], ["s5_ai"], lambda e: e.scalar_tensor_tensor(out=tb["ai"][:], in0=tb["ar"][:], scalar=2.0, in1=tb["ai"][:], op0=ALU.mult, op1=ALU.mult))
                ew("dve", tb["ar"][:], tb["t3"][:], tb["t2"][:], ALU.subtract, ["s5_t3", "s5_t2"], ["s5_ar"])
            ew("dve", tb["ar"][:], tb["ar"][:], tb["t1"][:], ALU.mult, ["s5_ar", "s5_t1"], ["s5_ar"])
            ew("dve", tb["ai"][:], tb["ai"][:], tb["t1"][:], ALU.mult, ["s5_ai", "s5_t1"], ["s5_ai"])
            tr.op("dve", ["s5_ar"], ["s5_t1"], lambda e: e.tensor_scalar(out=tb["t1"][:], in0=tb["ar"][:], scalar1=-1.0, scalar2=None, op0=ALU.add))
            ew("dve", tb["t2"][:], tb["lamr"][:], tb["lamr"][:], ALU.mult, ["s5_lamr"], ["s5_t2"])
            ew("dve", tb["t3"][:], tb["lami"][:], tb["lami"][:], ALU.mult, ["s5_lami"], ["s5_t3"])
            ew("dve", tb["t2"][:], tb["t2"][:], tb["t3"][:], ALU.add, ["s5_t2", "s5_t3"], ["s5_t2"])
            tr.op("dve", ["s5_t2"], ["s5_t2"], lambda e: e.reciprocal(out=tb["t2"][:], in_=tb["t2"][:]))
            ew("dve", tb["fr"][:], tb["t1"][:], tb["lamr"][:], ALU.mult, ["s5_t1", "s5_lamr"], ["s5_fr"])
            ew("dve", tb["t3"][:], tb["ai"][:], tb["lami"][:], ALU.mult, ["s5_ai", "s5_lami"], ["s5_t3"])
            ew("dve", tb["fr"][:], tb["fr"][:], tb["t3"][:], ALU.add, ["s5_fr", "s5_t3"], ["s5_fr"])
            ew("dve", tb["fr"][:], tb["fr"][:], tb["t2"][:], ALU.mult, ["s5_fr", "s5_t2"], ["s5_fr"])
            ew("dve", tb["fi"][:], tb["ai"][:], tb["lamr"][:], ALU.mult, ["s5_ai", "s5_lamr"], ["s5_fi"])
            ew("dve", tb["t3"][:], tb["t1"][:], tb["lami"][:], ALU.mult, ["s5_t1", "s5_lami"], ["s5_t3"])
            ew("dve", tb["fi"][:], tb["fi"][:], tb["t3"][:], ALU.subtract, ["s5_fi", "s5_t3"], ["s5_fi"])
            ew("dve", tb["fi"][:], tb["fi"][:], tb["t2"][:], ALU.mult, ["s5_fi", "s5_t2"], ["s5_fi"])

            dcol = sb("s5_dcol", [128, c.KW])
            with nc.allow_non_contiguous_dma(reason="tiny per-channel vector"):
                tr.dma("sp", chdc, [], ["s5_dcol"], dcol[:], d_ssm.rearrange("(k p) o -> p (k o)", p=128))
            wuy = sb("s5_wu", [128, max(KD * GB * 16, NT * TT)], BF16)
            wu3 = r3(wuy[:, 0:KD * GB * 16], KD, GB * 16)
            chwu = tr.chan()
            uT = sb("s5_uT", [128, NT * TT], BF16)
            uT3 = r3(uT[:], NT, TT)
            yT3 = r3(wuy[:, 0:NT * TT], NT, TT)
            chy = tr.chan()
            PWR = sb("s5_pwr", [64, 9 * GB])
            PWI = sb("s5_pwi", [64, 9 * GB])
            pw_r = lambda m: PWR[:, m * GB:(m + 1) * GB]
            pw_i = lambda m: PWI[:, m * GB:(m + 1) * GB]
            GC = GB * 16
            Bt = {k: sb("s5_" + k, [64, GC]) for k in ("Br", "Bi", "Bbr", "Bbi", "CTr", "CTi", "e1", "e2")}
            chB = tr.chan()
            XR = sb("s5_XR", [64, 8 * GC], BF16)
            XI = sb("s5_XI", [64, 8 * GC], BF16)
            CPR = sb("s5_CPR", [64, 8 * GC], BF16)
            CPI = sb("s5_CPI", [64, 8 * GC], BF16)
            CTrb = sb("s5_CTrb", [64, GC], BF16)
            NCTib = sb("s5_NCTib", [64, GC], BF16)
            crow = sb("s5_crow", [128, 64])
            Kbd = sb("s5_Kbd", [128, 8 * 128], BF16)
            YP = [sb("s5_YP%d" % i, [128, 8 * 8 * 64], BF16) for i in range(2)]
            YTs = sb("s5_YTs", [128, 128], BF16)
            CPpr = sb("s5_CPpr", [64, 8 * 128], BF16)
            CPpi = sb("s5_CPpi", [64, 8 * 128], BF16)
            VB = sb("s5_VB", [64, 2 * GB * NC1])
            VB4 = VB[:].rearrange("p (a g n) -> p a g n", a=2, g=GB, n=NC1)
            XH = sb("s5_XH", [64, 2 * GB * NCK], BF16)
            XH4 = XH[:].rearrange("p (a g n) -> p a g n", a=2, g=GB, n=NCK)
            A8a = sb("s5_A8a", [64, 2 * GB])
            A8b = sb("s5_A8b", [64, 2 * GB])
            A1a = sb("s5_A1a", [64, 2 * GB])
            A1b = sb("s5_A1b", [64, 2 * GB])
            sc1 = sb("s5_sc1", [64, 2 * GB])
            sc2 = sb("s5_sc2", [64, 2 * GB])
            VS = sb("s5_VS", [64, 2 * NS * GB])
            VS4 = VS[:].rearrange("p (a n g) -> p a n g", a=2, n=NS, g=GB)
            XS = sb("s5_XS", [64, 2 * NS * GB])
            XS4 = XS[:].rearrange("p (a n g) -> p a n g", a=2, n=NS, g=GB)
            XSb = sb("s5_XSb", [64, 2 * NS * GB], BF16)
            XSb4 = XSb[:].rearrange("p (a n g) -> p a n g", a=2, n=NS, g=GB)
            stmp = sb("s5_stmp", [64, max(2 * NS * GB, 2 * GB * 32)])
            ss1 = stmp
            APW = sb("s5_APW", [64, int(math.log2(NCK)) * 4 * GB])
            srow = sb("s5_srow", [128, 64])
            chs = tr.chan()
            orow = sb("s5_orow", [128, 64])
            cho = tr.chan()
            ytmp = sb("s5_ytmp", [128, max(NCK + NS, 256)])
            YT8 = ytmp[:, 0:256].bitcast(BF16)

            def bc3(ap2, n):
                return ap2.unsqueeze(2).to_broadcast([64, GB, n])

            for blk in range(G // GB):
                g0 = blk * GB
                ch0 = g0 * 16
                tr.dma("pool", chwu, [], ["s5_wu"], wu3, w_in_ssm[:, ch0:ch0 + GC].rearrange("(k p) n -> p k n", p=128))
                for tl in range(NT):
                    for t0 in range(0, TT, 512):
                        n = min(512, TT - t0)
                        p, pn = pf()
                        for k in range(KD):
                            tr.op("pe", ["s5_wu", "hT"], [pn], lambda e: e.matmul(p[:, 0:n], lhsT=wu3[:, k, tl * 128:(tl + 1) * 128], rhs=hT3[:, k, t0:t0 + n], start=(k == 0), stop=(k == KD - 1)))
                        tr.op("act", [pn], ["s5_uT"], lambda e: e.activation(out=uT3[:, tl, t0:t0 + n], in_=p[:, 0:n], func=AF.Copy))
                tr.op("pool", [], ["s5_pwr"], lambda e: e.memset(pw_r(0), 1.0))
                tr.op("pool", [], ["s5_pwi"], lambda e: e.memset(pw_i(0), 0.0))
                arb = tb["ar"][:, g0:g0 + GB]
                aib = tb["ai"][:, g0:g0 + GB]
                e1 = Bt["e1"][:, 0:GB]
                e2 = Bt["e2"][:, 0:GB]
                for m in range(8):
                    ew("dve", e1, pw_r(m), arb, ALU.mult, ["s5_pwr", "s5_ar"], ["s5_e1"])
                    ew("dve", e2, pw_i(m), aib, ALU.mult, ["s5_pwi", "s5_ai"], ["s5_e2"])
                    ew("dve", pw_r(m + 1), e1, e2, ALU.subtract, ["s5_e1", "s5_e2"], ["s5_pwr"])
                    ew("dve", e1, pw_r(m), aib, ALU.mult, ["s5_pwr", "s5_ai"], ["s5_e1"])
                    ew("dve", e2, pw_i(m), arb, ALU.mult, ["s5_pwi", "s5_ar"], ["s5_e2"])
                    ew("dve", pw_i(m + 1), e1, e2, ALU.add, ["s5_e1", "s5_e2"], ["s5_pwi"])
                for (Aa, Ab, an, bn, m) in ((A8a, A8b, "s5_A8a", "s5_A8b", 8), (A1a, A1b, "s5_A1a", "s5_A1b", 1)):
                    tr.op("dve", ["s5_pwr"], [an], lambda e: e.tensor_copy(out=Aa[:, 0:GB], in_=pw_r(m)))
                    tr.op("dve", ["s5_pwi"], [an], lambda e: e.tensor_copy(out=Aa[:, GB:2 * GB], in_=pw_i(m)))
                    tr.op("dve", ["s5_pwi"], [bn], lambda e: e.tensor_scalar(out=Ab[:, 0:GB], in0=pw_i(m), scalar1=-1.0, scalar2=None, op0=ALU.mult))
                    tr.op("dve", ["s5_pwr"], [bn], lambda e: e.tensor_copy(out=Ab[:, GB:2 * GB], in_=pw_r(m)))
                B3 = lambda k: r3(Bt[k][:], GB, 16)
                tr.dma("sp", chB, [], ["s5_Br"], B3("Br"), b_re[g0:g0 + GB].rearrange("g p c -> p g c"))
                tr.dma("sp", chBi, [], ["s5_Bi"], B3("Bi"), b_im[g0:g0 + GB].rearrange("g p c -> p g c"))
                frb = bc3(tb["fr"][:, g0:g0 + GB], 16)
                fib = bc3(tb["fi"][:, g0:g0 + GB], 16)
                ew("dve", B3("Bbr"), B3("Br"), frb, ALU.mult, ["s5_Br", "s5_fr"], ["s5_Bbr"])
                ew("dve", B3("e1"), B3("Bi"), fib, ALU.mult, ["s5_Bi", "s5_fi"], ["s5_e1"])
                ew("dve", B3("Bbr"), B3("Bbr"), B3("e1"), ALU.subtract, ["s5_Bbr", "s5_e1"], ["s5_Bbr"])
                ew("dve", B3("Bbi"), B3("Bi"), frb, ALU.mult, ["s5_Bi", "s5_fr"], ["s5_Bbi"])
                ew("dve", B3("e1"), B3("Br"), fib, ALU.mult, ["s5_Br", "s5_fi"], ["s5_e1"])
                ew("dve", B3("Bbi"), B3("Bbi"), B3("e1"), ALU.add, ["s5_Bbi", "s5_e1"], ["s5_Bbi"])
                for tau in range(8):
                    prb = bc3(pw_r(tau), 16)
                    pib = bc3(pw_i(tau), 16)
                    xr_o = r3(XR[:, tau * GC:(tau + 1) * GC], GB, 16)
                    xi_o = r3(XI[:, tau * GC:(tau + 1) * GC], GB, 16)
                    ew("dve", B3("e1"), B3("Bbr"), prb, ALU.mult, ["s5_Bbr", "s5_pwr"], ["s5_e1"])
                    ew("pool", B3("e2"), B3("Bbi"), pib, ALU.mult, ["s5_Bbi", "s5_pwi"], ["s5_e2"])
                    ew("dve", xr_o, B3("e1"), B3("e2"), ALU.subtract, ["s5_e1", "s5_e2"], ["s5_XR"])
                    ew("dve", B3("e1"), B3("Bbi"), prb, ALU.mult, ["s5_Bbi", "s5_pwr"], ["s5_e1"])
                    ew("pool", B3("e2"), B3("Bbr"), pib, ALU.mult, ["s5_Bbr", "s5_pwi"], ["s5_e2"])
                    ew("dve", xi_o, B3("e1"), B3("e2"), ALU.add, ["s5_e1", "s5_e2"], ["s5_XI"])
                for (src_, dk) in ((c_re, "CTr"), (c_im, "CTi")):
                    rows = src_[g0:g0 + GB].rearrange("g c p -> (g c) p")
                    for r0 in range(0, GC, 128):
                        tr.dma("sp", chcr, [], ["s5_crow"], crow[:], rows[r0:r0 + 128, :])
                        p, pn = pf()
                        tr.op("pe", ["s5_crow", "cst"], [pn], lambda e: e.transpose(out=p[0:64, 0:128], in_=crow[:], identity=ident))
                        tr.op("dve", [pn], ["s5_" + dk], lambda e: e.tensor_copy(out=Bt[dk][:, r0:r0 + 128], in_=p[0:64, 0:128]))
                tr.op("dve", ["s5_CTr"], ["s5_CTrb"], lambda e: e.tensor_copy(out=CTrb[:], in_=Bt["CTr"][:]))
                tr.op("dve", ["s5_CTi"], ["s5_NCTib"], lambda e: e.tensor_scalar(out=NCTib[:], in0=Bt["CTi"][:], scalar1=-1.0, scalar2=None, op0=ALU.mult))
                for r_ in range(8):
                    prb = bc3(pw_r(r_ + 1), 16)
                    pib = bc3(pw_i(r_ + 1), 16)
                    cr_o = r3(CPR[:, r_ * GC:(r_ + 1) * GC], GB, 16)
                    ci_o = r3(CPI[:, r_ * GC:(r_ + 1) * GC], GB, 16)
                    ew("dve", B3("e1"), B3("CTr"), prb, ALU.mult, ["s5_CTr", "s5_pwr"], ["s5_e1"])
                    ew("pool", B3("e2"), B3("CTi"), pib, ALU.mult, ["s5_CTi", "s5_pwi"], ["s5_e2"])
                    ew("dve", cr_o, B3("e1"), B3("e2"), ALU.subtract, ["s5_e1", "s5_e2"], ["s5_CPR"])
                    ew("dve", B3("e1"), B3("CTr"), pib, ALU.mult, ["s5_CTr", "s5_pwi"], ["s5_e1"])
                    ew("pool", B3("e2"), B3("CTi"), prb, ALU.mult, ["s5_CTi", "s5_pwr"], ["s5_e2"])
                    ew("dve", B3("e1"), B3("e1"), B3("e2"), ALU.add, ["s5_e1", "s5_e2"], ["s5_e1"])
                    tr.op("dve", ["s5_e1"], ["s5_CPI"], lambda e: e.tensor_scalar(out=ci_o, in0=B3("e1"), scalar1=-1.0, scalar2=None, op0=ALU.mult))
                tr.op("pool", [], ["s5_VB"], lambda e: e.memset(VB[:], 0.0))
                for tl in range(NT):
                    for part, Xs, xn in ((0, XR, "s5_XR"), (1, XI, "s5_XI")):
                        Y4 = YP[part][:].rearrange("p (t g s) -> p t g s", t=8, g=8, s=64)
                        ypn = "s5_YP%d" % part
                        p, pn = pb()
                        for tau in range(8):
                            tr.op("pe", [xn, "cstb"], [pn], lambda e: e.transpose(out=p[:, tau * 64:(tau + 1) * 64], in_=Xs[:, tau * GC + tl * 128: tau * GC + (tl + 1) * 128], identity=identb[0:64, 0:64]))
                        tr.op("act", [pn], ["s5_ytmp"], lambda e: e.activation(out=YT8, in_=p[:, 0:512], func=AF.Copy))
                        tr.op("dve", ["s5_ytmp", "cst"], [ypn], lambda e: e.tensor_tensor(out=Y4, in0=YT8.rearrange("p (t s) -> p t s", t=8, s=64).unsqueeze(2).to_broadcast([128, 8, 8, 64]), in1=GSEL.unsqueeze(1).unsqueeze(3).to_broadcast([128, 8, 8, 64]), op=ALU.mult))
                        for g in range(8):
                            p, pn = pf()
                            for tau in range(8):
                                tr.op("pe", [ypn, "s5_uT"], [pn], lambda e: e.matmul(p[0:64, 0:NCK], lhsT=Y4[:, tau, g, :], rhs=uT3[:, tl, (7 - tau):T:8], start=(tau == 0), stop=(tau == 7)))
                            tr.op("pe", [ypn, "s5_uT"], [pn], lambda e: e.matmul(p[0:64, NCK:NCK + NS], lhsT=Y4[:, 0, g, :], rhs=uT3[:, tl, T:TT], start=True, stop=True))
                            tr.op("act", [pn], ["s5_VB"], lambda e: e.activation(out=VB4[:, part, tl * 8 + g, 1:1 + NCK], in_=p[0:64, 0:NCK], func=AF.Copy))
                            tr.op("dve", [pn], ["s5_VS"], lambda e: e.tensor_copy(out=VS4[:, part, :, tl * 8 + g], in_=p[0:64, NCK:NCK + NS]))
                chk(30)
                A8a3 = A8a[:].rearrange("p (a g) -> p a g", a=2, g=GB)
                A8b3 = A8b[:].rearrange("p (a g) -> p a g", a=2, g=GB)
                LV = int(math.log2(NCK))
                assert (1 << LV) == NCK
                PWT = 32
                APW4 = APW[:].rearrange("p (l q g) -> p l q g", l=LV, q=4, g=GB)
                tr.op("dve", ["s5_A8a"], ["s5_APW"], lambda e: e.tensor_copy(out=APW4[:, 0, 0:2, :], in_=A8a3))
                tr.op("dve", ["s5_A8b"], ["s5_APW"], lambda e: e.tensor_copy(out=APW4[:, 0, 2:4, :], in_=A8b3))
                for l in range(1, LV):
                    pr_, pi_ = APW4[:, l - 1, 0, :], APW4[:, l - 1, 1, :]
                    ew("dve", sc1[:, 0:GB], pr_, pr_, ALU.mult, ["s5_APW"], ["s5_sc1"])
                    ew("dve", sc1[:, GB:2 * GB], pi_, pi_, ALU.mult, ["s5_APW"], ["s5_sc1"])
                    ew("dve", APW4[:, l, 0, :], sc1[:, 0:GB], sc1[:, GB:2 * GB], ALU.subtract, ["s5_sc1"], ["s5_APW"])
                    tr.op("dve", ["s5_APW"], ["s5_APW"], lambda e: e.scalar_tensor_tensor(out=APW4[:, l, 1, :], in0=pr_, scalar=2.0, in1=pi_, op0=ALU.mult, op1=ALU.mult))
                    tr.op("dve", ["s5_APW"], ["s5_APW"], lambda e: e.tensor_copy(out=APW4[:, l, 3, :], in_=APW4[:, l, 0, :]))
                    tr.op("dve", ["s5_APW"], ["s5_APW"], lambda e: e.tensor_scalar(out=APW4[:, l, 2, :], in0=APW4[:, l, 1, :], scalar1=-1.0, scalar2=None, op0=ALU.mult))

                def cacc(l, tgt_sl, src_sl, cnt):
                    for q0 in range(0, cnt, PWT):
                        qn = min(PWT, cnt - q0)
                        t_lo, t_st = tgt_sl
                        s_lo, s_st = src_sl
                        tg = VB4[:, :, :, t_lo + q0 * t_st: t_lo + (q0 + qn - 1) * t_st + 1: t_st]
                        sr = VB4[:, 0:1, :, s_lo + q0 * s_st: s_lo + (q0 + qn - 1) * s_st + 1: s_st].to_broadcast([64, 2, GB, qn])
                        si = VB4[:, 1:2, :, s_lo + q0 * s_st: s_lo + (q0 + qn - 1) * s_st + 1: s_st].to_broadcast([64, 2, GB, qn])
                        pa = APW4[:, l, 0:2, :].unsqueeze(3).to_broadcast([64, 2, GB, qn])
                        pb_ = APW4[:, l, 2:4, :].unsqueeze(3).to_broadcast([64, 2, GB, qn])
                        tm = stmp[:, 0:2 * GB * qn].rearrange("p (a g n) -> p a g n", a=2, g=GB, n=qn)
                        ew("dve", tm, pa, sr, ALU.mult, ["s5_APW", "s5_VB"], ["s5_stmp"])
                        ew("dve", tg, tg, tm, ALU.add, ["s5_VB", "s5_stmp"], ["s5_VB"])
                        ew("dve", tm, pb_, si, ALU.mult, ["s5_APW", "s5_VB"], ["s5_stmp"])
                        ew("dve", tg, tg, tm, ALU.add, ["s5_VB", "s5_stmp"], ["s5_VB"])

                for l in range(LV):
                    s_ = 1 << l
                    cacc(l, (2 * s_, 2 * s_), (s_, 2 * s_), NCK // (2 * s_))
                for l in range(LV - 2, -1, -1):
                    s_ = 1 << l
                    cacc(l, (3 * s_, 2 * s_), (2 * s_, 2 * s_), NCK // (2 * s_) - 1)
                tr.op("act", ["s5_VB"], ["s5_XH"], lambda e: e.activation(out=XH4, in_=VB4[:, :, :, 0:NCK], func=AF.Copy))
                for part, dst_, dn_ in ((0, re_p, "re_p"), (1, im_p, "im_p")):
                    tr.op("dve", ["s5_VB"], ["s5_sc1"], lambda e: e.tensor_copy(out=sc1[:, 0:GB], in_=VB4[:, part, :, NCK]))
                    p, pn = pf()
                    tr.op("pe", ["s5_sc1", "cst"], [pn], lambda e: e.transpose(out=p[0:GB, 0:64], in_=sc1[:, 0:GB], identity=ident[0:64, 0:64]))
                    tr.op("act", [pn], ["s5_orow"], lambda e: e.activation(out=orow[0:GB, :], in_=p[0:GB, 0:64], func=AF.Copy))
                    tr.dma("sp", cho, ["s5_orow"], [dn_], dst_[g0:g0 + GB, :], orow[0:GB, :])
                RW = min(128, NS * GB)
                for part, src_ in ((0, re_s), (1, im_s)):
                    for r0 in range(0, NS * GB, RW):
                        n0 = r0 // GB
                        nn = RW // GB
                        tr.dma("sp", chs, [], ["s5_srow"], srow[0:RW, :], src_[n0:n0 + nn, g0:g0 + GB, :])
                        p, pn = pf()
                        tr.op("pe", ["s5_srow", "cst"], [pn], lambda e: e.transpose(out=p[0:64, 0:RW], in_=srow[0:RW, :], identity=ident[0:RW, 0:RW]))
                        tr.op("dve", [pn], ["s5_XS"], lambda e: e.tensor_copy(out=XS[:, part * NS * GB + r0: part * NS * GB + r0 + RW], in_=p[0:64, 0:RW]))
                tr.op("act", ["s5_XS"], ["s5_XSb"], lambda e: e.activation(out=XSb[:], in_=XS[:], func=AF.Copy))
                A1a4 = A1a[:].rearrange("p (a g) -> p a g", a=2, g=GB).unsqueeze(2).to_broadcast([64, 2, NS, GB])
                A1b4 = A1b[:].rearrange("p (a g) -> p a g", a=2, g=GB).unsqueeze(2).to_broadcast([64, 2, NS, GB])
                ss4 = stmp[:, 0:2 * NS * GB].rearrange("p (a n g) -> p a n g", a=2, n=NS, g=GB)
                ew("dve", ss4, A1a4, XS4[:, 0:1].to_broadcast([64, 2, NS, GB]), ALU.mult, ["s5_A1a", "s5_XS"], ["s5_stmp"])
                ew("dve", VS4, VS4, ss4, ALU.add, ["s5_VS", "s5_stmp"], ["s5_VS"])
                ew("dve", ss4, A1b4, XS4[:, 1:2].to_broadcast([64, 2, NS, GB]), ALU.mult, ["s5_A1b", "s5_XS"], ["s5_stmp"])
                ew("dve", VS4, VS4, ss4, ALU.add, ["s5_VS", "s5_stmp"], ["s5_VS"])
                for part, dst_, dn_ in ((0, re_so, "re_so"), (1, im_so, "im_so")):
                    for r0 in range(0, NS * GB, RW):
                        n0 = r0 // GB
                        nn = RW // GB
                        p, pn = pf()
                        tr.op("pe", ["s5_VS", "cst"], [pn], lambda e: e.transpose(out=p[0:RW, 0:64], in_=VS[:, part * NS * GB + r0: part * NS * GB + r0 + RW], identity=ident[0:64, 0:64]))
                        tr.op("act", [pn], ["s5_orow"], lambda e: e.activation(out=orow[0:RW, :], in_=p[0:RW, 0:64], func=AF.Copy))
                        tr.dma("sp", cho, ["s5_orow"], [dn_], dst_[n0:n0 + nn, g0:g0 + GB, :], orow[0:RW, :])
                chk(31)
                for tl in range(NT):
                    for t4 in range(0, 8, 4):
                        p, pn = pf()
                        for tau in range(t4, t4 + 4):
                            o_ = p[:, (tau - t4) * 128:(tau - t4 + 1) * 128]
                            tr.op("pe", ["s5_XR", "s5_CTrb"], [pn], lambda e: e.matmul(o_, lhsT=XR[:, tau * GC + tl * 128: tau * GC + (tl + 1) * 128], rhs=CTrb[:, tl * 128:(tl + 1) * 128], start=True, stop=False))
                            tr.op("pe", ["s5_XI", "s5_NCTib"], [pn], lambda e: e.matmul(o_, lhsT=XI[:, tau * GC + tl * 128: tau * GC + (tl + 1) * 128], rhs=NCTib[:, tl * 128:(tl + 1) * 128], start=False, stop=True))
                        tr.op("dve", [pn, "cst"], ["s5_Kbd"], lambda e: e.tensor_tensor(out=r3(Kbd[:, t4 * 128:(t4 + 4) * 128], 4, 128), in0=r3(p[:, 0:512], 4, 128), in1=BD.unsqueeze(1).to_broadcast([128, 4, 128]), op=ALU.mult))
                    for r_ in range(8):
                        for CPs, CPp, cn in ((CPR, CPpr, "s5_CPpr"), (CPI, CPpi, "s5_CPpi")):
                            src4 = CPs[:, r_ * GC + tl * 128: r_ * GC + (tl + 1) * 128].rearrange("p (g c) -> p g c", g=8, c=16).unsqueeze(2).to_broadcast([64, 8, 8, 16])
                            gg4 = GG[0:64, :].rearrange("p (g h) -> p g h", g=8, h=8).unsqueeze(3).to_broadcast([64, 8, 8, 16])
                            tr.op("pool" if cn == "s5_CPpr" else "dve", ["s5_CPR", "s5_CPI", "cst"], [cn], lambda e: e.tensor_tensor(out=CPp[:].rearrange("p (g h c) -> p g h c", g=8, h=8, c=16), in0=src4, in1=gg4, op=ALU.mult))
                        p, pn = pf()
                        mms = [(Kbd[:, tau * 128:(tau + 1) * 128], uT3[:, tl, (r_ - tau):T:8], ["s5_Kbd", "s5_uT"]) for tau in range(r_ + 1)]
                        for g in range(8):
                            mms.append((CPpr[:, g * 128:(g + 1) * 128], XH4[:, 0, tl * 8 + g, :], ["s5_CPpr", "s5_XH"]))
                            mms.append((CPpi[:, g * 128:(g + 1) * 128], XH4[:, 1, tl * 8 + g, :], ["s5_CPpi", "s5_XH"]))
                        for i_, (l_, rh_, rd_) in enumerate(mms):
                            tr.op("pe", rd_, [pn], lambda e: e.matmul(p[:, 0:NCK], lhsT=l_, rhs=rh_, start=(i_ == 0), stop=(i_ == len(mms) - 1)))
                        dsc = dcol[:, blk * NT + tl: blk * NT + tl + 1]
                        tr.op("dve", [pn, "s5_uT", "s5_dcol"], ["s5_ytmp"], lambda e: e.scalar_tensor_tensor(out=ytmp[:, 0:NCK], in0=uT3[:, tl, r_:T:8], scalar=dsc, in1=p[:, 0:NCK], op0=ALU.mult, op1=ALU.add))
                        tr.op("act", ["s5_ytmp"], ["s5_wu"], lambda e: e.activation(out=yT3[:, tl, r_:T:8], in_=ytmp[:, 0:NCK], func=AF.Gelu))
                        if r_ == 0:
                            mms = [(Kbd[:, 0:128], uT3[:, tl, T:TT], ["s5_Kbd", "s5_uT"])]
                            for g in range(8):
                                mms.append((CPpr[:, g * 128:(g + 1) * 128], XSb4[:, 0, :, tl * 8 + g], ["s5_CPpr", "s5_XSb"]))
                                mms.append((CPpi[:, g * 128:(g + 1) * 128], XSb4[:, 1, :, tl * 8 + g], ["s5_CPpi", "s5_XSb"]))
                            p2, p2n = pf()
                            for i_, (l_, rh_, rd_) in enumerate(mms):
                                tr.op("pe", rd_, [p2n], lambda e: e.matmul(p2[:, 0:NS], lhsT=l_, rhs=rh_, start=(i_ == 0), stop=(i_ == len(mms) - 1)))
                            tr.op("dve", [p2n, "s5_uT", "s5_dcol"], ["s5_ytmp"], lambda e: e.scalar_tensor_tensor(out=ytmp[:, NCK:NCK + NS], in0=uT3[:, tl, T:TT], scalar=dsc, in1=p2[:, 0:NS], op0=ALU.mult, op1=ALU.add))
                            tr.op("act", ["s5_ytmp"], ["s5_wu"], lambda e: e.activation(out=yT3[:, tl, T:TT], in_=ytmp[:, NCK:NCK + NS], func=AF.Gelu))
                    tr.dma("sp", chy, ["s5_wu"], ["yT_scr"], yT_scr[ch0 + tl * 128: ch0 + (tl + 1) * 128, :], yT3[:, tl, :])
                chk(32)
            phase_end()
            chk(33)

            phase_begin()
            KW = c.KW
            TBM = min(TT, 1040)
            yTa = sb("g_yTa", [128, KW * TBM], BF16)
            yTa3 = r3(yTa[:], KW, TBM)
            chya = tr.chan()
            wg = [sb("g_wg%d" % i, [128, KW * 128], BF16) for i in range(2)]
            wgc = [tr.chan() for i in range(2)]
            wz = [sb("g_wz%d" % i, [128, KD * 128], BF16) for i in range(2)]
            wzc = [tr.chan() for i in range(2)]
            bgl = sb("g_bgl", [128, KW])
            chbg = tr.chan()
            with nc.allow_non_contiguous_dma(reason="tiny per-channel vector"):
                tr.dma("sp", chbg, [], ["g_bgl"], bgl[:], b_glu.rearrange("(k p) o -> p (k o)", p=128))
            sgt = sb("g_sg", [128, 512])
            szt = sb("g_sz", [128, 512])
            y2t = [sb("g_y2%d" % i, [128, TBM], BF16) for i in range(2)]
            y2c = [tr.chan() for i in range(2)]
            it = 0
            for b0 in range(0, TT, TBM):
                bn = min(TBM, TT - b0)
                tr.dma("sp", chya, ["yT_scr"], ["g_yTa"], yTa3[:, :, 0:bn], yT_scr[:, b0:b0 + bn].rearrange("(k p) t -> p k t", p=128))
                for m in range(KW):
                    wb = it % 2
                    it += 1
                    wg3 = r3(wg[wb][:], KW, 128)
                    wz3 = r3(wz[wb][:], KD, 128)
                    tr.dma("pool", wgc[wb], [], ["g_wg%d" % wb], wg3, w_glu[:, m * 128:(m + 1) * 128].rearrange("(k p) n -> p k n", p=128))
                    tr.dma("pool", wzc[wb], [], ["g_wz%d" % wb], wz3, w_in_ssm[:, c.W + m * 128: c.W + (m + 1) * 128].rearrange("(k p) n -> p k n", p=128))
                    for t0 in range(0, bn, 512):
                        n = min(512, bn - t0)
                        pg, pgn = pf()
                        for k in range(KW):
                            tr.op("pe", ["g_wg%d" % wb, "g_yTa"], [pgn], lambda e: e.matmul(pg[:, 0:n], lhsT=wg3[:, k, :], rhs=yTa3[:, k, t0:t0 + n], start=(k == 0), stop=(k == KW - 1)))
                        pz, pzn = pf()
                        for k in range(KD):
                            tr.op("pe", ["g_wz%d" % wb, "hT"], [pzn], lambda e: e.matmul(pz[:, 0:n], lhsT=wz3[:, k, :], rhs=hT3[:, k, b0 + t0:b0 + t0 + n], start=(k == 0), stop=(k == KD - 1)))
                        tr.op("act", [pgn, "g_bgl"], ["g_sg"], lambda e: e.activation(out=sgt[:, 0:n], in_=pg[:, 0:n], func=AF.Sigmoid, bias=bgl[:, m:m + 1]))
                        tr.op("act", [pzn], ["g_sz"], lambda e: e.activation(out=szt[:, 0:n], in_=pz[:, 0:n], func=AF.Silu))
                        tr.op("pool", ["g_sg", "g_sz"], ["g_sg"], lambda e: e.tensor_tensor(out=sgt[:, 0:n], in0=sgt[:, 0:n], in1=szt[:, 0:n], op=ALU.mult))
                        tr.op("dve", ["g_sg", "g_yTa"], ["g_y2%d" % wb], lambda e: e.tensor_tensor(out=y2t[wb][:, t0:t0 + n], in0=sgt[:, 0:n], in1=yTa3[:, m, t0:t0 + n], op=ALU.mult))
                    tr.dma("sp", y2c[wb], ["g_y2%d" % wb], ["y2_scr"], y2_scr[m * 128:(m + 1) * 128, b0:b0 + bn], y2t[wb][:, 0:bn])
            phase_end()
            chk(34)
            outproj(y2_scr, "y2_scr", c.KW, w_out_ssm, x1_scr, "x1_scr", x2_scr, "x2_scr", "b")
            chk(35)
            norm_to_hT(x2_scr, norm_final, "x2_scr", final_out=y_out)
        except _Stop:
            if cur[0] is not es:
                cur[0].close()
                cur[0] = es
        tr.finish()
    return nc


def make_consts():
    cs = np.zeros((128, 8 * 128), np.float32)
    i = np.arange(128)
    cs[:, 0:128] = np.eye(128)
    cs[:, 128:256] = (i[:, None] <= i[None, :])
    cs[:, 256:384] = 1.0
    cs[:, 384:512] = np.where(i[None, :] < i[:, None], 0.0, 30000.0)
    cs[:, 512:640] = np.where(i[:, None] <= i[None, :], 0.0, -30000.0)
    cs[:, 640:768] = ((i[None, :] // 16) >= (i[:, None] // 16))
    cs[:, 768:896] = ((i[None, :] // 16) == (i[:, None] // 16))
    cs[:, 896:904] = ((i[:, None] // 16) == np.arange(8)[None, :])
    cs[:, 904:968] = np.eye(8).reshape(1, 64)
    return cs


_NC_CACHE = {}


def kernel(x_prompt, x_sample, state_gdn_conv, state_gdn_delta, state_ssm_re, state_ssm_im,
           norm_gdn, w_in_gdn, conv_gdn, a_log_gdn, dt_bias_gdn, onorm_gdn, w_out_gdn,
           norm_ssm, w_in_ssm, lam_re, lam_im, b_re, b_im, c_re, c_im, d_ssm, log_dt_ssm,
           w_glu_ssm, b_glu_ssm, w_out_ssm, norm_final):
    cfg = Cfg(**FULL)
    f = lambda a: np.ascontiguousarray(np.asarray(a, dtype=np.float32))
    NS, T = cfg.NS, cfg.T
    B = x_prompt.shape[0]
    ncores = 8
    if "nc" not in _NC_CACHE:
        _NC_CACHE["nc"] = build(cfg)
    nc = _NC_CACHE["nc"]
    shared = {
        "norm_gdn": f(norm_gdn).reshape(1, -1), "w_in_gdn": f(w_in_gdn[0]), "conv_w": f(conv_gdn[0]),
        "a_log": f(a_log_gdn).reshape(1, -1), "dt_bias": f(dt_bias_gdn).reshape(1, -1), "onorm": f(onorm_gdn).reshape(1, -1),
        "w_out_gdn": f(w_out_gdn[0]), "norm_ssm": f(norm_ssm).reshape(1, -1), "w_in_ssm": f(w_in_ssm[0]),
        "lam_re": f(lam_re[0]), "lam_im": f(lam_im[0]), "b_re": f(b_re[0]), "b_im": f(b_im[0]), "c_re": f(c_re[0]), "c_im": f(c_im[0]),
        "d_ssm": f(d_ssm[0]).reshape(-1, 1), "log_dt": f(log_dt_ssm).reshape(1, -1), "w_glu": f(w_glu_ssm[0]),
        "b_glu": f(b_glu_ssm[0]).reshape(-1, 1), "w_out_ssm": f(w_out_ssm[0]), "norm_final": f(norm_final).reshape(1, -1),
        "consts": make_consts(),
    }
    in_maps = []
    for i in range(ncores):
        sq = i % B
        sl = slice(i * NS, (i + 1) * NS)
        m = dict(shared)
        m["xin"] = np.ascontiguousarray(np.concatenate([f(x_prompt[sq]), f(x_sample[sl, 0])], axis=0))
        m["conv_s"] = f(state_gdn_conv[0, sl])
        m["delta_s"] = f(state_gdn_delta[0, sl])
        m["re_s"] = f(state_ssm_re[0, sl])
        m["im_s"] = f(state_ssm_im[0, sl])
        in_maps.append(m)
    res = run_bass_kernel_spmd(nc, in_maps, core_ids=list(range(ncores))).results
    g = lambda i, k: np.asarray(res[i][k], dtype=np.float32)
    y_prompt = np.stack([g(i, "y_out")[:T] for i in range(B)])
    y_sample = np.concatenate([g(i, "y_out")[T:] for i in range(ncores)])[:, None, :]
    conv_prompt = np.stack([g(i, "conv_p") for i in range(B)])[None]
    delta_prompt = np.stack([g(i, "delta_p") for i in range(B)])[None]
    re_prompt = np.stack([g(i, "re_p") for i in range(B)])[None]
    im_prompt = np.stack([g(i, "im_p") for i in range(B)])[None]
    conv_sample = np.concatenate([g(i, "conv_so") for i in range(ncores)])[None]
    delta_sample = np.concatenate([g(i, "delta_so") for i in range(ncores)])[None]
    re_sample = np.concatenate([g(i, "re_so") for i in range(ncores)])[None]
    im_sample = np.concatenate([g(i, "im_so") for i in range(ncores)])[None]
    return (y_prompt, y_sample, conv_prompt, delta_prompt, re_prompt, im_prompt,
            conv_sample, delta_sample, re_sample, im_sample)
```

```python
import contextlib
import math
import numpy as np
import concourse.bass as bass
import concourse.mybir as mybir
from concourse.bass_utils import run_bass_kernel_spmd

F32 = mybir.dt.float32
BF16 = mybir.dt.bfloat16
AF = mybir.ActivationFunctionType
ALU = mybir.AluOpType
AX = mybir.AxisListType

FULL = dict(D=2048, T=2048, NS=16, HQK=16, G=256)


class Cfg:
    def __init__(self, D, T, NS, HQK, G):
        self.D, self.T, self.NS, self.HQK, self.G = D, T, NS, HQK, G
        self.KD = D // 128
        self.HV = 2 * HQK
        self.KEY = HQK * 128
        self.VAL = self.HV * 128
        self.CONV = 2 * self.KEY + self.VAL
        self.IN = self.CONV + self.VAL + 2 * self.HV
        self.W = 16 * G
        self.KW = self.W // 128
        self.TT = T + NS
        self.NCH = T // 128


class TR:
    def __init__(self, nc, es):
        self.nc, self.es = nc, es
        self.eng = dict(pe=nc.tensor, act=nc.scalar, dve=nc.vector, pool=nc.gpsimd, sp=nc.sync)
        self.sem = {}
        self.cnt = {}
        for k in ("pe", "act", "dve", "pool"):
            self.sem[k] = es.enter_context(nc.semaphore("s_" + k))
            self.cnt[k] = 0
        self.waited = {k: {} for k in self.eng}
        self.lastw = {}
        self.reads = {}
        self.nchan = 0

    def chan(self):
        self.nchan += 1
        k = "d%d" % self.nchan
        self.sem[k] = self.es.enter_context(self.nc.semaphore("s_" + k))
        self.cnt[k] = 0
        return k

    def _deps(self, e, reads, writes):
        deps = {}
        def add(ev):
            if ev is None:
                return
            k, v = ev
            if deps.get(k, 0) < v:
                deps[k] = v
        for r in reads:
            add(self.lastw.get(r))
            if r.startswith("pf") or r.startswith("pb"):
                for k, v in self.reads.get(r, {}).items():
                    if k != e:
                        add((k, v))
        for w in writes:
            add(self.lastw.get(w))
            for k, v in self.reads.get(w, {}).items():
                add((k, v))
        pend = []
        for k, v in deps.items():
            if k == "pe" and e == "pe":
                continue
            if self.waited[e].get(k, 0) >= v:
                continue
            pend.append((k, v))
            self.waited[e][k] = v
        for k, v in pend[:-1]:
            self.eng[e].wait_ge(self.sem[k], v)
        return pend[-1] if pend else None

    def _mark(self, ev, reads, writes):
        k, v = ev
        for r in reads:
            self.reads.setdefault(r, {})[k] = v
        for w in writes:
            self.lastw[w] = ev
            self.reads[w] = {}

    def op(self, e, reads, writes, fn):
        lw = self._deps(e, reads, writes)
        ins = fn(self.eng[e])
        if lw is not None:
            ins._wait_ge(self.sem[lw[0]], lw[1])
        self.cnt[e] += 1
        ins.then_inc(self.sem[e], 1)
        self._mark((e, self.cnt[e]), reads, writes)

    def dma(self, q, ch, reads, writes, out, in_, **kw):
        lw = self._deps(q, reads, writes)
        ins = self.eng[q].dma_start(out=out, in_=in_, **kw)
        if lw is not None:
            ins._wait_ge(self.sem[lw[0]], lw[1])
        ins.then_inc(self.sem[ch], 16)
        self.cnt[ch] += 16
        self._mark((ch, self.cnt[ch]), reads, writes)

    def barrier(self):
        for e in ("pe", "act", "dve", "pool", "sp"):
            for k in self.sem:
                if k != e and self.cnt[k] > 0 and self.waited[e].get(k, 0) < self.cnt[k]:
                    self.eng[e].wait_ge(self.sem[k], self.cnt[k])
                    self.waited[e][k] = self.cnt[k]

    def finish(self, q="sp"):
        for k in self.sem:
            if k.startswith("d") and self.cnt[k] > 0:
                self.eng[q].wait_ge(self.sem[k], self.cnt[k])
        for k in ("pe", "act", "dve", "pool"):
            if self.cnt[k] > 0:
                self.eng[q].wait_ge(self.sem[k], self.cnt[k])


def r3(ap, a, b):
    return ap.rearrange("p (a b) -> p a b", a=a, b=b)


class _Stop(Exception):
    pass


def build(cfg):
    c = cfg
    nc = bass.Bass("TRN2", target_bir_lowering=False)
    D, T, NS, TT, KD, HV, G = c.D, c.T, c.NS, c.TT, c.KD, c.HV, c.G

    def din(name, shape, dt=F32):
        return nc.dram_tensor(name, list(shape), dt, kind="ExternalInput").ap()

    def dout(name, shape, dt=F32):
        return nc.dram_tensor(name, list(shape), dt, kind="ExternalOutput").ap()

    def dscr(name, shape, dt):
        return nc.dram_tensor(name, list(shape), dt, kind="Internal").ap()

    xin = din("xin", [TT, D])
    conv_s = din("conv_s", [NS, 3, c.CONV])
    delta_s = din("delta_s", [NS, HV, 128, 128])
    re_s = din("re_s", [NS, G, 64])
    im_s = din("im_s", [NS, G, 64])
    norm_gdn = din("norm_gdn", [1, D])
    w_in_gdn = din("w_in_gdn", [D, c.IN])
    conv_w = din("conv_w", [4, c.CONV])
    a_log = din("a_log", [1, HV])
    dt_bias = din("dt_bias", [1, HV])
    onorm = din("onorm", [1, 128])
    w_out_gdn = din("w_out_gdn", [c.VAL, D])
    norm_ssm = din("norm_ssm", [1, D])
    w_in_ssm = din("w_in_ssm", [D, 2 * c.W])
    lam_re = din("lam_re", [G, 64])
    lam_im = din("lam_im", [G, 64])
    b_re = din("b_re", [G, 64, 16])
    b_im = din("b_im", [G, 64, 16])
    c_re = din("c_re", [G, 16, 64])
    c_im = din("c_im", [G, 16, 64])
    d_ssm = din("d_ssm", [c.W, 1])
    log_dt = din("log_dt", [1, G])
    w_glu = din("w_glu", [c.W, c.W])
    b_glu = din("b_glu", [c.W, 1])
    w_out_ssm = din("w_out_ssm", [c.W, D])
    norm_final = din("norm_final", [1, D])
    consts = din("consts", [128, 8 * 128])

    y_out = dout("y_out", [TT, D])
    conv_p = dout("conv_p", [3, c.CONV])
    delta_p = dout("delta_p", [HV, 128, 128])
    re_p = dout("re_p", [G, 64])
    im_p = dout("im_p", [G, 64])
    conv_so = dout("conv_so", [NS, 3, c.CONV])
    delta_so = dout("delta_so", [NS, HV, 128, 128])
    re_so = dout("re_so", [NS, G, 64])
    im_so = dout("im_so", [NS, G, 64])

    oT_scr = dscr("oT_scr", [c.VAL, TT], BF16)
    x1_scr = dscr("x1_scr", [TT, D], F32)
    yT_scr = dscr("yT_scr", [c.W, TT], BF16)
    y2_scr = dscr("y2_scr", [c.W, TT], BF16)
    x2_scr = dscr("x2_scr", [TT, D], F32)

    es = contextlib.ExitStack()
    with es:
        tr = TR(nc, es)
        cur = [es]
        try:

            def chk(k):
                if getattr(c, "stop", None) == k:
                    raise _Stop()

            def sb(name, shape, dt=F32):
                return cur[0].enter_context(nc.sbuf_tensor(name, list(shape), dt))

            def phase_begin():
                tr.barrier()
                cur[0] = contextlib.ExitStack()

            def phase_end():
                tr.barrier()
                cur[0].close()
                cur[0] = es

            def ps(name, shape, dt=F32):
                return es.enter_context(nc.psum_tensor(name, list(shape), dt))

            cst = sb("cst", [128, 8 * 128])
            ch_c = tr.chan()
            tr.dma("sp", ch_c, [], ["cst"], cst[:], consts[:, :])
            ident = cst[:, 0:128]
            triU = cst[:, 128:256]
            ones = cst[:, 256:384]
            MBIG = cst[:, 384:512]
            MNEG = cst[:, 512:640]
            CMASK = cst[:, 640:768]
            BD = cst[:, 768:896]
            GSEL = cst[:, 896:904]
            GG = cst[:, 904:968]
            cstb = sb("cstb", [128, 256], BF16)
            identb = cstb[:, 0:128]
            onesb = cstb[:, 128:256]
            tr.op("dve", ["cst"], ["cstb"], lambda e: e.tensor_copy(out=cstb[:, 0:128], in_=ident))
            tr.op("dve", ["cst"], ["cstb"], lambda e: e.tensor_copy(out=cstb[:, 128:256], in_=ones))

            def bcast_load(name, src, n):
                t = sb(name, [128, n])
                ch = tr.chan()
                tr.dma("sp", ch, [], [name], t[:], src[0:1, :].broadcast_to([128, n]))
                return t

            chg = tr.chan()
            nrm = {}
            alog_bc = bcast_load("alog_bc", a_log, HV)
            dtb_bc = bcast_load("dtb_bc", dt_bias, HV)
            ogain_bc = bcast_load("ogain_bc", onorm, 128)
            negA = sb("negA", [128, HV])
            tr.op("act", ["alog_bc"], ["negA"], lambda e: e.activation(out=negA[:], in_=alog_bc[:], func=AF.Exp))
            tr.op("dve", ["negA"], ["negA"], lambda e: e.tensor_scalar(out=negA[:], in0=negA[:], scalar1=-1.0, scalar2=None, op0=ALU.mult))

            hT = sb("hT", [128, KD * TT], BF16)
            hT3 = r3(hT[:], KD, TT)

            PF = [ps("pf%d" % i, [128, 512]) for i in range(6)]
            PB = [ps("pb%d" % i, [128, 1024], BF16) for i in range(2)]
            pf_i = [0]
            pb_i = [0]

            def pf():
                pf_i[0] = (pf_i[0] + 1) % len(PF)
                return PF[pf_i[0]], "pf%d" % pf_i[0]

            def pb():
                pb_i[0] = (pb_i[0] + 1) % len(PB)
                return PB[pb_i[0]], "pb%d" % pb_i[0]

            xtc = [tr.chan() for i in range(1)]
            stat = sb("stat", [128, 8])

            def tok_tiles():
                tl = [(i * 128, 128) for i in range(T // 128)]
                tl.append((T, NS))
                return tl

            def norm_to_hT(src, gsrc, srcname, addsrc=None, final_out=None):
                phase_begin()
                nrm["i"] = nrm.get("i", 0) + 1
                gbuf = sb("gbuf%d" % nrm["i"], [128, D])
                xt = [sb("xt%d_%d" % (nrm["i"], 0), [128, D])]
                hb = [sb("hb%d_%d" % (nrm["i"], 0), [128, D], BF16)]
                _norm_body(src, gsrc, srcname, final_out, gbuf, xt, hb)
                phase_end()

            def _norm_body(src, gsrc, srcname, final_out, gbuf, xt, hb):
                gain = gbuf
                gname = "gbuf"
                tr.dma("sp", chg, [], ["gbuf"], gbuf[:], gsrc[0:1, :].broadcast_to([128, D]))
                for it, (t0, n) in enumerate(tok_tiles()):
                    b = 0
                    tr.dma("sp", xtc[b], [srcname], ["xt%d" % b], xt[b][0:n, :], src[t0:t0 + n, :])
                    tr.op("act", ["xt%d" % b], ["hb%d" % b, "stat"], lambda e: e.activation(out=hb[b][0:n, :], in_=xt[b][0:n, :], func=AF.Square, accum_out=stat[0:n, 0:1]))
                    tr.op("dve", ["stat"], ["stat"], lambda e: e.tensor_scalar(out=stat[0:n, 1:2], in0=stat[0:n, 0:1], scalar1=1.0 / D, scalar2=1e-6, op0=ALU.mult, op1=ALU.add))
                    tr.op("act", ["stat"], ["stat"], lambda e: e.activation(out=stat[0:n, 3:4], in_=stat[0:n, 1:2], func=AF.Sqrt))
                    tr.op("dve", ["stat"], ["stat"], lambda e: e.reciprocal(out=stat[0:n, 2:3], in_=stat[0:n, 3:4]))
                    if final_out is not None:
                        tr.op("dve", ["xt%d" % b, "stat", gname], ["xt%d" % b], lambda e: e.scalar_tensor_tensor(out=xt[b][0:n, :], in0=xt[b][0:n, :], scalar=stat[0:n, 2:3], in1=gain[0:n, :], op0=ALU.mult, op1=ALU.mult))
                        tr.dma("sp", xtc[b], ["xt%d" % b], ["y_out"], final_out[t0:t0 + n, :], xt[b][0:n, :])
                        continue
                    tr.op("dve", ["xt%d" % b, "stat", gname], ["hb%d" % b], lambda e: e.scalar_tensor_tensor(out=hb[b][0:n, :], in0=xt[b][0:n, :], scalar=stat[0:n, 2:3], in1=gain[0:n, :], op0=ALU.mult, op1=ALU.mult))
                    for k0 in range(0, KD, 8):
                        kk = min(8, KD - k0)
                        p, pn = pb()
                        for k in range(kk):
                            tr.op("pe", ["hb%d" % b, "cstb"], [pn], lambda e: e.transpose(out=p[:, k * 128:k * 128 + n], in_=hb[b][0:n, (k0 + k) * 128:(k0 + k + 1) * 128], identity=identb[0:n, 0:n]))
                        tr.op("act" if (k0 // 8) % 2 == 0 else "dve", [pn], ["hT"],
                              (lambda e: e.activation(out=hT3[:, k0:k0 + kk, t0:t0 + n], in_=r3(p[:, 0:kk * 128], kk, 128)[:, :, 0:n], func=AF.Copy)) if (k0 // 8) % 2 == 0 else
                              (lambda e: e.tensor_copy(out=hT3[:, k0:k0 + kk, t0:t0 + n], in_=r3(p[:, 0:kk * 128], kk, 128)[:, :, 0:n])))

            chk(0)
            norm_to_hT(xin, norm_gdn, "xin")
            chk(1)

            phase_begin()
            NCOL = 772
            wj = [sb("wj%d" % i, [128, KD * NCOL], BF16) for i in range(1)]
            wjc = [tr.chan() for i in range(1)]
            NBLK = c.CONV // 128
            NR = 4 * NBLK
            cwT = sb("cwT", [128, NR])
            cwr = sb("cwr", [128, 128])
            chx = tr.chan()
            cw_rows = conv_w.rearrange("j (b c) -> (j b) c", c=128)
            for r0 in range(0, NR, 128):
                nr = min(128, NR - r0)
                tr.dma("sp", chx, [], ["cwr"], cwr[0:nr, :], cw_rows[r0:r0 + nr, :])
                p, pn = pf()
                tr.op("pe", ["cwr", "cst"], [pn], lambda e: e.transpose(out=p[:, 0:nr], in_=cwr[0:nr, :], identity=ident[0:nr, 0:nr]))
                tr.op("dve", [pn], ["cwT"], lambda e: e.tensor_copy(out=cwT[:, r0:r0 + nr], in_=p[:, 0:nr]))
            pre = [sb("pre0", [128, 3 + TT])] * 4
            if 3 + TT >= 1032:
                tail = pre[0][0:NS + 3, 8:520]
                cst48 = pre[0][0:NS * 3, 520:1032]
            else:
                tail = sb("tailx", [NS + 3, 512])[:, :]
                cst48 = sb("cst48x", [NS * 3, 512])[:, :]
            xp4 = [sb("xp4_0", [128, NS * 4])] * 4
            xs3 = sb("xs3", [128, 4 * NS * 3])
            ch48 = tr.chan()
            cv = [sb("cv0", [128, TT])] * 4
            tmpc = sb("tmpc", [128, TT])
            qT = sb("qT", [128, TT], BF16)
            kT = sb("kT", [128, TT], BF16)
            chtail = tr.chan()
            zba = sb("zba", [128, (c.NCH) * 260], BF16)
            zbas = sb("zbas", [1, NS * 260], BF16)
            gates = {}
            for nm, Cc, nch in (("p", 128, c.NCH), ("s", 1, NS)):
                for f in ("beta", "g", "gc", "egc", "bg", "ekd", "gl", "egl128", "tmp"):
                    gates[(nm, f)] = sb("gt_%s_%s" % (nm, f), [128, nch * 2])
            LANES = []
            for li in range(2):
                ln = {}
                ln["Sst"] = sb("Sst%d" % li, [128, 128]); ln["Sbf"] = sb("Sbf%d" % li, [128, 128], BF16)
                ln["chS"] = tr.chan(); ln["chSo"] = tr.chan(); ln["choT"] = tr.chan()
                ln["oTst"] = sb("oTst%d" % li, [128, TT], BF16)
                Wl = {}
                for nm, dt in (("kbg", F32), ("E1", F32), ("E2", F32), ("L", F32), ("N", F32),
                               ("P", F32), ("L2a", F32), ("L2b", F32), ("N2a", F32), ("N2b", F32), ("vn", BF16),
                               ("av", F32), ("gz", F32), ("og", BF16), ("sq", BF16)):
                    Wl[nm] = sb("w%d_%s" % (li, nm), [128, 128], dt)
                Wl["dg"] = sb("w%d_dg" % li, [128, 128])
                Wo = dict(Wl)
                for nm in ("kbg", "E1", "E2", "L", "N", "P", "L2a", "L2b", "N2a", "N2b", "dg"):
                    Wo[nm] = sb("w%do_%s" % (li, nm), [128, 128], F32)
                ln["WA"] = [Wl, Wo]
                ln["Adone"] = set()
                ln["H"] = []
                for par in range(2):
                    Hd = {}
                    for nm, dt in (("vb", F32), ("kd", BF16), ("AT", BF16), ("u", F32), ("wT", BF16)):
                        Hd[nm] = sb("h%d_%d_%s" % (li, par, nm), [128, 128], dt)
                    ln["H"].append(Hd)
                ln["A_done"] = 0
                ln["B_done"] = 0
                ln["C_done"] = 0
                ln["O"] = [sb("ho%d_%d" % (li, par), [128, 128]) for par in range(2)]
                ln["SS"] = [(sb("Sss%d_%d" % (li, par), [128, 128]), sb("Ssb%d_%d" % (li, par), [128, 128], BF16), tr.chan(), tr.chan()) for par in range(2)]
                ln["W"] = Wl
                ln["colst"] = sb("colst%d" % li, [128, 8])
                ln["id"] = li
                ln["pfb"] = [3 * li, 3 * li + 1, 3 * li + 2]
                ln["pfi"] = [0]
                ln["pbb"] = li
                LANES.append(ln)

            def load_wj(j, b):
                base = wj[b]
                w3 = r3(base[:], KD, NCOL)
                segs = [(0, j * 128, 128), (128, c.KEY + j * 128, 128), (256, 2 * c.KEY + j * 256, 256),
                        (512, c.CONV + j * 256, 256), (768, c.CONV + c.VAL + 2 * j, 2), (770, c.CONV + c.VAL + HV + 2 * j, 2)]
                for (o, s0, n) in segs:
                    tr.dma("pool", wjc[b], [], ["wj%d" % b], w3[:, :, o:o + n], w_in_gdn[:, s0:s0 + n].rearrange("(k p) n -> p k n", p=128))

            def chunk(ln, part, seq, nm, Cc, ci, cols, j, hh, first, last, n_idx):
                c0 = cols
                W = ln["W"]; Sst = ln["Sst"]; Sbf = ln["Sbf"]; chS = ln["chS"]; chSo = ln["chSo"]; oTst = ln["oTst"]; colst = ln["colst"]
                WN = "w%d_" % ln["id"]; SN = "Sst%d" % ln["id"]; BN = "Sbf%d" % ln["id"]; ON = "oTst%d" % ln["id"]; CN = "colst%d" % ln["id"]

                H = ln["H"][seq % 2]
                HN = "h%d_%d_" % (ln["id"], seq % 2)
                KO = 256 * (seq % 2)
                if part == "A":
                    W = ln["WA"][seq % 2]
                    WN = "w%d%s_" % (ln["id"], "o" if seq % 2 else "")

                def pf():
                    if part == "B":
                        i_ = ln["pfb"][2]
                    else:
                        i_ = ln["pfb"][seq % 2]
                    return PF[i_], "pf%d" % i_

                def pb():
                    return PB[ln["pbb"]], "pb%d" % ln["pbb"]
                h = 2 * j + hh
                gi = ci * 2 + hh
                G_ = lambda f: gates[(nm, f)]
                gname = lambda f: "gt_%s_%s" % (nm, f)
                if part == "A":
                    while ln["B_done"] < seq - 1:
                        yield "blocked"
                    p, pn = pb()
                    tr.op("pe", ["kT", "cstb"], [pn], lambda e: e.transpose(out=p[0:Cc, KO:KO + 128], in_=kT[:, c0:c0 + Cc], identity=identb))
                    yield
                    tr.op("pe", ["cvb%d" % hh, "cstb"], [pn], lambda e: e.transpose(out=p[0:Cc, KO + 128:KO + 256], in_=cvb[hh][:, c0:c0 + Cc], identity=identb))
                    yield
                    tr.op("dve", [pn, gname("beta")], [HN + "vb"], lambda e: e.tensor_scalar(out=H["vb"][0:Cc, :], in0=p[0:Cc, KO + 128:KO + 256], scalar1=G_("beta")[0:Cc, gi:gi + 1], scalar2=None, op0=ALU.mult))
                    yield
                    tr.op("dve", [pn, gname("bg")], [WN + "kbg"], lambda e: e.tensor_scalar(out=W["kbg"][0:Cc, :], in0=p[0:Cc, KO:KO + 128], scalar1=G_("bg")[0:Cc, gi:gi + 1], scalar2=None, op0=ALU.mult))
                    yield
                    tr.op("act", [pn, gname("ekd")], [HN + "kd"], lambda e: e.activation(out=H["kd"][0:Cc, :], in_=p[0:Cc, KO:KO + 128], func=AF.Copy, scale=G_("ekd")[0:Cc, gi:gi + 1]))
                    yield
                    chk(10)
                    if Cc > 1:
                        pk, pkn = pf()
                        tr.op("pe", ["kT"], [pkn], lambda e: e.matmul(pk[0:Cc, 0:Cc], lhsT=kT[:, c0:c0 + Cc], rhs=kT[:, c0:c0 + Cc], start=True, stop=True))
                        yield
                        tr.op("pe", ["kT", "qT"], [pkn], lambda e: e.matmul(pk[0:Cc, 128:128 + Cc], lhsT=kT[:, c0:c0 + Cc], rhs=qT[:, c0:c0 + Cc], start=True, stop=True))
                        yield
                        tr.op("dve", ["cst", gname("gc")], [WN + "dg"], lambda e: e.tensor_scalar(out=W["dg"][0:Cc, 0:Cc], in0=ident[0:Cc, 0:Cc], scalar1=G_("gc")[0:Cc, gi:gi + 1], scalar2=None, op0=ALU.mult))
                        yield
                        tr.op("pe", ["cst", WN + "dg"], [pkn], lambda e: e.matmul(pk[0:Cc, 256:256 + Cc], lhsT=ones[0:Cc, 0:Cc], rhs=W["dg"][0:Cc, 0:Cc], start=True, stop=True))
                        yield
                        R = pk[0:Cc, 256:256 + Cc]
                        chk(11)
                        tr.op("dve", [pkn, gname("gc"), "cst"], [WN + "E1"], lambda e: e.scalar_tensor_tensor(out=W["E1"][0:Cc, 0:Cc], in0=R, scalar=G_("gc")[0:Cc, gi:gi + 1], in1=MBIG[0:Cc, 0:Cc], op0=ALU.subtract, op1=ALU.max))
                        yield
                        tr.op("dve", [pkn, gname("gc"), "cst"], [WN + "E2"], lambda e: e.scalar_tensor_tensor(out=W["E2"][0:Cc, 0:Cc], in0=R, scalar=G_("gc")[0:Cc, gi:gi + 1], in1=MNEG[0:Cc, 0:Cc], op0=ALU.subtract, op1=ALU.min))
                        yield
                        tr.op("act", [WN + "E1"], [WN + "E1"], lambda e: e.activation(out=W["E1"][0:Cc, 0:Cc], in_=W["E1"][0:Cc, 0:Cc], func=AF.Exp, scale=-1.0))
                        yield
                        tr.op("act", [WN + "E2"], [WN + "E2"], lambda e: e.activation(out=W["E2"][0:Cc, 0:Cc], in_=W["E2"][0:Cc, 0:Cc], func=AF.Exp))
                        yield
                        tr.op("dve", [pkn, gname("beta"), WN + "E1"], [WN + "L"], lambda e: e.scalar_tensor_tensor(out=W["L"][0:Cc, 0:Cc], in0=pk[0:Cc, 0:Cc], scalar=G_("beta")[0:Cc, gi:gi + 1], in1=W["E1"][0:Cc, 0:Cc], op0=ALU.mult, op1=ALU.mult))
                        yield
                        tr.op("dve", [pkn, WN + "E2"], [HN + "AT"], lambda e: e.tensor_tensor(out=H["AT"][0:Cc, 0:Cc], in0=pk[0:Cc, 128:128 + Cc], in1=W["E2"][0:Cc, 0:Cc], op=ALU.mult))
                        yield
                    else:
                        pk, pkn = pf()
                        tr.op("pe", ["kT", "qT"], [pkn], lambda e: e.matmul(pk[0:1, 128:129], lhsT=kT[:, c0:c0 + 1], rhs=qT[:, c0:c0 + 1], start=True, stop=True))
                        yield
                        tr.op("dve", [pkn], [HN + "AT"], lambda e: e.tensor_copy(out=H["AT"][0:1, 0:1], in_=pk[0:1, 128:129]))
                        yield
                    chk(12)
                    if Cc > 1:
                        p2, p2n = pf()
                        tr.op("pe", [WN + "L", "cst"], [p2n], lambda e: e.transpose(out=p2[0:Cc, 0:Cc], in_=W["L"][0:Cc, 0:Cc], identity=ident[0:Cc, 0:Cc]))
                        yield
                        tr.op("act", [p2n], [WN + "N"], lambda e: e.activation(out=W["N"][0:Cc, 0:Cc], in_=p2[0:Cc, 0:Cc], func=AF.Copy))
                        yield
                        tr.op("dve", ["cstb", WN + "N"], [WN + "P"], lambda e: e.tensor_tensor(out=W["P"][0:Cc, 0:Cc], in0=ident[0:Cc, 0:Cc], in1=W["N"][0:Cc, 0:Cc], op=ALU.subtract))
                        yield
                        chk(120)
                        Lk, Nk = "L", "N"
                        nsteps = int(math.log2(Cc)) - 1
                        for st in range(nsteps):
                            L2, N2 = ("L2a", "N2a") if st % 2 == 0 else ("L2b", "N2b")
                            pq, pqn = pf()
                            tr.op("pe", [WN + Nk, WN + Lk], [pqn], lambda e: e.matmul(pq[0:Cc, 0:Cc], lhsT=W[Nk][0:Cc, 0:Cc], rhs=W[Lk][0:Cc, 0:Cc], start=True, stop=True))
                            yield
                            if st < nsteps - 1:
                                tr.op("pe", [WN + Nk, WN + Lk], [pqn], lambda e: e.matmul(pq[0:Cc, 128:128 + Cc], lhsT=W[Lk][0:Cc, 0:Cc], rhs=W[Nk][0:Cc, 0:Cc], start=True, stop=True))
                                yield
                            tr.op("act", [pqn], [WN + L2], lambda e: e.activation(out=W[L2][0:Cc, 0:Cc], in_=pq[0:Cc, 0:Cc], func=AF.Copy))
                            yield
                            if st < nsteps - 1:
                                tr.op("dve", [pqn], [WN + N2], lambda e: e.tensor_copy(out=W[N2][0:Cc, 0:Cc], in_=pq[0:Cc, 128:128 + Cc]))
                                yield
                            tr.op("pe", [WN + L2, WN + "P"], [pqn], lambda e: e.matmul(pq[0:Cc, 256:256 + Cc], lhsT=W[L2][0:Cc, 0:Cc], rhs=W["P"][0:Cc, 0:Cc], start=True, stop=True))
                            yield
                            tr.op("dve", [pqn, WN + "P"], [WN + "P"], lambda e: e.tensor_tensor(out=W["P"][0:Cc, 0:Cc], in0=pq[0:Cc, 256:256 + Cc], in1=W["P"][0:Cc, 0:Cc], op=ALU.add))
                            yield
                            Lk, Nk = L2, N2
                            chk(121 + st)
                        TTm, TTn = W["P"][0:Cc, 0:Cc], WN + "P"
                    else:
                        TTm, TTn = ident[0:1, 0:1], "cst"
                    chk(13)
                    pu, pun = pf()
                    if Cc > 1:
                        tr.op("pe", [TTn, HN + "vb"], [pun], lambda e: e.matmul(pu[0:Cc, 0:128], lhsT=TTm, rhs=H["vb"][0:Cc, :], start=True, stop=True))
                        yield
                        tr.op("act", [pun], [HN + "u"], lambda e: e.activation(out=H["u"][0:Cc, :], in_=pu[0:Cc, 0:128], func=AF.Copy))
                        yield
                        Uap, Un = H["u"], HN + "u"
                    else:
                        Uap, Un = H["vb"], HN + "vb"
                    tr.op("pe", [TTn, WN + "kbg"], [pun], lambda e: e.matmul(pu[:, 128:128 + Cc], lhsT=W["kbg"][0:Cc, :], rhs=TTm, start=True, stop=True))
                    yield
                    tr.op("act", [pun], [HN + "wT"], lambda e: e.activation(out=H["wT"][:, 0:Cc], in_=pu[:, 128:128 + Cc], func=AF.Copy))
                    yield
                    chk(14)
                    ln["Adone"].add(seq)
                    return
                Otile = ln["O"][seq % 2]
                OnN = "ho%d_%d" % (ln["id"], seq % 2)
                if nm == "s":
                    Sst, Sbf, chS, chSo = ln["SS"][seq % 2]
                    SN = "Sss%d_%d" % (ln["id"], seq % 2)
                    BN = "Ssb%d_%d" % (ln["id"], seq % 2)
                if part == "C":
                    while ln["B_done"] <= seq:
                        yield "blocked"
                else:
                    while (seq not in ln["Adone"]) or ln["C_done"] < seq - 1:
                        yield "blocked"
                    if Cc > 1:
                        Uap, Un = H["u"], HN + "u"
                    else:
                        Uap, Un = H["vb"], HN + "vb"
                    if nm == "s":
                        tr.dma("sp", chS, ["delta_s"], [SN], Sst[:], delta_s[n_idx, h, :, :])
                        yield
                        tr.op("act", [SN], [BN], lambda e: e.activation(out=Sbf[:], in_=Sst[:], func=AF.Copy))
                        yield
                    elif first:
                        tr.op("pool", [], [SN], lambda e: e.memset(Sst[:], 0.0))
                        yield
                        tr.op("pool", [], [BN], lambda e: e.memset(Sbf[:], 0.0))
                        yield
                    pw, pwn = pf()
                    tr.op("pe", [HN + "wT", BN], [pwn], lambda e: e.matmul(pw[0:Cc, 0:128], lhsT=H["wT"][:, 0:Cc], rhs=Sbf[:], start=True, stop=True))
                    yield
                    tr.op("pe", ["qT", BN], [pwn], lambda e: e.matmul(pw[0:Cc, 128:256], lhsT=qT[:, c0:c0 + Cc], rhs=Sbf[:], start=True, stop=True))
                    yield
                    tr.op("dve", [pwn, Un], [WN + "vn"], lambda e: e.tensor_tensor(out=W["vn"][0:Cc, :], in0=Uap[0:Cc, :], in1=pw[0:Cc, 0:128], op=ALU.subtract))
                    yield
                    tr.op("pe", [HN + "kd", WN + "vn"], [pwn], lambda e: e.matmul(pw[:, 384:512], lhsT=H["kd"][0:Cc, :], rhs=W["vn"][0:Cc, :], start=True, stop=True))
                    yield
                    tr.op("pe", [HN + "AT", WN + "vn"], [pwn], lambda e: e.matmul(pw[0:Cc, 256:384], lhsT=H["AT"][0:Cc, 0:Cc], rhs=W["vn"][0:Cc, :], start=True, stop=True))
                    yield
                    tr.op("dve", [pwn, SN, gname("egl128")], [SN], lambda e: e.scalar_tensor_tensor(out=Sst[:], in0=Sst[:], scalar=G_("egl128")[:, gi:gi + 1], in1=pw[:, 384:512], op0=ALU.mult, op1=ALU.add))
                    yield
                    if nm == "s":
                        tr.dma("pool", chSo, [SN], ["delta_so"], delta_so[n_idx, h, :, :], Sst[:])
                        yield
                    elif last:
                        tr.dma("pool", chSo, [SN], ["delta_p"], delta_p[h, :, :], Sst[:])
                        yield
                    else:
                        tr.op("act", [SN], [BN], lambda e: e.activation(out=Sbf[:], in_=Sst[:], func=AF.Copy))
                        yield
                    tr.op("act", [pwn], [WN + "av"], lambda e: e.activation(out=W["av"][0:Cc, :], in_=pw[0:Cc, 256:384], func=AF.Copy))
                    yield
                    tr.op("dve", [pwn, WN + "av", gname("egc")], [OnN], lambda e: e.scalar_tensor_tensor(out=Otile[0:Cc, :], in0=pw[0:Cc, 128:256], scalar=G_("egc")[0:Cc, gi:gi + 1], in1=W["av"][0:Cc, :], op0=ALU.mult, op1=ALU.add))
                    yield
                    ln["B_done"] = seq + 1
                    return
                zsrc = (zba if nm == "p" else zbas)
                zname = "zba" if nm == "p" else "zbas"
                zap = zsrc[0:Cc, ci * 260 + hh * 128: ci * 260 + hh * 128 + 128]
                tr.op("act", [OnN], [WN + "sq", CN], lambda e: e.activation(out=W["sq"][0:Cc, :], in_=Otile[0:Cc, :], func=AF.Square, accum_out=colst[0:Cc, 0:1]))
                yield
                tr.op("dve", [CN], [CN], lambda e: e.tensor_scalar(out=colst[0:Cc, 1:2], in0=colst[0:Cc, 0:1], scalar1=1.0 / 128, scalar2=1e-6, op0=ALU.mult, op1=ALU.add))
                yield
                tr.op("act", [CN], [CN], lambda e: e.activation(out=colst[0:Cc, 3:4], in_=colst[0:Cc, 1:2], func=AF.Sqrt))
                yield
                tr.op("dve", [CN], [CN], lambda e: e.reciprocal(out=colst[0:Cc, 2:3], in_=colst[0:Cc, 3:4]))
                yield
                tr.op("act", [zname], [WN + "gz"], lambda e: e.activation(out=W["gz"][0:Cc, :], in_=zap, func=AF.Silu))
                yield
                tr.op("pool", [WN + "gz", "ogain_bc"], [WN + "gz"], lambda e: e.tensor_tensor(out=W["gz"][0:Cc, :], in0=W["gz"][0:Cc, :], in1=ogain_bc[0:Cc, :], op=ALU.mult))
                yield
                tr.op("dve", [OnN, CN, WN + "gz"], [WN + "og"], lambda e: e.scalar_tensor_tensor(out=W["og"][0:Cc, :], in0=Otile[0:Cc, :], scalar=colst[0:Cc, 2:3], in1=W["gz"][0:Cc, :], op0=ALU.mult, op1=ALU.mult))
                yield
                p3, p3n = pb()
                tr.op("pe", [WN + "og", "cstb"], [p3n], lambda e: e.transpose(out=p3[:, 512:512 + Cc], in_=W["og"][0:Cc, :], identity=identb[0:Cc, 0:Cc]))
                yield
                tr.op("act", [p3n], [ON], lambda e: e.activation(out=oTst[:, c0:c0 + Cc], in_=p3[:, 512:512 + Cc], func=AF.Copy))
                yield
                ln["C_done"] = seq + 1

            cvb = [sb("cvb%d" % i, [128, TT], BF16) for i in range(2)]

            for j in range(c.HQK):
                chk(100 + j)
                b = 0
                load_wj(j, b)
                w3 = r3(wj[b][:], KD, NCOL)
                wn = "wj%d" % b
                chk(2)
                for (lo, m, dname) in ((T - 3, 3, "conv_p"), (T, NS, "conv_so")):
                    p, pn = pf()
                    for k in range(KD):
                        tr.op("pe", [wn, "hT"], [pn], lambda e: e.matmul(p[0:m, 0:512], lhsT=hT3[:, k, lo:lo + m], rhs=w3[:, k, 0:512], start=(k == 0), stop=(k == KD - 1)))
                    tr.op("act", [pn], ["pre0"], lambda e: e.activation(out=tail[0:m, :], in_=p[0:m, 0:512], func=AF.Copy))
                    for (o, s0, n) in ((0, j * 128, 128), (128, c.KEY + j * 128, 128), (256, 2 * c.KEY + j * 256, 256)):
                        if dname == "conv_p":
                            tr.dma("sp", chtail, ["pre0"], ["conv_p"], conv_p[0:3, s0:s0 + n], tail[0:3, o:o + n])
                        else:
                            tr.dma("sp", chtail, ["pre0"], ["conv_so"], conv_so[:, 2, s0:s0 + n], tail[0:NS, o:o + n])
                            tr.dma("sp", chtail, [], ["conv_so"], conv_so[:, 0:2, s0:s0 + n], conv_s[:, 1:3, s0:s0 + n])
                chk(3)
                for (o, s0, n) in ((0, j * 128, 128), (128, c.KEY + j * 128, 128), (256, 2 * c.KEY + j * 256, 256)):
                    tr.dma("sp", ch48, [], ["pre0"], cst48[:, o:o + n], conv_s[:, :, s0:s0 + n].rearrange("n j c -> (n j) c"))
                x4 = r3(xp4[0][:], NS, 4)
                for fb in range(4):
                    p, pn = pf()
                    tr.op("pe", ["pre0", "cst"], [pn], lambda e: e.transpose(out=p[:, 0:NS * 3], in_=cst48[:, fb * 128:(fb + 1) * 128], identity=ident[0:NS * 3, 0:NS * 3]))
                    tr.op("dve", [pn], ["xs3"], lambda e: e.tensor_copy(out=xs3[:, fb * NS * 3:(fb + 1) * NS * 3], in_=p[:, 0:NS * 3]))
                tr.op("pool", [], ["pre0"], lambda e: e.memset(pre[0][:, 0:3], 0.0))
                for fb in range(4):
                    for t0 in range(0, TT, 512):
                        n = min(512, TT - t0)
                        p, pn = pf()
                        for k in range(KD):
                            tr.op("pe", [wn, "hT"], [pn], lambda e: e.matmul(p[:, 0:n], lhsT=w3[:, k, fb * 128:(fb + 1) * 128], rhs=hT3[:, k, t0:t0 + n], start=(k == 0), stop=(k == KD - 1)))
                        np_ = max(0, min(n, T - t0))
                        if np_ > 0:
                            tr.op("act", [pn], ["pre0"], lambda e: e.activation(out=pre[0][:, 3 + t0:3 + t0 + np_], in_=p[:, 0:np_], func=AF.Copy))
                        if np_ < n:
                            s0 = t0 + np_ - T
                            ns_ = n - np_
                            tr.op("dve", [pn], ["xp4_0"], lambda e: e.tensor_copy(out=x4[:, s0:s0 + ns_, 3], in_=p[:, np_:n]))
                    tr.op("dve", ["xs3"], ["xp4_0"], lambda e: e.tensor_copy(out=x4[:, :, 0:3], in_=r3(xs3[:, fb * NS * 3:(fb + 1) * NS * 3], NS, 3)))
                    blk = [j, c.KEY // 128 + j, 2 * c.KEY // 128 + 2 * j, 2 * c.KEY // 128 + 2 * j + 1][fb]
                    cwb = lambda tp: cwT[:, tp * NBLK + blk:tp * NBLK + blk + 1]
                    acc, an, eng = tmpc, "tmpc", "dve"
                    tr.op(eng, ["pre0", "cwT"], [an], lambda e: e.tensor_scalar(out=acc[:, 0:T], in0=pre[0][:, 0:T], scalar1=cwb(0), scalar2=None, op0=ALU.mult))
                    for tp in (1, 2, 3):
                        tr.op(eng, ["pre0", "cwT", an], [an], lambda e: e.scalar_tensor_tensor(out=acc[:, 0:T], in0=pre[0][:, tp:tp + T], scalar=cwb(tp), in1=acc[:, 0:T], op0=ALU.mult, op1=ALU.add))
                    tr.op(eng, ["xp4_0", "cwT"], [an], lambda e: e.tensor_scalar(out=acc[:, T:TT], in0=x4[:, :, 0], scalar1=cwb(0), scalar2=None, op0=ALU.mult))
                    for tp in (1, 2, 3):
                        tr.op(eng, ["xp4_0", "cwT", an], [an], lambda e: e.scalar_tensor_tensor(out=acc[:, T:TT], in0=x4[:, :, tp], scalar=cwb(tp), in1=acc[:, T:TT], op0=ALU.mult, op1=ALU.add))
                    tr.op("act", [an], ["cv0"], lambda e: e.activation(out=cv[0][:], in_=acc[:], func=AF.Silu))
                    if fb < 2:
                        dstT, dn, scl = ((qT, "qT", 128 ** -0.5), (kT, "kT", 1.0))[fb]
                        sqb = pre[0][:, 3:3 + TT]
                        rinv = tmpc
                        tr.op("act", ["cv0"], ["pre0"], lambda e: e.activation(out=sqb, in_=cv[0][:], func=AF.Square))
                        for t0 in range(0, TT, 512):
                            n = min(512, TT - t0)
                            p, pn = pf()
                            tr.op("pe", ["cst", "pre0"], [pn], lambda e: e.matmul(p[:, 0:n], lhsT=ones, rhs=sqb[:, t0:t0 + n], start=True, stop=True))
                            tr.op("dve", [pn], ["tmpc"], lambda e: e.tensor_scalar(out=rinv[:, t0:t0 + n], in0=p[:, 0:n], scalar1=1e-6, scalar2=None, op0=ALU.add))
                            tr.op("act", ["tmpc"], ["tmpc"], lambda e: e.activation(out=rinv[:, t0:t0 + n], in_=rinv[:, t0:t0 + n], func=AF.Sqrt))
                            tr.op("dve", ["tmpc"], ["tmpc"], lambda e: e.reciprocal(out=rinv[:, t0:t0 + n], in_=rinv[:, t0:t0 + n]))
                        tr.op("dve", ["cv0", "tmpc"], [dn], lambda e: e.scalar_tensor_tensor(out=dstT[:], in0=cv[0][:], scalar=scl, in1=rinv[:], op0=ALU.mult, op1=ALU.mult))
                    else:
                        hh = fb - 2
                        tr.op("pool", ["cv0"], ["cvb%d" % hh], lambda e: e.tensor_copy(out=cvb[hh][:], in_=cv[0][:]))
                chk(4)
                for nm, Cc, nch, zt, zn in (("p", 128, c.NCH, zba, "zba"), ("s", 1, NS, zbas, "zbas")):
                    for ci in range(nch):
                        c0 = ci * 128 if nm == "p" else T + ci
                        p, pn = pf()
                        for k in range(KD):
                            tr.op("pe", [wn, "hT"], [pn], lambda e: e.matmul(p[0:Cc, 0:260], lhsT=hT3[:, k, c0:c0 + Cc], rhs=w3[:, k, 512:772], start=(k == 0), stop=(k == KD - 1)))
                        tr.op("act", [pn], [zn], lambda e: e.activation(out=zt[0:Cc, ci * 260:(ci + 1) * 260], in_=p[0:Cc, 0:260], func=AF.Copy))
                    z3 = r3(zt[0:Cc, 0:nch * 260], nch, 260)
                    Gt = lambda f: r3(gates[(nm, f)][:, 0:nch * 2], nch, 2)
                    gn = lambda f: "gt_%s_%s" % (nm, f)
                    tr.op("act", [zn], [gn("beta")], lambda e: e.activation(out=Gt("beta")[0:Cc], in_=z3[:, :, 256:258], func=AF.Sigmoid))
                    for hh in range(2):
                        tr.op("act", [zn, "dtb_bc"], [gn("tmp")], lambda e: e.activation(out=Gt("tmp")[0:Cc, :, hh], in_=z3[:, :, 258 + hh], func=AF.Exp, bias=dtb_bc[0:Cc, 2 * j + hh:2 * j + hh + 1]))
                    tr.op("act", [gn("tmp")], [gn("tmp")], lambda e: e.activation(out=gates[(nm, "tmp")][0:Cc, 0:nch * 2], in_=gates[(nm, "tmp")][0:Cc, 0:nch * 2], func=AF.Ln, bias=1.0))
                    for hh in range(2):
                        tr.op("dve", [gn("tmp"), "negA"], [gn("g")], lambda e: e.tensor_scalar(out=Gt("g")[0:Cc, :, hh], in0=Gt("tmp")[0:Cc, :, hh], scalar1=negA[0:Cc, 2 * j + hh:2 * j + hh + 1], scalar2=None, op0=ALU.mult))
                    p, pn = pf()
                    n2 = nch * 2
                    gg = gates[(nm, "g")]
                    tr.op("pe", ["cst", gn("g")], [pn], lambda e: e.matmul(p[0:Cc, 0:n2], lhsT=triU[0:Cc, 0:Cc], rhs=gg[0:Cc, 0:n2], start=True, stop=True))
                    tr.op("pe", ["cst", gn("g")], [pn], lambda e: e.matmul(p[0:Cc, 64:64 + n2], lhsT=ones[0:Cc, 0:Cc], rhs=gg[0:Cc, 0:n2], start=True, stop=True))
                    tr.op("pe", ["cst", gn("g")], [pn], lambda e: e.matmul(p[:, 128:128 + n2], lhsT=ones[0:Cc, :], rhs=gg[0:Cc, 0:n2], start=True, stop=True))
                    gt = lambda f: gates[(nm, f)]
                    tr.op("dve", [pn], [gn("gc")], lambda e: e.tensor_copy(out=gt("gc")[0:Cc, 0:n2], in_=p[0:Cc, 0:n2]))
                    tr.op("act", [pn], [gn("egc")], lambda e: e.activation(out=gt("egc")[0:Cc, 0:n2], in_=p[0:Cc, 0:n2], func=AF.Exp))
                    tr.op("dve", [gn("egc"), gn("beta")], [gn("bg")], lambda e: e.tensor_tensor(out=gt("bg")[0:Cc, 0:n2], in0=gt("egc")[0:Cc, 0:n2], in1=gt("beta")[0:Cc, 0:n2], op=ALU.mult))
                    tr.op("dve", [pn, gn("gc")], [gn("gl")], lambda e: e.tensor_tensor(out=gt("gl")[0:Cc, 0:n2], in0=p[0:Cc, 64:64 + n2], in1=gt("gc")[0:Cc, 0:n2], op=ALU.subtract))
                    tr.op("act", [gn("gl")], [gn("ekd")], lambda e: e.activation(out=gt("ekd")[0:Cc, 0:n2], in_=gt("gl")[0:Cc, 0:n2], func=AF.Exp))
                    tr.op("act", [pn], [gn("egl128")], lambda e: e.activation(out=gt("egl128")[:, 0:n2], in_=p[:, 128:128 + n2], func=AF.Exp))
                chk(5)
                def lane_gen(hh, part, parity=None):
                    ln = LANES[hh]
                    seq = 0
                    for ci in range(c.NCH):
                        if parity is None or seq % 2 == parity:
                            yield from chunk(ln, part, seq, "p", 128, ci, ci * 128, j, hh, ci == 0, ci == c.NCH - 1, None)
                        seq += 1
                    for n_ in range(NS):
                        if parity is None or seq % 2 == parity:
                            yield from chunk(ln, part, seq, "s", 1, n_, T + n_, j, hh, False, False, n_)
                        seq += 1
                    if part == "C":
                        h = 2 * j + hh
                        tr.dma("sp", ln["choT"], ["oTst%d" % hh], ["oT_scr"], oT_scr[h * 128:(h + 1) * 128, :], ln["oTst"][:])

                for ln_ in LANES:
                    ln_["A_done"] = 0
                    ln_["B_done"] = 0
                    ln_["C_done"] = 0
                    ln_["Adone"] = set()
                active = [lane_gen(0, "A", 0), lane_gen(1, "A", 0), lane_gen(0, "A", 1), lane_gen(1, "A", 1),
                          lane_gen(0, "B"), lane_gen(1, "B"), lane_gen(0, "C"), lane_gen(1, "C")]
                while active:
                    for g_ in list(active):
                        try:
                            next(g_)
                        except StopIteration:
                            active.remove(g_)

            phase_end()
            chk(20)

            def outproj(srcT, sname, KS, wsrc, resid, rname, dst, dname, tag):
                phase_begin()
                CB = min(1024, D)
                NWB = 1 if CB > 512 else 2
                wo = [sb("wo%s%d" % (tag, i), [128, KS * CB], BF16) for i in range(NWB)]
                woc = [tr.chan() for i in range(NWB)]
                ot = [sb("ot%s%d" % (tag, i), [128, KS * 128], BF16) for i in range(2)]
                otc = [tr.chan() for i in range(2)]
                xr = [sb("xr%s%d" % (tag, i), [128, CB]) for i in range(2)]
                xrc = [tr.chan() for i in range(2)]
                it = 0
                for cb in range(D // CB):
                    wb = cb % NWB
                    tr.dma("pool", woc[wb], [], ["wo%s%d" % (tag, wb)], r3(wo[wb][:], KS, CB), wsrc[:, cb * CB:(cb + 1) * CB].rearrange("(k p) n -> p k n", p=128))
                    w3_ = r3(wo[wb][:], KS, CB)
                    for (t0, n) in tok_tiles():
                        b = it % 2
                        it += 1
                        o3 = r3(ot[b][:], KS, 128)
                        tr.dma("sp", otc[b], [sname], ["ot%s%d" % (tag, b)], o3[:, :, 0:n], srcT[:, t0:t0 + n].rearrange("(k p) t -> p k t", p=128))
                        tr.dma("sp", xrc[b], [rname], ["xr%s%d" % (tag, b)], xr[b][0:n, :], resid[t0:t0 + n, cb * CB:(cb + 1) * CB])
                        for h0 in range(0, CB, 512):
                            hn = min(512, CB - h0)
                            p, pn = pf()
                            for k in range(KS):
                                tr.op("pe", ["ot%s%d" % (tag, b), "wo%s%d" % (tag, wb)], [pn], lambda e: e.matmul(p[0:n, 0:hn], lhsT=o3[:, k, 0:n], rhs=w3_[:, k, h0:h0 + hn], start=(k == 0), stop=(k == KS - 1)))
                            tr.op("dve", [pn, "xr%s%d" % (tag, b)], ["xr%s%d" % (tag, b)], lambda e: e.tensor_tensor(out=xr[b][0:n, h0:h0 + hn], in0=p[0:n, 0:hn], in1=xr[b][0:n, h0:h0 + hn], op=ALU.add))
                        tr.dma("sp", xrc[b], ["xr%s%d" % (tag, b)], [dname], dst[t0:t0 + n, cb * CB:(cb + 1) * CB], xr[b][0:n, :])
                phase_end()

            outproj(oT_scr, "oT_scr", c.VAL // 128, w_out_gdn, xin, "xin", x1_scr, "x1_scr", "a")
            chk(21)
            norm_to_hT(x1_scr, norm_ssm, "x1_scr")
            chk(22)

            phase_begin()
            GB = min(16, G)
            NT = GB // 8
            NCK = T // 8
            NC1 = 1 + NCK
            PI = math.pi

            def ew(e, out, in0, in1, op, r=(), w=()):
                tr.op(e, list(r), list(w), lambda en: en.tensor_tensor(out=out, in0=in0, in1=in1, op=op))

            tb = {k: sb("s5_" + k, [64, G]) for k in ("lamr", "lami", "dt", "ar", "ai", "fr", "fi", "t1", "t2", "t3")}
            lt = sb("s5_lt", [128, 64])
            chl = tr.chan()
            chdt = tr.chan()
            chdc = tr.chan()
            chBi = tr.chan()
            chcr = tr.chan()
            for (src_, dst_) in ((lam_re, "lamr"), (lam_im, "lami")):
                for r0 in range(0, G, 128):
                    nr = min(128, G - r0)
                    tr.dma("sp", chl, [], ["s5_lt"], lt[0:nr, :], src_[r0:r0 + nr, :])
                    p, pn = pf()
                    tr.op("pe", ["s5_lt", "cst"], [pn], lambda e: e.transpose(out=p[0:64, 0:nr], in_=lt[0:nr, :], identity=ident[0:nr, 0:nr]))
                    tr.op("dve", [pn], ["s5_" + dst_], lambda e: e.tensor_copy(out=tb[dst_][:, r0:r0 + nr], in_=p[0:64, 0:nr]))
            tr.dma("sp", chdt, [], ["s5_dt"], tb["dt"][:], log_dt[0:1, :].broadcast_to([64, G]))
            tr.op("act", ["s5_dt"], ["s5_dt"], lambda e: e.activation(out=tb["dt"][:], in_=tb["dt"][:], func=AF.Exp))
            tr.op("dve", ["s5_lamr"], ["s5_lamr"], lambda e: e.tensor_scalar(out=tb["lamr"][:], in0=tb["lamr"][:], scalar1=-1e-4, scalar2=None, op0=ALU.min))
            ew("dve", tb["t1"][:], tb["lamr"][:], tb["dt"][:], ALU.mult, ["s5_lamr", "s5_dt"], ["s5_t1"])
            tr.op("act", ["s5_t1"], ["s5_t1"], lambda e: e.activation(out=tb["t1"][:], in_=tb["t1"][:], func=AF.Exp))
            ew("dve", tb["t2"][:], tb["lami"][:], tb["dt"][:], ALU.mult, ["s5_lami", "s5_dt"], ["s5_t2"])
            tr.op("act", ["s5_t2"], ["s5_ai"], lambda e: e.activation(out=tb["ai"][:], in_=tb["t2"][:], func=AF.Sin, scale=1.0 / 32))
            tr.op("dve", ["s5_t2"], ["s5_t3"], lambda e: e.tensor_scalar(out=tb["t3"][:], in0=tb["t2"][:], scalar1=1.0 / 32, scalar2=PI / 2, op0=ALU.mult, op1=ALU.add))
            tr.op("act", ["s5_t3"], ["s5_ar"], lambda e: e.activation(out=tb["ar"][:], in_=tb["t3"][:], func=AF.Sin))
            for _ in range(5):
                ew("dve", tb["t3"][:], tb["ar"][:], tb["ar"][:], ALU.mult, ["s5_ar"], ["s5_t3"])
                ew("dve", tb["t2"][:], tb["ai"][:], tb["ai"][:], ALU.mult, ["s5_ai"], ["s5_t2"])
                tr.op("dve", ["s5_ar", "s5_ai"], ["s5_ai"], lambda e: e.scalar_tensor_tensor(out=tb["ai"][:], in0=tb["ar"][:], scalar=2.0, in1=tb["ai"][:], op0=ALU.mult, op1=ALU.mult))
                ew("dve", tb["ar"][:], tb["t3"][:], tb["t2"][:], ALU.subtract, ["s5_t3", "s5_t2"], ["s5_ar"])
            ew("dve", tb["ar"][:], tb["ar"][:], tb["t1"][:], ALU.mult, ["s5_ar", "s5_t1"], ["s5_ar"])
            ew("dve", tb["ai"][:], tb["ai"][:], tb["t1"][:], ALU.mult, ["s5_ai", "s5_t1"], ["s5_ai"])
            tr.op("dve", ["s5_ar"], ["s5_t1"], lambda e: e.tensor_scalar(out=tb["t1"][:], in0=tb["ar"][:], scalar1=-1.0, scalar2=None, op0=ALU.add))
            ew("dve", tb["t2"][:], tb["lamr"][:], tb["lamr"][:], ALU.mult, ["s5_lamr"], ["s5_t2"])
            ew("dve", tb["t3"][:], tb["lami"][:], tb["lami"][:], ALU.mult, ["s5_lami"], ["s5_t3"])
            ew("dve", tb["t2"][:], tb["t2"][:], tb["t3"][:], ALU.add, ["s5_t2", "s5_t3"], ["s5_t2"])
            tr.op("dve", ["s5_t2"], ["s5_t2"], lambda e: e.reciprocal(out=tb["t2"][:], in_=tb["t2"][:]))
            ew("dve", tb["fr"][:], tb["t1"][:], tb["lamr"][:], ALU.mult, ["s5_t1", "s5_lamr"], ["s5_fr"])
            ew("dve", tb["t3"][:], tb["ai"][:], tb["lami"][:], ALU.mult, ["s5_ai", "s5_lami"], ["s5_t3"])
            ew("dve", tb["fr"][:], tb["fr"][:], tb["t3"][:], ALU.add, ["s5_fr", "s5_t3"], ["s5_fr"])
            ew("dve", tb["fr"][:], tb["fr"][:], tb["t2"][:], ALU.mult, ["s5_fr", "s5_t2"], ["s5_fr"])
            ew("dve", tb["fi"][:], tb["ai"][:], tb["lamr"][:], ALU.mult, ["s5_ai", "s5_lamr"], ["s5_fi"])
            ew("dve", tb["t3"][:], tb["t1"][:], tb["lami"][:], ALU.mult, ["s5_t1", "s5_lami"], ["s5_t3"])
            ew("dve", tb["fi"][:], tb["fi"][:], tb["t3"][:], ALU.subtract, ["s5_fi", "s5_t3"], ["s5_fi"])
            ew("dve", tb["fi"][:], tb["fi"][:], tb["t2"][:], ALU.mult, ["s5_fi", "s5_t2"], ["s5_fi"])

            dcol = sb("s5_dcol", [128, c.KW])
            with nc.allow_non_contiguous_dma(reason="tiny per-channel vector"):
                tr.dma("sp", chdc, [], ["s5_dcol"], dcol[:], d_ssm.rearrange("(k p) o -> p (k o)", p=128))
            wuy = sb("s5_wu", [128, max(KD * GB * 16, NT * TT)], BF16)
            wu3 = r3(wuy[:, 0:KD * GB * 16], KD, GB * 16)
            chwu = tr.chan()
            uT = sb("s5_uT", [128, NT * TT], BF16)
            uT3 = r3(uT[:], NT, TT)
            yT3 = r3(wuy[:, 0:NT * TT], NT, TT)
            chy = tr.chan()
            PWR = sb("s5_pwr", [64, 9 * GB])
            PWI = sb("s5_pwi", [64, 9 * GB])
            pw_r = lambda m: PWR[:, m * GB:(m + 1) * GB]
            pw_i = lambda m: PWI[:, m * GB:(m + 1) * GB]
            GC = GB * 16
            Bt = {k: sb("s5_" + k, [64, GC]) for k in ("Br", "Bi", "Bbr", "Bbi", "CTr", "CTi", "e1", "e2")}
            chB = tr.chan()
            XR = sb("s5_XR", [64, 8 * GC], BF16)
            XI = sb("s5_XI", [64, 8 * GC], BF16)
            CPR = sb("s5_CPR", [64, 8 * GC], BF16)
            CPI = sb("s5_CPI", [64, 8 * GC], BF16)
            CTrb = sb("s5_CTrb", [64, GC], BF16)
            NCTib = sb("s5_NCTib", [64, GC], BF16)
            crow = sb("s5_crow", [128, 64])
            Kbd = sb("s5_Kbd", [128, 8 * 128], BF16)
            YP = [sb("s5_YP%d" % i, [128, 8 * 8 * 64], BF16) for i in range(2)]
            YTs = sb("s5_YTs", [128, 128], BF16)
            CPpr = sb("s5_CPpr", [64, 8 * 128], BF16)
            CPpi = sb("s5_CPpi", [64, 8 * 128], BF16)
            VB = sb("s5_VB", [64, 2 * GB * NC1])
            VB4 = VB[:].rearrange("p (a g n) -> p a g n", a=2, g=GB, n=NC1)
            XH = sb("s5_XH", [64, 2 * GB * NCK], BF16)
            XH4 = XH[:].rearrange("p (a g n) -> p a g n", a=2, g=GB, n=NCK)
            A8a = sb("s5_A8a", [64, 2 * GB])
            A8b = sb("s5_A8b", [64, 2 * GB])
            A1a = sb("s5_A1a", [64, 2 * GB])
            A1b = sb("s5_A1b", [64, 2 * GB])
            sc1 = sb("s5_sc1", [64, 2 * GB])
            sc2 = sb("s5_sc2", [64, 2 * GB])
            VS = sb("s5_VS", [64, 2 * NS * GB])
            VS4 = VS[:].rearrange("p (a n g) -> p a n g", a=2, n=NS, g=GB)
            XS = sb("s5_XS", [64, 2 * NS * GB])
            XS4 = XS[:].rearrange("p (a n g) -> p a n g", a=2, n=NS, g=GB)
            XSb = sb("s5_XSb", [64, 2 * NS * GB], BF16)
            XSb4 = XSb[:].rearrange("p (a n g) -> p a n g", a=2, n=NS, g=GB)
            stmp = sb("s5_stmp", [64, max(2 * NS * GB, 2 * GB * 32)])
            ss1 = stmp
            APW = sb("s5_APW", [64, int(math.log2(NCK)) * 4 * GB])
            srow = sb("s5_srow", [128, 64])
            chs = tr.chan()
            orow = sb("s5_orow", [128, 64])
            cho = tr.chan()
            ytmp = sb("s5_ytmp", [128, max(NCK + NS, 256)])
            YT8 = ytmp[:, 0:256].bitcast(BF16)

            def bc3(ap2, n):
                return ap2.unsqueeze(2).to_broadcast([64, GB, n])

            for blk in range(G // GB):
                g0 = blk * GB
                ch0 = g0 * 16
                tr.dma("pool", chwu, [], ["s5_wu"], wu3, w_in_ssm[:, ch0:ch0 + GC].rearrange("(k p) n -> p k n", p=128))
                for tl in range(NT):
                    for t0 in range(0, TT, 512):
                        n = min(512, TT - t0)
                        p, pn = pf()
                        for k in range(KD):
                            tr.op("pe", ["s5_wu", "hT"], [pn], lambda e: e.matmul(p[:, 0:n], lhsT=wu3[:, k, tl * 128:(tl + 1) * 128], rhs=hT3[:, k, t0:t0 + n], start=(k == 0), stop=(k == KD - 1)))
                        tr.op("act", [pn], ["s5_uT"], lambda e: e.activation(out=uT3[:, tl, t0:t0 + n], in_=p[:, 0:n], func=AF.Copy))
                tr.op("pool", [], ["s5_pwr"], lambda e: e.memset(pw_r(0), 1.0))
                tr.op("pool", [], ["s5_pwi"], lambda e: e.memset(pw_i(0), 0.0))
                arb = tb["ar"][:, g0:g0 + GB]
                aib = tb["ai"][:, g0:g0 + GB]
                e1 = Bt["e1"][:, 0:GB]
                e2 = Bt["e2"][:, 0:GB]
                for m in range(8):
                    ew("dve", e1, pw_r(m), arb, ALU.mult, ["s5_pwr", "s5_ar"], ["s5_e1"])
                    ew("dve", e2, pw_i(m), aib, ALU.mult, ["s5_pwi", "s5_ai"], ["s5_e2"])
                    ew("dve", pw_r(m + 1), e1, e2, ALU.subtract, ["s5_e1", "s5_e2"], ["s5_pwr"])
                    ew("dve", e1, pw_r(m), aib, ALU.mult, ["s5_pwr", "s5_ai"], ["s5_e1"])
                    ew("dve", e2, pw_i(m), arb, ALU.mult, ["s5_pwi", "s5_ar"], ["s5_e2"])
                    ew("dve", pw_i(m + 1), e1, e2, ALU.add, ["s5_e1", "s5_e2"], ["s5_pwi"])
                for (Aa, Ab, an, bn, m) in ((A8a, A8b, "s5_A8a", "s5_A8b", 8), (A1a, A1b, "s5_A1a", "s5_A1b", 1)):
                    tr.op("dve", ["s5_pwr"], [an], lambda e: e.tensor_copy(out=Aa[:, 0:GB], in_=pw_r(m)))
                    tr.op("dve", ["s5_pwi"], [an], lambda e: e.tensor_copy(out=Aa[:, GB:2 * GB], in_=pw_i(m)))
                    tr.op("dve", ["s5_pwi"], [bn], lambda e: e.tensor_scalar(out=Ab[:, 0:GB], in0=pw_i(m), scalar1=-1.0, scalar2=None, op0=ALU.mult))
                    tr.op("dve", ["s5_pwr"], [bn], lambda e: e.tensor_copy(out=Ab[:, GB:2 * GB], in_=pw_r(m)))
                B3 = lambda k: r3(Bt[k][:], GB, 16)
                tr.dma("sp", chB, [], ["s5_Br"], B3("Br"), b_re[g0:g0 + GB].rearrange("g p c -> p g c"))
                tr.dma("sp", chBi, [], ["s5_Bi"], B3("Bi"), b_im[g0:g0 + GB].rearrange("g p c -> p g c"))
                frb = bc3(tb["fr"][:, g0:g0 + GB], 16)
                fib = bc3(tb["fi"][:, g0:g0 + GB], 16)
                ew("dve", B3("Bbr"), B3("Br"), frb, ALU.mult, ["s5_Br", "s5_fr"], ["s5_Bbr"])
                ew("dve", B3("e1"), B3("Bi"), fib, ALU.mult, ["s5_Bi", "s5_fi"], ["s5_e1"])
                ew("dve", B3("Bbr"), B3("Bbr"), B3("e1"), ALU.subtract, ["s5_Bbr", "s5_e1"], ["s5_Bbr"])
                ew("dve", B3("Bbi"), B3("Bi"), frb, ALU.mult, ["s5_Bi", "s5_fr"], ["s5_Bbi"])
                ew("dve", B3("e1"), B3("Br"), fib, ALU.mult, ["s5_Br", "s5_fi"], ["s5_e1"])
                ew("dve", B3("Bbi"), B3("Bbi"), B3("e1"), ALU.add, ["s5_Bbi", "s5_e1"], ["s5_Bbi"])
                for tau in range(8):
                    prb = bc3(pw_r(tau), 16)
                    pib = bc3(pw_i(tau), 16)
                    xr_o = r3(XR[:, tau * GC:(tau + 1) * GC], GB, 16)
                    xi_o = r3(XI[:, tau * GC:(tau + 1) * GC], GB, 16)
                    ew("dve", B3("e1"), B3("Bbr"), prb, ALU.mult, ["s5_Bbr", "s5_pwr"], ["s5_e1"])
                    ew("pool", B3("e2"), B3("Bbi"), pib, ALU.mult, ["s5_Bbi", "s5_pwi"], ["s5_e2"])
                    ew("dve", xr_o, B3("e1"), B3("e2"), ALU.subtract, ["s5_e1", "s5_e2"], ["s5_XR"])
                    ew("dve", B3("e1"), B3("Bbi"), prb, ALU.mult, ["s5_Bbi", "s5_pwr"], ["s5_e1"])
                    ew("pool", B3("e2"), B3("Bbr"), pib, ALU.mult, ["s5_Bbr", "s5_pwi"], ["s5_e2"])
                    ew("dve", xi_o, B3("e1"), B3("e2"), ALU.add, ["s5_e1", "s5_e2"], ["s5_XI"])
                for (src_, dk) in ((c_re, "CTr"), (c_im, "CTi")):
                    rows = src_[g0:g0 + GB].rearrange("g c p -> (g c) p")
                    for r0 in range(0, GC, 128):
                        tr.dma("sp", chcr, [], ["s5_crow"], crow[:], rows[r0:r0 + 128, :])
                        p, pn = pf()
                        tr.op("pe", ["s5_crow", "cst"], [pn], lambda e: e.transpose(out=p[0:64, 0:128], in_=crow[:], identity=ident))
                        tr.op("dve", [pn], ["s5_" + dk], lambda e: e.tensor_copy(out=Bt[dk][:, r0:r0 + 128], in_=p[0:64, 0:128]))
                tr.op("dve", ["s5_CTr"], ["s5_CTrb"], lambda e: e.tensor_copy(out=CTrb[:], in_=Bt["CTr"][:]))
                tr.op("dve", ["s5_CTi"], ["s5_NCTib"], lambda e: e.tensor_scalar(out=NCTib[:], in0=Bt["CTi"][:], scalar1=-1.0, scalar2=None, op0=ALU.mult))
                for r_ in range(8):
                    prb = bc3(pw_r(r_ + 1), 16)
                    pib = bc3(pw_i(r_ + 1), 16)
                    cr_o = r3(CPR[:, r_ * GC:(r_ + 1) * GC], GB, 16)
                    ci_o = r3(CPI[:, r_ * GC:(r_ + 1) * GC], GB, 16)
                    ew("dve", B3("e1"), B3("CTr"), prb, ALU.mult, ["s5_CTr", "s5_pwr"], ["s5_e1"])
                    ew("pool", B3("e2"), B3("CTi"), pib, ALU.mult, ["s5_CTi", "s5_pwi"], ["s5_e2"])
                    ew("dve", cr_o, B3("e1"), B3("e2"), ALU.subtract, ["s5_e1", "s5_e2"], ["s5_CPR"])
                    ew("dve", B3("e1"), B3("CTr"), pib, ALU.mult, ["s5_CTr", "s5_pwi"], ["s5_e1"])
                    ew("pool", B3("e2"), B3("CTi"), prb, ALU.mult, ["s5_CTi", "s5_pwr"], ["s5_e2"])
                    ew("dve", B3("e1"), B3("e1"), B3("e2"), ALU.add, ["s5_e1", "s5_e2"], ["s5_e1"])
                    tr.op("dve", ["s5_e1"], ["s5_CPI"], lambda e: e.tensor_scalar(out=ci_o, in0=B3("e1"), scalar1=-1.0, scalar2=None, op0=ALU.mult))
                tr.op("pool", [], ["s5_VB"], lambda e: e.memset(VB[:], 0.0))
                for tl in range(NT):
                    for part, Xs, xn in ((0, XR, "s5_XR"), (1, XI, "s5_XI")):
                        Y4 = YP[part][:].rearrange("p (t g s) -> p t g s", t=8, g=8, s=64)
                        ypn = "s5_YP%d" % part
                        p, pn = pb()
                        for tau in range(8):
                            tr.op("pe", [xn, "cstb"], [pn], lambda e: e.transpose(out=p[:, tau * 64:(tau + 1) * 64], in_=Xs[:, tau * GC + tl * 128: tau * GC + (tl + 1) * 128], identity=identb[0:64, 0:64]))
                        tr.op("act", [pn], ["s5_ytmp"], lambda e: e.activation(out=YT8, in_=p[:, 0:512], func=AF.Copy))
                        tr.op("dve", ["s5_ytmp", "cst"], [ypn], lambda e: e.tensor_tensor(out=Y4, in0=YT8.rearrange("p (t s) -> p t s", t=8, s=64).unsqueeze(2).to_broadcast([128, 8, 8, 64]), in1=GSEL.unsqueeze(1).unsqueeze(3).to_broadcast([128, 8, 8, 64]), op=ALU.mult))
                        for g in range(8):
                            p, pn = pf()
                            for tau in range(8):
                                tr.op("pe", [ypn, "s5_uT"], [pn], lambda e: e.matmul(p[0:64, 0:NCK], lhsT=Y4[:, tau, g, :], rhs=uT3[:, tl, (7 - tau):T:8], start=(tau == 0), stop=(tau == 7)))
                            tr.op("pe", [ypn, "s5_uT"], [pn], lambda e: e.matmul(p[0:64, NCK:NCK + NS], lhsT=Y4[:, 0, g, :], rhs=uT3[:, tl, T:TT], start=True, stop=True))
                            tr.op("act", [pn], ["s5_VB"], lambda e: e.activation(out=VB4[:, part, tl * 8 + g, 1:1 + NCK], in_=p[0:64, 0:NCK], func=AF.Copy))
                            tr.op("dve", [pn], ["s5_VS"], lambda e: e.tensor_copy(out=VS4[:, part, :, tl * 8 + g], in_=p[0:64, NCK:NCK + NS]))
                chk(30)
                A8a3 = A8a[:].rearrange("p (a g) -> p a g", a=2, g=GB)
                A8b3 = A8b[:].rearrange("p (a g) -> p a g", a=2, g=GB)
                LV = int(math.log2(NCK))
                assert (1 << LV) == NCK
                PWT = 32
                APW4 = APW[:].rearrange("p (l q g) -> p l q g", l=LV, q=4, g=GB)
                tr.op("dve", ["s5_A8a"], ["s5_APW"], lambda e: e.tensor_copy(out=APW4[:, 0, 0:2, :], in_=A8a3))
                tr.op("dve", ["s5_A8b"], ["s5_APW"], lambda e: e.tensor_copy(out=APW4[:, 0, 2:4, :], in_=A8b3))
                for l in range(1, LV):
                    pr_, pi_ = APW4[:, l - 1, 0, :], APW4[:, l - 1, 1, :]
                    ew("dve", sc1[:, 0:GB], pr_, pr_, ALU.mult, ["s5_APW"], ["s5_sc1"])
                    ew("dve", sc1[:, GB:2 * GB], pi_, pi_, ALU.mult, ["s5_APW"], ["s5_sc1"])
                    ew("dve", APW4[:, l, 0, :], sc1[:, 0:GB], sc1[:, GB:2 * GB], ALU.subtract, ["s5_sc1"], ["s5_APW"])
                    tr.op("dve", ["s5_APW"], ["s5_APW"], lambda e: e.scalar_tensor_tensor(out=APW4[:, l, 1, :], in0=pr_, scalar=2.0, in1=pi_, op0=ALU.mult, op1=ALU.mult))
                    tr.op("dve", ["s5_APW"], ["s5_APW"], lambda e: e.tensor_copy(out=APW4[:, l, 3, :], in_=APW4[:, l, 0, :]))
                    tr.op("dve", ["s5_APW"], ["s5_APW"], lambda e: e.tensor_scalar(out=APW4[:, l, 2, :], in0=APW4[:, l, 1, :], scalar1=-1.0, scalar2=None, op0=ALU.mult))

                def cacc(l, tgt_sl, src_sl, cnt):
                    for q0 in range(0, cnt, PWT):
                        qn = min(PWT, cnt - q0)
                        t_lo, t_st = tgt_sl
                        s_lo, s_st = src_sl
                        tg = VB4[:, :, :, t_lo + q0 * t_st: t_lo + (q0 + qn - 1) * t_st + 1: t_st]
                        sr = VB4[:, 0:1, :, s_lo + q0 * s_st: s_lo + (q0 + qn - 1) * s_st + 1: s_st].to_broadcast([64, 2, GB, qn])
                        si = VB4[:, 1:2, :, s_lo + q0 * s_st: s_lo + (q0 + qn - 1) * s_st + 1: s_st].to_broadcast([64, 2, GB, qn])
                        pa = APW4[:, l, 0:2, :].unsqueeze(3).to_broadcast([64, 2, GB, qn])
                        pb_ = APW4[:, l, 2:4, :].unsqueeze(3).to_broadcast([64, 2, GB, qn])
                        tm = stmp[:, 0:2 * GB * qn].rearrange("p (a g n) -> p a g n", a=2, g=GB, n=qn)
                        ew("dve", tm, pa, sr, ALU.mult, ["s5_APW", "s5_VB"], ["s5_stmp"])
                        ew("dve", tg, tg, tm, ALU.add, ["s5_VB", "s5_stmp"], ["s5_VB"])
                        ew("dve", tm, pb_, si, ALU.mult, ["s5_APW", "s5_VB"], ["s5_stmp"])
                        ew("dve", tg, tg, tm, ALU.add, ["s5_VB", "s5_stmp"], ["s5_VB"])

                for l in range(LV):
                    s_ = 1 << l
                    cacc(l, (2 * s_, 2 * s_), (s_, 2 * s_), NCK // (2 * s_))
                for l in range(LV - 2, -1, -1):
                    s_ = 1 << l
                    cacc(l, (3 * s_, 2 * s_), (2 * s_, 2 * s_), NCK // (2 * s_) - 1)
                tr.op("act", ["s5_VB"], ["s5_XH"], lambda e: e.activation(out=XH4, in_=VB4[:, :, :, 0:NCK], func=AF.Copy))
                for part, dst_, dn_ in ((0, re_p, "re_p"), (1, im_p, "im_p")):
                    tr.op("dve", ["s5_VB"], ["s5_sc1"], lambda e: e.tensor_copy(out=sc1[:, 0:GB], in_=VB4[:, part, :, NCK]))
                    p, pn = pf()
                    tr.op("pe", ["s5_sc1", "cst"], [pn], lambda e: e.transpose(out=p[0:GB, 0:64], in_=sc1[:, 0:GB], identity=ident[0:64, 0:64]))
                    tr.op("act", [pn], ["s5_orow"], lambda e: e.activation(out=orow[0:GB, :], in_=p[0:GB, 0:64], func=AF.Copy))
                    tr.dma("sp", cho, ["s5_orow"], [dn_], dst_[g0:g0 + GB, :], orow[0:GB, :])
                RW = min(128, NS * GB)
                for part, src_ in ((0, re_s), (1, im_s)):
                    for r0 in range(0, NS * GB, RW):
                        n0 = r0 // GB
                        nn = RW // GB
                        tr.dma("sp", chs, [], ["s5_srow"], srow[0:RW, :], src_[n0:n0 + nn, g0:g0 + GB, :])
                        p, pn = pf()
                        tr.op("pe", ["s5_srow", "cst"], [pn], lambda e: e.transpose(out=p[0:64, 0:RW], in_=srow[0:RW, :], identity=ident[0:RW, 0:RW]))
                        tr.op("dve", [pn], ["s5_XS"], lambda e: e.tensor_copy(out=XS[:, part * NS * GB + r0: part * NS * GB + r0 + RW], in_=p[0:64, 0:RW]))
                tr.op("act", ["s5_XS"], ["s5_XSb"], lambda e: e.activation(out=XSb[:], in_=XS[:], func=AF.Copy))
                A1a4 = A1a[:].rearrange("p (a g) -> p a g", a=2, g=GB).unsqueeze(2).to_broadcast([64, 2, NS, GB])
                A1b4 = A1b[:].rearrange("p (a g) -> p a g", a=2, g=GB).unsqueeze(2).to_broadcast([64, 2, NS, GB])
                ss4 = stmp[:, 0:2 * NS * GB].rearrange("p (a n g) -> p a n g", a=2, n=NS, g=GB)
                ew("dve", ss4, A1a4, XS4[:, 0:1].to_broadcast([64, 2, NS, GB]), ALU.mult, ["s5_A1a", "s5_XS"], ["s5_stmp"])
                ew("dve", VS4, VS4, ss4, ALU.add, ["s5_VS", "s5_stmp"], ["s5_VS"])
                ew("dve", ss4, A1b4, XS4[:, 1:2].to_broadcast([64, 2, NS, GB]), ALU.mult, ["s5_A1b", "s5_XS"], ["s5_stmp"])
                ew("dve", VS4, VS4, ss4, ALU.add, ["s5_VS", "s5_stmp"], ["s5_VS"])
                for part, dst_, dn_ in ((0, re_so, "re_so"), (1, im_so, "im_so")):
                    for r0 in range(0, NS * GB, RW):
                        n0 = r0 // GB
                        nn = RW // GB
                        p, pn = pf()
                        tr.op("pe", ["s5_VS", "cst"], [pn], lambda e: e.transpose(out=p[0:RW, 0:64], in_=VS[:, part * NS * GB + r0: part * NS * GB + r0 + RW], identity=ident[0:64, 0:64]))
                        tr.op("act", [pn], ["s5_orow"], lambda e: e.activation(out=orow[0:RW, :], in_=p[0:RW, 0:64], func=AF.Copy))
                        tr.dma("sp", cho, ["s5_orow"], [dn_], dst_[n0:n0 + nn, g0:g0 + GB, :], orow[0:RW, :])
                chk(31)
                for tl in range(NT):
                    for t4 in range(0, 8, 4):
                        p, pn = pf()
                        for tau in range(t4, t4 + 4):
                            o_ = p[:, (tau - t4) * 128:(tau - t4 + 1) * 128]
                            tr.op("pe", ["s5_XR", "s5_CTrb"], [pn], lambda e: e.matmul(o_, lhsT=XR[:, tau * GC + tl * 128: tau * GC + (tl + 1) * 128], rhs=CTrb[:, tl * 128:(tl + 1) * 128], start=True, stop=False))
                            tr.op("pe", ["s5_XI", "s5_NCTib"], [pn], lambda e: e.matmul(o_, lhsT=XI[:, tau * GC + tl * 128: tau * GC + (tl + 1) * 128], rhs=NCTib[:, tl * 128:(tl + 1) * 128], start=False, stop=True))
                        tr.op("dve", [pn, "cst"], ["s5_Kbd"], lambda e: e.tensor_tensor(out=r3(Kbd[:, t4 * 128:(t4 + 4) * 128], 4, 128), in0=r3(p[:, 0:512], 4, 128), in1=BD.unsqueeze(1).to_broadcast([128, 4, 128]), op=ALU.mult))
                    for r_ in range(8):
                        for CPs, CPp, cn in ((CPR, CPpr, "s5_CPpr"), (CPI, CPpi, "s5_CPpi")):
                            src4 = CPs[:, r_ * GC + tl * 128: r_ * GC + (tl + 1) * 128].rearrange("p (g c) -> p g c", g=8, c=16).unsqueeze(2).to_broadcast([64, 8, 8, 16])
                            gg4 = GG[0:64, :].rearrange("p (g h) -> p g h", g=8, h=8).unsqueeze(3).to_broadcast([64, 8, 8, 16])
                            tr.op("pool" if cn == "s5_CPpr" else "dve", ["s5_CPR", "s5_CPI", "cst"], [cn], lambda e: e.tensor_tensor(out=CPp[:].rearrange("p (g h c) -> p g h c", g=8, h=8, c=16), in0=src4, in1=gg4, op=ALU.mult))
                        p, pn = pf()
                        mms = [(Kbd[:, tau * 128:(tau + 1) * 128], uT3[:, tl, (r_ - tau):T:8], ["s5_Kbd", "s5_uT"]) for tau in range(r_ + 1)]
                        for g in range(8):
                            mms.append((CPpr[:, g * 128:(g + 1) * 128], XH4[:, 0, tl * 8 + g, :], ["s5_CPpr", "s5_XH"]))
                            mms.append((CPpi[:, g * 128:(g + 1) * 128], XH4[:, 1, tl * 8 + g, :], ["s5_CPpi", "s5_XH"]))
                        for i_, (l_, rh_, rd_) in enumerate(mms):
                            tr.op("pe", rd_, [pn], lambda e: e.matmul(p[:, 0:NCK], lhsT=l_, rhs=rh_, start=(i_ == 0), stop=(i_ == len(mms) - 1)))
                        dsc = dcol[:, blk * NT + tl: blk * NT + tl + 1]
                        tr.op("dve", [pn, "s5_uT", "s5_dcol"], ["s5_ytmp"], lambda e: e.scalar_tensor_tensor(out=ytmp[:, 0:NCK], in0=uT3[:, tl, r_:T:8], scalar=dsc, in1=p[:, 0:NCK], op0=ALU.mult, op1=ALU.add))
                        tr.op("act", ["s5_ytmp"], ["s5_wu"], lambda e: e.activation(out=yT3[:, tl, r_:T:8], in_=ytmp[:, 0:NCK], func=AF.Gelu))
                        if r_ == 0:
                            mms = [(Kbd[:, 0:128], uT3[:, tl, T:TT], ["s5_Kbd", "s5_uT"])]
                            for g in range(8):
                                mms.append((CPpr[:, g * 128:(g + 1) * 128], XSb4[:, 0, :, tl * 8 + g], ["s5_CPpr", "s5_XSb"]))
                                mms.append((CPpi[:, g * 128:(g + 1) * 128], XSb4[:, 1, :, tl * 8 + g], ["s5_CPpi", "s5_XSb"]))
                            p2, p2n = pf()
                            for i_, (l_, rh_, rd_) in enumerate(mms):
                                tr.op("pe", rd_, [p2n], lambda e: e.matmul(p2[:, 0:NS], lhsT=l_, rhs=rh_, start=(i_ == 0), stop=(i_ == len(mms) - 1)))
                            tr.op("dve", [p2n, "s5_uT", "s5_dcol"], ["s5_ytmp"], lambda e: e.scalar_tensor_tensor(out=ytmp[:, NCK:NCK + NS], in0=uT3[:, tl, T:TT], scalar=dsc, in1=p2[:, 0:NS], op0=ALU.mult, op1=ALU.add))
                            tr.op("act", ["s5_ytmp"], ["s5_wu"], lambda e: e.activation(out=yT3[:, tl, T:TT], in_=ytmp[:, NCK:NCK + NS], func=AF.Gelu))
                    tr.dma("sp", chy, ["s5_wu"], ["yT_scr"], yT_scr[ch0 + tl * 128: ch0 + (tl + 1) * 128, :], yT3[:, tl, :])
                chk(32)
            phase_end()
            chk(33)

            phase_begin()
            KW = c.KW
            TBM = min(TT, 1040)
            yTa = sb("g_yTa", [128, KW * TBM], BF16)
            yTa3 = r3(yTa[:], KW, TBM)
            chya = tr.chan()
            wg = [sb("g_wg%d" % i, [128, KW * 128], BF16) for i in range(2)]
            wgc = [tr.chan() for i in range(2)]
            wz = [sb("g_wz%d" % i, [128, KD * 128], BF16) for i in range(2)]
            wzc = [tr.chan() for i in range(2)]
            bgl = sb("g_bgl", [128, KW])
            chbg = tr.chan()
            with nc.allow_non_contiguous_dma(reason="tiny per-channel vector"):
                tr.dma("sp", chbg, [], ["g_bgl"], bgl[:], b_glu.rearrange("(k p) o -> p (k o)", p=128))
            sgt = sb("g_sg", [128, 512])
            szt = sb("g_sz", [128, 512])
            y2t = [sb("g_y2%d" % i, [128, TBM], BF16) for i in range(2)]
            y2c = [tr.chan() for i in range(2)]
            it = 0
            for b0 in range(0, TT, TBM):
                bn = min(TBM, TT - b0)
                tr.dma("sp", chya, ["yT_scr"], ["g_yTa"], yTa3[:, :, 0:bn], yT_scr[:, b0:b0 + bn].rearrange("(k p) t -> p k t", p=128))
                for m in range(KW):
                    wb = it % 2
                    it += 1
                    wg3 = r3(wg[wb][:], KW, 128)
                    wz3 = r3(wz[wb][:], KD, 128)
                    tr.dma("pool", wgc[wb], [], ["g_wg%d" % wb], wg3, w_glu[:, m * 128:(m + 1) * 128].rearrange("(k p) n -> p k n", p=128))
                    tr.dma("pool", wzc[wb], [], ["g_wz%d" % wb], wz3, w_in_ssm[:, c.W + m * 128: c.W + (m + 1) * 128].rearrange("(k p) n -> p k n", p=128))
                    for t0 in range(0, bn, 512):
                        n = min(512, bn - t0)
                        pg, pgn = pf()
                        for k in range(KW):
                            tr.op("pe", ["g_wg%d" % wb, "g_yTa"], [pgn], lambda e: e.matmul(pg[:, 0:n], lhsT=wg3[:, k, :], rhs=yTa3[:, k, t0:t0 + n], start=(k == 0), stop=(k == KW - 1)))
                        pz, pzn = pf()
                        for k in range(KD):
                            tr.op("pe", ["g_wz%d" % wb, "hT"], [pzn], lambda e: e.matmul(pz[:, 0:n], lhsT=wz3[:, k, :], rhs=hT3[:, k, b0 + t0:b0 + t0 + n], start=(k == 0), stop=(k == KD - 1)))
                        tr.op("act", [pgn, "g_bgl"], ["g_sg"], lambda e: e.activation(out=sgt[:, 0:n], in_=pg[:, 0:n], func=AF.Sigmoid, bias=bgl[:, m:m + 1]))
                        tr.op("act", [pzn], ["g_sz"], lambda e: e.activation(out=szt[:, 0:n], in_=pz[:, 0:n], func=AF.Silu))
                        tr.op("pool", ["g_sg", "g_sz"], ["g_sg"], lambda e: e.tensor_tensor(out=sgt[:, 0:n], in0=sgt[:, 0:n], in1=szt[:, 0:n], op=ALU.mult))
                        tr.op("dve", ["g_sg", "g_yTa"], ["g_y2%d" % wb], lambda e: e.tensor_tensor(out=y2t[wb][:, t0:t0 + n], in0=sgt[:, 0:n], in1=yTa3[:, m, t0:t0 + n], op=ALU.mult))
                    tr.dma("sp", y2c[wb], ["g_y2%d" % wb], ["y2_scr"], y2_scr[m * 128:(m + 1) * 128, b0:b0 + bn], y2t[wb][:, 0:bn])
            phase_end()
            chk(34)
            outproj(y2_scr, "y2_scr", c.KW, w_out_ssm, x1_scr, "x1_scr", x2_scr, "x2_scr", "b")
            chk(35)
            norm_to_hT(x2_scr, norm_final, "x2_scr", final_out=y_out)
        except _Stop:
            if cur[0] is not es:
                cur[0].close()
                cur[0] = es
        tr.finish()
    return nc


def make_consts():
    cs = np.zeros((128, 8 * 128), np.float32)
    i = np.arange(128)
    cs[:, 0:128] = np.eye(128)
    cs[:, 128:256] = (i[:, None] <= i[None, :])
    cs[:, 256:384] = 1.0
    cs[:, 384:512] = np.where(i[None, :] < i[:, None], 0.0, 30000.0)
    cs[:, 512:640] = np.where(i[:, None] <= i[None, :], 0.0, -30000.0)
    cs[:, 640:768] = ((i[None, :] // 16) >= (i[:, None] // 16))
    cs[:, 768:896] = ((i[None, :] // 16) == (i[:, None] // 16))
    cs[:, 896:904] = ((i[:, None] // 16) == np.arange(8)[None, :])
    cs[:, 904:968] = np.eye(8).reshape(1, 64)
    return cs


_NC_CACHE = {}


def kernel(x_prompt, x_sample, state_gdn_conv, state_gdn_delta, state_ssm_re, state_ssm_im,
           norm_gdn, w_in_gdn, conv_gdn, a_log_gdn, dt_bias_gdn, onorm_gdn, w_out_gdn,
           norm_ssm, w_in_ssm, lam_re, lam_im, b_re, b_im, c_re, c_im, d_ssm, log_dt_ssm,
           w_glu_ssm, b_glu_ssm, w_out_ssm, norm_final):
    cfg = Cfg(**FULL)
    f = lambda a: np.ascontiguousarray(np.asarray(a, dtype=np.float32))
    NS, T = cfg.NS, cfg.T
    B = x_prompt.shape[0]
    ncores = 8
    if "nc" not in _NC_CACHE:
        _NC_CACHE["nc"] = build(cfg)
    nc = _NC_CACHE["nc"]
    shared = {
        "norm_gdn": f(norm_gdn).reshape(1, -1), "w_in_gdn": f(w_in_gdn[0]), "conv_w": f(conv_gdn[0]),
        "a_log": f(a_log_gdn).reshape(1, -1), "dt_bias": f(dt_bias_gdn).reshape(1, -1), "onorm": f(onorm_gdn).reshape(1, -1),
        "w_out_gdn": f(w_out_gdn[0]), "norm_ssm": f(norm_ssm).reshape(1, -1), "w_in_ssm": f(w_in_ssm[0]),
        "lam_re": f(lam_re[0]), "lam_im": f(lam_im[0]), "b_re": f(b_re[0]), "b_im": f(b_im[0]), "c_re": f(c_re[0]), "c_im": f(c_im[0]),
        "d_ssm": f(d_ssm[0]).reshape(-1, 1), "log_dt": f(log_dt_ssm).reshape(1, -1), "w_glu": f(w_glu_ssm[0]),
        "b_glu": f(b_glu_ssm[0]).reshape(-1, 1), "w_out_ssm": f(w_out_ssm[0]), "norm_final": f(norm_final).reshape(1, -1),
        "consts": make_consts(),
    }
    in_maps = []
    for i in range(ncores):
        sq = i % B
        sl = slice(i * NS, (i + 1) * NS)
        m = dict(shared)
        m["xin"] = np.ascontiguousarray(np.concatenate([f(x_prompt[sq]), f(x_sample[sl, 0])], axis=0))
        m["conv_s"] = f(state_gdn_conv[0, sl])
        m["delta_s"] = f(state_gdn_delta[0, sl])
        m["re_s"] = f(state_ssm_re[0, sl])
        m["im_s"] = f(state_ssm_im[0, sl])
        in_maps.append(m)
    res = run_bass_kernel_spmd(nc, in_maps, core_ids=list(range(ncores))).results
    g = lambda i, k: np.asarray(res[i][k], dtype=np.float32)
    y_prompt = np.stack([g(i, "y_out")[:T] for i in range(B)])
    y_sample = np.concatenate([g(i, "y_out")[T:] for i in range(ncores)])[:, None, :]
    conv_prompt = np.stack([g(i, "conv_p") for i in range(B)])[None]
    delta_prompt = np.stack([g(i, "delta_p") for i in range(B)])[None]
    re_prompt = np.stack([g(i, "re_p") for i in range(B)])[None]
    im_prompt = np.stack([g(i, "im_p") for i in range(B)])[None]
    conv_sample = np.concatenate([g(i, "conv_so") for i in range(ncores)])[None]
    delta_sample = np.concatenate([g(i, "delta_so") for i in range(ncores)])[None]
    re_sample = np.concatenate([g(i, "re_so") for i in range(ncores)])[None]
    im_sample = np.concatenate([g(i, "im_so") for i in range(ncores)])[None]
    return (y_prompt, y_sample, conv_prompt, delta_prompt, re_prompt, im_prompt,
            conv_sample, delta_sample, re_sample, im_sample)
```

```python
import contextlib
import math
import numpy as np
import concourse.bass as bass
import concourse.mybir as mybir
from concourse.bass_utils import run_bass_kernel_spmd

F32 = mybir.dt.float32
BF16 = mybir.dt.bfloat16
AF = mybir.ActivationFunctionType
ALU = mybir.AluOpType
AX = mybir.AxisListType

FULL = dict(D=2048, T=2048, NS=16, HQK=16, G=256)


class Cfg:
    def __init__(self, D, T, NS, HQK, G):
        self.D, self.T, self.NS, self.HQK, self.G = D, T, NS, HQK, G
        self.KD = D // 128
        self.HV = 2 * HQK
        self.KEY = HQK * 128
        self.VAL = self.HV * 128
        self.CONV = 2 * self.KEY + self.VAL
        self.IN = self.CONV + self.VAL + 2 * self.HV
        self.W = 16 * G
        self.KW = self.W // 128
        self.TT = T + NS
        self.NCH = T // 128


class TR:
    def __init__(self, nc, es):
        self.nc, self.es = nc, es
        self.eng = dict(pe=nc.tensor, act=nc.scalar, dve=nc.vector, pool=nc.gpsimd, sp=nc.sync)
        self.sem = {}
        self.cnt = {}
        for k in ("pe", "act", "dve", "pool"):
            self.sem[k] = es.enter_context(nc.semaphore("s_" + k))
            self.cnt[k] = 0
        self.waited = {k: {} for k in self.eng}
        self.lastw = {}
        self.reads = {}
        self.nchan = 0

    def chan(self):
        self.nchan += 1
        k = "d%d" % self.nchan
        self.sem[k] = self.es.enter_context(self.nc.semaphore("s_" + k))
        self.cnt[k] = 0
        return k

    def _deps(self, e, reads, writes):
        deps = {}
        def add(ev):
            if ev is None:
                return
            k, v = ev
            if deps.get(k, 0) < v:
                deps[k] = v
        for r in reads:
            add(self.lastw.get(r))
            if r.startswith("pf") or r.startswith("pb"):
                for k, v in self.reads.get(r, {}).items():
                    if k != e:
                        add((k, v))
        for w in writes:
            add(self.lastw.get(w))
            for k, v in self.reads.get(w, {}).items():
                add((k, v))
        pend = []
        for k, v in deps.items():
            if k == "pe" and e == "pe":
                continue
            if self.waited[e].get(k, 0) >= v:
                continue
            pend.append((k, v))
            self.waited[e][k] = v
        for k, v in pend[:-1]:
            self.eng[e].wait_ge(self.sem[k], v)
        return pend[-1] if pend else None

    def _mark(self, ev, reads, writes):
        k, v = ev
        for r in reads:
            self.reads.setdefault(r, {})[k] = v
        for w in writes:
            self.lastw[w] = ev
            self.reads[w] = {}

    def op(self, e, reads, writes, fn):
        lw = self._deps(e, reads, writes)
        ins = fn(self.eng[e])
        if lw is not None:
            ins._wait_ge(self.sem[lw[0]], lw[1])
        self.cnt[e] += 1
        ins.then_inc(self.sem[e], 1)
        self._mark((e, self.cnt[e]), reads, writes)

    def dma(self, q, ch, reads, writes, out, in_, **kw):
        lw = self._deps(q, reads, writes)
        ins = self.eng[q].dma_start(out=out, in_=in_, **kw)
        if lw is not None:
            ins._wait_ge(self.sem[lw[0]], lw[1])
        ins.then_inc(self.sem[ch], 16)
        self.cnt[ch] += 16
        self._mark((ch, self.cnt[ch]), reads, writes)

    def barrier(self):
        for e in ("pe", "act", "dve", "pool", "sp"):
            for k in self.sem:
                if k != e and self.cnt[k] > 0 and self.waited[e].get(k, 0) < self.cnt[k]:
                    self.eng[e].wait_ge(self.sem[k], self.cnt[k])
                    self.waited[e][k] = self.cnt[k]

    def finish(self, q="sp"):
        for k in self.sem:
            if k.startswith("d") and self.cnt[k] > 0:
                self.eng[q].wait_ge(self.sem[k], self.cnt[k])
        for k in ("pe", "act", "dve", "pool"):
            if self.cnt[k] > 0:
                self.eng[q].wait_ge(self.sem[k], self.cnt[k])


def r3(ap, a, b):
    return ap.rearrange("p (a b) -> p a b", a=a, b=b)


class _Stop(Exception):
    pass


def build(cfg):
    c = cfg
    nc = bass.Bass("TRN2", target_bir_lowering=False)
    D, T, NS, TT, KD, HV, G = c.D, c.T, c.NS, c.TT, c.KD, c.HV, c.G

    def din(name, shape, dt=F32):
        return nc.dram_tensor(name, list(shape), dt, kind="ExternalInput").ap()

    def dout(name, shape, dt=F32):
        return nc.dram_tensor(name, list(shape), dt, kind="ExternalOutput").ap()

    def dscr(name, shape, dt):
        return nc.dram_tensor(name, list(shape), dt, kind="Internal").ap()

    xin = din("xin", [TT, D])
    conv_s = din("conv_s", [NS, 3, c.CONV])
    delta_s = din("delta_s", [NS, HV, 128, 128])
    re_s = din("re_s", [NS, G, 64])
    im_s = din("im_s", [NS, G, 64])
    norm_gdn = din("norm_gdn", [1, D])
    w_in_gdn = din("w_in_gdn", [D, c.IN])
    conv_w = din("conv_w", [4, c.CONV])
    a_log = din("a_log", [1, HV])
    dt_bias = din("dt_bias", [1, HV])
    onorm = din("onorm", [1, 128])
    w_out_gdn = din("w_out_gdn", [c.VAL, D])
    norm_ssm = din("norm_ssm", [1, D])
    w_in_ssm = din("w_in_ssm", [D, 2 * c.W])
    lam_re = din("lam_re", [G, 64])
    lam_im = din("lam_im", [G, 64])
    b_re = din("b_re", [G, 64, 16])
    b_im = din("b_im", [G, 64, 16])
    c_re = din("c_re", [G, 16, 64])
    c_im = din("c_im", [G, 16, 64])
    d_ssm = din("d_ssm", [c.W, 1])
    log_dt = din("log_dt", [1, G])
    w_glu = din("w_glu", [c.W, c.W])
    b_glu = din("b_glu", [c.W, 1])
    w_out_ssm = din("w_out_ssm", [c.W, D])
    norm_final = din("norm_final", [1, D])
    consts = din("consts", [128, 8 * 128])

    y_out = dout("y_out", [TT, D])
    conv_p = dout("conv_p", [3, c.CONV])
    delta_p = dout("delta_p", [HV, 128, 128])
    re_p = dout("re_p", [G, 64])
    im_p = dout("im_p", [G, 64])
    conv_so = dout("conv_so", [NS, 3, c.CONV])
    delta_so = dout("delta_so", [NS, HV, 128, 128])
    re_so = dout("re_so", [NS, G, 64])
    im_so = dout("im_so", [NS, G, 64])

    oT_scr = dscr("oT_scr", [c.VAL, TT], BF16)
    x1_scr = dscr("x1_scr", [TT, D], F32)
    yT_scr = dscr("yT_scr", [c.W, TT], BF16)
    y2_scr = dscr("y2_scr", [c.W, TT], BF16)
    x2_scr = dscr("x2_scr", [TT, D], F32)

    es = contextlib.ExitStack()
    with es:
        tr = TR(nc, es)
        cur = [es]
        try:

            def chk(k):
                if getattr(c, "stop", None) == k:
                    raise _Stop()

            def sb(name, shape, dt=F32):
                return cur[0].enter_context(nc.sbuf_tensor(name, list(shape), dt))

            def phase_begin():
                tr.barrier()
                cur[0] = contextlib.ExitStack()

            def phase_end():
                tr.barrier()
                cur[0].close()
                cur[0] = es

            def ps(name, shape, dt=F32):
                return es.enter_context(nc.psum_tensor(name, list(shape), dt))

            cst = sb("cst", [128, 8 * 128])
            ch_c = tr.chan()
            tr.dma("sp", ch_c, [], ["cst"], cst[:], consts[:, :])
            ident = cst[:, 0:128]
            triU = cst[:, 128:256]
            ones = cst[:, 256:384]
            MBIG = cst[:, 384:512]
            MNEG = cst[:, 512:640]
            CMASK = cst[:, 640:768]
            BD = cst[:, 768:896]
            GSEL = cst[:, 896:904]
            GG = cst[:, 904:968]
            cstb = sb("cstb", [128, 256], BF16)
            identb = cstb[:, 0:128]
            onesb = cstb[:, 128:256]
            tr.op("dve", ["cst"], ["cstb"], lambda e: e.tensor_copy(out=cstb[:, 0:128], in_=ident))
            tr.op("dve", ["cst"], ["cstb"], lambda e: e.tensor_copy(out=cstb[:, 128:256], in_=ones))

            def bcast_load(name, src, n):
                t = sb(name, [128, n])
                ch = tr.chan()
                tr.dma("sp", ch, [], [name], t[:], src[0:1, :].broadcast_to([128, n]))
                return t

            chg = tr.chan()
            nrm = {}
            alog_bc = bcast_load("alog_bc", a_log, HV)
            dtb_bc = bcast_load("dtb_bc", dt_bias, HV)
            ogain_bc = bcast_load("ogain_bc", onorm, 128)
            negA = sb("negA", [128, HV])
            tr.op("act", ["alog_bc"], ["negA"], lambda e: e.activation(out=negA[:], in_=alog_bc[:], func=AF.Exp))
            tr.op("dve", ["negA"], ["negA"], lambda e: e.tensor_scalar(out=negA[:], in0=negA[:], scalar1=-1.0, scalar2=None, op0=ALU.mult))

            hT = sb("hT", [128, KD * TT], BF16)
            hT3 = r3(hT[:], KD, TT)

            PF = [ps("pf%d" % i, [128, 512]) for i in range(6)]
            PB = [ps("pb%d" % i, [128, 1024], BF16) for i in range(2)]
            pf_i = [0]
            pb_i = [0]

            def pf():
                pf_i[0] = (pf_i[0] + 1) % len(PF)
                return PF[pf_i[0]], "pf%d" % pf_i[0]

            def pb():
                pb_i[0] = (pb_i[0] + 1) % len(PB)
                return PB[pb_i[0]], "pb%d" % pb_i[0]

            xtc = [tr.chan() for i in range(1)]
            stat = sb("stat", [128, 8])

            def tok_tiles():
                tl = [(i * 128, 128) for i in range(T // 128)]
                tl.append((T, NS))
                return tl

            def norm_to_hT(src, gsrc, srcname, addsrc=None, final_out=None):
                phase_begin()
                nrm["i"] = nrm.get("i", 0) + 1
                gbuf = sb("gbuf%d" % nrm["i"], [128, D])
                xt = [sb("xt%d_%d" % (nrm["i"], 0), [128, D])]
                hb = [sb("hb%d_%d" % (nrm["i"], 0), [128, D], BF16)]
                _norm_body(src, gsrc, srcname, final_out, gbuf, xt, hb)
                phase_end()

            def _norm_body(src, gsrc, srcname, final_out, gbuf, xt, hb):
                gain = gbuf
                gname = "gbuf"
                tr.dma("sp", chg, [], ["gbuf"], gbuf[:], gsrc[0:1, :].broadcast_to([128, D]))
                for it, (t0, n) in enumerate(tok_tiles()):
                    b = 0
                    tr.dma("sp", xtc[b], [srcname], ["xt%d" % b], xt[b][0:n, :], src[t0:t0 + n, :])
                    tr.op("act", ["xt%d" % b], ["hb%d" % b, "stat"], lambda e: e.activation(out=hb[b][0:n, :], in_=xt[b][0:n, :], func=AF.Square, accum_out=stat[0:n, 0:1]))
                    tr.op("dve", ["stat"], ["stat"], lambda e: e.tensor_scalar(out=stat[0:n, 1:2], in0=stat[0:n, 0:1], scalar1=1.0 / D, scalar2=1e-6, op0=ALU.mult, op1=ALU.add))
                    tr.op("act", ["stat"], ["stat"], lambda e: e.activation(out=stat[0:n, 3:4], in_=stat[0:n, 1:2], func=AF.Sqrt))
                    tr.op("dve", ["stat"], ["stat"], lambda e: e.reciprocal(out=stat[0:n, 2:3], in_=stat[0:n, 3:4]))
                    if final_out is not None:
                        tr.op("dve", ["xt%d" % b, "stat", gname], ["xt%d" % b], lambda e: e.scalar_tensor_tensor(out=xt[b][0:n, :], in0=xt[b][0:n, :], scalar=stat[0:n, 2:3], in1=gain[0:n, :], op0=ALU.mult, op1=ALU.mult))
                        tr.dma("sp", xtc[b], ["xt%d" % b], ["y_out"], final_out[t0:t0 + n, :], xt[b][0:n, :])
                        continue
                    tr.op("dve", ["xt%d" % b, "stat", gname], ["hb%d" % b], lambda e: e.scalar_tensor_tensor(out=hb[b][0:n, :], in0=xt[b][0:n, :], scalar=stat[0:n, 2:3], in1=gain[0:n, :], op0=ALU.mult, op1=ALU.mult))
                    for k0 in range(0, KD, 8):
                        kk = min(8, KD - k0)
                        p, pn = pb()
                        for k in range(kk):
                            tr.op("pe", ["hb%d" % b, "cstb"], [pn], lambda e: e.transpose(out=p[:, k * 128:k * 128 + n], in_=hb[b][0:n, (k0 + k) * 128:(k0 + k + 1) * 128], identity=identb[0:n, 0:n]))
                        tr.op("act" if (k0 // 8) % 2 == 0 else "dve", [pn], ["hT"],
                              (lambda e: e.activation(out=hT3[:, k0:k0 + kk, t0:t0 + n], in_=r3(p[:, 0:kk * 128], kk, 128)[:, :, 0:n], func=AF.Copy)) if (k0 // 8) % 2 == 0 else
                              (lambda e: e.tensor_copy(out=hT3[:, k0:k0 + kk, t0:t0 + n], in_=r3(p[:, 0:kk * 128], kk, 128)[:, :, 0:n])))

            chk(0)
            norm_to_hT(xin, norm_gdn, "xin")
            chk(1)

            phase_begin()
            NCOL = 772
            wj = [sb("wj%d" % i, [128, KD * NCOL], BF16) for i in range(1)]
            wjc = [tr.chan() for i in range(1)]
            NBLK = c.CONV // 128
            NR = 4 * NBLK
            cwT = sb("cwT", [128, NR])
            cwr = sb("cwr", [128, 128])
            chx = tr.chan()
            cw_rows = conv_w.rearrange("j (b c) -> (j b) c", c=128)
            for r0 in range(0, NR, 128):
                nr = min(128, NR - r0)
                tr.dma("sp", chx, [], ["cwr"], cwr[0:nr, :], cw_rows[r0:r0 + nr, :])
                p, pn = pf()
                tr.op("pe", ["cwr", "cst"], [pn], lambda e: e.transpose(out=p[:, 0:nr], in_=cwr[0:nr, :], identity=ident[0:nr, 0:nr]))
                tr.op("dve", [pn], ["cwT"], lambda e: e.tensor_copy(out=cwT[:, r0:r0 + nr], in_=p[:, 0:nr]))
            pre = [sb("pre0", [128, 3 + TT])] * 4
            if 3 + TT >= 1032:
                tail = pre[0][0:NS + 3, 8:520]
                cst48 = pre[0][0:NS * 3, 520:1032]
            else:
                tail = sb("tailx", [NS + 3, 512])[:, :]
                cst48 = sb("cst48x", [NS * 3, 512])[:, :]
            xp4 = [sb("xp4_0", [128, NS * 4])] * 4
            xs3 = sb("xs3", [128, 4 * NS * 3])
            ch48 = tr.chan()
            cv = [sb("cv0", [128, TT])] * 4
            tmpc = sb("tmpc", [128, TT])
            qT = sb("qT", [128, TT], BF16)
            kT = sb("kT", [128, TT], BF16)
            chtail = tr.chan()
            zba = sb("zba", [128, (c.NCH) * 260], BF16)
            zbas = sb("zbas", [1, NS * 260], BF16)
            gates = {}
            for nm, Cc, nch in (("p", 128, c.NCH), ("s", 1, NS)):
                for f in ("beta", "g", "gc", "egc", "bg", "ekd", "gl", "egl128", "tmp"):
                    gates[(nm, f)] = sb("gt_%s_%s" % (nm, f), [128, nch * 2])
            LANES = []
            for li in range(2):
                ln = {}
                ln["Sst"] = sb("Sst%d" % li, [128, 128]); ln["Sbf"] = sb("Sbf%d" % li, [128, 128], BF16)
                ln["chS"] = tr.chan(); ln["chSo"] = tr.chan(); ln["choT"] = tr.chan()
                ln["oTst"] = sb("oTst%d" % li, [128, TT], BF16)
                Wl = {}
                for nm, dt in (("kbg", F32), ("E1", F32), ("E2", F32), ("L", F32), ("N", F32),
                               ("P", F32), ("L2a", F32), ("L2b", F32), ("N2a", F32), ("N2b", F32), ("vn", BF16),
                               ("av", F32), ("gz", F32), ("og", BF16), ("sq", BF16)):
                    Wl[nm] = sb("w%d_%s" % (li, nm), [128, 128], dt)
                Wl["dg"] = sb("w%d_dg" % li, [128, 128])
                Wo = dict(Wl)
                for nm in ("kbg", "E1", "E2", "L", "N", "P", "L2a", "L2b", "N2a", "N2b", "dg"):
                    Wo[nm] = sb("w%do_%s" % (li, nm), [128, 128], F32)
                ln["WA"] = [Wl, Wo]
                ln["Adone"] = set()
                ln["H"] = []
                for par in range(2):
                    Hd = {}
                    for nm, dt in (("vb", F32), ("kd", BF16), ("AT", BF16), ("u", F32), ("wT", BF16)):
                        Hd[nm] = sb("h%d_%d_%s" % (li, par, nm), [128, 128], dt)
                    ln["H"].append(Hd)
                ln["A_done"] = 0
                ln["B_done"] = 0
                ln["C_done"] = 0
                ln["O"] = [sb("ho%d_%d" % (li, par), [128, 128]) for par in range(2)]
                ln["SS"] = [(sb("Sss%d_%d" % (li, par), [128, 128]), sb("Ssb%d_%d" % (li, par), [128, 128], BF16), tr.chan(), tr.chan()) for par in range(2)]
                ln["W"] = Wl
                ln["colst"] = sb("colst%d" % li, [128, 8])
                ln["id"] = li
                ln["pfb"] = [3 * li, 3 * li + 1, 3 * li + 2]
                ln["pfi"] = [0]
                ln["pbb"] = li
                LANES.append(ln)

            def load_wj(j, b):
                base = wj[b]
                w3 = r3(base[:], KD, NCOL)
                segs = [(0, j * 128, 128), (128, c.KEY + j * 128, 128), (256, 2 * c.KEY + j * 256, 256),
                        (512, c.CONV + j * 256, 256), (768, c.CONV + c.VAL + 2 * j, 2), (770, c.CONV + c.VAL + HV + 2 * j, 2)]
                for (o, s0, n) in segs:
                    tr.dma("pool", wjc[b], [], ["wj%d" % b], w3[:, :, o:o + n], w_in_gdn[:, s0:s0 + n].rearrange("(k p) n -> p k n", p=128))

            def chunk(ln, part, seq, nm, Cc, ci, cols, j, hh, first, last, n_idx):
                c0 = cols
                W = ln["W"]; Sst = ln["Sst"]; Sbf = ln["Sbf"]; chS = ln["chS"]; chSo = ln["chSo"]; oTst = ln["oTst"]; colst = ln["colst"]
                WN = "w%d_" % ln["id"]; SN = "Sst%d" % ln["id"]; BN = "Sbf%d" % ln["id"]; ON = "oTst%d" % ln["id"]; CN = "colst%d" % ln["id"]

                H = ln["H"][seq % 2]
                HN = "h%d_%d_" % (ln["id"], seq % 2)
                KO = 256 * (seq % 2)
                if part == "A":
                    W = ln["WA"][seq % 2]
                    WN = "w%d%s_" % (ln["id"], "o" if seq % 2 else "")

                def pf():
                    if part == "B":
                        i_ = ln["pfb"][2]
                    else:
                        i_ = ln["pfb"][seq % 2]
                    return PF[i_], "pf%d" % i_

                def pb():
                    return PB[ln["pbb"]], "pb%d" % ln["pbb"]
                h = 2 * j + hh
                gi = ci * 2 + hh
                G_ = lambda f: gates[(nm, f)]
                gname = lambda f: "gt_%s_%s" % (nm, f)
                if part == "A":
                    while ln["B_done"] < seq - 1:
                        yield "blocked"
                    p, pn = pb()
                    tr.op("pe", ["kT", "cstb"], [pn], lambda e: e.transpose(out=p[0:Cc, KO:KO + 128], in_=kT[:, c0:c0 + Cc], identity=identb))
                    yield
                    tr.op("pe", ["cvb%d" % hh, "cstb"], [pn], lambda e: e.transpose(out=p[0:Cc, KO + 128:KO + 256], in_=cvb[hh][:, c0:c0 + Cc], identity=identb))
                    yield
                    tr.op("dve", [pn, gname("beta")], [HN + "vb"], lambda e: e.tensor_scalar(out=H["vb"][0:Cc, :], in0=p[0:Cc, KO + 128:KO + 256], scalar1=G_("beta")[0:Cc, gi:gi + 1], scalar2=None, op0=ALU.mult))
                    yield
                    tr.op("dve", [pn, gname("bg")], [WN + "kbg"], lambda e: e.tensor_scalar(out=W["kbg"][0:Cc, :], in0=p[0:Cc, KO:KO + 128], scalar1=G_("bg")[0:Cc, gi:gi + 1], scalar2=None, op0=ALU.mult))
                    yield
                    tr.op("act", [pn, gname("ekd")], [HN + "kd"], lambda e: e.activation(out=H["kd"][0:Cc, :], in_=p[0:Cc, KO:KO + 128], func=AF.Copy, scale=G_("ekd")[0:Cc, gi:gi + 1]))
                    yield
                    chk(10)
                    if Cc > 1:
                        pk, pkn = pf()
                        tr.op("pe", ["kT"], [pkn], lambda e: e.matmul(pk[0:Cc, 0:Cc], lhsT=kT[:, c0:c0 + Cc], rhs=kT[:, c0:c0 + Cc], start=True, stop=True))
                        yield
                        tr.op("pe", ["kT", "qT"], [pkn], lambda e: e.matmul(pk[0:Cc, 128:128 + Cc], lhsT=kT[:, c0:c0 + Cc], rhs=qT[:, c0:c0 + Cc], start=True, stop=True))
                        yield
                        tr.op("dve", ["cst", gname("gc")], [WN + "dg"], lambda e: e.tensor_scalar(out=W["dg"][0:Cc, 0:Cc], in0=ident[0:Cc, 0:Cc], scalar1=G_("gc")[0:Cc, gi:gi + 1], scalar2=None, op0=ALU.mult))
                        yield
                        tr.op("pe", ["cst", WN + "dg"], [pkn], lambda e: e.matmul(pk[0:Cc, 256:256 + Cc], lhsT=ones[0:Cc, 0:Cc], rhs=W["dg"][0:Cc, 0:Cc], start=True, stop=True))
                        yield
                        R = pk[0:Cc, 256:256 + Cc]
                        chk(11)
                        tr.op("dve", [pkn, gname("gc"), "cst"], [WN + "E1"], lambda e: e.scalar_tensor_tensor(out=W["E1"][0:Cc, 0:Cc], in0=R, scalar=G_("gc")[0:Cc, gi:gi + 1], in1=MBIG[0:Cc, 0:Cc], op0=ALU.subtract, op1=ALU.max))
                        yield
                        tr.op("dve", [pkn, gname("gc"), "cst"], [WN + "E2"], lambda e: e.scalar_tensor_tensor(out=W["E2"][0:Cc, 0:Cc], in0=R, scalar=G_("gc")[0:Cc, gi:gi + 1], in1=MNEG[0:Cc, 0:Cc], op0=ALU.subtract, op1=ALU.min))
                        yield
                        tr.op("act", [WN + "E1"], [WN + "E1"], lambda e: e.activation(out=W["E1"][0:Cc, 0:Cc], in_=W["E1"][0:Cc, 0:Cc], func=AF.Exp, scale=-1.0))
                        yield
                        tr.op("act", [WN + "E2"], [WN + "E2"], lambda e: e.activation(out=W["E2"][0:Cc, 0:Cc], in_=W["E2"][0:Cc, 0:Cc], func=AF.Exp))
                        yield
                        tr.op("dve", [pkn, gname("beta"), WN + "E1"], [WN + "L"], lambda e: e.scalar_tensor_tensor(out=W["L"][0:Cc, 0:Cc], in0=pk[0:Cc, 0:Cc], scalar=G_("beta")[0:Cc, gi:gi + 1], in1=W["E1"][0:Cc, 0:Cc], op0=ALU.mult, op1=ALU.mult))
                        yield
                        tr.op("dve", [pkn, WN + "E2"], [HN + "AT"], lambda e: e.tensor_tensor(out=H["AT"][0:Cc, 0:Cc], in0=pk[0:Cc, 128:128 + Cc], in1=W["E2"][0:Cc, 0:Cc], op=ALU.mult))
                        yield
                    else:
                        pk, pkn = pf()
                        tr.op("pe", ["kT", "qT"], [pkn], lambda e: e.matmul(pk[0:1, 128:129], lhsT=kT[:, c0:c0 + 1], rhs=qT[:, c0:c0 + 1], start=True, stop=True))
                        yield
                        tr.op("dve", [pkn], [HN + "AT"], lambda e: e.tensor_copy(out=H["AT"][0:1, 0:1], in_=pk[0:1, 128:129]))
                        yield
                    chk(12)
                    if Cc > 1:
                        p2, p2n = pf()
                        tr.op("pe", [WN + "L", "cst"], [p2n], lambda e: e.transpose(out=p2[0:Cc, 0:Cc], in_=W["L"][0:Cc, 0:Cc], identity=ident[0:Cc, 0:Cc]))
                        yield
                        tr.op("act", [p2n], [WN + "N"], lambda e: e.activation(out=W["N"][0:Cc, 0:Cc], in_=p2[0:Cc, 0:Cc], func=AF.Copy))
                        yield
                        tr.op("dve", ["cstb", WN + "N"], [WN + "P"], lambda e: e.tensor_tensor(out=W["P"][0:Cc, 0:Cc], in0=ident[0:Cc, 0:Cc], in1=W["N"][0:Cc, 0:Cc], op=ALU.subtract))
                        yield
                        chk(120)
                        Lk, Nk = "L", "N"
                        nsteps = int(math.log2(Cc)) - 1
                        for st in range(nsteps):
                            L2, N2 = ("L2a", "N2a") if st % 2 == 0 else ("L2b", "N2b")
                            pq, pqn = pf()
                            tr.op("pe", [WN + Nk, WN + Lk], [pqn], lambda e: e.matmul(pq[0:Cc, 0:Cc], lhsT=W[Nk][0:Cc, 0:Cc], rhs=W[Lk][0:Cc, 0:Cc], start=True, stop=True))
                            yield
                            if st < nsteps - 1:
                                tr.op("pe", [WN + Nk, WN + Lk], [pqn], lambda e: e.matmul(pq[0:Cc, 128:128 + Cc], lhsT=W[Lk][0:Cc, 0:Cc], rhs=W[Nk][0:Cc, 0:Cc], start=True, stop=True))
                                yield
                            tr.op("act", [pqn], [WN + L2], lambda e: e.activation(out=W[L2][0:Cc, 0:Cc], in_=pq[0:Cc, 0:Cc], func=AF.Copy))
                            yield
                            if st < nsteps - 1:
                                tr.op("dve", [pqn], [WN + N2], lambda e: e.tensor_copy(out=W[N2][0:Cc, 0:Cc], in_=pq[0:Cc, 128:128 + Cc]))
                                yield
                            tr.op("pe", [WN + L2, WN + "P"], [pqn], lambda e: e.matmul(pq[0:Cc, 256:256 + Cc], lhsT=W[L2][0:Cc, 0:Cc], rhs=W["P"][0:Cc, 0:Cc], start=True, stop=True))
                            yield
                            tr.op("dve", [pqn, WN + "P"], [WN + "P"], lambda e: e.tensor_tensor(out=W["P"][0:Cc, 0:Cc], in0=pq[0:Cc, 256:256 + Cc], in1=W["P"][0:Cc, 0:Cc], op=ALU.add))
                            yield
                            Lk, Nk = L2, N2
                            chk(121 + st)
                        TTm, TTn = W["P"][0:Cc, 0:Cc], WN + "P"
                    else:
                        TTm, TTn = ident[0:1, 0:1], "cst"
                    chk(13)
                    pu, pun = pf()
                    if Cc > 1:
                        tr.op("pe", [TTn, HN + "vb"], [pun], lambda e: e.matmul(pu[0:Cc, 0:128], lhsT=TTm, rhs=H["vb"][0:Cc, :], start=True, stop=True))
                        yield
                        tr.op("act", [pun], [HN + "u"], lambda e: e.activation(out=H["u"][0:Cc, :], in_=pu[0:Cc, 0:128], func=AF.Copy))
                        yield
                        Uap, Un = H["u"], HN + "u"
                    else:
                        Uap, Un = H["vb"], HN + "vb"
                    tr.op("pe", [TTn, WN + "kbg"], [pun], lambda e: e.matmul(pu[:, 128:128 + Cc], lhsT=W["kbg"][0:Cc, :], rhs=TTm, start=True, stop=True))
                    yield
                    tr.op("act", [pun], [HN + "wT"], lambda e: e.activation(out=H["wT"][:, 0:Cc], in_=pu[:, 128:128 + Cc], func=AF.Copy))
                    yield
                    chk(14)
                    ln["Adone"].add(seq)
                    return
                Otile = ln["O"][seq % 2]
                OnN = "ho%d_%d" % (ln["id"], seq % 2)
                if nm == "s":
                    Sst, Sbf, chS, chSo = ln["SS"][seq % 2]
                    SN = "Sss%d_%d" % (ln["id"], seq % 2)
                    BN = "Ssb%d_%d" % (ln["id"], seq % 2)
                if part == "C":
                    while ln["B_done"] <= seq:
                        yield "blocked"
                else:
                    while (seq not in ln["Adone"]) or ln["C_done"] < seq - 1:
                        yield "blocked"
                    if Cc > 1:
                        Uap, Un = H["u"], HN + "u"
                    else:
                        Uap, Un = H["vb"], HN + "vb"
                    if nm == "s":
                        tr.dma("sp", chS, ["delta_s"], [SN], Sst[:], delta_s[n_idx, h, :, :])
                        yield
                        tr.op("act", [SN], [BN], lambda e: e.activation(out=Sbf[:], in_=Sst[:], func=AF.Copy))
                        yield
                    elif first:
                        tr.op("pool", [], [SN], lambda e: e.memset(Sst[:], 0.0))
                        yield
                        tr.op("pool", [], [BN], lambda e: e.memset(Sbf[:], 0.0))
                        yield
                    pw, pwn = pf()
                    tr.op("pe", [HN + "wT", BN], [pwn], lambda e: e.matmul(pw[0:Cc, 0:128], lhsT=H["wT"][:, 0:Cc], rhs=Sbf[:], start=True, stop=True))
                    yield
                    tr.op("pe", ["qT", BN], [pwn], lambda e: e.matmul(pw[0:Cc, 128:256], lhsT=qT[:, c0:c0 + Cc], rhs=Sbf[:], start=True, stop=True))
                    yield
                    tr.op("dve", [pwn, Un], [WN + "vn"], lambda e: e.tensor_tensor(out=W["vn"][0:Cc, :], in0=Uap[0:Cc, :], in1=pw[0:Cc, 0:128], op=ALU.subtract))
                    yield
                    tr.op("pe", [HN + "kd", WN + "vn"], [pwn], lambda e: e.matmul(pw[:, 384:512], lhsT=H["kd"][0:Cc, :], rhs=W["vn"][0:Cc, :], start=True, stop=True))
                    yield
                    tr.op("pe", [HN + "AT", WN + "vn"], [pwn], lambda e: e.matmul(pw[0:Cc, 256:384], lhsT=H["AT"][0:Cc, 0:Cc], rhs=W["vn"][0:Cc, :], start=True, stop=True))
                    yield
                    tr.op("dve", [pwn, SN, gname("egl128")], [SN], lambda e: e.scalar_tensor_tensor(out=Sst[:], in0=Sst[:], scalar=G_("egl128")[:, gi:gi + 1], in1=pw[:, 384:512], op0=ALU.mult, op1=ALU.add))
                    yield
                    if nm == "s":
                        tr.dma("pool", chSo, [SN], ["delta_so"], delta_so[n_idx, h, :, :], Sst[:])
                        yield
                    elif last:
                        tr.dma("pool", chSo, [SN], ["delta_p"], delta_p[h, :, :], Sst[:])
                        yield
                    else:
                        tr.op("act", [SN], [BN], lambda e: e.activation(out=Sbf[:], in_=Sst[:], func=AF.Copy))
                        yield
                    tr.op("act", [pwn], [WN + "av"], lambda e: e.activation(out=W["av"][0:Cc, :], in_=pw[0:Cc, 256:384], func=AF.Copy))
                    yield
                    tr.op("dve", [pwn, WN + "av", gname("egc")], [OnN], lambda e: e.scalar_tensor_tensor(out=Otile[0:Cc, :], in0=pw[0:Cc, 128:256], scalar=G_("egc")[0:Cc, gi:gi + 1], in1=W["av"][0:Cc, :], op0=ALU.mult, op1=ALU.add))
                    yield
                    ln["B_done"] = seq + 1
                    return
                zsrc = (zba if nm == "p" else zbas)
                zname = "zba" if nm == "p" else "zbas"
                zap = zsrc[0:Cc, ci * 260 + hh * 128: ci * 260 + hh * 128 + 128]
                tr.op("act", [OnN], [WN + "sq", CN], lambda e: e.activation(out=W["sq"][0:Cc, :], in_=Otile[0:Cc, :], func=AF.Square, accum_out=colst[0:Cc, 0:1]))
                yield
                tr.op("dve", [CN], [CN], lambda e: e.tensor_scalar(out=colst[0:Cc, 1:2], in0=colst[0:Cc, 0:1], scalar1=1.0 / 128, scalar2=1e-6, op0=ALU.mult, op1=ALU.add))
                yield
                tr.op("act", [CN], [CN], lambda e: e.activation(out=colst[0:Cc, 3:4], in_=colst[0:Cc, 1:2], func=AF.Sqrt))
                yield
                tr.op("dve", [CN], [CN], lambda e: e.reciprocal(out=colst[0:Cc, 2:3], in_=colst[0:Cc, 3:4]))
                yield
                tr.op("act", [zname], [WN + "gz"], lambda e: e.activation(out=W["gz"][0:Cc, :], in_=zap, func=AF.Silu))
                yield
                tr.op("pool", [WN + "gz", "ogain_bc"], [WN + "gz"], lambda e: e.tensor_tensor(out=W["gz"][0:Cc, :], in0=W["gz"][0:Cc, :], in1=ogain_bc[0:Cc, :], op=ALU.mult))
                yield
                tr.op("dve", [OnN, CN, WN + "gz"], [WN + "og"], lambda e: e.scalar_tensor_tensor(out=W["og"][0:Cc, :], in0=Otile[0:Cc, :], scalar=colst[0:Cc, 2:3], in1=W["gz"][0:Cc, :], op0=ALU.mult, op1=ALU.mult))
                yield
                p3, p3n = pb()
                tr.op("pe", [WN + "og", "cstb"], [p3n], lambda e: e.transpose(out=p3[:, 512:512 + Cc], in_=W["og"][0:Cc, :], identity=identb[0:Cc, 0:Cc]))
                yield
                tr.op("act", [p3n], [ON], lambda e: e.activation(out=oTst[:, c0:c0 + Cc], in_=p3[:, 512:512 + Cc], func=AF.Copy))
                yield
                ln["C_done"] = seq + 1

            cvb = [sb("cvb%d" % i, [128, TT], BF16) for i in range(2)]

            for j in range(c.HQK):
                chk(100 + j)
                b = 0
                load_wj(j, b)
                w3 = r3(wj[b][:], KD, NCOL)
                wn = "wj%d" % b
                chk(2)
                for (lo, m, dname) in ((T - 3, 3, "conv_p"), (T, NS, "conv_so")):
                    p, pn = pf()
                    for k in range(KD):
                        tr.op("pe", [wn, "hT"], [pn], lambda e: e.matmul(p[0:m, 0:512], lhsT=hT3[:, k, lo:lo + m], rhs=w3[:, k, 0:512], start=(k == 0), stop=(k == KD - 1)))
                    tr.op("act", [pn], ["pre0"], lambda e: e.activation(out=tail[0:m, :], in_=p[0:m, 0:512], func=AF.Copy))
                    for (o, s0, n) in ((0, j * 128, 128), (128, c.KEY + j * 128, 128), (256, 2 * c.KEY + j * 256, 256)):
                        if dname == "conv_p":
                            tr.dma("sp", chtail, ["pre0"], ["conv_p"], conv_p[0:3, s0:s0 + n], tail[0:3, o:o + n])
                        else:
                            tr.dma("sp", chtail, ["pre0"], ["conv_so"], conv_so[:, 2, s0:s0 + n], tail[0:NS, o:o + n])
                            tr.dma("sp", chtail, [], ["conv_so"], conv_so[:, 0:2, s0:s0 + n], conv_s[:, 1:3, s0:s0 + n])
                chk(3)
                for (o, s0, n) in ((0, j * 128, 128), (128, c.KEY + j * 128, 128), (256, 2 * c.KEY + j * 256, 256)):
                    tr.dma("sp", ch48, [], ["pre0"], cst48[:, o:o + n], conv_s[:, :, s0:s0 + n].rearrange("n j c -> (n j) c"))
                x4 = r3(xp4[0][:], NS, 4)
                for fb in range(4):
                    p, pn = pf()
                    tr.op("pe", ["pre0", "cst"], [pn], lambda e: e.transpose(out=p[:, 0:NS * 3], in_=cst48[:, fb * 128:(fb + 1) * 128], identity=ident[0:NS * 3, 0:NS * 3]))
                    tr.op("dve", [pn], ["xs3"], lambda e: e.tensor_copy(out=xs3[:, fb * NS * 3:(fb + 1) * NS * 3], in_=p[:, 0:NS * 3]))
                tr.op("pool", [], ["pre0"], lambda e: e.memset(pre[0][:, 0:3], 0.0))
                for fb in range(4):
                    for t0 in range(0, TT, 512):
                        n = min(512, TT - t0)
                        p, pn = pf()
                        for k in range(KD):
                            tr.op("pe", [wn, "hT"], [pn], lambda e: e.matmul(p[:, 0:n], lhsT=w3[:, k, fb * 128:(fb + 1) * 128], rhs=hT3[:, k, t0:t0 + n], start=(k == 0), stop=(k == KD - 1)))
                        np_ = max(0, min(n, T - t0))
                        if np_ > 0:
                            tr.op("act", [pn], ["pre0"], lambda e: e.activation(out=pre[0][:, 3 + t0:3 + t0 + np_], in_=p[:, 0:np_], func=AF.Copy))
                        if np_ < n:
                            s0 = t0 + np_ - T
                            ns_ = n - np_
                            tr.op("dve", [pn], ["xp4_0"], lambda e: e.tensor_copy(out=x4[:, s0:s0 + ns_, 3], in_=p[:, np_:n]))
                    tr.op("dve", ["xs3"], ["xp4_0"], lambda e: e.tensor_copy(out=x4[:, :, 0:3], in_=r3(xs3[:, fb * NS * 3:(fb + 1) * NS * 3], NS, 3)))
                    blk = [j, c.KEY // 128 + j, 2 * c.KEY // 128 + 2 * j, 2 * c.KEY // 128 + 2 * j + 1][fb]
                    cwb = lambda tp: cwT[:, tp * NBLK + blk:tp * NBLK + blk + 1]
                    acc, an, eng = tmpc, "tmpc", "dve"
                    tr.op(eng, ["pre0", "cwT"], [an], lambda e: e.tensor_scalar(out=acc[:, 0:T], in0=pre[0][:, 0:T], scalar1=cwb(0), scalar2=None, op0=ALU.mult))
                    for tp in (1, 2, 3):
                        tr.op(eng, ["pre0", "cwT", an], [an], lambda e: e.scalar_tensor_tensor(out=acc[:, 0:T], in0=pre[0][:, tp:tp + T], scalar=cwb(tp), in1=acc[:, 0:T], op0=ALU.mult, op1=ALU.add))
                    tr.op(eng, ["xp4_0", "cwT"], [an], lambda e: e.tensor_scalar(out=acc[:, T:TT], in0=x4[:, :, 0], scalar1=cwb(0), scalar2=None, op0=ALU.mult))
                    for tp in (1, 2, 3):
                        tr.op(eng, ["xp4_0", "cwT", an], [an], lambda e: e.scalar_tensor_tensor(out=acc[:, T:TT], in0=x4[:, :, tp], scalar=cwb(tp), in1=acc[:, T:TT], op0=ALU.mult, op1=ALU.add))
                    tr.op("act", [an], ["cv0"], lambda e: e.activation(out=cv[0][:], in_=acc[:], func=AF.Silu))
                    if fb < 2:
                        dstT, dn, scl = ((qT, "qT", 128 ** -0.5), (kT, "kT", 1.0))[fb]
                        sqb = pre[0][:, 3:3 + TT]
                        rinv = tmpc
                        tr.op("act", ["cv0"], ["pre0"], lambda e: e.activation(out=sqb, in_=cv[0][:], func=AF.Square))
                        for t0 in range(0, TT, 512):
                            n = min(512, TT - t0)
                            p, pn = pf()
                            tr.op("pe", ["cst", "pre0"], [pn], lambda e: e.matmul(p[:, 0:n], lhsT=ones, rhs=sqb[:, t0:t0 + n], start=True, stop=True))
                            tr.op("dve", [pn], ["tmpc"], lambda e: e.tensor_scalar(out=rinv[:, t0:t0 + n], in0=p[:, 0:n], scalar1=1e-6, scalar2=None, op0=ALU.add))
                            tr.op("act", ["tmpc"], ["tmpc"], lambda e: e.activation(out=rinv[:, t0:t0 + n], in_=rinv[:, t0:t0 + n], func=AF.Sqrt))
                            tr.op("dve", ["tmpc"], ["tmpc"], lambda e: e.reciprocal(out=rinv[:, t0:t0 + n], in_=rinv[:, t0:t0 + n]))
                        tr.op("dve", ["cv0", "tmpc"], [dn], lambda e: e.scalar_tensor_tensor(out=dstT[:], in0=cv[0][:], scalar=scl, in1=rinv[:], op0=ALU.mult, op1=ALU.mult))
                    else:
                        hh = fb - 2
                        tr.op("pool", ["cv0"], ["cvb%d" % hh], lambda e: e.tensor_copy(out=cvb[hh][:], in_=cv[0][:]))
                chk(4)
                for nm, Cc, nch, zt, zn in (("p", 128, c.NCH, zba, "zba"), ("s", 1, NS, zbas, "zbas")):
                    for ci in range(nch):
                        c0 = ci * 128 if nm == "p" else T + ci
                        p, pn = pf()
                        for k in range(KD):
                            tr.op("pe", [wn, "hT"], [pn], lambda e: e.matmul(p[0:Cc, 0:260], lhsT=hT3[:, k, c0:c0 + Cc], rhs=w3[:, k, 512:772], start=(k == 0), stop=(k == KD - 1)))
                        tr.op("act", [pn], [zn], lambda e: e.activation(out=zt[0:Cc, ci * 260:(ci + 1) * 260], in_=p[0:Cc, 0:260], func=AF.Copy))
                    z3 = r3(zt[0:Cc, 0:nch * 260], nch, 260)
                    Gt = lambda f: r3(gates[(nm, f)][:, 0:nch * 2], nch, 2)
                    gn = lambda f: "gt_%s_%s" % (nm, f)
                    tr.op("act", [zn], [gn("beta")], lambda e: e.activation(out=Gt("beta")[0:Cc], in_=z3[:, :, 256:258], func=AF.Sigmoid))
                    for hh in range(2):
                        tr.op("act", [zn, "dtb_bc"], [gn("tmp")], lambda e: e.activation(out=Gt("tmp")[0:Cc, :, hh], in_=z3[:, :, 258 + hh], func=AF.Exp, bias=dtb_bc[0:Cc, 2 * j + hh:2 * j + hh + 1]))
                    tr.op("act", [gn("tmp")], [gn("tmp")], lambda e: e.activation(out=gates[(nm, "tmp")][0:Cc, 0:nch * 2], in_=gates[(nm, "tmp")][0:Cc, 0:nch * 2], func=AF.Ln, bias=1.0))
                    for hh in range(2):
                        tr.op("dve", [gn("tmp"), "negA"], [gn("g")], lambda e: e.tensor_scalar(out=Gt("g")[0:Cc, :, hh], in0=Gt("tmp")[0:Cc, :, hh], scalar1=negA[0:Cc, 2 * j + hh:2 * j + hh + 1], scalar2=None, op0=ALU.mult))
                    p, pn = pf()
                    n2 = nch * 2
                    gg = gates[(nm, "g")]
                    tr.op("pe", ["cst", gn("g")], [pn], lambda e: e.matmul(p[0:Cc, 0:n2], lhsT=triU[0:Cc, 0:Cc], rhs=gg[0:Cc, 0:n2], start=True, stop=True))
                    tr.op("pe", ["cst", gn("g")], [pn], lambda e: e.matmul(p[0:Cc, 64:64 + n2], lhsT=ones[0:Cc, 0:Cc], rhs=gg[0:Cc, 0:n2], start=True, stop=True))
                    tr.op("pe", ["cst", gn("g")], [pn], lambda e: e.matmul(p[:, 128:128 + n2], lhsT=ones[0:Cc, :], rhs=gg[0:Cc, 0:n2], start=True, stop=True))
                    gt = lambda f: gates[(nm, f)]
                    tr.op("dve", [pn], [gn("gc")], lambda e: e.tensor_copy(out=gt("gc")[0:Cc, 0:n2], in_=p[0:Cc, 0:n2]))
                    tr.op("act", [pn], [gn("egc")], lambda e: e.activation(out=gt("egc")[0:Cc, 0:n2], in_=p[0:Cc, 0:n2], func=AF.Exp))
                    tr.op("dve", [gn("egc"), gn("beta")], [gn("bg")], lambda e: e.tensor_tensor(out=gt("bg")[0:Cc, 0:n2], in0=gt("egc")[0:Cc, 0:n2], in1=gt("beta")[0:Cc, 0:n2], op=ALU.mult))
                    tr.op("dve", [pn, gn("gc")], [gn("gl")], lambda e: e.tensor_tensor(out=gt("gl")[0:Cc, 0:n2], in0=p[0:Cc, 64:64 + n2], in1=gt("gc")[0:Cc, 0:n2], op=ALU.subtract))
                    tr.op("act", [gn("gl")], [gn("ekd")], lambda e: e.activation(out=gt("ekd")[0:Cc, 0:n2], in_=gt("gl")[0:Cc, 0:n2], func=AF.Exp))
                    tr.op("act", [pn], [gn("egl128")], lambda e: e.activation(out=gt("egl128")[:, 0:n2], in_=p[:, 128:128 + n2], func=AF.Exp))
                chk(5)
                def lane_gen(hh, part, parity=None):
                    ln = LANES[hh]
                    seq = 0
                    for ci in range(c.NCH):
                        if parity is None or seq % 2 == parity:
                            yield from chunk(ln, part, seq, "p", 128, ci, ci * 128, j, hh, ci == 0, ci == c.NCH - 1, None)
                        seq += 1
                    for n_ in range(NS):
                        if parity is None or seq % 2 == parity:
                            yield from chunk(ln, part, seq, "s", 1, n_, T + n_, j, hh, False, False, n_)
                        seq += 1
                    if part == "C":
                        h = 2 * j + hh
                        tr.dma("sp", ln["choT"], ["oTst%d" % hh], ["oT_scr"], oT_scr[h * 128:(h + 1) * 128, :], ln["oTst"][:])

                for ln_ in LANES:
                    ln_["A_done"] = 0
                    ln_["B_done"] = 0
                    ln_["C_done"] = 0
                    ln_["Adone"] = set()
                active = [lane_gen(0, "A", 0), lane_gen(1, "A", 0), lane_gen(0, "A", 1), lane_gen(1, "A", 1),
                          lane_gen(0, "B"), lane_gen(1, "B"), lane_gen(0, "C"), lane_gen(1, "C")]
                while active:
                    for g_ in list(active):
                        try:
                            next(g_)
                        except StopIteration:
                            active.remove(g_)

            phase_end()
            chk(20)

            def outproj(srcT, sname, KS, wsrc, resid, rname, dst, dname, tag):
                phase_begin()
                CB = min(1024, D)
                NWB = 1 if CB > 512 else 2
                wo = [sb("wo%s%d" % (tag, i), [128, KS * CB], BF16) for i in range(NWB)]
                woc = [tr.chan() for i in range(NWB)]
                ot = [sb("ot%s%d" % (tag, i), [128, KS * 128], BF16) for i in range(2)]
                otc = [tr.chan() for i in range(2)]
                xr = [sb("xr%s%d" % (tag, i), [128, CB]) for i in range(2)]
                xrc = [tr.chan() for i in range(2)]
                xrs = [tr.chan() for i in range(2)]
                it = 0
                for cb in range(D // CB):
                    wb = cb % NWB
                    tr.dma("pool", woc[wb], [], ["wo%s%d" % (tag, wb)], r3(wo[wb][:], KS, CB), wsrc[:, cb * CB:(cb + 1) * CB].rearrange("(k p) n -> p k n", p=128))
                    w3_ = r3(wo[wb][:], KS, CB)
                    for (t0, n) in tok_tiles():
                        b = it % 2
                        it += 1
                        o3 = r3(ot[b][:], KS, 128)
                        tr.dma("sp", otc[b], [sname], ["ot%s%d" % (tag, b)], o3[:, :, 0:n], srcT[:, t0:t0 + n].rearrange("(k p) t -> p k t", p=128))
                        tr.dma("sp", xrc[b], [rname], ["xr%s%d" % (tag, b)], xr[b][0:n, :], resid[t0:t0 + n, cb * CB:(cb + 1) * CB])
                        for h0 in range(0, CB, 512):
                            hn = min(512, CB - h0)
                            p, pn = pf()
                            for k in range(KS):
                                tr.op("pe", ["ot%s%d" % (tag, b), "wo%s%d" % (tag, wb)], [pn], lambda e: e.matmul(p[0:n, 0:hn], lhsT=o3[:, k, 0:n], rhs=w3_[:, k, h0:h0 + hn], start=(k == 0), stop=(k == KS - 1)))
                            tr.op("dve", [pn, "xr%s%d" % (tag, b)], ["xr%s%d" % (tag, b)], lambda e: e.tensor_tensor(out=xr[b][0:n, h0:h0 + hn], in0=p[0:n, 0:hn], in1=xr[b][0:n, h0:h0 + hn], op=ALU.add))
                        tr.dma("pool", xrs[b], ["xr%s%d" % (tag, b)], [dname], dst[t0:t0 + n, cb * CB:(cb + 1) * CB], xr[b][0:n, :])
                phase_end()

            outproj(oT_scr, "oT_scr", c.VAL // 128, w_out_gdn, xin, "xin", x1_scr, "x1_scr", "a")
            chk(21)
            norm_to_hT(x1_scr, norm_ssm, "x1_scr")
            chk(22)

            phase_begin()
            GB = min(16, G)
            NT = GB // 8
            NCK = T // 8
            NC1 = 1 + NCK
            PI = math.pi

            def ew(e, out, in0, in1, op, r=(), w=()):
                tr.op(e, list(r), list(w), lambda en: en.tensor_tensor(out=out, in0=in0, in1=in1, op=op))

            tb = {k: sb("s5_" + k, [64, G]) for k in ("lamr", "lami", "dt", "ar", "ai", "fr", "fi", "t1", "t2", "t3")}
            lt = sb("s5_lt", [128, 64])
            chl = tr.chan()
            chdt = tr.chan()
            chdc = tr.chan()
            chBi = tr.chan()
            chcr = tr.chan()
            for (src_, dst_) in ((lam_re, "lamr"), (lam_im, "lami")):
                for r0 in range(0, G, 128):
                    nr = min(128, G - r0)
                    tr.dma("sp", chl, [], ["s5_lt"], lt[0:nr, :], src_[r0:r0 + nr, :])
                    p, pn = pf()
                    tr.op("pe", ["s5_lt", "cst"], [pn], lambda e: e.transpose(out=p[0:64, 0:nr], in_=lt[0:nr, :], identity=ident[0:nr, 0:nr]))
                    tr.op("dve", [pn], ["s5_" + dst_], lambda e: e.tensor_copy(out=tb[dst_][:, r0:r0 + nr], in_=p[0:64, 0:nr]))
            tr.dma("sp", chdt, [], ["s5_dt"], tb["dt"][:], log_dt[0:1, :].broadcast_to([64, G]))
            tr.op("act", ["s5_dt"], ["s5_dt"], lambda e: e.activation(out=tb["dt"][:], in_=tb["dt"][:], func=AF.Exp))
            tr.op("dve", ["s5_lamr"], ["s5_lamr"], lambda e: e.tensor_scalar(out=tb["lamr"][:], in0=tb["lamr"][:], scalar1=-1e-4, scalar2=None, op0=ALU.min))
            ew("dve", tb["t1"][:], tb["lamr"][:], tb["dt"][:], ALU.mult, ["s5_lamr", "s5_dt"], ["s5_t1"])
            tr.op("act", ["s5_t1"], ["s5_t1"], lambda e: e.activation(out=tb["t1"][:], in_=tb["t1"][:], func=AF.Exp))
            ew("dve", tb["t2"][:], tb["lami"][:], tb["dt"][:], ALU.mult, ["s5_lami", "s5_dt"], ["s5_t2"])
            tr.op("act", ["s5_t2"], ["s5_ai"], lambda e: e.activation(out=tb["ai"][:], in_=tb["t2"][:], func=AF.Sin, scale=1.0 / 32))
            tr.op("dve", ["s5_t2"], ["s5_t3"], lambda e: e.tensor_scalar(out=tb["t3"][:], in0=tb["t2"][:], scalar1=1.0 / 32, scalar2=PI / 2, op0=ALU.mult, op1=ALU.add))
            tr.op("act", ["s5_t3"], ["s5_ar"], lambda e: e.activation(out=tb["ar"][:], in_=tb["t3"][:], func=AF.Sin))
            for _ in range(5):
                ew("dve", tb["t3"][:], tb["ar"][:], tb["ar"][:], ALU.mult, ["s5_ar"], ["s5_t3"])
                ew("dve", tb["t2"][:], tb["ai"][:], tb["ai"][:], ALU.mult, ["s5_ai"], ["s5_t2"])
                tr.op("dve", ["s5_ar", "s5_ai"], ["s5_ai"], lambda e: e.scalar_tensor_tensor(out=tb["ai"][:], in0=tb["ar"][:], scalar=2.0, in1=tb["ai"][:], op0=ALU.mult, op1=ALU.mult))
                ew("dve", tb["ar"][:], tb["t3"][:], tb["t2"][:], ALU.subtract, ["s5_t3", "s5_t2"], ["s5_ar"])
            ew("dve", tb["ar"][:], tb["ar"][:], tb["t1"][:], ALU.mult, ["s5_ar", "s5_t1"], ["s5_ar"])
            ew("dve", tb["ai"][:], tb["ai"][:], tb["t1"][:], ALU.mult, ["s5_ai", "s5_t1"], ["s5_ai"])
            tr.op("dve", ["s5_ar"], ["s5_t1"], lambda e: e.tensor_scalar(out=tb["t1"][:], in0=tb["ar"][:], scalar1=-1.0, scalar2=None, op0=ALU.add))
            ew("dve", tb["t2"][:], tb["lamr"][:], tb["lamr"][:], ALU.mult, ["s5_lamr"], ["s5_t2"])
            ew("dve", tb["t3"][:], tb["lami"][:], tb["lami"][:], ALU.mult, ["s5_lami"], ["s5_t3"])
            ew("dve", tb["t2"][:], tb["t2"][:], tb["t3"][:], ALU.add, ["s5_t2", "s5_t3"], ["s5_t2"])
            tr.op("dve", ["s5_t2"], ["s5_t2"], lambda e: e.reciprocal(out=tb["t2"][:], in_=tb["t2"][:]))
            ew("dve", tb["fr"][:], tb["t1"][:], tb["lamr"][:], ALU.mult, ["s5_t1", "s5_lamr"], ["s5_fr"])
            ew("dve", tb["t3"][:], tb["ai"][:], tb["lami"][:], ALU.mult, ["s5_ai", "s5_lami"], ["s5_t3"])
            ew("dve", tb["fr"][:], tb["fr"][:], tb["t3"][:], ALU.add, ["s5_fr", "s5_t3"], ["s5_fr"])
            ew("dve", tb["fr"][:], tb["fr"][:], tb["t2"][:], ALU.mult, ["s5_fr", "s5_t2"], ["s5_fr"])
            ew("dve", tb["fi"][:], tb["ai"][:], tb["lamr"][:], ALU.mult, ["s5_ai", "s5_lamr"], ["s5_fi"])
            ew("dve", tb["t3"][:], tb["t1"][:], tb["lami"][:], ALU.mult, ["s5_t1", "s5_lami"], ["s5_t3"])
            ew("dve", tb["fi"][:], tb["fi"][:], tb["t3"][:], ALU.subtract, ["s5_fi", "s5_t3"], ["s5_fi"])
            ew("dve", tb["fi"][:], tb["fi"][:], tb["t2"][:], ALU.mult, ["s5_fi", "s5_t2"], ["s5_fi"])

            dcol = sb("s5_dcol", [128, c.KW])
            with nc.allow_non_contiguous_dma(reason="tiny per-channel vector"):
                tr.dma("sp", chdc, [], ["s5_dcol"], dcol[:], d_ssm.rearrange("(k p) o -> p (k o)", p=128))
            wuy = sb("s5_wu", [128, max(KD * GB * 16, NT * TT)], BF16)
            wu3 = r3(wuy[:, 0:KD * GB * 16], KD, GB * 16)
            chwu = tr.chan()
            uT = sb("s5_uT", [128, NT * TT], BF16)
            uT3 = r3(uT[:], NT, TT)
            yT3 = r3(wuy[:, 0:NT * TT], NT, TT)
            chy = tr.chan()
            PWR = sb("s5_pwr", [64, 9 * GB])
            PWI = sb("s5_pwi", [64, 9 * GB])
            pw_r = lambda m: PWR[:, m * GB:(m + 1) * GB]
            pw_i = lambda m: PWI[:, m * GB:(m + 1) * GB]
            GC = GB * 16
            Bt = {k: sb("s5_" + k, [64, GC]) for k in ("Br", "Bi", "Bbr", "Bbi", "CTr", "CTi", "e1", "e2")}
            chB = tr.chan()
            XR = sb("s5_XR", [64, 8 * GC], BF16)
            XI = sb("s5_XI", [64, 8 * GC], BF16)
            CPR = sb("s5_CPR", [64, 8 * GC], BF16)
            CPI = sb("s5_CPI", [64, 8 * GC], BF16)
            CTrb = sb("s5_CTrb", [64, GC], BF16)
            NCTib = sb("s5_NCTib", [64, GC], BF16)
            crow = sb("s5_crow", [128, 64])
            Kbd = sb("s5_Kbd", [128, 8 * 128], BF16)
            YP = [sb("s5_YP%d" % i, [128, 8 * 8 * 64], BF16) for i in range(2)]
            YTs = sb("s5_YTs", [128, 128], BF16)
            CPpr = sb("s5_CPpr", [64, 8 * 128], BF16)
            CPpi = sb("s5_CPpi", [64, 8 * 128], BF16)
            VB = sb("s5_VB", [64, 2 * GB * NC1])
            VB4 = VB[:].rearrange("p (a g n) -> p a g n", a=2, g=GB, n=NC1)
            XH = sb("s5_XH", [64, 2 * GB * NCK], BF16)
            XH4 = XH[:].rearrange("p (a g n) -> p a g n", a=2, g=GB, n=NCK)
            A8a = sb("s5_A8a", [64, 2 * GB])
            A8b = sb("s5_A8b", [64, 2 * GB])
            A1a = sb("s5_A1a", [64, 2 * GB])
            A1b = sb("s5_A1b", [64, 2 * GB])
            sc1 = sb("s5_sc1", [64, 2 * GB])
            sc2 = sb("s5_sc2", [64, 2 * GB])
            VS = sb("s5_VS", [64, 2 * NS * GB])
            VS4 = VS[:].rearrange("p (a n g) -> p a n g", a=2, n=NS, g=GB)
            XS = sb("s5_XS", [64, 2 * NS * GB])
            XS4 = XS[:].rearrange("p (a n g) -> p a n g", a=2, n=NS, g=GB)
            XSb = sb("s5_XSb", [64, 2 * NS * GB], BF16)
            XSb4 = XSb[:].rearrange("p (a n g) -> p a n g", a=2, n=NS, g=GB)
            stmp = sb("s5_stmp", [64, max(2 * NS * GB, 2 * GB * 32)])
            ss1 = stmp
            APW = sb("s5_APW", [64, int(math.log2(NCK)) * 4 * GB])
            srow = sb("s5_srow", [128, 64])
            chs = tr.chan()
            orow = sb("s5_orow", [128, 64])
            cho = tr.chan()
            ytmp = sb("s5_ytmp", [128, max(NCK + NS, 256)])
            YT8 = ytmp[:, 0:256].bitcast(BF16)

            def bc3(ap2, n):
                return ap2.unsqueeze(2).to_broadcast([64, GB, n])

            for blk in range(G // GB):
                g0 = blk * GB
                ch0 = g0 * 16
                tr.dma("pool", chwu, [], ["s5_wu"], wu3, w_in_ssm[:, ch0:ch0 + GC].rearrange("(k p) n -> p k n", p=128))
                for tl in range(NT):
                    for t0 in range(0, TT, 512):
                        n = min(512, TT - t0)
                        p, pn = pf()
                        for k in range(KD):
                            tr.op("pe", ["s5_wu", "hT"], [pn], lambda e: e.matmul(p[:, 0:n], lhsT=wu3[:, k, tl * 128:(tl + 1) * 128], rhs=hT3[:, k, t0:t0 + n], start=(k == 0), stop=(k == KD - 1)))
                        tr.op("act", [pn], ["s5_uT"], lambda e: e.activation(out=uT3[:, tl, t0:t0 + n], in_=p[:, 0:n], func=AF.Copy))
                tr.op("pool", [], ["s5_pwr"], lambda e: e.memset(pw_r(0), 1.0))
                tr.op("pool", [], ["s5_pwi"], lambda e: e.memset(pw_i(0), 0.0))
                arb = tb["ar"][:, g0:g0 + GB]
                aib = tb["ai"][:, g0:g0 + GB]
                e1 = Bt["e1"][:, 0:GB]
                e2 = Bt["e2"][:, 0:GB]
                for m in range(8):
                    ew("dve", e1, pw_r(m), arb, ALU.mult, ["s5_pwr", "s5_ar"], ["s5_e1"])
                    ew("dve", e2, pw_i(m), aib, ALU.mult, ["s5_pwi", "s5_ai"], ["s5_e2"])
                    ew("dve", pw_r(m + 1), e1, e2, ALU.subtract, ["s5_e1", "s5_e2"], ["s5_pwr"])
                    ew("dve", e1, pw_r(m), aib, ALU.mult, ["s5_pwr", "s5_ai"], ["s5_e1"])
                    ew("dve", e2, pw_i(m), arb, ALU.mult, ["s5_pwi", "s5_ar"], ["s5_e2"])
                    ew("dve", pw_i(m + 1), e1, e2, ALU.add, ["s5_e1", "s5_e2"], ["s5_pwi"])
                for (Aa, Ab, an, bn, m) in ((A8a, A8b, "s5_A8a", "s5_A8b", 8), (A1a, A1b, "s5_A1a", "s5_A1b", 1)):
                    tr.op("dve", ["s5_pwr"], [an], lambda e: e.tensor_copy(out=Aa[:, 0:GB], in_=pw_r(m)))
                    tr.op("dve", ["s5_pwi"], [an], lambda e: e.tensor_copy(out=Aa[:, GB:2 * GB], in_=pw_i(m)))
                    tr.op("dve", ["s5_pwi"], [bn], lambda e: e.tensor_scalar(out=Ab[:, 0:GB], in0=pw_i(m), scalar1=-1.0, scalar2=None, op0=ALU.mult))
                    tr.op("dve", ["s5_pwr"], [bn], lambda e: e.tensor_copy(out=Ab[:, GB:2 * GB], in_=pw_r(m)))
                B3 = lambda k: r3(Bt[k][:], GB, 16)
                tr.dma("sp", chB, [], ["s5_Br"], B3("Br"), b_re[g0:g0 + GB].rearrange("g p c -> p g c"))
                tr.dma("sp", chBi, [], ["s5_Bi"], B3("Bi"), b_im[g0:g0 + GB].rearrange("g p c -> p g c"))
                frb = bc3(tb["fr"][:, g0:g0 + GB], 16)
                fib = bc3(tb["fi"][:, g0:g0 + GB], 16)
                ew("dve", B3("Bbr"), B3("Br"), frb, ALU.mult, ["s5_Br", "s5_fr"], ["s5_Bbr"])
                ew("dve", B3("e1"), B3("Bi"), fib, ALU.mult, ["s5_Bi", "s5_fi"], ["s5_e1"])
                ew("dve", B3("Bbr"), B3("Bbr"), B3("e1"), ALU.subtract, ["s5_Bbr", "s5_e1"], ["s5_Bbr"])
                ew("dve", B3("Bbi"), B3("Bi"), frb, ALU.mult, ["s5_Bi", "s5_fr"], ["s5_Bbi"])
                ew("dve", B3("e1"), B3("Br"), fib, ALU.mult, ["s5_Br", "s5_fi"], ["s5_e1"])
                ew("dve", B3("Bbi"), B3("Bbi"), B3("e1"), ALU.add, ["s5_Bbi", "s5_e1"], ["s5_Bbi"])
                for tau in range(8):
                    prb = bc3(pw_r(tau), 16)
                    pib = bc3(pw_i(tau), 16)
                    xr_o = r3(XR[:, tau * GC:(tau + 1) * GC], GB, 16)
                    xi_o = r3(XI[:, tau * GC:(tau + 1) * GC], GB, 16)
                    ew("dve", B3("e1"), B3("Bbr"), prb, ALU.mult, ["s5_Bbr", "s5_pwr"], ["s5_e1"])
                    ew("pool", B3("e2"), B3("Bbi"), pib, ALU.mult, ["s5_Bbi", "s5_pwi"], ["s5_e2"])
                    ew("dve", xr_o, B3("e1"), B3("e2"), ALU.subtract, ["s5_e1", "s5_e2"], ["s5_XR"])
                    ew("dve", B3("e1"), B3("Bbi"), prb, ALU.mult, ["s5_Bbi", "s5_pwr"], ["s5_e1"])
                    ew("pool", B3("e2"), B3("Bbr"), pib, ALU.mult, ["s5_Bbr", "s5_pwi"], ["s5_e2"])
                    ew("dve", xi_o, B3("e1"), B3("e2"), ALU.add, ["s5_e1", "s5_e2"], ["s5_XI"])
                for (src_, dk) in ((c_re, "CTr"), (c_im, "CTi")):
                    rows = src_[g0:g0 + GB].rearrange("g c p -> (g c) p")
                    for r0 in range(0, GC, 128):
                        tr.dma("sp", chcr, [], ["s5_crow"], crow[:], rows[r0:r0 + 128, :])
                        p, pn = pf()
                        tr.op("pe", ["s5_crow", "cst"], [pn], lambda e: e.transpose(out=p[0:64, 0:128], in_=crow[:], identity=ident))
                        tr.op("dve", [pn], ["s5_" + dk], lambda e: e.tensor_copy(out=Bt[dk][:, r0:r0 + 128], in_=p[0:64, 0:128]))
                tr.op("dve", ["s5_CTr"], ["s5_CTrb"], lambda e: e.tensor_copy(out=CTrb[:], in_=Bt["CTr"][:]))
                tr.op("dve", ["s5_CTi"], ["s5_NCTib"], lambda e: e.tensor_scalar(out=NCTib[:], in0=Bt["CTi"][:], scalar1=-1.0, scalar2=None, op0=ALU.mult))
                for r_ in range(8):
                    prb = bc3(pw_r(r_ + 1), 16)
                    pib = bc3(pw_i(r_ + 1), 16)
                    cr_o = r3(CPR[:, r_ * GC:(r_ + 1) * GC], GB, 16)
                    ci_o = r3(CPI[:, r_ * GC:(r_ + 1) * GC], GB, 16)
                    ew("dve", B3("e1"), B3("CTr"), prb, ALU.mult, ["s5_CTr", "s5_pwr"], ["s5_e1"])
                    ew("pool", B3("e2"), B3("CTi"), pib, ALU.mult, ["s5_CTi", "s5_pwi"], ["s5_e2"])
                    ew("dve", cr_o, B3("e1"), B3("e2"), ALU.subtract, ["s5_e1", "s5_e2"], ["s5_CPR"])
                    ew("dve", B3("e1"), B3("CTr"), pib, ALU.mult, ["s5_CTr", "s5_pwi"], ["s5_e1"])
                    ew("pool", B3("e2"), B3("CTi"), prb, ALU.mult, ["s5_CTi", "s5_pwr"], ["s5_e2"])
                    ew("dve", B3("e1"), B3("e1"), B3("e2"), ALU.add, ["s5_e1", "s5_e2"], ["s5_e1"])
                    tr.op("dve", ["s5_e1"], ["s5_CPI"], lambda e: e.tensor_scalar(out=ci_o, in0=B3("e1"), scalar1=-1.0, scalar2=None, op0=ALU.mult))
                tr.op("pool", [], ["s5_VB"], lambda e: e.memset(VB[:], 0.0))
                for tl in range(NT):
                    for part, Xs, xn in ((0, XR, "s5_XR"), (1, XI, "s5_XI")):
                        Y4 = YP[part][:].rearrange("p (t g s) -> p t g s", t=8, g=8, s=64)
                        ypn = "s5_YP%d" % part
                        p, pn = pb()
                        for tau in range(8):
                            tr.op("pe", [xn, "cstb"], [pn], lambda e: e.transpose(out=p[:, tau * 64:(tau + 1) * 64], in_=Xs[:, tau * GC + tl * 128: tau * GC + (tl + 1) * 128], identity=identb[0:64, 0:64]))
                        tr.op("act", [pn], ["s5_ytmp"], lambda e: e.activation(out=YT8, in_=p[:, 0:512], func=AF.Copy))
                        tr.op("dve", ["s5_ytmp", "cst"], [ypn], lambda e: e.tensor_tensor(out=Y4, in0=YT8.rearrange("p (t s) -> p t s", t=8, s=64).unsqueeze(2).to_broadcast([128, 8, 8, 64]), in1=GSEL.unsqueeze(1).unsqueeze(3).to_broadcast([128, 8, 8, 64]), op=ALU.mult))
                        for g in range(8):
                            p, pn = pf()
                            for tau in range(8):
                                tr.op("pe", [ypn, "s5_uT"], [pn], lambda e: e.matmul(p[0:64, 0:NCK], lhsT=Y4[:, tau, g, :], rhs=uT3[:, tl, (7 - tau):T:8], start=(tau == 0), stop=(tau == 7)))
                            tr.op("pe", [ypn, "s5_uT"], [pn], lambda e: e.matmul(p[0:64, NCK:NCK + NS], lhsT=Y4[:, 0, g, :], rhs=uT3[:, tl, T:TT], start=True, stop=True))
                            tr.op("act", [pn], ["s5_VB"], lambda e: e.activation(out=VB4[:, part, tl * 8 + g, 1:1 + NCK], in_=p[0:64, 0:NCK], func=AF.Copy))
                            tr.op("dve", [pn], ["s5_VS"], lambda e: e.tensor_copy(out=VS4[:, part, :, tl * 8 + g], in_=p[0:64, NCK:NCK + NS]))
                chk(30)
                A8a3 = A8a[:].rearrange("p (a g) -> p a g", a=2, g=GB)
                A8b3 = A8b[:].rearrange("p (a g) -> p a g", a=2, g=GB)
                LV = int(math.log2(NCK))
                assert (1 << LV) == NCK
                PWT = 32
                APW4 = APW[:].rearrange("p (l q g) -> p l q g", l=LV, q=4, g=GB)
                tr.op("dve", ["s5_A8a"], ["s5_APW"], lambda e: e.tensor_copy(out=APW4[:, 0, 0:2, :], in_=A8a3))
                tr.op("dve", ["s5_A8b"], ["s5_APW"], lambda e: e.tensor_copy(out=APW4[:, 0, 2:4, :], in_=A8b3))
                for l in range(1, LV):
                    pr_, pi_ = APW4[:, l - 1, 0, :], APW4[:, l - 1, 1, :]
                    ew("dve", sc1[:, 0:GB], pr_, pr_, ALU.mult, ["s5_APW"], ["s5_sc1"])
                    ew("dve", sc1[:, GB:2 * GB], pi_, pi_, ALU.mult, ["s5_APW"], ["s5_sc1"])
                    ew("dve", APW4[:, l, 0, :], sc1[:, 0:GB], sc1[:, GB:2 * GB], ALU.subtract, ["s5_sc1"], ["s5_APW"])
                    tr.op("dve", ["s5_APW"], ["s5_APW"], lambda e: e.scalar_tensor_tensor(out=APW4[:, l, 1, :], in0=pr_, scalar=2.0, in1=pi_, op0=ALU.mult, op1=ALU.mult))
                    tr.op("dve", ["s5_APW"], ["s5_APW"], lambda e: e.tensor_copy(out=APW4[:, l, 3, :], in_=APW4[:, l, 0, :]))
                    tr.op("dve", ["s5_APW"], ["s5_APW"], lambda e: e.tensor_scalar(out=APW4[:, l, 2, :], in0=APW4[:, l, 1, :], scalar1=-1.0, scalar2=None, op0=ALU.mult))

                def cacc(l, tgt_sl, src_sl, cnt):
                    for q0 in range(0, cnt, PWT):
                        qn = min(PWT, cnt - q0)
                        t_lo, t_st = tgt_sl
                        s_lo, s_st = src_sl
                        tg = VB4[:, :, :, t_lo + q0 * t_st: t_lo + (q0 + qn - 1) * t_st + 1: t_st]
                        sr = VB4[:, 0:1, :, s_lo + q0 * s_st: s_lo + (q0 + qn - 1) * s_st + 1: s_st].to_broadcast([64, 2, GB, qn])
                        si = VB4[:, 1:2, :, s_lo + q0 * s_st: s_lo + (q0 + qn - 1) * s_st + 1: s_st].to_broadcast([64, 2, GB, qn])
                        pa = APW4[:, l, 0:2, :].unsqueeze(3).to_broadcast([64, 2, GB, qn])
                        pb_ = APW4[:, l, 2:4, :].unsqueeze(3).to_broadcast([64, 2, GB, qn])
                        tm = stmp[:, 0:2 * GB * qn].rearrange("p (a g n) -> p a g n", a=2, g=GB, n=qn)
                        ew("dve", tm, pa, sr, ALU.mult, ["s5_APW", "s5_VB"], ["s5_stmp"])
                        ew("dve", tg, tg, tm, ALU.add, ["s5_VB", "s5_stmp"], ["s5_VB"])
                        ew("dve", tm, pb_, si, ALU.mult, ["s5_APW", "s5_VB"], ["s5_stmp"])
                        ew("dve", tg, tg, tm, ALU.add, ["s5_VB", "s5_stmp"], ["s5_VB"])

                for l in range(LV):
                    s_ = 1 << l
                    cacc(l, (2 * s_, 2 * s_), (s_, 2 * s_), NCK // (2 * s_))
                for l in range(LV - 2, -1, -1):
                    s_ = 1 << l
                    cacc(l, (3 * s_, 2 * s_), (2 * s_, 2 * s_), NCK // (2 * s_) - 1)
                tr.op("act", ["s5_VB"], ["s5_XH"], lambda e: e.activation(out=XH4, in_=VB4[:, :, :, 0:NCK], func=AF.Copy))
                for part, dst_, dn_ in ((0, re_p, "re_p"), (1, im_p, "im_p")):
                    tr.op("dve", ["s5_VB"], ["s5_sc1"], lambda e: e.tensor_copy(out=sc1[:, 0:GB], in_=VB4[:, part, :, NCK]))
                    p, pn = pf()
                    tr.op("pe", ["s5_sc1", "cst"], [pn], lambda e: e.transpose(out=p[0:GB, 0:64], in_=sc1[:, 0:GB], identity=ident[0:64, 0:64]))
                    tr.op("act", [pn], ["s5_orow"], lambda e: e.activation(out=orow[0:GB, :], in_=p[0:GB, 0:64], func=AF.Copy))
                    tr.dma("sp", cho, ["s5_orow"], [dn_], dst_[g0:g0 + GB, :], orow[0:GB, :])
                RW = min(128, NS * GB)
                for part, src_ in ((0, re_s), (1, im_s)):
                    for r0 in range(0, NS * GB, RW):
                        n0 = r0 // GB
                        nn = RW // GB
                        tr.dma("sp", chs, [], ["s5_srow"], srow[0:RW, :], src_[n0:n0 + nn, g0:g0 + GB, :])
                        p, pn = pf()
                        tr.op("pe", ["s5_srow", "cst"], [pn], lambda e: e.transpose(out=p[0:64, 0:RW], in_=srow[0:RW, :], identity=ident[0:RW, 0:RW]))
                        tr.op("dve", [pn], ["s5_XS"], lambda e: e.tensor_copy(out=XS[:, part * NS * GB + r0: part * NS * GB + r0 + RW], in_=p[0:64, 0:RW]))
                tr.op("act", ["s5_XS"], ["s5_XSb"], lambda e: e.activation(out=XSb[:], in_=XS[:], func=AF.Copy))
                A1a4 = A1a[:].rearrange("p (a g) -> p a g", a=2, g=GB).unsqueeze(2).to_broadcast([64, 2, NS, GB])
                A1b4 = A1b[:].rearrange("p (a g) -> p a g", a=2, g=GB).unsqueeze(2).to_broadcast([64, 2, NS, GB])
                ss4 = stmp[:, 0:2 * NS * GB].rearrange("p (a n g) -> p a n g", a=2, n=NS, g=GB)
                ew("dve", ss4, A1a4, XS4[:, 0:1].to_broadcast([64, 2, NS, GB]), ALU.mult, ["s5_A1a", "s5_XS"], ["s5_stmp"])
                ew("dve", VS4, VS4, ss4, ALU.add, ["s5_VS", "s5_stmp"], ["s5_VS"])
                ew("dve", ss4, A1b4, XS4[:, 1:2].to_broadcast([64, 2, NS, GB]), ALU.mult, ["s5_A1b", "s5_XS"], ["s5_stmp"])
                ew("dve", VS4, VS4, ss4, ALU.add, ["s5_VS", "s5_stmp"], ["s5_VS"])
                for part, dst_, dn_ in ((0, re_so, "re_so"), (1, im_so, "im_so")):
                    for r0 in range(0, NS * GB, RW):
                        n0 = r0 // GB
                        nn = RW // GB
                        p, pn = pf()
                        tr.op("pe", ["s5_VS", "cst"], [pn], lambda e: e.transpose(out=p[0:RW, 0:64], in_=VS[:, part * NS * GB + r0: part * NS * GB + r0 + RW], identity=ident[0:64, 0:64]))
                        tr.op("act", [pn], ["s5_orow"], lambda e: e.activation(out=orow[0:RW, :], in_=p[0:RW, 0:64], func=AF.Copy))
                        tr.dma("sp", cho, ["s5_orow"], [dn_], dst_[n0:n0 + nn, g0:g0 + GB, :], orow[0:RW, :])
                chk(31)
                for tl in range(NT):
                    for t4 in range(0, 8, 4):
                        p, pn = pf()
                        for tau in range(t4, t4 + 4):
                            o_ = p[:, (tau - t4) * 128:(tau - t4 + 1) * 128]
                            tr.op("pe", ["s5_XR", "s5_CTrb"], [pn], lambda e: e.matmul(o_, lhsT=XR[:, tau * GC + tl * 128: tau * GC + (tl + 1) * 128], rhs=CTrb[:, tl * 128:(tl + 1) * 128], start=True, stop=False))
                            tr.op("pe", ["s5_XI", "s5_NCTib"], [pn], lambda e: e.matmul(o_, lhsT=XI[:, tau * GC + tl * 128: tau * GC + (tl + 1) * 128], rhs=NCTib[:, tl * 128:(tl + 1) * 128], start=False, stop=True))
                        tr.op("dve", [pn, "cst"], ["s5_Kbd"], lambda e: e.tensor_tensor(out=r3(Kbd[:, t4 * 128:(t4 + 4) * 128], 4, 128), in0=r3(p[:, 0:512], 4, 128), in1=BD.unsqueeze(1).to_broadcast([128, 4, 128]), op=ALU.mult))
                    for r_ in range(8):
                        for CPs, CPp, cn in ((CPR, CPpr, "s5_CPpr"), (CPI, CPpi, "s5_CPpi")):
                            src4 = CPs[:, r_ * GC + tl * 128: r_ * GC + (tl + 1) * 128].rearrange("p (g c) -> p g c", g=8, c=16).unsqueeze(2).to_broadcast([64, 8, 8, 16])
                            gg4 = GG[0:64, :].rearrange("p (g h) -> p g h", g=8, h=8).unsqueeze(3).to_broadcast([64, 8, 8, 16])
                            tr.op("pool" if cn == "s5_CPpr" else "dve", ["s5_CPR", "s5_CPI", "cst"], [cn], lambda e: e.tensor_tensor(out=CPp[:].rearrange("p (g h c) -> p g h c", g=8, h=8, c=16), in0=src4, in1=gg4, op=ALU.mult))
                        p, pn = pf()
                        mms = [(Kbd[:, tau * 128:(tau + 1) * 128], uT3[:, tl, (r_ - tau):T:8], ["s5_Kbd", "s5_uT"]) for tau in range(r_ + 1)]
                        for g in range(8):
                            mms.append((CPpr[:, g * 128:(g + 1) * 128], XH4[:, 0, tl * 8 + g, :], ["s5_CPpr", "s5_XH"]))
                            mms.append((CPpi[:, g * 128:(g + 1) * 128], XH4[:, 1, tl * 8 + g, :], ["s5_CPpi", "s5_XH"]))
                        for i_, (l_, rh_, rd_) in enumerate(mms):
                            tr.op("pe", rd_, [pn], lambda e: e.matmul(p[:, 0:NCK], lhsT=l_, rhs=rh_, start=(i_ == 0), stop=(i_ == len(mms) - 1)))
                        dsc = dcol[:, blk * NT + tl: blk * NT + tl + 1]
                        tr.op("dve", [pn, "s5_uT", "s5_dcol"], ["s5_ytmp"], lambda e: e.scalar_tensor_tensor(out=ytmp[:, 0:NCK], in0=uT3[:, tl, r_:T:8], scalar=dsc, in1=p[:, 0:NCK], op0=ALU.mult, op1=ALU.add))
                        tr.op("act", ["s5_ytmp"], ["s5_wu"], lambda e: e.activation(out=yT3[:, tl, r_:T:8], in_=ytmp[:, 0:NCK], func=AF.Gelu))
                        if r_ == 0:
                            mms = [(Kbd[:, 0:128], uT3[:, tl, T:TT], ["s5_Kbd", "s5_uT"])]
                            for g in range(8):
                                mms.append((CPpr[:, g * 128:(g + 1) * 128], XSb4[:, 0, :, tl * 8 + g], ["s5_CPpr", "s5_XSb"]))
                                mms.append((CPpi[:, g * 128:(g + 1) * 128], XSb4[:, 1, :, tl * 8 + g], ["s5_CPpi", "s5_XSb"]))
                            p2, p2n = pf()
                            for i_, (l_, rh_, rd_) in enumerate(mms):
                                tr.op("pe", rd_, [p2n], lambda e: e.matmul(p2[:, 0:NS], lhsT=l_, rhs=rh_, start=(i_ == 0), stop=(i_ == len(mms) - 1)))
                            tr.op("dve", [p2n, "s5_uT", "s5_dcol"], ["s5_ytmp"], lambda e: e.scalar_tensor_tensor(out=ytmp[:, NCK:NCK + NS], in0=uT3[:, tl, T:TT], scalar=dsc, in1=p2[:, 0:NS], op0=ALU.mult, op1=ALU.add))
                            tr.op("act", ["s5_ytmp"], ["s5_wu"], lambda e: e.activation(out=yT3[:, tl, T:TT], in_=ytmp[:, NCK:NCK + NS], func=AF.Gelu))
                    tr.dma("sp", chy, ["s5_wu"], ["yT_scr"], yT_scr[ch0 + tl * 128: ch0 + (tl + 1) * 128, :], yT3[:, tl, :])
                chk(32)
            phase_end()
            chk(33)

            phase_begin()
            KW = c.KW
            TBM = min(TT, 1040)
            yTa = sb("g_yTa", [128, KW * TBM], BF16)
            yTa3 = r3(yTa[:], KW, TBM)
            chya = tr.chan()
            wg = [sb("g_wg%d" % i, [128, KW * 128], BF16) for i in range(2)]
            wgc = [tr.chan() for i in range(2)]
            wz = [sb("g_wz%d" % i, [128, KD * 128], BF16) for i in range(2)]
            wzc = [tr.chan() for i in range(2)]
            bgl = sb("g_bgl", [128, KW])
            chbg = tr.chan()
            with nc.allow_non_contiguous_dma(reason="tiny per-channel vector"):
                tr.dma("sp", chbg, [], ["g_bgl"], bgl[:], b_glu.rearrange("(k p) o -> p (k o)", p=128))
            sgt = sb("g_sg", [128, 512])
            szt = sb("g_sz", [128, 512])
            y2t = [sb("g_y2%d" % i, [128, TBM], BF16) for i in range(2)]
            y2c = [tr.chan() for i in range(2)]
            it = 0
            for b0 in range(0, TT, TBM):
                bn = min(TBM, TT - b0)
                tr.dma("sp", chya, ["yT_scr"], ["g_yTa"], yTa3[:, :, 0:bn], yT_scr[:, b0:b0 + bn].rearrange("(k p) t -> p k t", p=128))
                for m in range(KW):
                    wb = it % 2
                    it += 1
                    wg3 = r3(wg[wb][:], KW, 128)
                    wz3 = r3(wz[wb][:], KD, 128)
                    tr.dma("pool", wgc[wb], [], ["g_wg%d" % wb], wg3, w_glu[:, m * 128:(m + 1) * 128].rearrange("(k p) n -> p k n", p=128))
                    tr.dma("pool", wzc[wb], [], ["g_wz%d" % wb], wz3, w_in_ssm[:, c.W + m * 128: c.W + (m + 1) * 128].rearrange("(k p) n -> p k n", p=128))
                    for t0 in range(0, bn, 512):
                        n = min(512, bn - t0)
                        pg, pgn = pf()
                        for k in range(KW):
                            tr.op("pe", ["g_wg%d" % wb, "g_yTa"], [pgn], lambda e: e.matmul(pg[:, 0:n], lhsT=wg3[:, k, :], rhs=yTa3[:, k, t0:t0 + n], start=(k == 0), stop=(k == KW - 1)))
                        pz, pzn = pf()
                        for k in range(KD):
                            tr.op("pe", ["g_wz%d" % wb, "hT"], [pzn], lambda e: e.matmul(pz[:, 0:n], lhsT=wz3[:, k, :], rhs=hT3[:, k, b0 + t0:b0 + t0 + n], start=(k == 0), stop=(k == KD - 1)))
                        tr.op("act", [pgn, "g_bgl"], ["g_sg"], lambda e: e.activation(out=sgt[:, 0:n], in_=pg[:, 0:n], func=AF.Sigmoid, bias=bgl[:, m:m + 1]))
                        tr.op("act", [pzn], ["g_sz"], lambda e: e.activation(out=szt[:, 0:n], in_=pz[:, 0:n], func=AF.Silu))
                        tr.op("pool", ["g_sg", "g_sz"], ["g_sg"], lambda e: e.tensor_tensor(out=sgt[:, 0:n], in0=sgt[:, 0:n], in1=szt[:, 0:n], op=ALU.mult))
                        tr.op("dve", ["g_sg", "g_yTa"], ["g_y2%d" % wb], lambda e: e.tensor_tensor(out=y2t[wb][:, t0:t0 + n], in0=sgt[:, 0:n], in1=yTa3[:, m, t0:t0 + n], op=ALU.mult))
                    tr.dma("sp", y2c[wb], ["g_y2%d" % wb], ["y2_scr"], y2_scr[m * 128:(m + 1) * 128, b0:b0 + bn], y2t[wb][:, 0:bn])
            phase_end()
            chk(34)
            outproj(y2_scr, "y2_scr", c.KW, w_out_ssm, x1_scr, "x1_scr", x2_scr, "x2_scr", "b")
            chk(35)
            norm_to_hT(x2_scr, norm_final, "x2_scr", final_out=y_out)
        except _Stop:
            if cur[0] is not es:
                cur[0].close()
                cur[0] = es
        tr.finish()
    return nc


def make_consts():
    cs = np.zeros((128, 8 * 128), np.float32)
    i = np.arange(128)
    cs[:, 0:128] = np.eye(128)
    cs[:, 128:256] = (i[:, None] <= i[None, :])
    cs[:, 256:384] = 1.0
    cs[:, 384:512] = np.where(i[None, :] < i[:, None], 0.0, 30000.0)
    cs[:, 512:640] = np.where(i[:, None] <= i[None, :], 0.0, -30000.0)
    cs[:, 640:768] = ((i[None, :] // 16) >= (i[:, None] // 16))
    cs[:, 768:896] = ((i[None, :] // 16) == (i[:, None] // 16))
    cs[:, 896:904] = ((i[:, None] // 16) == np.arange(8)[None, :])
    cs[:, 904:968] = np.eye(8).reshape(1, 64)
    return cs


_NC_CACHE = {}


def kernel(x_prompt, x_sample, state_gdn_conv, state_gdn_delta, state_ssm_re, state_ssm_im,
           norm_gdn, w_in_gdn, conv_gdn, a_log_gdn, dt_bias_gdn, onorm_gdn, w_out_gdn,
           norm_ssm, w_in_ssm, lam_re, lam_im, b_re, b_im, c_re, c_im, d_ssm, log_dt_ssm,
           w_glu_ssm, b_glu_ssm, w_out_ssm, norm_final):
    cfg = Cfg(**FULL)
    f = lambda a: np.ascontiguousarray(np.asarray(a, dtype=np.float32))
    NS, T = cfg.NS, cfg.T
    B = x_prompt.shape[0]
    ncores = 8
    if "nc" not in _NC_CACHE:
        _NC_CACHE["nc"] = build(cfg)
    nc = _NC_CACHE["nc"]
    shared = {
        "norm_gdn": f(norm_gdn).reshape(1, -1), "w_in_gdn": f(w_in_gdn[0]), "conv_w": f(conv_gdn[0]),
        "a_log": f(a_log_gdn).reshape(1, -1), "dt_bias": f(dt_bias_gdn).reshape(1, -1), "onorm": f(onorm_gdn).reshape(1, -1),
        "w_out_gdn": f(w_out_gdn[0]), "norm_ssm": f(norm_ssm).reshape(1, -1), "w_in_ssm": f(w_in_ssm[0]),
        "lam_re": f(lam_re[0]), "lam_im": f(lam_im[0]), "b_re": f(b_re[0]), "b_im": f(b_im[0]), "c_re": f(c_re[0]), "c_im": f(c_im[0]),
        "d_ssm": f(d_ssm[0]).reshape(-1, 1), "log_dt": f(log_dt_ssm).reshape(1, -1), "w_glu": f(w_glu_ssm[0]),
        "b_glu": f(b_glu_ssm[0]).reshape(-1, 1), "w_out_ssm": f(w_out_ssm[0]), "norm_final": f(norm_final).reshape(1, -1),
        "consts": make_consts(),
    }
    in_maps = []
    for i in range(ncores):
        sq = i % B
        sl = slice(i * NS, (i + 1) * NS)
        m = dict(shared)
        m["xin"] = np.ascontiguousarray(np.concatenate([f(x_prompt[sq]), f(x_sample[sl, 0])], axis=0))
        m["conv_s"] = f(state_gdn_conv[0, sl])
        m["delta_s"] = f(state_gdn_delta[0, sl])
        m["re_s"] = f(state_ssm_re[0, sl])
        m["im_s"] = f(state_ssm_im[0, sl])
        in_maps.append(m)
    res = run_bass_kernel_spmd(nc, in_maps, core_ids=list(range(ncores))).results
    g = lambda i, k: np.asarray(res[i][k], dtype=np.float32)
    y_prompt = np.stack([g(i, "y_out")[:T] for i in range(B)])
    y_sample = np.concatenate([g(i, "y_out")[T:] for i in range(ncores)])[:, None, :]
    conv_prompt = np.stack([g(i, "conv_p") for i in range(B)])[None]
    delta_prompt = np.stack([g(i, "delta_p") for i in range(B)])[None]
    re_prompt = np.stack([g(i, "re_p") for i in range(B)])[None]
    im_prompt = np.stack([g(i, "im_p") for i in range(B)])[None]
    conv_sample = np.concatenate([g(i, "conv_so") for i in range(ncores)])[None]
    delta_sample = np.concatenate([g(i, "delta_so") for i in range(ncores)])[None]
    re_sample = np.concatenate([g(i, "re_so") for i in range(ncores)])[None]
    im_sample = np.concatenate([g(i, "im_so") for i in range(ncores)])[None]
    return (y_prompt, y_sample, conv_prompt, delta_prompt, re_prompt, im_prompt,
            conv_sample, delta_sample, re_sample, im_sample)
```

```python
import contextlib
import math
import numpy as np
import concourse.bass as bass
import concourse.mybir as mybir
from concourse.bass_utils import run_bass_kernel_spmd

F32 = mybir.dt.float32
BF16 = mybir.dt.bfloat16
AF = mybir.ActivationFunctionType
ALU = mybir.AluOpType
AX = mybir.AxisListType

FULL = dict(D=2048, T=2048, NS=16, HQK=16, G=256)


class Cfg:
    def __init__(self, D, T, NS, HQK, G):
        self.D, self.T, self.NS, self.HQK, self.G = D, T, NS, HQK, G
        self.KD = D // 128
        self.HV = 2 * HQK
        self.KEY = HQK * 128
        self.VAL = self.HV * 128
        self.CONV = 2 * self.KEY + self.VAL
        self.IN = self.CONV + self.VAL + 2 * self.HV
        self.W = 16 * G
        self.KW = self.W // 128
        self.TT = T + NS
        self.NCH = T // 128


class TR:
    def __init__(self, nc, es):
        self.nc, self.es = nc, es
        self.eng = dict(pe=nc.tensor, act=nc.scalar, dve=nc.vector, pool=nc.gpsimd, sp=nc.sync)
        self.sem = {}
        self.cnt = {}
        for k in ("pe", "act", "dve", "pool"):
            self.sem[k] = es.enter_context(nc.semaphore("s_" + k))
            self.cnt[k] = 0
        self.waited = {k: {} for k in self.eng}
        self.lastw = {}
        self.reads = {}
        self.nchan = 0

    def chan(self):
        self.nchan += 1
        k = "d%d" % self.nchan
        self.sem[k] = self.es.enter_context(self.nc.semaphore("s_" + k))
        self.cnt[k] = 0
        return k

    def _deps(self, e, reads, writes):
        deps = {}
        def add(ev):
            if ev is None:
                return
            k, v = ev
            if deps.get(k, 0) < v:
                deps[k] = v
        for r in reads:
            add(self.lastw.get(r))
            if r.startswith("pf") or r.startswith("pb"):
                for k, v in self.reads.get(r, {}).items():
                    if k != e:
                        add((k, v))
        for w in writes:
            add(self.lastw.get(w))
            for k, v in self.reads.get(w, {}).items():
                add((k, v))
        pend = []
        for k, v in deps.items():
            if k == "pe" and e == "pe":
                continue
            if self.waited[e].get(k, 0) >= v:
                continue
            pend.append((k, v))
            self.waited[e][k] = v
        for k, v in pend[:-1]:
            self.eng[e].wait_ge(self.sem[k], v)
        return pend[-1] if pend else None

    def _mark(self, ev, reads, writes):
        k, v = ev
        for r in reads:
            self.reads.setdefault(r, {})[k] = v
        for w in writes:
            self.lastw[w] = ev
            self.reads[w] = {}

    def op(self, e, reads, writes, fn):
        lw = self._deps(e, reads, writes)
        ins = fn(self.eng[e])
        if lw is not None:
            ins._wait_ge(self.sem[lw[0]], lw[1])
        self.cnt[e] += 1
        ins.then_inc(self.sem[e], 1)
        self._mark((e, self.cnt[e]), reads, writes)

    def dma(self, q, ch, reads, writes, out, in_, **kw):
        lw = self._deps(q, reads, writes)
        ins = self.eng[q].dma_start(out=out, in_=in_, **kw)
        if lw is not None:
            ins._wait_ge(self.sem[lw[0]], lw[1])
        ins.then_inc(self.sem[ch], 16)
        self.cnt[ch] += 16
        self._mark((ch, self.cnt[ch]), reads, writes)

    def barrier(self):
        for e in ("pe", "act", "dve", "pool", "sp"):
            for k in self.sem:
                if k != e and self.cnt[k] > 0 and self.waited[e].get(k, 0) < self.cnt[k]:
                    self.eng[e].wait_ge(self.sem[k], self.cnt[k])
                    self.waited[e][k] = self.cnt[k]

    def finish(self, q="sp"):
        for k in self.sem:
            if k.startswith("d") and self.cnt[k] > 0:
                self.eng[q].wait_ge(self.sem[k], self.cnt[k])
        for k in ("pe", "act", "dve", "pool"):
            if self.cnt[k] > 0:
                self.eng[q].wait_ge(self.sem[k], self.cnt[k])


def r3(ap, a, b):
    return ap.rearrange("p (a b) -> p a b", a=a, b=b)


class _Stop(Exception):
    pass


def build(cfg):
    c = cfg
    nc = bass.Bass("TRN2", target_bir_lowering=False)
    D, T, NS, TT, KD, HV, G = c.D, c.T, c.NS, c.TT, c.KD, c.HV, c.G

    def din(name, shape, dt=F32):
        return nc.dram_tensor(name, list(shape), dt, kind="ExternalInput").ap()

    def dout(name, shape, dt=F32):
        return nc.dram_tensor(name, list(shape), dt, kind="ExternalOutput").ap()

    def dscr(name, shape, dt):
        return nc.dram_tensor(name, list(shape), dt, kind="Internal").ap()

    xin = din("xin", [TT, D])
    conv_s = din("conv_s", [NS, 3, c.CONV])
    delta_s = din("delta_s", [NS, HV, 128, 128])
    re_s = din("re_s", [NS, G, 64])
    im_s = din("im_s", [NS, G, 64])
    norm_gdn = din("norm_gdn", [1, D])
    w_in_gdn = din("w_in_gdn", [D, c.IN])
    conv_w = din("conv_w", [4, c.CONV])
    a_log = din("a_log", [1, HV])
    dt_bias = din("dt_bias", [1, HV])
    onorm = din("onorm", [1, 128])
    w_out_gdn = din("w_out_gdn", [c.VAL, D])
    norm_ssm = din("norm_ssm", [1, D])
    w_in_ssm = din("w_in_ssm", [D, 2 * c.W])
    lam_re = din("lam_re", [G, 64])
    lam_im = din("lam_im", [G, 64])
    b_re = din("b_re", [G, 64, 16])
    b_im = din("b_im", [G, 64, 16])
    c_re = din("c_re", [G, 16, 64])
    c_im = din("c_im", [G, 16, 64])
    d_ssm = din("d_ssm", [c.W, 1])
    log_dt = din("log_dt", [1, G])
    w_glu = din("w_glu", [c.W, c.W])
    b_glu = din("b_glu", [c.W, 1])
    w_out_ssm = din("w_out_ssm", [c.W, D])
    norm_final = din("norm_final", [1, D])
    consts = din("consts", [128, 8 * 128])

    y_out = dout("y_out", [TT, D])
    conv_p = dout("conv_p", [3, c.CONV])
    delta_p = dout("delta_p", [HV, 128, 128])
    re_p = dout("re_p", [G, 64])
    im_p = dout("im_p", [G, 64])
    conv_so = dout("conv_so", [NS, 3, c.CONV])
    delta_so = dout("delta_so", [NS, HV, 128, 128])
    re_so = dout("re_so", [NS, G, 64])
    im_so = dout("im_so", [NS, G, 64])

    oT_scr = dscr("oT_scr", [c.VAL, TT], BF16)
    x1_scr = dscr("x1_scr", [TT, D], F32)
    yT_scr = dscr("yT_scr", [c.W, TT], BF16)
    y2_scr = dscr("y2_scr", [c.W, TT], BF16)
    x2_scr = dscr("x2_scr", [TT, D], F32)

    es = contextlib.ExitStack()
    with es:
        tr = TR(nc, es)
        cur = [es]
        try:

            def chk(k):
                if getattr(c, "stop", None) == k:
                    raise _Stop()

            def sb(name, shape, dt=F32):
                return cur[0].enter_context(nc.sbuf_tensor(name, list(shape), dt))

            def phase_begin():
                tr.barrier()
                cur[0] = contextlib.ExitStack()

            def phase_end():
                tr.barrier()
                cur[0].close()
                cur[0] = es

            def ps(name, shape, dt=F32):
                return es.enter_context(nc.psum_tensor(name, list(shape), dt))

            cst = sb("cst", [128, 8 * 128])
            ch_c = tr.chan()
            tr.dma("sp", ch_c, [], ["cst"], cst[:], consts[:, :])
            ident = cst[:, 0:128]
            triU = cst[:, 128:256]
            ones = cst[:, 256:384]
            MBIG = cst[:, 384:512]
            MNEG = cst[:, 512:640]
            CMASK = cst[:, 640:768]
            BD = cst[:, 768:896]
            GSEL = cst[:, 896:904]
            GG = cst[:, 904:968]
            cstb = sb("cstb", [128, 256], BF16)
            identb = cstb[:, 0:128]
            onesb = cstb[:, 128:256]
            tr.op("dve", ["cst"], ["cstb"], lambda e: e.tensor_copy(out=cstb[:, 0:128], in_=ident))
            tr.op("dve", ["cst"], ["cstb"], lambda e: e.tensor_copy(out=cstb[:, 128:256], in_=ones))

            def bcast_load(name, src, n):
                t = sb(name, [128, n])
                ch = tr.chan()
                tr.dma("sp", ch, [], [name], t[:], src[0:1, :].broadcast_to([128, n]))
                return t

            chg = tr.chan()
            nrm = {}
            alog_bc = bcast_load("alog_bc", a_log, HV)
            dtb_bc = bcast_load("dtb_bc", dt_bias, HV)
            ogain_bc = bcast_load("ogain_bc", onorm, 128)
            negA = sb("negA", [128, HV])
            tr.op("act", ["alog_bc"], ["negA"], lambda e: e.activation(out=negA[:], in_=alog_bc[:], func=AF.Exp))
            tr.op("dve", ["negA"], ["negA"], lambda e: e.tensor_scalar(out=negA[:], in0=negA[:], scalar1=-1.0, scalar2=None, op0=ALU.mult))

            hT = sb("hT", [128, KD * TT], BF16)
            hT3 = r3(hT[:], KD, TT)

            PF = [ps("pf%d" % i, [128, 512]) for i in range(6)]
            PB = [ps("pb%d" % i, [128, 1024], BF16) for i in range(2)]
            pf_i = [0]
            pb_i = [0]

            def pf():
                pf_i[0] = (pf_i[0] + 1) % len(PF)
                return PF[pf_i[0]], "pf%d" % pf_i[0]

            def pb():
                pb_i[0] = (pb_i[0] + 1) % len(PB)
                return PB[pb_i[0]], "pb%d" % pb_i[0]

            xtc = [tr.chan() for i in range(2)]
            xts = [tr.chan() for i in range(2)]
            stat = sb("stat", [128, 8])

            def tok_tiles():
                tl = [(i * 128, 128) for i in range(T // 128)]
                tl.append((T, NS))
                return tl

            def norm_to_hT(src, gsrc, srcname, addsrc=None, final_out=None):
                phase_begin()
                nrm["i"] = nrm.get("i", 0) + 1
                gbuf = sb("gbuf%d" % nrm["i"], [128, D])
                xt = [sb("xt%d_%d" % (nrm["i"], i_), [128, D]) for i_ in range(2)]
                hb = [sb("hb%d_%d" % (nrm["i"], i_), [128, D], BF16) for i_ in range(2)]
                _norm_body(src, gsrc, srcname, final_out, gbuf, xt, hb)
                phase_end()

            def _norm_body(src, gsrc, srcname, final_out, gbuf, xt, hb):
                gain = gbuf
                gname = "gbuf"
                tr.dma("sp", chg, [], ["gbuf"], gbuf[:], gsrc[0:1, :].broadcast_to([128, D]))
                for it, (t0, n) in enumerate(tok_tiles()):
                    b = it % 2
                    tr.dma("sp", xtc[b], [srcname], ["xt%d" % b], xt[b][0:n, :], src[t0:t0 + n, :])
                    tr.op("act", ["xt%d" % b], ["hb%d" % b, "stat"], lambda e: e.activation(out=hb[b][0:n, :], in_=xt[b][0:n, :], func=AF.Square, accum_out=stat[0:n, 0:1]))
                    tr.op("dve", ["stat"], ["stat"], lambda e: e.tensor_scalar(out=stat[0:n, 1:2], in0=stat[0:n, 0:1], scalar1=1.0 / D, scalar2=1e-6, op0=ALU.mult, op1=ALU.add))
                    tr.op("act", ["stat"], ["stat"], lambda e: e.activation(out=stat[0:n, 3:4], in_=stat[0:n, 1:2], func=AF.Sqrt))
                    tr.op("dve", ["stat"], ["stat"], lambda e: e.reciprocal(out=stat[0:n, 2:3], in_=stat[0:n, 3:4]))
                    if final_out is not None:
                        tr.op("dve", ["xt%d" % b, "stat", gname], ["xt%d" % b], lambda e: e.scalar_tensor_tensor(out=xt[b][0:n, :], in0=xt[b][0:n, :], scalar=stat[0:n, 2:3], in1=gain[0:n, :], op0=ALU.mult, op1=ALU.mult))
                        tr.dma("pool", xts[b], ["xt%d" % b], ["y_out"], final_out[t0:t0 + n, :], xt[b][0:n, :])
                        continue
                    tr.op("dve", ["xt%d" % b, "stat", gname], ["hb%d" % b], lambda e: e.scalar_tensor_tensor(out=hb[b][0:n, :], in0=xt[b][0:n, :], scalar=stat[0:n, 2:3], in1=gain[0:n, :], op0=ALU.mult, op1=ALU.mult))
                    for k0 in range(0, KD, 8):
                        kk = min(8, KD - k0)
                        p, pn = pb()
                        for k in range(kk):
                            tr.op("pe", ["hb%d" % b, "cstb"], [pn], lambda e: e.transpose(out=p[:, k * 128:k * 128 + n], in_=hb[b][0:n, (k0 + k) * 128:(k0 + k + 1) * 128], identity=identb[0:n, 0:n]))
                        tr.op("act" if (k0 // 8) % 2 == 0 else "dve", [pn], ["hT"],
                              (lambda e: e.activation(out=hT3[:, k0:k0 + kk, t0:t0 + n], in_=r3(p[:, 0:kk * 128], kk, 128)[:, :, 0:n], func=AF.Copy)) if (k0 // 8) % 2 == 0 else
                              (lambda e: e.tensor_copy(out=hT3[:, k0:k0 + kk, t0:t0 + n], in_=r3(p[:, 0:kk * 128], kk, 128)[:, :, 0:n])))

            chk(0)
            norm_to_hT(xin, norm_gdn, "xin")
            chk(1)

            phase_begin()
            NCOL = 772
            wj = [sb("wj%d" % i, [128, KD * NCOL], BF16) for i in range(1)]
            wjc = [tr.chan() for i in range(1)]
            NBLK = c.CONV // 128
            NR = 4 * NBLK
            cwT = sb("cwT", [128, NR])
            cwr = sb("cwr", [128, 128])
            chx = tr.chan()
            cw_rows = conv_w.rearrange("j (b c) -> (j b) c", c=128)
            for r0 in range(0, NR, 128):
                nr = min(128, NR - r0)
                tr.dma("sp", chx, [], ["cwr"], cwr[0:nr, :], cw_rows[r0:r0 + nr, :])
                p, pn = pf()
                tr.op("pe", ["cwr", "cst"], [pn], lambda e: e.transpose(out=p[:, 0:nr], in_=cwr[0:nr, :], identity=ident[0:nr, 0:nr]))
                tr.op("dve", [pn], ["cwT"], lambda e: e.tensor_copy(out=cwT[:, r0:r0 + nr], in_=p[:, 0:nr]))
            pre = [sb("pre0", [128, 3 + TT])] * 4
            if 3 + TT >= 1032:
                tail = pre[0][0:NS + 3, 8:520]
                cst48 = pre[0][0:NS * 3, 520:1032]
            else:
                tail = sb("tailx", [NS + 3, 512])[:, :]
                cst48 = sb("cst48x", [NS * 3, 512])[:, :]
            xp4 = [sb("xp4_0", [128, NS * 4])] * 4
            xs3 = sb("xs3", [128, 4 * NS * 3])
            ch48 = tr.chan()
            cv = [sb("cv0", [128, TT])] * 4
            tmpc = sb("tmpc", [128, TT])
            qT = sb("qT", [128, TT], BF16)
            kT = sb("kT", [128, TT], BF16)
            chtail = tr.chan()
            zba = sb("zba", [128, (c.NCH) * 260], BF16)
            zbas = sb("zbas", [1, NS * 260], BF16)
            gates = {}
            for nm, Cc, nch in (("p", 128, c.NCH), ("s", 1, NS)):
                for f in ("beta", "g", "gc", "egc", "bg", "ekd", "gl", "egl128", "tmp"):
                    gates[(nm, f)] = sb("gt_%s_%s" % (nm, f), [128, nch * 2])
            LANES = []
            for li in range(2):
                ln = {}
                ln["Sst"] = sb("Sst%d" % li, [128, 128]); ln["Sbf"] = sb("Sbf%d" % li, [128, 128], BF16)
                ln["chS"] = tr.chan(); ln["chSo"] = tr.chan(); ln["choT"] = tr.chan()
                ln["oTst"] = sb("oTst%d" % li, [128, TT], BF16)
                Wl = {}
                for nm, dt in (("kbg", F32), ("E1", F32), ("E2", F32), ("L", F32), ("N", F32),
                               ("P", F32), ("L2a", F32), ("L2b", F32), ("N2a", F32), ("N2b", F32), ("vn", BF16),
                               ("av", F32), ("gz", F32), ("og", BF16), ("sq", BF16)):
                    Wl[nm] = sb("w%d_%s" % (li, nm), [128, 128], dt)
                Wl["dg"] = sb("w%d_dg" % li, [128, 128])
                Wo = dict(Wl)
                for nm in ("kbg", "E1", "E2", "L", "N", "P", "L2a", "L2b", "N2a", "N2b", "dg"):
                    Wo[nm] = sb("w%do_%s" % (li, nm), [128, 128], F32)
                ln["WA"] = [Wl, Wo]
                ln["Adone"] = set()
                ln["H"] = []
                for par in range(2):
                    Hd = {}
                    for nm, dt in (("vb", F32), ("kd", BF16), ("AT", BF16), ("u", F32), ("wT", BF16)):
                        Hd[nm] = sb("h%d_%d_%s" % (li, par, nm), [128, 128], dt)
                    ln["H"].append(Hd)
                ln["A_done"] = 0
                ln["B_done"] = 0
                ln["C_done"] = 0
                ln["O"] = [sb("ho%d_%d" % (li, par), [128, 128]) for par in range(2)]
                ln["SS"] = [(sb("Sss%d_%d" % (li, par), [128, 128]), sb("Ssb%d_%d" % (li, par), [128, 128], BF16), tr.chan(), tr.chan()) for par in range(2)]
                ln["W"] = Wl
                ln["colst"] = sb("colst%d" % li, [128, 8])
                ln["id"] = li
                ln["pfb"] = [3 * li, 3 * li + 1, 3 * li + 2]
                ln["pfi"] = [0]
                ln["pbb"] = li
                LANES.append(ln)

            def load_wj(j, b):
                base = wj[b]
                w3 = r3(base[:], KD, NCOL)
                segs = [(0, j * 128, 128), (128, c.KEY + j * 128, 128), (256, 2 * c.KEY + j * 256, 256),
                        (512, c.CONV + j * 256, 256), (768, c.CONV + c.VAL + 2 * j, 2), (770, c.CONV + c.VAL + HV + 2 * j, 2)]
                for (o, s0, n) in segs:
                    tr.dma("pool", wjc[b], [], ["wj%d" % b], w3[:, :, o:o + n], w_in_gdn[:, s0:s0 + n].rearrange("(k p) n -> p k n", p=128))

            def chunk(ln, part, seq, nm, Cc, ci, cols, j, hh, first, last, n_idx):
                c0 = cols
                W = ln["W"]; Sst = ln["Sst"]; Sbf = ln["Sbf"]; chS = ln["chS"]; chSo = ln["chSo"]; oTst = ln["oTst"]; colst = ln["colst"]
                WN = "w%d_" % ln["id"]; SN = "Sst%d" % ln["id"]; BN = "Sbf%d" % ln["id"]; ON = "oTst%d" % ln["id"]; CN = "colst%d" % ln["id"]

                H = ln["H"][seq % 2]
                HN = "h%d_%d_" % (ln["id"], seq % 2)
                KO = 256 * (seq % 2)
                if part == "A":
                    W = ln["WA"][seq % 2]
                    WN = "w%d%s_" % (ln["id"], "o" if seq % 2 else "")

                def pf():
                    if part == "B":
                        i_ = ln["pfb"][2]
                    else:
                        i_ = ln["pfb"][seq % 2]
                    return PF[i_], "pf%d" % i_

                def pb():
                    return PB[ln["pbb"]], "pb%d" % ln["pbb"]
                h = 2 * j + hh
                gi = ci * 2 + hh
                G_ = lambda f: gates[(nm, f)]
                gname = lambda f: "gt_%s_%s" % (nm, f)
                if part == "A":
                    while ln["B_done"] < seq - 1:
                        yield "blocked"
                    p, pn = pb()
                    tr.op("pe", ["kT", "cstb"], [pn], lambda e: e.transpose(out=p[0:Cc, KO:KO + 128], in_=kT[:, c0:c0 + Cc], identity=identb))
                    yield
                    tr.op("pe", ["cvb%d" % hh, "cstb"], [pn], lambda e: e.transpose(out=p[0:Cc, KO + 128:KO + 256], in_=cvb[hh][:, c0:c0 + Cc], identity=identb))
                    yield
                    tr.op("dve", [pn, gname("beta")], [HN + "vb"], lambda e: e.tensor_scalar(out=H["vb"][0:Cc, :], in0=p[0:Cc, KO + 128:KO + 256], scalar1=G_("beta")[0:Cc, gi:gi + 1], scalar2=None, op0=ALU.mult))
                    yield
                    tr.op("dve", [pn, gname("bg")], [WN + "kbg"], lambda e: e.tensor_scalar(out=W["kbg"][0:Cc, :], in0=p[0:Cc, KO:KO + 128], scalar1=G_("bg")[0:Cc, gi:gi + 1], scalar2=None, op0=ALU.mult))
                    yield
                    tr.op("act", [pn, gname("ekd")], [HN + "kd"], lambda e: e.activation(out=H["kd"][0:Cc, :], in_=p[0:Cc, KO:KO + 128], func=AF.Copy, scale=G_("ekd")[0:Cc, gi:gi + 1]))
                    yield
                    chk(10)
                    if Cc > 1:
                        pk, pkn = pf()
                        tr.op("pe", ["kT"], [pkn], lambda e: e.matmul(pk[0:Cc, 0:Cc], lhsT=kT[:, c0:c0 + Cc], rhs=kT[:, c0:c0 + Cc], start=True, stop=True))
                        yield
                        tr.op("pe", ["kT", "qT"], [pkn], lambda e: e.matmul(pk[0:Cc, 128:128 + Cc], lhsT=kT[:, c0:c0 + Cc], rhs=qT[:, c0:c0 + Cc], start=True, stop=True))
                        yield
                        tr.op("dve", ["cst", gname("gc")], [WN + "dg"], lambda e: e.tensor_scalar(out=W["dg"][0:Cc, 0:Cc], in0=ident[0:Cc, 0:Cc], scalar1=G_("gc")[0:Cc, gi:gi + 1], scalar2=None, op0=ALU.mult))
                        yield
                        tr.op("pe", ["cst", WN + "dg"], [pkn], lambda e: e.matmul(pk[0:Cc, 256:256 + Cc], lhsT=ones[0:Cc, 0:Cc], rhs=W["dg"][0:Cc, 0:Cc], start=True, stop=True))
                        yield
                        R = pk[0:Cc, 256:256 + Cc]
                        chk(11)
                        tr.op("dve", [pkn, gname("gc"), "cst"], [WN + "E1"], lambda e: e.scalar_tensor_tensor(out=W["E1"][0:Cc, 0:Cc], in0=R, scalar=G_("gc")[0:Cc, gi:gi + 1], in1=MBIG[0:Cc, 0:Cc], op0=ALU.subtract, op1=ALU.max))
                        yield
                        tr.op("dve", [pkn, gname("gc"), "cst"], [WN + "E2"], lambda e: e.scalar_tensor_tensor(out=W["E2"][0:Cc, 0:Cc], in0=R, scalar=G_("gc")[0:Cc, gi:gi + 1], in1=MNEG[0:Cc, 0:Cc], op0=ALU.subtract, op1=ALU.min))
                        yield
                        tr.op("act", [WN + "E1"], [WN + "E1"], lambda e: e.activation(out=W["E1"][0:Cc, 0:Cc], in_=W["E1"][0:Cc, 0:Cc], func=AF.Exp, scale=-1.0))
                        yield
                        tr.op("act", [WN + "E2"], [WN + "E2"], lambda e: e.activation(out=W["E2"][0:Cc, 0:Cc], in_=W["E2"][0:Cc, 0:Cc], func=AF.Exp))
                        yield
                        tr.op("dve", [pkn, gname("beta"), WN + "E1"], [WN + "L"], lambda e: e.scalar_tensor_tensor(out=W["L"][0:Cc, 0:Cc], in0=pk[0:Cc, 0:Cc], scalar=G_("beta")[0:Cc, gi:gi + 1], in1=W["E1"][0:Cc, 0:Cc], op0=ALU.mult, op1=ALU.mult))
                        yield
                        tr.op("dve", [pkn, WN + "E2"], [HN + "AT"], lambda e: e.tensor_tensor(out=H["AT"][0:Cc, 0:Cc], in0=pk[0:Cc, 128:128 + Cc], in1=W["E2"][0:Cc, 0:Cc], op=ALU.mult))
                        yield
                    else:
                        pk, pkn = pf()
                        tr.op("pe", ["kT", "qT"], [pkn], lambda e: e.matmul(pk[0:1, 128:129], lhsT=kT[:, c0:c0 + 1], rhs=qT[:, c0:c0 + 1], start=True, stop=True))
                        yield
                        tr.op("dve", [pkn], [HN + "AT"], lambda e: e.tensor_copy(out=H["AT"][0:1, 0:1], in_=pk[0:1, 128:129]))
                        yield
                    chk(12)
                    if Cc > 1:
                        p2, p2n = pf()
                        tr.op("pe", [WN + "L", "cst"], [p2n], lambda e: e.transpose(out=p2[0:Cc, 0:Cc], in_=W["L"][0:Cc, 0:Cc], identity=ident[0:Cc, 0:Cc]))
                        yield
                        tr.op("act", [p2n], [WN + "N"], lambda e: e.activation(out=W["N"][0:Cc, 0:Cc], in_=p2[0:Cc, 0:Cc], func=AF.Copy))
                        yield
                        tr.op("dve", ["cstb", WN + "N"], [WN + "P"], lambda e: e.tensor_tensor(out=W["P"][0:Cc, 0:Cc], in0=ident[0:Cc, 0:Cc], in1=W["N"][0:Cc, 0:Cc], op=ALU.subtract))
                        yield
                        chk(120)
                        Lk, Nk = "L", "N"
                        nsteps = int(math.log2(Cc)) - 1
                        for st in range(nsteps):
                            L2, N2 = ("L2a", "N2a") if st % 2 == 0 else ("L2b", "N2b")
                            pq, pqn = pf()
                            tr.op("pe", [WN + Nk, WN + Lk], [pqn], lambda e: e.matmul(pq[0:Cc, 0:Cc], lhsT=W[Nk][0:Cc, 0:Cc], rhs=W[Lk][0:Cc, 0:Cc], start=True, stop=True))
                            yield
                            if st < nsteps - 1:
                                tr.op("pe", [WN + Nk, WN + Lk], [pqn], lambda e: e.matmul(pq[0:Cc, 128:128 + Cc], lhsT=W[Lk][0:Cc, 0:Cc], rhs=W[Nk][0:Cc, 0:Cc], start=True, stop=True))
                                yield
                            tr.op("act", [pqn], [WN + L2], lambda e: e.activation(out=W[L2][0:Cc, 0:Cc], in_=pq[0:Cc, 0:Cc], func=AF.Copy))
                            yield
                            if st < nsteps - 1:
                                tr.op("dve", [pqn], [WN + N2], lambda e: e.tensor_copy(out=W[N2][0:Cc, 0:Cc], in_=pq[0:Cc, 128:128 + Cc]))
                                yield
                            tr.op("pe", [WN + L2, WN + "P"], [pqn], lambda e: e.matmul(pq[0:Cc, 256:256 + Cc], lhsT=W[L2][0:Cc, 0:Cc], rhs=W["P"][0:Cc, 0:Cc], start=True, stop=True))
                            yield
                            tr.op("dve", [pqn, WN + "P"], [WN + "P"], lambda e: e.tensor_tensor(out=W["P"][0:Cc, 0:Cc], in0=pq[0:Cc, 256:256 + Cc], in1=W["P"][0:Cc, 0:Cc], op=ALU.add))
                            yield
                            Lk, Nk = L2, N2
                            chk(121 + st)
                        TTm, TTn = W["P"][0:Cc, 0:Cc], WN + "P"
                    else:
                        TTm, TTn = ident[0:1, 0:1], "cst"
                    chk(13)
                    pu, pun = pf()
                    if Cc > 1:
                        tr.op("pe", [TTn, HN + "vb"], [pun], lambda e: e.matmul(pu[0:Cc, 0:128], lhsT=TTm, rhs=H["vb"][0:Cc, :], start=True, stop=True))
                        yield
                        tr.op("act", [pun], [HN + "u"], lambda e: e.activation(out=H["u"][0:Cc, :], in_=pu[0:Cc, 0:128], func=AF.Copy))
                        yield
                        Uap, Un = H["u"], HN + "u"
                    else:
                        Uap, Un = H["vb"], HN + "vb"
                    tr.op("pe", [TTn, WN + "kbg"], [pun], lambda e: e.matmul(pu[:, 128:128 + Cc], lhsT=W["kbg"][0:Cc, :], rhs=TTm, start=True, stop=True))
                    yield
                    tr.op("act", [pun], [HN + "wT"], lambda e: e.activation(out=H["wT"][:, 0:Cc], in_=pu[:, 128:128 + Cc], func=AF.Copy))
                    yield
                    chk(14)
                    ln["Adone"].add(seq)
                    return
                Otile = ln["O"][seq % 2]
                OnN = "ho%d_%d" % (ln["id"], seq % 2)
                if nm == "s":
                    Sst, Sbf, chS, chSo = ln["SS"][seq % 2]
                    SN = "Sss%d_%d" % (ln["id"], seq % 2)
                    BN = "Ssb%d_%d" % (ln["id"], seq % 2)
                if part == "C":
                    while ln["B_done"] <= seq:
                        yield "blocked"
                else:
                    while (seq not in ln["Adone"]) or ln["C_done"] < seq - 1:
                        yield "blocked"
                    if Cc > 1:
                        Uap, Un = H["u"], HN + "u"
                    else:
                        Uap, Un = H["vb"], HN + "vb"
                    if nm == "s":
                        tr.dma("sp", chS, ["delta_s"], [SN], Sst[:], delta_s[n_idx, h, :, :])
                        yield
                        tr.op("act", [SN], [BN], lambda e: e.activation(out=Sbf[:], in_=Sst[:], func=AF.Copy))
                        yield
                    elif first:
                        tr.op("pool", [], [SN], lambda e: e.memset(Sst[:], 0.0))
                        yield
                        tr.op("pool", [], [BN], lambda e: e.memset(Sbf[:], 0.0))
                        yield
                    pw, pwn = pf()
                    tr.op("pe", [HN + "wT", BN], [pwn], lambda e: e.matmul(pw[0:Cc, 0:128], lhsT=H["wT"][:, 0:Cc], rhs=Sbf[:], start=True, stop=True))
                    yield
                    tr.op("pe", ["qT", BN], [pwn], lambda e: e.matmul(pw[0:Cc, 128:256], lhsT=qT[:, c0:c0 + Cc], rhs=Sbf[:], start=True, stop=True))
                    yield
                    tr.op("dve", [pwn, Un], [WN + "vn"], lambda e: e.tensor_tensor(out=W["vn"][0:Cc, :], in0=Uap[0:Cc, :], in1=pw[0:Cc, 0:128], op=ALU.subtract))
                    yield
                    tr.op("pe", [HN + "kd", WN + "vn"], [pwn], lambda e: e.matmul(pw[:, 384:512], lhsT=H["kd"][0:Cc, :], rhs=W["vn"][0:Cc, :], start=True, stop=True))
                    yield
                    tr.op("pe", [HN + "AT", WN + "vn"], [pwn], lambda e: e.matmul(pw[0:Cc, 256:384], lhsT=H["AT"][0:Cc, 0:Cc], rhs=W["vn"][0:Cc, :], start=True, stop=True))
                    yield
                    tr.op("dve", [pwn, SN, gname("egl128")], [SN], lambda e: e.scalar_tensor_tensor(out=Sst[:], in0=Sst[:], scalar=G_("egl128")[:, gi:gi + 1], in1=pw[:, 384:512], op0=ALU.mult, op1=ALU.add))
                    yield
                    if nm == "s":
                        tr.dma("pool", chSo, [SN], ["delta_so"], delta_so[n_idx, h, :, :], Sst[:])
                        yield
                    elif last:
                        tr.dma("pool", chSo, [SN], ["delta_p"], delta_p[h, :, :], Sst[:])
                        yield
                    else:
                        tr.op("act", [SN], [BN], lambda e: e.activation(out=Sbf[:], in_=Sst[:], func=AF.Copy))
                        yield
                    tr.op("act", [pwn], [WN + "av"], lambda e: e.activation(out=W["av"][0:Cc, :], in_=pw[0:Cc, 256:384], func=AF.Copy))
                    yield
                    tr.op("dve", [pwn, WN + "av", gname("egc")], [OnN], lambda e: e.scalar_tensor_tensor(out=Otile[0:Cc, :], in0=pw[0:Cc, 128:256], scalar=G_("egc")[0:Cc, gi:gi + 1], in1=W["av"][0:Cc, :], op0=ALU.mult, op1=ALU.add))
                    yield
                    ln["B_done"] = seq + 1
                    return
                zsrc = (zba if nm == "p" else zbas)
                zname = "zba" if nm == "p" else "zbas"
                zap = zsrc[0:Cc, ci * 260 + hh * 128: ci * 260 + hh * 128 + 128]
                tr.op("act", [OnN], [WN + "sq", CN], lambda e: e.activation(out=W["sq"][0:Cc, :], in_=Otile[0:Cc, :], func=AF.Square, accum_out=colst[0:Cc, 0:1]))
                yield
                tr.op("dve", [CN], [CN], lambda e: e.tensor_scalar(out=colst[0:Cc, 1:2], in0=colst[0:Cc, 0:1], scalar1=1.0 / 128, scalar2=1e-6, op0=ALU.mult, op1=ALU.add))
                yield
                tr.op("act", [CN], [CN], lambda e: e.activation(out=colst[0:Cc, 3:4], in_=colst[0:Cc, 1:2], func=AF.Sqrt))
                yield
                tr.op("dve", [CN], [CN], lambda e: e.reciprocal(out=colst[0:Cc, 2:3], in_=colst[0:Cc, 3:4]))
                yield
                tr.op("act", [zname], [WN + "gz"], lambda e: e.activation(out=W["gz"][0:Cc, :], in_=zap, func=AF.Silu))
                yield
                tr.op("pool", [WN + "gz", "ogain_bc"], [WN + "gz"], lambda e: e.tensor_tensor(out=W["gz"][0:Cc, :], in0=W["gz"][0:Cc, :], in1=ogain_bc[0:Cc, :], op=ALU.mult))
                yield
                tr.op("dve", [OnN, CN, WN + "gz"], [WN + "og"], lambda e: e.scalar_tensor_tensor(out=W["og"][0:Cc, :], in0=Otile[0:Cc, :], scalar=colst[0:Cc, 2:3], in1=W["gz"][0:Cc, :], op0=ALU.mult, op1=ALU.mult))
                yield
                p3, p3n = pb()
                tr.op("pe", [WN + "og", "cstb"], [p3n], lambda e: e.transpose(out=p3[:, 512:512 + Cc], in_=W["og"][0:Cc, :], identity=identb[0:Cc, 0:Cc]))
                yield
                tr.op("act", [p3n], [ON], lambda e: e.activation(out=oTst[:, c0:c0 + Cc], in_=p3[:, 512:512 + Cc], func=AF.Copy))
                yield
                ln["C_done"] = seq + 1

            cvb = [sb("cvb%d" % i, [128, TT], BF16) for i in range(2)]

            for j in range(c.HQK):
                chk(100 + j)
                b = 0
                load_wj(j, b)
                w3 = r3(wj[b][:], KD, NCOL)
                wn = "wj%d" % b
                chk(2)
                for (lo, m, dname) in ((T - 3, 3, "conv_p"), (T, NS, "conv_so")):
                    p, pn = pf()
                    for k in range(KD):
                        tr.op("pe", [wn, "hT"], [pn], lambda e: e.matmul(p[0:m, 0:512], lhsT=hT3[:, k, lo:lo + m], rhs=w3[:, k, 0:512], start=(k == 0), stop=(k == KD - 1)))
                    tr.op("act", [pn], ["pre0"], lambda e: e.activation(out=tail[0:m, :], in_=p[0:m, 0:512], func=AF.Copy))
                    for (o, s0, n) in ((0, j * 128, 128), (128, c.KEY + j * 128, 128), (256, 2 * c.KEY + j * 256, 256)):
                        if dname == "conv_p":
                            tr.dma("sp", chtail, ["pre0"], ["conv_p"], conv_p[0:3, s0:s0 + n], tail[0:3, o:o + n])
                        else:
                            tr.dma("sp", chtail, ["pre0"], ["conv_so"], conv_so[:, 2, s0:s0 + n], tail[0:NS, o:o + n])
                            tr.dma("sp", chtail, [], ["conv_so"], conv_so[:, 0:2, s0:s0 + n], conv_s[:, 1:3, s0:s0 + n])
                chk(3)
                for (o, s0, n) in ((0, j * 128, 128), (128, c.KEY + j * 128, 128), (256, 2 * c.KEY + j * 256, 256)):
                    tr.dma("sp", ch48, [], ["pre0"], cst48[:, o:o + n], conv_s[:, :, s0:s0 + n].rearrange("n j c -> (n j) c"))
                x4 = r3(xp4[0][:], NS, 4)
                for fb in range(4):
                    p, pn = pf()
                    tr.op("pe", ["pre0", "cst"], [pn], lambda e: e.transpose(out=p[:, 0:NS * 3], in_=cst48[:, fb * 128:(fb + 1) * 128], identity=ident[0:NS * 3, 0:NS * 3]))
                    tr.op("dve", [pn], ["xs3"], lambda e: e.tensor_copy(out=xs3[:, fb * NS * 3:(fb + 1) * NS * 3], in_=p[:, 0:NS * 3]))
                tr.op("pool", [], ["pre0"], lambda e: e.memset(pre[0][:, 0:3], 0.0))
                for fb in range(4):
                    for t0 in range(0, TT, 512):
                        n = min(512, TT - t0)
                        p, pn = pf()
                        for k in range(KD):
                            tr.op("pe", [wn, "hT"], [pn], lambda e: e.matmul(p[:, 0:n], lhsT=w3[:, k, fb * 128:(fb + 1) * 128], rhs=hT3[:, k, t0:t0 + n], start=(k == 0), stop=(k == KD - 1)))
                        np_ = max(0, min(n, T - t0))
                        if np_ > 0:
                            tr.op("act", [pn], ["pre0"], lambda e: e.activation(out=pre[0][:, 3 + t0:3 + t0 + np_], in_=p[:, 0:np_], func=AF.Copy))
                        if np_ < n:
                            s0 = t0 + np_ - T
                            ns_ = n - np_
                            tr.op("dve", [pn], ["xp4_0"], lambda e: e.tensor_copy(out=x4[:, s0:s0 + ns_, 3], in_=p[:, np_:n]))
                    tr.op("dve", ["xs3"], ["xp4_0"], lambda e: e.tensor_copy(out=x4[:, :, 0:3], in_=r3(xs3[:, fb * NS * 3:(fb + 1) * NS * 3], NS, 3)))
                    blk = [j, c.KEY // 128 + j, 2 * c.KEY // 128 + 2 * j, 2 * c.KEY // 128 + 2 * j + 1][fb]
                    cwb = lambda tp: cwT[:, tp * NBLK + blk:tp * NBLK + blk + 1]
                    acc, an, eng = tmpc, "tmpc", "dve"
                    tr.op(eng, ["pre0", "cwT"], [an], lambda e: e.tensor_scalar(out=acc[:, 0:T], in0=pre[0][:, 0:T], scalar1=cwb(0), scalar2=None, op0=ALU.mult))
                    for tp in (1, 2, 3):
                        tr.op(eng, ["pre0", "cwT", an], [an], lambda e: e.scalar_tensor_tensor(out=acc[:, 0:T], in0=pre[0][:, tp:tp + T], scalar=cwb(tp), in1=acc[:, 0:T], op0=ALU.mult, op1=ALU.add))
                    tr.op(eng, ["xp4_0", "cwT"], [an], lambda e: e.tensor_scalar(out=acc[:, T:TT], in0=x4[:, :, 0], scalar1=cwb(0), scalar2=None, op0=ALU.mult))
                    for tp in (1, 2, 3):
                        tr.op(eng, ["xp4_0", "cwT", an], [an], lambda e: e.scalar_tensor_tensor(out=acc[:, T:TT], in0=x4[:, :, tp], scalar=cwb(tp), in1=acc[:, T:TT], op0=ALU.mult, op1=ALU.add))
                    tr.op("act", [an], ["cv0"], lambda e: e.activation(out=cv[0][:], in_=acc[:], func=AF.Silu))
                    if fb < 2:
                        dstT, dn, scl = ((qT, "qT", 128 ** -0.5), (kT, "kT", 1.0))[fb]
                        sqb = pre[0][:, 3:3 + TT]
                        rinv = tmpc
                        tr.op("act", ["cv0"], ["pre0"], lambda e: e.activation(out=sqb, in_=cv[0][:], func=AF.Square))
                        for t0 in range(0, TT, 512):
                            n = min(512, TT - t0)
                            p, pn = pf()
                            tr.op("pe", ["cst", "pre0"], [pn], lambda e: e.matmul(p[:, 0:n], lhsT=ones, rhs=sqb[:, t0:t0 + n], start=True, stop=True))
                            tr.op("dve", [pn], ["tmpc"], lambda e: e.tensor_scalar(out=rinv[:, t0:t0 + n], in0=p[:, 0:n], scalar1=1e-6, scalar2=None, op0=ALU.add))
                            tr.op("act", ["tmpc"], ["tmpc"], lambda e: e.activation(out=rinv[:, t0:t0 + n], in_=rinv[:, t0:t0 + n], func=AF.Sqrt))
                            tr.op("dve", ["tmpc"], ["tmpc"], lambda e: e.reciprocal(out=rinv[:, t0:t0 + n], in_=rinv[:, t0:t0 + n]))
                        tr.op("dve", ["cv0", "tmpc"], [dn], lambda e: e.scalar_tensor_tensor(out=dstT[:], in0=cv[0][:], scalar=scl, in1=rinv[:], op0=ALU.mult, op1=ALU.mult))
                    else:
                        hh = fb - 2
                        tr.op("pool", ["cv0"], ["cvb%d" % hh], lambda e: e.tensor_copy(out=cvb[hh][:], in_=cv[0][:]))
                chk(4)
                for nm, Cc, nch, zt, zn in (("p", 128, c.NCH, zba, "zba"), ("s", 1, NS, zbas, "zbas")):
                    for ci in range(nch):
                        c0 = ci * 128 if nm == "p" else T + ci
                        p, pn = pf()
                        for k in range(KD):
                            tr.op("pe", [wn, "hT"], [pn], lambda e: e.matmul(p[0:Cc, 0:260], lhsT=hT3[:, k, c0:c0 + Cc], rhs=w3[:, k, 512:772], start=(k == 0), stop=(k == KD - 1)))
                        tr.op("act", [pn], [zn], lambda e: e.activation(out=zt[0:Cc, ci * 260:(ci + 1) * 260], in_=p[0:Cc, 0:260], func=AF.Copy))
                    z3 = r3(zt[0:Cc, 0:nch * 260], nch, 260)
                    Gt = lambda f: r3(gates[(nm, f)][:, 0:nch * 2], nch, 2)
                    gn = lambda f: "gt_%s_%s" % (nm, f)
                    tr.op("act", [zn], [gn("beta")], lambda e: e.activation(out=Gt("beta")[0:Cc], in_=z3[:, :, 256:258], func=AF.Sigmoid))
                    for hh in range(2):
                        tr.op("act", [zn, "dtb_bc"], [gn("tmp")], lambda e: e.activation(out=Gt("tmp")[0:Cc, :, hh], in_=z3[:, :, 258 + hh], func=AF.Exp, bias=dtb_bc[0:Cc, 2 * j + hh:2 * j + hh + 1]))
                    tr.op("act", [gn("tmp")], [gn("tmp")], lambda e: e.activation(out=gates[(nm, "tmp")][0:Cc, 0:nch * 2], in_=gates[(nm, "tmp")][0:Cc, 0:nch * 2], func=AF.Ln, bias=1.0))
                    for hh in range(2):
                        tr.op("dve", [gn("tmp"), "negA"], [gn("g")], lambda e: e.tensor_scalar(out=Gt("g")[0:Cc, :, hh], in0=Gt("tmp")[0:Cc, :, hh], scalar1=negA[0:Cc, 2 * j + hh:2 * j + hh + 1], scalar2=None, op0=ALU.mult))
                    p, pn = pf()
                    n2 = nch * 2
                    gg = gates[(nm, "g")]
                    tr.op("pe", ["cst", gn("g")], [pn], lambda e: e.matmul(p[0:Cc, 0:n2], lhsT=triU[0:Cc, 0:Cc], rhs=gg[0:Cc, 0:n2], start=True, stop=True))
                    tr.op("pe", ["cst", gn("g")], [pn], lambda e: e.matmul(p[0:Cc, 64:64 + n2], lhsT=ones[0:Cc, 0:Cc], rhs=gg[0:Cc, 0:n2], start=True, stop=True))
                    tr.op("pe", ["cst", gn("g")], [pn], lambda e: e.matmul(p[:, 128:128 + n2], lhsT=ones[0:Cc, :], rhs=gg[0:Cc, 0:n2], start=True, stop=True))
                    gt = lambda f: gates[(nm, f)]
                    tr.op("dve", [pn], [gn("gc")], lambda e: e.tensor_copy(out=gt("gc")[0:Cc, 0:n2], in_=p[0:Cc, 0:n2]))
                    tr.op("act", [pn], [gn("egc")], lambda e: e.activation(out=gt("egc")[0:Cc, 0:n2], in_=p[0:Cc, 0:n2], func=AF.Exp))
                    tr.op("dve", [gn("egc"), gn("beta")], [gn("bg")], lambda e: e.tensor_tensor(out=gt("bg")[0:Cc, 0:n2], in0=gt("egc")[0:Cc, 0:n2], in1=gt("beta")[0:Cc, 0:n2], op=ALU.mult))
                    tr.op("dve", [pn, gn("gc")], [gn("gl")], lambda e: e.tensor_tensor(out=gt("gl")[0:Cc, 0:n2], in0=p[0:Cc, 64:64 + n2], in1=gt("gc")[0:Cc, 0:n2], op=ALU.subtract))
                    tr.op("act", [gn("gl")], [gn("ekd")], lambda e: e.activation(out=gt("ekd")[0:Cc, 0:n2], in_=gt("gl")[0:Cc, 0:n2], func=AF.Exp))
                    tr.op("act", [pn], [gn("egl128")], lambda e: e.activation(out=gt("egl128")[:, 0:n2], in_=p[:, 128:128 + n2], func=AF.Exp))
                chk(5)
                def lane_gen(hh, part, parity=None):
                    ln = LANES[hh]
                    seq = 0
                    for ci in range(c.NCH):
                        if parity is None or seq % 2 == parity:
                            yield from chunk(ln, part, seq, "p", 128, ci, ci * 128, j, hh, ci == 0, ci == c.NCH - 1, None)
                        seq += 1
                    for n_ in range(NS):
                        if parity is None or seq % 2 == parity:
                            yield from chunk(ln, part, seq, "s", 1, n_, T + n_, j, hh, False, False, n_)
                        seq += 1
                    if part == "C":
                        h = 2 * j + hh
                        tr.dma("sp", ln["choT"], ["oTst%d" % hh], ["oT_scr"], oT_scr[h * 128:(h + 1) * 128, :], ln["oTst"][:])

                for ln_ in LANES:
                    ln_["A_done"] = 0
                    ln_["B_done"] = 0
                    ln_["C_done"] = 0
                    ln_["Adone"] = set()
                active = [lane_gen(0, "A", 0), lane_gen(1, "A", 0), lane_gen(0, "A", 1), lane_gen(1, "A", 1),
                          lane_gen(0, "B"), lane_gen(1, "B"), lane_gen(0, "C"), lane_gen(1, "C")]
                while active:
                    for g_ in list(active):
                        try:
                            next(g_)
                        except StopIteration:
                            active.remove(g_)

            phase_end()
            chk(20)

            def outproj(srcT, sname, KS, wsrc, resid, rname, dst, dname, tag):
                phase_begin()
                CB = min(1024, D)
                NWB = 1 if CB > 512 else 2
                wo = [sb("wo%s%d" % (tag, i), [128, KS * CB], BF16) for i in range(NWB)]
                woc = [tr.chan() for i in range(NWB)]
                ot = [sb("ot%s%d" % (tag, i), [128, KS * 128], BF16) for i in range(2)]
                otc = [tr.chan() for i in range(2)]
                xr = [sb("xr%s%d" % (tag, i), [128, CB]) for i in range(2)]
                xrc = [tr.chan() for i in range(2)]
                xrs = [tr.chan() for i in range(2)]
                it = 0
                for cb in range(D // CB):
                    wb = cb % NWB
                    tr.dma("pool", woc[wb], [], ["wo%s%d" % (tag, wb)], r3(wo[wb][:], KS, CB), wsrc[:, cb * CB:(cb + 1) * CB].rearrange("(k p) n -> p k n", p=128))
                    w3_ = r3(wo[wb][:], KS, CB)
                    for (t0, n) in tok_tiles():
                        b = it % 2
                        it += 1
                        o3 = r3(ot[b][:], KS, 128)
                        tr.dma("sp", otc[b], [sname], ["ot%s%d" % (tag, b)], o3[:, :, 0:n], srcT[:, t0:t0 + n].rearrange("(k p) t -> p k t", p=128))
                        tr.dma("sp", xrc[b], [rname], ["xr%s%d" % (tag, b)], xr[b][0:n, :], resid[t0:t0 + n, cb * CB:(cb + 1) * CB])
                        for h0 in range(0, CB, 512):
                            hn = min(512, CB - h0)
                            p, pn = pf()
                            for k in range(KS):
                                tr.op("pe", ["ot%s%d" % (tag, b), "wo%s%d" % (tag, wb)], [pn], lambda e: e.matmul(p[0:n, 0:hn], lhsT=o3[:, k, 0:n], rhs=w3_[:, k, h0:h0 + hn], start=(k == 0), stop=(k == KS - 1)))
                            tr.op("dve", [pn, "xr%s%d" % (tag, b)], ["xr%s%d" % (tag, b)], lambda e: e.tensor_tensor(out=xr[b][0:n, h0:h0 + hn], in0=p[0:n, 0:hn], in1=xr[b][0:n, h0:h0 + hn], op=ALU.add))
                        tr.dma("pool", xrs[b], ["xr%s%d" % (tag, b)], [dname], dst[t0:t0 + n, cb * CB:(cb + 1) * CB], xr[b][0:n, :])
                phase_end()

            outproj(oT_scr, "oT_scr", c.VAL // 128, w_out_gdn, xin, "xin", x1_scr, "x1_scr", "a")
            chk(21)
            norm_to_hT(x1_scr, norm_ssm, "x1_scr")
            chk(22)

            phase_begin()
            GB = min(16, G)
            NT = GB // 8
            NCK = T // 8
            NC1 = 1 + NCK
            PI = math.pi

            def ew(e, out, in0, in1, op, r=(), w=()):
                tr.op(e, list(r), list(w), lambda en: en.tensor_tensor(out=out, in0=in0, in1=in1, op=op))

            tb = {k: sb("s5_" + k, [64, G]) for k in ("lamr", "lami", "dt", "ar", "ai", "fr", "fi", "t1", "t2", "t3")}
            lt = sb("s5_lt", [128, 64])
            chl = tr.chan()
            chdt = tr.chan()
            chdc = tr.chan()
            chBi = tr.chan()
            chcr = tr.chan()
            for (src_, dst_) in ((lam_re, "lamr"), (lam_im, "lami")):
                for r0 in range(0, G, 128):
                    nr = min(128, G - r0)
                    tr.dma("sp", chl, [], ["s5_lt"], lt[0:nr, :], src_[r0:r0 + nr, :])
                    p, pn = pf()
                    tr.op("pe", ["s5_lt", "cst"], [pn], lambda e: e.transpose(out=p[0:64, 0:nr], in_=lt[0:nr, :], identity=ident[0:nr, 0:nr]))
                    tr.op("dve", [pn], ["s5_" + dst_], lambda e: e.tensor_copy(out=tb[dst_][:, r0:r0 + nr], in_=p[0:64, 0:nr]))
            tr.dma("sp", chdt, [], ["s5_dt"], tb["dt"][:], log_dt[0:1, :].broadcast_to([64, G]))
            tr.op("act", ["s5_dt"], ["s5_dt"], lambda e: e.activation(out=tb["dt"][:], in_=tb["dt"][:], func=AF.Exp))
            tr.op("dve", ["s5_lamr"], ["s5_lamr"], lambda e: e.tensor_scalar(out=tb["lamr"][:], in0=tb["lamr"][:], scalar1=-1e-4, scalar2=None, op0=ALU.min))
            ew("dve", tb["t1"][:], tb["lamr"][:], tb["dt"][:], ALU.mult, ["s5_lamr", "s5_dt"], ["s5_t1"])
            tr.op("act", ["s5_t1"], ["s5_t1"], lambda e: e.activation(out=tb["t1"][:], in_=tb["t1"][:], func=AF.Exp))
            ew("dve", tb["t2"][:], tb["lami"][:], tb["dt"][:], ALU.mult, ["s5_lami", "s5_dt"], ["s5_t2"])
            tr.op("act", ["s5_t2"], ["s5_ai"], lambda e: e.activation(out=tb["ai"][:], in_=tb["t2"][:], func=AF.Sin, scale=1.0 / 32))
            tr.op("dve", ["s5_t2"], ["s5_t3"], lambda e: e.tensor_scalar(out=tb["t3"][:], in0=tb["t2"][:], scalar1=1.0 / 32, scalar2=PI / 2, op0=ALU.mult, op1=ALU.add))
            tr.op("act", ["s5_t3"], ["s5_ar"], lambda e: e.activation(out=tb["ar"][:], in_=tb["t3"][:], func=AF.Sin))
            for _ in range(5):
                ew("dve", tb["t3"][:], tb["ar"][:], tb["ar"][:], ALU.mult, ["s5_ar"], ["s5_t3"])
                ew("dve", tb["t2"][:], tb["ai"][:], tb["ai"][:], ALU.mult, ["s5_ai"], ["s5_t2"])
                tr.op("dve", ["s5_ar", "s5_ai"], ["s5_ai"], lambda e: e.scalar_tensor_tensor(out=tb["ai"][:], in0=tb["ar"][:], scalar=2.0, in1=tb["ai"][:], op0=ALU.mult, op1=ALU.mult))
                ew("dve", tb["ar"][:], tb["t3"][:], tb["t2"][:], ALU.subtract, ["s5_t3", "s5_t2"], ["s5_ar"])
            ew("dve", tb["ar"][:], tb["ar"][:], tb["t1"][:], ALU.mult, ["s5_ar", "s5_t1"], ["s5_ar"])
            ew("dve", tb["ai"][:], tb["ai"][:], tb["t1"][:], ALU.mult, ["s5_ai", "s5_t1"], ["s5_ai"])
            tr.op("dve", ["s5_ar"], ["s5_t1"], lambda e: e.tensor_scalar(out=tb["t1"][:], in0=tb["ar"][:], scalar1=-1.0, scalar2=None, op0=ALU.add))
            ew("dve", tb["t2"][:], tb["lamr"][:], tb["lamr"][:], ALU.mult, ["s5_lamr"], ["s5_t2"])
            ew("dve", tb["t3"][:], tb["lami"][:], tb["lami"][:], ALU.mult, ["s5_lami"], ["s5_t3"])
            ew("dve", tb["t2"][:], tb["t2"][:], tb["t3"][:], ALU.add, ["s5_t2", "s5_t3"], ["s5_t2"])
            tr.op("dve", ["s5_t2"], ["s5_t2"], lambda e: e.reciprocal(out=tb["t2"][:], in_=tb["t2"][:]))
            ew("dve", tb["fr"][:], tb["t1"][:], tb["lamr"][:], ALU.mult, ["s5_t1", "s5_lamr"], ["s5_fr"])
            ew("dve", tb["t3"][:], tb["ai"][:], tb["lami"][:], ALU.mult, ["s5_ai", "s5_lami"], ["s5_t3"])
            ew("dve", tb["fr"][:], tb["fr"][:], tb["t3"][:], ALU.add, ["s5_fr", "s5_t3"], ["s5_fr"])
            ew("dve", tb["fr"][:], tb["fr"][:], tb["t2"][:], ALU.mult, ["s5_fr", "s5_t2"], ["s5_fr"])
            ew("dve", tb["fi"][:], tb["ai"][:], tb["lamr"][:], ALU.mult, ["s5_ai", "s5_lamr"], ["s5_fi"])
            ew("dve", tb["t3"][:], tb["t1"][:], tb["lami"][:], ALU.mult, ["s5_t1", "s5_lami"], ["s5_t3"])
            ew("dve", tb["fi"][:], tb["fi"][:], tb["t3"][:], ALU.subtract, ["s5_fi", "s5_t3"], ["s5_fi"])
            ew("dve", tb["fi"][:], tb["fi"][:], tb["t2"][:], ALU.mult, ["s5_fi", "s5_t2"], ["s5_fi"])

            dcol = sb("s5_dcol", [128, c.KW])
            with nc.allow_non_contiguous_dma(reason="tiny per-channel vector"):
                tr.dma("sp", chdc, [], ["s5_dcol"], dcol[:], d_ssm.rearrange("(k p) o -> p (k o)", p=128))
            wuy = sb("s5_wu", [128, max(KD * GB * 16, NT * TT)], BF16)
            wu3 = r3(wuy[:, 0:KD * GB * 16], KD, GB * 16)
            chwu = tr.chan()
            uT = sb("s5_uT", [128, NT * TT], BF16)
            uT3 = r3(uT[:], NT, TT)
            yT3 = r3(wuy[:, 0:NT * TT], NT, TT)
            chy = tr.chan()
            PWR = sb("s5_pwr", [64, 9 * GB])
            PWI = sb("s5_pwi", [64, 9 * GB])
            pw_r = lambda m: PWR[:, m * GB:(m + 1) * GB]
            pw_i = lambda m: PWI[:, m * GB:(m + 1) * GB]
            GC = GB * 16
            Bt = {k: sb("s5_" + k, [64, GC]) for k in ("Br", "Bi", "Bbr", "Bbi", "CTr", "CTi", "e1", "e2")}
            chB = tr.chan()
            XR = sb("s5_XR", [64, 8 * GC], BF16)
            XI = sb("s5_XI", [64, 8 * GC], BF16)
            CPR = sb("s5_CPR", [64, 8 * GC], BF16)
            CPI = sb("s5_CPI", [64, 8 * GC], BF16)
            CTrb = sb("s5_CTrb", [64, GC], BF16)
            NCTib = sb("s5_NCTib", [64, GC], BF16)
            crow = sb("s5_crow", [128, 64])
            Kbd = sb("s5_Kbd", [128, 8 * 128], BF16)
            YP = [sb("s5_YP%d" % i, [128, 8 * 8 * 64], BF16) for i in range(2)]
            YTs = sb("s5_YTs", [128, 128], BF16)
            CPpr = sb("s5_CPpr", [64, 8 * 128], BF16)
            CPpi = sb("s5_CPpi", [64, 8 * 128], BF16)
            VB = sb("s5_VB", [64, 2 * GB * NC1])
            VB4 = VB[:].rearrange("p (a g n) -> p a g n", a=2, g=GB, n=NC1)
            XH = sb("s5_XH", [64, 2 * GB * NCK], BF16)
            XH4 = XH[:].rearrange("p (a g n) -> p a g n", a=2, g=GB, n=NCK)
            A8a = sb("s5_A8a", [64, 2 * GB])
            A8b = sb("s5_A8b", [64, 2 * GB])
            A1a = sb("s5_A1a", [64, 2 * GB])
            A1b = sb("s5_A1b", [64, 2 * GB])
            sc1 = sb("s5_sc1", [64, 2 * GB])
            sc2 = sb("s5_sc2", [64, 2 * GB])
            VS = sb("s5_VS", [64, 2 * NS * GB])
            VS4 = VS[:].rearrange("p (a n g) -> p a n g", a=2, n=NS, g=GB)
            XS = sb("s5_XS", [64, 2 * NS * GB])
            XS4 = XS[:].rearrange("p (a n g) -> p a n g", a=2, n=NS, g=GB)
            XSb = sb("s5_XSb", [64, 2 * NS * GB], BF16)
            XSb4 = XSb[:].rearrange("p (a n g) -> p a n g", a=2, n=NS, g=GB)
            stmp = sb("s5_stmp", [64, max(2 * NS * GB, 2 * GB * 32)])
            ss1 = stmp
            APW = sb("s5_APW", [64, int(math.log2(NCK)) * 4 * GB])
            srow = sb("s5_srow", [128, 64])
            chs = tr.chan()
            orow = sb("s5_orow", [128, 64])
            cho = tr.chan()
            ytmp = sb("s5_ytmp", [128, max(NCK + NS, 256)])
            YT8 = ytmp[:, 0:256].bitcast(BF16)

            def bc3(ap2, n):
                return ap2.unsqueeze(2).to_broadcast([64, GB, n])

            for blk in range(G // GB):
                g0 = blk * GB
                ch0 = g0 * 16
                tr.dma("pool", chwu, [], ["s5_wu"], wu3, w_in_ssm[:, ch0:ch0 + GC].rearrange("(k p) n -> p k n", p=128))
                for tl in range(NT):
                    for t0 in range(0, TT, 512):
                        n = min(512, TT - t0)
                        p, pn = pf()
                        for k in range(KD):
                            tr.op("pe", ["s5_wu", "hT"], [pn], lambda e: e.matmul(p[:, 0:n], lhsT=wu3[:, k, tl * 128:(tl + 1) * 128], rhs=hT3[:, k, t0:t0 + n], start=(k == 0), stop=(k == KD - 1)))
                        tr.op("act", [pn], ["s5_uT"], lambda e: e.activation(out=uT3[:, tl, t0:t0 + n], in_=p[:, 0:n], func=AF.Copy))
                tr.op("pool", [], ["s5_pwr"], lambda e: e.memset(pw_r(0), 1.0))
                tr.op("pool", [], ["s5_pwi"], lambda e: e.memset(pw_i(0), 0.0))
                arb = tb["ar"][:, g0:g0 + GB]
                aib = tb["ai"][:, g0:g0 + GB]
                e1 = Bt["e1"][:, 0:GB]
                e2 = Bt["e2"][:, 0:GB]
                for m in range(8):
                    ew("dve", e1, pw_r(m), arb, ALU.mult, ["s5_pwr", "s5_ar"], ["s5_e1"])
                    ew("dve", e2, pw_i(m), aib, ALU.mult, ["s5_pwi", "s5_ai"], ["s5_e2"])
                    ew("dve", pw_r(m + 1), e1, e2, ALU.subtract, ["s5_e1", "s5_e2"], ["s5_pwr"])
                    ew("dve", e1, pw_r(m), aib, ALU.mult, ["s5_pwr", "s5_ai"], ["s5_e1"])
                    ew("dve", e2, pw_i(m), arb, ALU.mult, ["s5_pwi", "s5_ar"], ["s5_e2"])
                    ew("dve", pw_i(m + 1), e1, e2, ALU.add, ["s5_e1", "s5_e2"], ["s5_pwi"])
                for (Aa, Ab, an, bn, m) in ((A8a, A8b, "s5_A8a", "s5_A8b", 8), (A1a, A1b, "s5_A1a", "s5_A1b", 1)):
                    tr.op("dve", ["s5_pwr"], [an], lambda e: e.tensor_copy(out=Aa[:, 0:GB], in_=pw_r(m)))
                    tr.op("dve", ["s5_pwi"], [an], lambda e: e.tensor_copy(out=Aa[:, GB:2 * GB], in_=pw_i(m)))
                    tr.op("dve", ["s5_pwi"], [bn], lambda e: e.tensor_scalar(out=Ab[:, 0:GB], in0=pw_i(m), scalar1=-1.0, scalar2=None, op0=ALU.mult))
                    tr.op("dve", ["s5_pwr"], [bn], lambda e: e.tensor_copy(out=Ab[:, GB:2 * GB], in_=pw_r(m)))
                B3 = lambda k: r3(Bt[k][:], GB, 16)
                tr.dma("sp", chB, [], ["s5_Br"], B3("Br"), b_re[g0:g0 + GB].rearrange("g p c -> p g c"))
                tr.dma("sp", chBi, [], ["s5_Bi"], B3("Bi"), b_im[g0:g0 + GB].rearrange("g p c -> p g c"))
                frb = bc3(tb["fr"][:, g0:g0 + GB], 16)
                fib = bc3(tb["fi"][:, g0:g0 + GB], 16)
                ew("dve", B3("Bbr"), B3("Br"), frb, ALU.mult, ["s5_Br", "s5_fr"], ["s5_Bbr"])
                ew("dve", B3("e1"), B3("Bi"), fib, ALU.mult, ["s5_Bi", "s5_fi"], ["s5_e1"])
                ew("dve", B3("Bbr"), B3("Bbr"), B3("e1"), ALU.subtract, ["s5_Bbr", "s5_e1"], ["s5_Bbr"])
                ew("dve", B3("Bbi"), B3("Bi"), frb, ALU.mult, ["s5_Bi", "s5_fr"], ["s5_Bbi"])
                ew("dve", B3("e1"), B3("Br"), fib, ALU.mult, ["s5_Br", "s5_fi"], ["s5_e1"])
                ew("dve", B3("Bbi"), B3("Bbi"), B3("e1"), ALU.add, ["s5_Bbi", "s5_e1"], ["s5_Bbi"])
                for tau in range(8):
                    prb = bc3(pw_r(tau), 16)
                    pib = bc3(pw_i(tau), 16)
                    xr_o = r3(XR[:, tau * GC:(tau + 1) * GC], GB, 16)
                    xi_o = r3(XI[:, tau * GC:(tau + 1) * GC], GB, 16)
                    ew("dve", B3("e1"), B3("Bbr"), prb, ALU.mult, ["s5_Bbr", "s5_pwr"], ["s5_e1"])
                    ew("pool", B3("e2"), B3("Bbi"), pib, ALU.mult, ["s5_Bbi", "s5_pwi"], ["s5_e2"])
                    ew("dve", xr_o, B3("e1"), B3("e2"), ALU.subtract, ["s5_e1", "s5_e2"], ["s5_XR"])
                    ew("dve", B3("e1"), B3("Bbi"), prb, ALU.mult, ["s5_Bbi", "s5_pwr"], ["s5_e1"])
                    ew("pool", B3("e2"), B3("Bbr"), pib, ALU.mult, ["s5_Bbr", "s5_pwi"], ["s5_e2"])
                    ew("dve", xi_o, B3("e1"), B3("e2"), ALU.add, ["s5_e1", "s5_e2"], ["s5_XI"])
                for (src_, dk) in ((c_re, "CTr"), (c_im, "CTi")):
                    rows = src_[g0:g0 + GB].rearrange("g c p -> (g c) p")
                    for r0 in range(0, GC, 128):
                        tr.dma("sp", chcr, [], ["s5_crow"], crow[:], rows[r0:r0 + 128, :])
                        p, pn = pf()
                        tr.op("pe", ["s5_crow", "cst"], [pn], lambda e: e.transpose(out=p[0:64, 0:128], in_=crow[:], identity=ident))
                        tr.op("dve", [pn], ["s5_" + dk], lambda e: e.tensor_copy(out=Bt[dk][:, r0:r0 + 128], in_=p[0:64, 0:128]))
                tr.op("dve", ["s5_CTr"], ["s5_CTrb"], lambda e: e.tensor_copy(out=CTrb[:], in_=Bt["CTr"][:]))
                tr.op("dve", ["s5_CTi"], ["s5_NCTib"], lambda e: e.tensor_scalar(out=NCTib[:], in0=Bt["CTi"][:], scalar1=-1.0, scalar2=None, op0=ALU.mult))
                for r_ in range(8):
                    prb = bc3(pw_r(r_ + 1), 16)
                    pib = bc3(pw_i(r_ + 1), 16)
                    cr_o = r3(CPR[:, r_ * GC:(r_ + 1) * GC], GB, 16)
                    ci_o = r3(CPI[:, r_ * GC:(r_ + 1) * GC], GB, 16)
                    ew("dve", B3("e1"), B3("CTr"), prb, ALU.mult, ["s5_CTr", "s5_pwr"], ["s5_e1"])
                    ew("pool", B3("e2"), B3("CTi"), pib, ALU.mult, ["s5_CTi", "s5_pwi"], ["s5_e2"])
                    ew("dve", cr_o, B3("e1"), B3("e2"), ALU.subtract, ["s5_e1", "s5_e2"], ["s5_CPR"])
                    ew("dve", B3("e1"), B3("CTr"), pib, ALU.mult, ["s5_CTr", "s5_pwi"], ["s5_e1"])
                    ew("pool", B3("e2"), B3("CTi"), prb, ALU.mult, ["s5_CTi", "s5_pwr"], ["s5_e2"])
                    ew("dve", B3("e1"), B3("e1"), B3("e2"), ALU.add, ["s5_e1", "s5_e2"], ["s5_e1"])
                    tr.op("dve", ["s5_e1"], ["s5_CPI"], lambda e: e.tensor_scalar(out=ci_o, in0=B3("e1"), scalar1=-1.0, scalar2=None, op0=ALU.mult))
                tr.op("pool", [], ["s5_VB"], lambda e: e.memset(VB[:], 0.0))
                for tl in range(NT):
                    for part, Xs, xn in ((0, XR, "s5_XR"), (1, XI, "s5_XI")):
                        Y4 = YP[part][:].rearrange("p (t g s) -> p t g s", t=8, g=8, s=64)
                        ypn = "s5_YP%d" % part
                        p, pn = pb()
                        for tau in range(8):
                            tr.op("pe", [xn, "cstb"], [pn], lambda e: e.transpose(out=p[:, tau * 64:(tau + 1) * 64], in_=Xs[:, tau * GC + tl * 128: tau * GC + (tl + 1) * 128], identity=identb[0:64, 0:64]))
                        tr.op("act", [pn], ["s5_ytmp"], lambda e: e.activation(out=YT8, in_=p[:, 0:512], func=AF.Copy))
                        tr.op("dve", ["s5_ytmp", "cst"], [ypn], lambda e: e.tensor_tensor(out=Y4, in0=YT8.rearrange("p (t s) -> p t s", t=8, s=64).unsqueeze(2).to_broadcast([128, 8, 8, 64]), in1=GSEL.unsqueeze(1).unsqueeze(3).to_broadcast([128, 8, 8, 64]), op=ALU.mult))
                        for g in range(8):
                            p, pn = pf()
                            for tau in range(8):
                                tr.op("pe", [ypn, "s5_uT"], [pn], lambda e: e.matmul(p[0:64, 0:NCK], lhsT=Y4[:, tau, g, :], rhs=uT3[:, tl, (7 - tau):T:8], start=(tau == 0), stop=(tau == 7)))
                            tr.op("pe", [ypn, "s5_uT"], [pn], lambda e: e.matmul(p[0:64, NCK:NCK + NS], lhsT=Y4[:, 0, g, :], rhs=uT3[:, tl, T:TT], start=True, stop=True))
                            tr.op("act", [pn], ["s5_VB"], lambda e: e.activation(out=VB4[:, part, tl * 8 + g, 1:1 + NCK], in_=p[0:64, 0:NCK], func=AF.Copy))
                            tr.op("dve", [pn], ["s5_VS"], lambda e: e.tensor_copy(out=VS4[:, part, :, tl * 8 + g], in_=p[0:64, NCK:NCK + NS]))
                chk(30)
                A8a3 = A8a[:].rearrange("p (a g) -> p a g", a=2, g=GB)
                A8b3 = A8b[:].rearrange("p (a g) -> p a g", a=2, g=GB)
                LV = int(math.log2(NCK))
                assert (1 << LV) == NCK
                PWT = 32
                APW4 = APW[:].rearrange("p (l q g) -> p l q g", l=LV, q=4, g=GB)
                tr.op("dve", ["s5_A8a"], ["s5_APW"], lambda e: e.tensor_copy(out=APW4[:, 0, 0:2, :], in_=A8a3))
                tr.op("dve", ["s5_A8b"], ["s5_APW"], lambda e: e.tensor_copy(out=APW4[:, 0, 2:4, :], in_=A8b3))
                for l in range(1, LV):
                    pr_, pi_ = APW4[:, l - 1, 0, :], APW4[:, l - 1, 1, :]
                    ew("dve", sc1[:, 0:GB], pr_, pr_, ALU.mult, ["s5_APW"], ["s5_sc1"])
                    ew("dve", sc1[:, GB:2 * GB], pi_, pi_, ALU.mult, ["s5_APW"], ["s5_sc1"])
                    ew("dve", APW4[:, l, 0, :], sc1[:, 0:GB], sc1[:, GB:2 * GB], ALU.subtract, ["s5_sc1"], ["s5_APW"])
                    tr.op("dve", ["s5_APW"], ["s5_APW"], lambda e: e.scalar_tensor_tensor(out=APW4[:, l, 1, :], in0=pr_, scalar=2.0, in1=pi_, op0=ALU.mult, op1=ALU.mult))
                    tr.op("dve", ["s5_APW"], ["s5_APW"], lambda e: e.tensor_copy(out=APW4[:, l, 3, :], in_=APW4[:, l, 0, :]))
                    tr.op("dve", ["s5_APW"], ["s5_APW"], lambda e: e.tensor_scalar(out=APW4[:, l, 2, :], in0=APW4[:, l, 1, :], scalar1=-1.0, scalar2=None, op0=ALU.mult))

                def cacc(l, tgt_sl, src_sl, cnt):
                    for q0 in range(0, cnt, PWT):
                        qn = min(PWT, cnt - q0)
                        t_lo, t_st = tgt_sl
                        s_lo, s_st = src_sl
                        tg = VB4[:, :, :, t_lo + q0 * t_st: t_lo + (q0 + qn - 1) * t_st + 1: t_st]
                        sr = VB4[:, 0:1, :, s_lo + q0 * s_st: s_lo + (q0 + qn - 1) * s_st + 1: s_st].to_broadcast([64, 2, GB, qn])
                        si = VB4[:, 1:2, :, s_lo + q0 * s_st: s_lo + (q0 + qn - 1) * s_st + 1: s_st].to_broadcast([64, 2, GB, qn])
                        pa = APW4[:, l, 0:2, :].unsqueeze(3).to_broadcast([64, 2, GB, qn])
                        pb_ = APW4[:, l, 2:4, :].unsqueeze(3).to_broadcast([64, 2, GB, qn])
                        tm = stmp[:, 0:2 * GB * qn].rearrange("p (a g n) -> p a g n", a=2, g=GB, n=qn)
                        ew("dve", tm, pa, sr, ALU.mult, ["s5_APW", "s5_VB"], ["s5_stmp"])
                        ew("dve", tg, tg, tm, ALU.add, ["s5_VB", "s5_stmp"], ["s5_VB"])
                        ew("dve", tm, pb_, si, ALU.mult, ["s5_APW", "s5_VB"], ["s5_stmp"])
                        ew("dve", tg, tg, tm, ALU.add, ["s5_VB", "s5_stmp"], ["s5_VB"])

                for l in range(LV):
                    s_ = 1 << l
                    cacc(l, (2 * s_, 2 * s_), (s_, 2 * s_), NCK // (2 * s_))
                for l in range(LV - 2, -1, -1):
                    s_ = 1 << l
                    cacc(l, (3 * s_, 2 * s_), (2 * s_, 2 * s_), NCK // (2 * s_) - 1)
                tr.op("act", ["s5_VB"], ["s5_XH"], lambda e: e.activation(out=XH4, in_=VB4[:, :, :, 0:NCK], func=AF.Copy))
                for part, dst_, dn_ in ((0, re_p, "re_p"), (1, im_p, "im_p")):
                    tr.op("dve", ["s5_VB"], ["s5_sc1"], lambda e: e.tensor_copy(out=sc1[:, 0:GB], in_=VB4[:, part, :, NCK]))
                    p, pn = pf()
                    tr.op("pe", ["s5_sc1", "cst"], [pn], lambda e: e.transpose(out=p[0:GB, 0:64], in_=sc1[:, 0:GB], identity=ident[0:64, 0:64]))
                    tr.op("act", [pn], ["s5_orow"], lambda e: e.activation(out=orow[0:GB, :], in_=p[0:GB, 0:64], func=AF.Copy))
                    tr.dma("sp", cho, ["s5_orow"], [dn_], dst_[g0:g0 + GB, :], orow[0:GB, :])
                RW = min(128, NS * GB)
                for part, src_ in ((0, re_s), (1, im_s)):
                    for r0 in range(0, NS * GB, RW):
                        n0 = r0 // GB
                        nn = RW // GB
                        tr.dma("sp", chs, [], ["s5_srow"], srow[0:RW, :], src_[n0:n0 + nn, g0:g0 + GB, :])
                        p, pn = pf()
                        tr.op("pe", ["s5_srow", "cst"], [pn], lambda e: e.transpose(out=p[0:64, 0:RW], in_=srow[0:RW, :], identity=ident[0:RW, 0:RW]))
                        tr.op("dve", [pn], ["s5_XS"], lambda e: e.tensor_copy(out=XS[:, part * NS * GB + r0: part * NS * GB + r0 + RW], in_=p[0:64, 0:RW]))
                tr.op("act", ["s5_XS"], ["s5_XSb"], lambda e: e.activation(out=XSb[:], in_=XS[:], func=AF.Copy))
                A1a4 = A1a[:].rearrange("p (a g) -> p a g", a=2, g=GB).unsqueeze(2).to_broadcast([64, 2, NS, GB])
                A1b4 = A1b[:].rearrange("p (a g) -> p a g", a=2, g=GB).unsqueeze(2).to_broadcast([64, 2, NS, GB])
                ss4 = stmp[:, 0:2 * NS * GB].rearrange("p (a n g) -> p a n g", a=2, n=NS, g=GB)
                ew("dve", ss4, A1a4, XS4[:, 0:1].to_broadcast([64, 2, NS, GB]), ALU.mult, ["s5_A1a", "s5_XS"], ["s5_stmp"])
                ew("dve", VS4, VS4, ss4, ALU.add, ["s5_VS", "s5_stmp"], ["s5_VS"])
                ew("dve", ss4, A1b4, XS4[:, 1:2].to_broadcast([64, 2, NS, GB]), ALU.mult, ["s5_A1b", "s5_XS"], ["s5_stmp"])
                ew("dve", VS4, VS4, ss4, ALU.add, ["s5_VS", "s5_stmp"], ["s5_VS"])
                for part, dst_, dn_ in ((0, re_so, "re_so"), (1, im_so, "im_so")):
                    for r0 in range(0, NS * GB, RW):
                        n0 = r0 // GB
                        nn = RW // GB
                        p, pn = pf()
                        tr.op("pe", ["s5_VS", "cst"], [pn], lambda e: e.transpose(out=p[0:RW, 0:64], in_=VS[:, part * NS * GB + r0: part * NS * GB + r0 + RW], identity=ident[0:64, 0:64]))
                        tr.op("act", [pn], ["s5_orow"], lambda e: e.activation(out=orow[0:RW, :], in_=p[0:RW, 0:64], func=AF.Copy))
                        tr.dma("sp", cho, ["s5_orow"], [dn_], dst_[n0:n0 + nn, g0:g0 + GB, :], orow[0:RW, :])
                chk(31)
                for tl in range(NT):
                    for t4 in range(0, 8, 4):
                        p, pn = pf()
                        for tau in range(t4, t4 + 4):
                            o_ = p[:, (tau - t4) * 128:(tau - t4 + 1) * 128]
                            tr.op("pe", ["s5_XR", "s5_CTrb"], [pn], lambda e: e.matmul(o_, lhsT=XR[:, tau * GC + tl * 128: tau * GC + (tl + 1) * 128], rhs=CTrb[:, tl * 128:(tl + 1) * 128], start=True, stop=False))
                            tr.op("pe", ["s5_XI", "s5_NCTib"], [pn], lambda e: e.matmul(o_, lhsT=XI[:, tau * GC + tl * 128: tau * GC + (tl + 1) * 128], rhs=NCTib[:, tl * 128:(tl + 1) * 128], start=False, stop=True))
                        tr.op("dve", [pn, "cst"], ["s5_Kbd"], lambda e: e.tensor_tensor(out=r3(Kbd[:, t4 * 128:(t4 + 4) * 128], 4, 128), in0=r3(p[:, 0:512], 4, 128), in1=BD.unsqueeze(1).to_broadcast([128, 4, 128]), op=ALU.mult))
                    for r_ in range(8):
                        for CPs, CPp, cn in ((CPR, CPpr, "s5_CPpr"), (CPI, CPpi, "s5_CPpi")):
                            src4 = CPs[:, r_ * GC + tl * 128: r_ * GC + (tl + 1) * 128].rearrange("p (g c) -> p g c", g=8, c=16).unsqueeze(2).to_broadcast([64, 8, 8, 16])
                            gg4 = GG[0:64, :].rearrange("p (g h) -> p g h", g=8, h=8).unsqueeze(3).to_broadcast([64, 8, 8, 16])
                            tr.op("pool" if cn == "s5_CPpr" else "dve", ["s5_CPR", "s5_CPI", "cst"], [cn], lambda e: e.tensor_tensor(out=CPp[:].rearrange("p (g h c) -> p g h c", g=8, h=8, c=16), in0=src4, in1=gg4, op=ALU.mult))
                        p, pn = pf()
                        mms = [(Kbd[:, tau * 128:(tau + 1) * 128], uT3[:, tl, (r_ - tau):T:8], ["s5_Kbd", "s5_uT"]) for tau in range(r_ + 1)]
                        for g in range(8):
                            mms.append((CPpr[:, g * 128:(g + 1) * 128], XH4[:, 0, tl * 8 + g, :], ["s5_CPpr", "s5_XH"]))
                            mms.append((CPpi[:, g * 128:(g + 1) * 128], XH4[:, 1, tl * 8 + g, :], ["s5_CPpi", "s5_XH"]))
                        for i_, (l_, rh_, rd_) in enumerate(mms):
                            tr.op("pe", rd_, [pn], lambda e: e.matmul(p[:, 0:NCK], lhsT=l_, rhs=rh_, start=(i_ == 0), stop=(i_ == len(mms) - 1)))
                        dsc = dcol[:, blk * NT + tl: blk * NT + tl + 1]
                        tr.op("dve", [pn, "s5_uT", "s5_dcol"], ["s5_ytmp"], lambda e: e.scalar_tensor_tensor(out=ytmp[:, 0:NCK], in0=uT3[:, tl, r_:T:8], scalar=dsc, in1=p[:, 0:NCK], op0=ALU.mult, op1=ALU.add))
                        tr.op("act", ["s5_ytmp"], ["s5_wu"], lambda e: e.activation(out=yT3[:, tl, r_:T:8], in_=ytmp[:, 0:NCK], func=AF.Gelu))
                        if r_ == 0:
                            mms = [(Kbd[:, 0:128], uT3[:, tl, T:TT], ["s5_Kbd", "s5_uT"])]
                            for g in range(8):
                                mms.append((CPpr[:, g * 128:(g + 1) * 128], XSb4[:, 0, :, tl * 8 + g], ["s5_CPpr", "s5_XSb"]))
                                mms.append((CPpi[:, g * 128:(g + 1) * 128], XSb4[:, 1, :, tl * 8 + g], ["s5_CPpi", "s5_XSb"]))
                            p2, p2n = pf()
                            for i_, (l_, rh_, rd_) in enumerate(mms):
                                tr.op("pe", rd_, [p2n], lambda e: e.matmul(p2[:, 0:NS], lhsT=l_, rhs=rh_, start=(i_ == 0), stop=(i_ == len(mms) - 1)))
                            tr.op("dve", [p2n, "s5_uT", "s5_dcol"], ["s5_ytmp"], lambda e: e.scalar_tensor_tensor(out=ytmp[:, NCK:NCK + NS], in0=uT3[:, tl, T:TT], scalar=dsc, in1=p2[:, 0:NS], op0=ALU.mult, op1=ALU.add))
                            tr.op("act", ["s5_ytmp"], ["s5_wu"], lambda e: e.activation(out=yT3[:, tl, T:TT], in_=ytmp[:, NCK:NCK + NS], func=AF.Gelu))
                    tr.dma("sp", chy, ["s5_wu"], ["yT_scr"], yT_scr[ch0 + tl * 128: ch0 + (tl + 1) * 128, :], yT3[:, tl, :])
                chk(32)
            phase_end()
            chk(33)

            phase_begin()
            KW = c.KW
            TBM = min(TT, 1040)
            yTa = sb("g_yTa", [128, KW * TBM], BF16)
            yTa3 = r3(yTa[:], KW, TBM)
            chya = tr.chan()
            wg = [sb("g_wg%d" % i, [128, KW * 128], BF16) for i in range(2)]
            wgc = [tr.chan() for i in range(2)]
            wz = [sb("g_wz%d" % i, [128, KD * 128], BF16) for i in range(2)]
            wzc = [tr.chan() for i in range(2)]
            bgl = sb("g_bgl", [128, KW])
            chbg = tr.chan()
            with nc.allow_non_contiguous_dma(reason="tiny per-channel vector"):
                tr.dma("sp", chbg, [], ["g_bgl"], bgl[:], b_glu.rearrange("(k p) o -> p (k o)", p=128))
            sgt = sb("g_sg", [128, 512])
            szt = sb("g_sz", [128, 512])
            y2t = [sb("g_y2%d" % i, [128, TBM], BF16) for i in range(2)]
            y2c = [tr.chan() for i in range(2)]
            it = 0
            for b0 in range(0, TT, TBM):
                bn = min(TBM, TT - b0)
                tr.dma("sp", chya, ["yT_scr"], ["g_yTa"], yTa3[:, :, 0:bn], yT_scr[:, b0:b0 + bn].rearrange("(k p) t -> p k t", p=128))
                for m in range(KW):
                    wb = it % 2
                    it += 1
                    wg3 = r3(wg[wb][:], KW, 128)
                    wz3 = r3(wz[wb][:], KD, 128)
                    tr.dma("pool", wgc[wb], [], ["g_wg%d" % wb], wg3, w_glu[:, m * 128:(m + 1) * 128].rearrange("(k p) n -> p k n", p=128))
                    tr.dma("pool", wzc[wb], [], ["g_wz%d" % wb], wz3, w_in_ssm[:, c.W + m * 128: c.W + (m + 1) * 128].rearrange("(k p) n -> p k n", p=128))
                    for t0 in range(0, bn, 512):
                        n = min(512, bn - t0)
                        pg, pgn = pf()
                        for k in range(KW):
                            tr.op("pe", ["g_wg%d" % wb, "g_yTa"], [pgn], lambda e: e.matmul(pg[:, 0:n], lhsT=wg3[:, k, :], rhs=yTa3[:, k, t0:t0 + n], start=(k == 0), stop=(k == KW - 1)))
                        pz, pzn = pf()
                        for k in range(KD):
                            tr.op("pe", ["g_wz%d" % wb, "hT"], [pzn], lambda e: e.matmul(pz[:, 0:n], lhsT=wz3[:, k, :], rhs=hT3[:, k, b0 + t0:b0 + t0 + n], start=(k == 0), stop=(k == KD - 1)))
                        tr.op("act", [pgn, "g_bgl"], ["g_sg"], lambda e: e.activation(out=sgt[:, 0:n], in_=pg[:, 0:n], func=AF.Sigmoid, bias=bgl[:, m:m + 1]))
                        tr.op("act", [pzn], ["g_sz"], lambda e: e.activation(out=szt[:, 0:n], in_=pz[:, 0:n], func=AF.Silu))
                        tr.op("pool", ["g_sg", "g_sz"], ["g_sg"], lambda e: e.tensor_tensor(out=sgt[:, 0:n], in0=sgt[:, 0:n], in1=szt[:, 0:n], op=ALU.mult))
                        tr.op("dve", ["g_sg", "g_yTa"], ["g_y2%d" % wb], lambda e: e.tensor_tensor(out=y2t[wb][:, t0:t0 + n], in0=sgt[:, 0:n], in1=yTa3[:, m, t0:t0 + n], op=ALU.mult))
                    tr.dma("sp", y2c[wb], ["g_y2%d" % wb], ["y2_scr"], y2_scr[m * 128:(m + 1) * 128, b0:b0 + bn], y2t[wb][:, 0:bn])
            phase_end()
            chk(34)
            outproj(y2_scr, "y2_scr", c.KW, w_out_ssm, x1_scr, "x1_scr", x2_scr, "x2_scr", "b")
            chk(35)
            norm_to_hT(x2_scr, norm_final, "x2_scr", final_out=y_out)
        except _Stop:
            if cur[0] is not es:
                cur[0].close()
                cur[0] = es
        tr.finish()
    return nc


def make_consts():
    cs = np.zeros((128, 8 * 128), np.float32)
    i = np.arange(128)
    cs[:, 0:128] = np.eye(128)
    cs[:, 128:256] = (i[:, None] <= i[None, :])
    cs[:, 256:384] = 1.0
    cs[:, 384:512] = np.where(i[None, :] < i[:, None], 0.0, 30000.0)
    cs[:, 512:640] = np.where(i[:, None] <= i[None, :], 0.0, -30000.0)
    cs[:, 640:768] = ((i[None, :] // 16) >= (i[:, None] // 16))
    cs[:, 768:896] = ((i[None, :] // 16) == (i[:, None] // 16))
    cs[:, 896:904] = ((i[:, None] // 16) == np.arange(8)[None, :])
    cs[:, 904:968] = np.eye(8).reshape(1, 64)
    return cs


_NC_CACHE = {}


def kernel(x_prompt, x_sample, state_gdn_conv, state_gdn_delta, state_ssm_re, state_ssm_im,
           norm_gdn, w_in_gdn, conv_gdn, a_log_gdn, dt_bias_gdn, onorm_gdn, w_out_gdn,
           norm_ssm, w_in_ssm, lam_re, lam_im, b_re, b_im, c_re, c_im, d_ssm, log_dt_ssm,
           w_glu_ssm, b_glu_ssm, w_out_ssm, norm_final):
    cfg = Cfg(**FULL)
    f = lambda a: np.ascontiguousarray(np.asarray(a, dtype=np.float32))
    NS, T = cfg.NS, cfg.T
    B = x_prompt.shape[0]
    ncores = 8
    if "nc" not in _NC_CACHE:
        _NC_CACHE["nc"] = build(cfg)
    nc = _NC_CACHE["nc"]
    shared = {
        "norm_gdn": f(norm_gdn).reshape(1, -1), "w_in_gdn": f(w_in_gdn[0]), "conv_w": f(conv_gdn[0]),
        "a_log": f(a_log_gdn).reshape(1, -1), "dt_bias": f(dt_bias_gdn).reshape(1, -1), "onorm": f(onorm_gdn).reshape(1, -1),
        "w_out_gdn": f(w_out_gdn[0]), "norm_ssm": f(norm_ssm).reshape(1, -1), "w_in_ssm": f(w_in_ssm[0]),
        "lam_re": f(lam_re[0]), "lam_im": f(lam_im[0]), "b_re": f(b_re[0]), "b_im": f(b_im[0]), "c_re": f(c_re[0]), "c_im": f(c_im[0]),
        "d_ssm": f(d_ssm[0]).reshape(-1, 1), "log_dt": f(log_dt_ssm).reshape(1, -1), "w_glu": f(w_glu_ssm[0]),
        "b_glu": f(b_glu_ssm[0]).reshape(-1, 1), "w_out_ssm": f(w_out_ssm[0]), "norm_final": f(norm_final).reshape(1, -1),
        "consts": make_consts(),
    }
    in_maps = []
    for i in range(ncores):
        sq = i % B
        sl = slice(i * NS, (i + 1) * NS)
        m = dict(shared)
        m["xin"] = np.ascontiguousarray(np.concatenate([f(x_prompt[sq]), f(x_sample[sl, 0])], axis=0))
        m["conv_s"] = f(state_gdn_conv[0, sl])
        m["delta_s"] = f(state_gdn_delta[0, sl])
        m["re_s"] = f(state_ssm_re[0, sl])
        m["im_s"] = f(state_ssm_im[0, sl])
        in_maps.append(m)
    res = run_bass_kernel_spmd(nc, in_maps, core_ids=list(range(ncores))).results
    g = lambda i, k: np.asarray(res[i][k], dtype=np.float32)
    y_prompt = np.stack([g(i, "y_out")[:T] for i in range(B)])
    y_sample = np.concatenate([g(i, "y_out")[T:] for i in range(ncores)])[:, None, :]
    conv_prompt = np.stack([g(i, "conv_p") for i in range(B)])[None]
    delta_prompt = np.stack([g(i, "delta_p") for i in range(B)])[None]
    re_prompt = np.stack([g(i, "re_p") for i in range(B)])[None]
    im_prompt = np.stack([g(i, "im_p") for i in range(B)])[None]
    conv_sample = np.concatenate([g(i, "conv_so") for i in range(ncores)])[None]
    delta_sample = np.concatenate([g(i, "delta_so") for i in range(ncores)])[None]
    re_sample = np.concatenate([g(i, "re_so") for i in range(ncores)])[None]
    im_sample = np.concatenate([g(i, "im_so") for i in range(ncores)])[None]
    return (y_prompt, y_sample, conv_prompt, delta_prompt, re_prompt, im_prompt,
            conv_sample, delta_sample, re_sample, im_sample)
```
